# Optimizing a Trainium2 kernel written in Bass

```python
import math
import jax, jax.numpy as jnp
from jax import lax
import numpy as np

D_MODEL = 1024
BATCH = 2
SEQ = 8192
DEPTH = 4

MIX_WIDTH = D_MODEL
DA_WIDTH = MIX_WIDTH // 2
DA_HEADS = 4
DA_DV = DA_WIDTH // DA_HEADS
DA_DK = DA_DV // 2
DA_QBLOCK = 128
ML_WIDTH = MIX_WIDTH - DA_WIDTH
ML_HEADS = 4
ML_DH = ML_WIDTH // ML_HEADS
ML_CHUNK = 64
CONV_K = 4
MEM_LEN = 256
MEM_HEADS = 4
MEM_DH = D_MODEL // MEM_HEADS
PEER_HEADS = 8
PEER_NKEYS = 128
PEER_N = PEER_NKEYS * PEER_NKEYS
PEER_DK = 256
PEER_TOPK = 16
PEER_TOKBLOCK = 128
ALPHA = (2.0 * DEPTH) ** 0.25
BETA = (8.0 * DEPTH) ** -0.25
LN_EPS = 1e-5
IN_SIZES = (DA_HEADS * 2 * DA_DK, DA_HEADS * 2 * DA_DK, DA_HEADS * DA_DV,
            ML_WIDTH, ML_WIDTH, ML_WIDTH, ML_WIDTH, ML_HEADS, ML_HEADS)
P_IN = sum(IN_SIZES)

kernel_name = 'hymba_diffattn_mlstm_peer_deepnorm'


def layer_norm(x, g, b):
    xf = x.astype(jnp.float32)
    mu = jnp.mean(xf, axis=-1, keepdims=True)
    var = jnp.mean(jnp.square(xf - mu), axis=-1, keepdims=True)
    return ((xf - mu) * lax.rsqrt(var + LN_EPS) * g + b).astype(x.dtype)


def causal_dwconv(x, w, b):
    y = lax.conv_general_dilated(x, w[:, None, :], window_strides=(1,),
                                 padding=[(CONV_K - 1, 0)],
                                 dimension_numbers=('NWC', 'WIO', 'NWC'),
                                 feature_group_count=x.shape[-1])
    return y + b


def diff_attention(q, k, v, lam, lam_init, norm_g):
    B, S, H, _, dk = q.shape
    nq = S // DA_QBLOCK
    q = q * (dk ** -0.5)
    qb = q.reshape(B, nq, DA_QBLOCK, H, 2, dk).transpose(1, 0, 3, 4, 2, 5)
    kt = k.transpose(0, 2, 3, 1, 4)
    vt = v.transpose(0, 2, 1, 3)
    kpos = jnp.arange(S)

    def block(args):
        qblk, start = args
        s = jnp.einsum('bhcqd,bhckd->bhcqk', qblk, kt).astype(jnp.float32)
        qpos = start + jnp.arange(DA_QBLOCK)
        mask = kpos[None, :] <= qpos[:, None]
        p = jax.nn.softmax(jnp.where(mask, s, -jnp.inf), axis=-1)
        a = p[:, :, 0] - lam * p[:, :, 1]
        return jnp.einsum('bhqk,bhkv->bhqv', a.astype(vt.dtype), vt)

    o = lax.map(block, (qb, jnp.arange(nq) * DA_QBLOCK))
    o = o.transpose(1, 0, 3, 2, 4).reshape(B, S, H, -1).astype(jnp.float32)
    o = o * lax.rsqrt(jnp.mean(jnp.square(o), axis=-1, keepdims=True) + LN_EPS)
    o = o * norm_g * (1.0 - lam_init)
    return o.reshape(B, S, H * o.shape[-1])


def mlstm_chunkwise(q, k, v, i_pre, f_pre):
    B, S, H, dh = q.shape
    L = ML_CHUNK
    nc = S // L
    f32 = jnp.float32

    def to_chunks(t):
        t = t.astype(f32).reshape((B, nc, L, H) + t.shape[3:])
        return jnp.moveaxis(t, (1, 3), (0, 2))

    qc = to_chunks(q)
    kc = to_chunks(k) * (dh ** -0.5)
    vc = to_chunks(v)
    ic = to_chunks(i_pre)
    bc = jnp.cumsum(jax.nn.log_sigmoid(to_chunks(f_pre)), axis=-1)
    causal = jnp.tril(jnp.ones((L, L), dtype=bool))

    def step(carry, xs):
        C, n, m = carry
        qt, kt, vt, it, bt = xs
        a = bt + m[..., None]
        d = jnp.where(causal, bt[..., :, None] - bt[..., None, :] + it[..., None, :], -jnp.inf)
        m_t = jnp.maximum(a, jnp.max(d, axis=-1))
        w_inter = jnp.exp(a - m_t)
        w_intra = jnp.exp(d - m_t[..., None]) * jnp.einsum('bhtk,bhsk->bhts', qt, kt)
        num = (w_inter[..., None] * jnp.einsum('bhtk,bhkv->bhtv', qt, C)
               + jnp.einsum('bhts,bhsv->bhtv', w_intra, vt))
        nq = w_inter * jnp.einsum('bhtk,bhk->bht', qt, n) + jnp.sum(w_intra, axis=-1)
        h = num / jnp.maximum(jnp.abs(nq), jnp.exp(-m_t))[..., None]
        b_last = bt[..., -1]
        g = b_last[..., None] - bt + it
        m_new = jnp.maximum(b_last + m, jnp.max(g, axis=-1))
        decay = jnp.exp(b_last + m - m_new)
        wk = jnp.exp(g - m_new[..., None])[..., None] * kt
        C_new = decay[..., None, None] * C + jnp.einsum('bhsk,bhsv->bhkv', wk, vt)
        n_new = decay[..., None] * n + jnp.sum(wk, axis=2)
        return (C_new, n_new, m_new), h

    init = (jnp.zeros((B, H, dh, dh), f32), jnp.zeros((B, H, dh), f32), jnp.zeros((B, H), f32))
    _, hs = lax.scan(step, init, (qc, kc, vc, ic, bc))
    return jnp.moveaxis(hs, (0, 2), (1, 3)).reshape(B, S, H, dh)


def hybrid_mixer(x, w_in, i_bias, f_bias, conv_w, conv_b, lam_qk, lam_init, da_g, ml_g, w_out):
    B, S, _ = x.shape
    h = x @ w_in
    offs = np.cumsum(IN_SIZES)[:-1].tolist()
    da_q, da_k, da_v, ml_q, ml_k, ml_v, ml_o, ml_i, ml_f = jnp.split(h, offs, axis=-1)
    lq = lam_qk.astype(jnp.float32)
    lam = jnp.exp(jnp.sum(lq[0] * lq[1])) - jnp.exp(jnp.sum(lq[2] * lq[3])) + lam_init
    da = diff_attention(da_q.reshape(B, S, DA_HEADS, 2, DA_DK),
                        da_k.reshape(B, S, DA_HEADS, 2, DA_DK),
                        da_v.reshape(B, S, DA_HEADS, DA_DV), lam, lam_init, da_g)
    qk = jax.nn.silu(causal_dwconv(jnp.concatenate([ml_q, ml_k], axis=-1), conv_w, conv_b))
    ml_q, ml_k = jnp.split(qk, 2, axis=-1)
    hm = mlstm_chunkwise(ml_q.reshape(B, S, ML_HEADS, ML_DH), ml_k.reshape(B, S, ML_HEADS, ML_DH),
                         ml_v.reshape(B, S, ML_HEADS, ML_DH), ml_i + i_bias, ml_f + f_bias)
    mu = jnp.mean(hm, axis=-1, keepdims=True)
    var = jnp.mean(jnp.square(hm - mu), axis=-1, keepdims=True)
    hm = ((hm - mu) * lax.rsqrt(var + LN_EPS)).reshape(B, S, ML_WIDTH) * ml_g
    hm = hm * jax.nn.sigmoid(ml_o.astype(jnp.float32))
    mixed = jnp.concatenate([da, hm], axis=-1).astype(x.dtype)
    return mixed @ w_out


def memory_attention(x, mem, wq, wkv, wo):
    B, S, D = x.shape
    q = (x @ wq).reshape(B, S, MEM_HEADS, MEM_DH) * (MEM_DH ** -0.5)
    k, v = jnp.split(mem @ wkv, 2, axis=-1)
    k = k.reshape(B, -1, MEM_HEADS, MEM_DH)
    v = v.reshape(B, -1, MEM_HEADS, MEM_DH)
    s = jnp.einsum('bshd,bmhd->bhsm', q, k).astype(jnp.float32)
    p = jax.nn.softmax(s, axis=-1).astype(x.dtype)
    o = jnp.einsum('bhsm,bmhd->bshd', p, v).reshape(B, S, D)
    return o @ wo


def peer_ffn(x, w_pq, sub_keys, u_tab, v_tab):
    B, S, D = x.shape
    half = PEER_DK // 2
    xt = x.reshape(-1, PEER_TOKBLOCK, D)

    def block(xc):
        T = xc.shape[0]
        q = (xc @ w_pq).reshape(T, PEER_HEADS, 2, half)
        s = jnp.einsum('thcd,cnd->thcn', q, sub_keys).astype(jnp.float32)
        sc, idx = lax.top_k(s, PEER_TOPK)
        cand = sc[:, :, 0, :, None] + sc[:, :, 1, None, :]
        cs, ci = lax.top_k(cand.reshape(T, PEER_HEADS, PEER_TOPK * PEER_TOPK), PEER_TOPK)
        e = (jnp.take_along_axis(idx[:, :, 0], ci // PEER_TOPK, axis=-1) * PEER_NKEYS
             + jnp.take_along_axis(idx[:, :, 1], ci % PEER_TOPK, axis=-1))
        g = jax.nn.softmax(cs, axis=-1)
        act = jax.nn.gelu(jnp.einsum('td,thkd->thk', xc, u_tab[e]).astype(jnp.float32),
                          approximate=False)
        return jnp.einsum('thk,thkd->td', (g * act).astype(xc.dtype), v_tab[e])

    return lax.map(block, xt).reshape(B, S, D)


def setup_inputs(seed: int = 0) -> dict:
    key = jax.random.key(seed)
    ks = jax.random.split(key, 24)
    f32 = jnp.float32

    def nrm(k, shape, scale):
        return jax.random.normal(k, shape, f32) * scale

    def gain(k, shape):
        return 1.0 + nrm(k, shape, 0.02)

    return {
        'x': nrm(ks[0], (BATCH, SEQ, D_MODEL), 1.0),
        'mem': nrm(ks[1], (BATCH, MEM_LEN, D_MODEL), 1.0),
        'w_in': nrm(ks[2], (DEPTH, D_MODEL, P_IN), D_MODEL ** -0.5),
        'i_bias': nrm(ks[3], (DEPTH, ML_HEADS), 0.1),
        'f_bias': jnp.broadcast_to(jnp.linspace(3.0, 6.0, ML_HEADS, dtype=f32), (DEPTH, ML_HEADS))
                  + nrm(ks[4], (DEPTH, ML_HEADS), 0.01),
        'conv_w': nrm(ks[5], (DEPTH, CONV_K, 2 * ML_WIDTH), CONV_K ** -0.5),
        'conv_b': nrm(ks[6], (DEPTH, 2 * ML_WIDTH), 0.01),
        'lam_qk': nrm(ks[7], (DEPTH, 4, DA_DK), 0.1),
        'da_norm_g': gain(ks[8], (DEPTH, DA_DV)),
        'ml_norm_g': gain(ks[9], (DEPTH, ML_WIDTH)),
        'w_out': nrm(ks[10], (DEPTH, MIX_WIDTH, D_MODEL), MIX_WIDTH ** -0.5 * BETA),
        'ln1_g': gain(ks[11], (DEPTH, D_MODEL)),
        'ln1_b': nrm(ks[12], (DEPTH, D_MODEL), 0.01),
        'wq_mem': nrm(ks[13], (DEPTH, D_MODEL, D_MODEL), D_MODEL ** -0.5),
        'wkv_mem': nrm(ks[14], (DEPTH, D_MODEL, 2 * D_MODEL), D_MODEL ** -0.5),
        'wo_mem': nrm(ks[15], (DEPTH, D_MODEL, D_MODEL), D_MODEL ** -0.5 * BETA),
        'ln2_g': gain(ks[16], (DEPTH, D_MODEL)),
        'ln2_b': nrm(ks[17], (DEPTH, D_MODEL), 0.01),
        'w_pq': nrm(ks[18], (DEPTH, D_MODEL, PEER_HEADS * PEER_DK), D_MODEL ** -0.5),
        'sub_keys': nrm(ks[19], (DEPTH, 2, PEER_NKEYS, PEER_DK // 2), (PEER_DK // 2) ** -0.5),
        'u_tab': nrm(ks[20], (DEPTH, PEER_N, D_MODEL), D_MODEL ** -0.5),
        'v_tab': nrm(ks[21], (DEPTH, PEER_N, D_MODEL), BETA * PEER_HEADS ** -0.5),
        'ln3_g': gain(ks[22], (DEPTH, D_MODEL)),
        'ln3_b': nrm(ks[23], (DEPTH, D_MODEL), 0.01),
    }


def reference(x, mem, w_in, i_bias, f_bias, conv_w, conv_b, lam_qk, da_norm_g, ml_norm_g, w_out,
              ln1_g, ln1_b, wq_mem, wkv_mem, wo_mem, ln2_g, ln2_b, w_pq, sub_keys, u_tab, v_tab,
              ln3_g, ln3_b):
    for l in range(DEPTH):
        lam_init = 0.8 - 0.6 * math.exp(-0.3 * l)
        mix = hybrid_mixer(x, w_in[l], i_bias[l], f_bias[l], conv_w[l], conv_b[l], lam_qk[l],
                           lam_init, da_norm_g[l], ml_norm_g[l], w_out[l])
        x = layer_norm(ALPHA * x + mix, ln1_g[l], ln1_b[l])
        x = layer_norm(ALPHA * x + memory_attention(x, mem, wq_mem[l], wkv_mem[l], wo_mem[l]),
                       ln2_g[l], ln2_b[l])
        x = layer_norm(ALPHA * x + peer_ffn(x, w_pq[l], sub_keys[l], u_tab[l], v_tab[l]),
                       ln3_g[l], ln3_b[l])
    return x
```

```python
import math
import contextlib
import numpy as np
import ml_dtypes
import concourse.bass as bass
import concourse.mybir as mybir
from concourse.bass_utils import run_bass_kernel_spmd

F32 = mybir.dt.float32
BF16 = mybir.dt.bfloat16
I32 = mybir.dt.int32
U32 = mybir.dt.uint32
AF = mybir.ActivationFunctionType
ALU = mybir.AluOpType
AX = mybir.AxisListType

SEM_LIMIT = 12000


class Res:
    __slots__ = ("w", "r", "name")

    def __init__(self, name=""):
        self.w = None
        self.r = []
        self.name = name


class Tile:
    def __init__(self, t, res=None, name=""):
        self.t = t
        self.res = res if res is not None else Res(name)

    def __getitem__(self, k):
        return self.t[k]


def _res(x):
    return x.res if isinstance(x, Tile) else x


class Prog:
    ENGS = ("pe", "act", "dve", "pool", "sp")

    def __init__(self, nc, n_dma_sems=6):
        self.nc = nc
        self.es = contextlib.ExitStack()
        self.cur_es = self.es
        self.nsem = 0
        self.streams = {e: [] for e in self.ENGS}
        self.cur_sem = {}
        self.cur_val = {}
        for e in self.ENGS:
            self._new_eng_sem(e)
        self.known = {e: {} for e in self.ENGS}
        self.dsem = {}
        self.nds = n_dma_sems
        self.dma_rr = {e: 0 for e in self.ENGS}
        self.ntile = 0
        self.last_tok = {}

    def sem(self, name):
        self.nsem += 1
        return self.es.enter_context(self.nc.semaphore(f"{name}_{self.nsem}"))

    def _new_eng_sem(self, e):
        self.nsem = getattr(self, "nsem", 0)
        self.cur_sem[e] = self.sem("c" + e)
        self.cur_val[e] = 0

    def sb(self, shape, dt, name=None):
        self.ntile += 1
        nm = f"{name or 't'}_{self.ntile}"
        t = self.cur_es.enter_context(self.nc.sbuf_tensor(nm, list(shape), dt))
        return Tile(t, name=nm)

    def ps(self, shape, dt=F32, name=None):
        self.ntile += 1
        nm = f"{name or 'p'}_{self.ntile}"
        t = self.cur_es.enter_context(self.nc.psum_tensor(nm, list(shape), dt))
        return Tile(t, name=nm)

    def _collect(self, eng, reads, writes, excl=()):
        toks = []
        for r in reads:
            r = _res(r)
            if r.w is not None:
                toks.append(r.w)
        for w in list(writes) + list(excl):
            w = _res(w)
            if w.w is not None:
                toks.append(w.w)
            toks.extend(w.r)
        waits = []
        kn = self.known[eng]
        best = {}
        for (s, v, src) in toks:
            if src == "pe" and eng == "pe":
                continue
            if kn.get(s, 0) >= v:
                continue
            if best.get(s, (None, 0))[1] < v:
                best[s] = (s, v)
        for s, (sh, v) in best.items():
            kn[s] = v
            waits.append((sh, v))
        return waits

    def _commit(self, tok, reads, writes, excl=()):
        for r in reads:
            _res(r).r.append(tok)
        for w in list(writes) + list(excl):
            w = _res(w)
            w.w = tok
            w.r = []

    def op(self, eng, fn, reads=(), writes=(), excl=()):
        waits = self._collect(eng, reads, writes, excl)
        if self.cur_val[eng] >= SEM_LIMIT:
            self._new_eng_sem(eng)
        s = self.cur_sem[eng]
        self.cur_val[eng] += 1
        v = self.cur_val[eng]
        tok = (s, v, eng)
        self.last_tok[eng] = tok
        self._commit(tok, reads, writes, excl)
        self.streams[eng].append((waits, fn, s, 1))
        return tok

    def dma(self, q, out, in_, reads=(), writes=(), fn=None, inc=16):
        key = q
        nds = self.nds[q] if isinstance(self.nds, dict) else self.nds
        if key not in self.dsem:
            self.dsem[key] = [[self.sem("d" + q), 0] for _ in range(nds)]
        k = self.dma_rr[q]
        self.dma_rr[q] = (k + 1) % nds
        slot = self.dsem[key][k]
        waits = self._collect(q, reads, writes)
        kn = self.known[q]
        if slot[1] > 0 and kn.get(slot[0], 0) < slot[1]:
            waits.append((slot[0], slot[1]))
            kn[slot[0]] = slot[1]
        if slot[1] + inc > SEM_LIMIT:
            slot[0] = self.sem("d" + q)
            slot[1] = 0
        slot[1] += inc
        tok = (slot[0], slot[1], "dma")
        self._commit(tok, reads, writes)
        if fn is None:
            fn = lambda e, o=out, i=in_: e.dma_start(out=o, in_=i)
        self.streams[q].append((waits, fn, slot[0], inc))
        return tok


    def cc(self, kind, groups, in_ap, out_ap, reads=(), writes=(), inc=16):
        fn = lambda e: e.collective_compute(kind, ALU.bypass, replica_groups=groups, ins=[in_ap], outs=[out_ap])
        self._cc_inc = inc
        return self.dma('pool', None, None, reads=reads, writes=writes, fn=fn, inc=inc)


    @contextlib.contextmanager
    def scope(self):
        prev = self.cur_es
        es = contextlib.ExitStack()
        self.cur_es = es
        try:
            yield
            self.barrier()
            self.emit()
        finally:
            self.cur_es = prev
            es.close()

    def barrier(self):
        toks = []
        for e in self.ENGS:
            if e in self.last_tok:
                toks.append(self.last_tok[e])
        for q, slots in self.dsem.items():
            for (sh, v) in slots:
                if v > 0:
                    toks.append((sh, v, "dma"))
        for e in self.ENGS:
            kn = self.known[e]
            waits = []
            for (sh, v, src) in toks:
                if src == e and e == "pe":
                    continue
                if kn.get(sh, 0) < v:
                    kn[sh] = v
                    waits.append((sh, v))
            if waits:
                self.streams[e].append((waits, None, None, 0))

    def finish(self, all_res):
        waits = self._collect("sp", all_res, [])
        kn = self.known["sp"]
        for q, slots in self.dsem.items():
            for (sh, v) in slots:
                if v > 0 and kn.get(sh, 0) < v:
                    waits.append((sh, v))
                    kn[sh] = v
        self.streams["sp"].append((waits, None, None, 0))

    def emit(self):
        nc = self.nc
        streams = self.streams

        def run(engname, eng):
            for (waits, fn, s, inc) in streams[engname]:
                for (sh, v) in waits:
                    eng.wait_ge(sh, v)
                if fn is not None:
                    ins = fn(eng)
                    ins.then_inc(s, inc)

        with nc.Block() as block:
            @block.tensor
            def _(e):
                run("pe", e)

            @block.scalar
            def _(e):
                run("act", e)

            @block.vector
            def _(e):
                run("dve", e)

            @block.gpsimd
            def _(e):
                run("pool", e)

            @block.sync
            def _(e):
                run("sp", e)
        self.streams = {e: [] for e in self.ENGS}

    def close(self):
        self.es.close()


LN_EPS = 1e-5
ALPHA = 8.0 ** 0.25
S = 8192
NWA = 898
NTOK = 2048


def AP(t, off, dims):
    return bass.AP(t.t if isinstance(t, Tile) else t, off, dims)


def make_consts(P):
    C = {}
    identf = P.sb([128, 128], F32, "identf")
    ident = P.sb([128, 128], BF16, "ident")
    trif = P.sb([128, 128], F32, "trif")
    trib = P.sb([128, 128], BF16, "trib")
    onesf = P.sb([128, 128], F32, "onesf")
    ones_bf = P.sb([128, 128], BF16, "ones")
    iota16 = P.sb([128, 16], F32, "iota16")
    thr16 = P.sb([128, 16], F32, "thr16")
    P.op('dve', lambda e: e.memset(identf[:], 0.0), writes=[identf])
    P.op('pool', lambda e: e.affine_select(out=identf[:], in_=identf[:], pattern=[[-1, 128]],
                                           compare_op=ALU.not_equal, fill=1.0, base=0, channel_multiplier=1),
         reads=[identf], writes=[identf])
    P.op('dve', lambda e: e.tensor_copy(ident[:], identf[:]), reads=[identf], writes=[ident])
    P.op('dve', lambda e: e.memset(onesf[:], 1.0), writes=[onesf])
    P.op('dve', lambda e: e.memset(ones_bf[:], 1.0), writes=[ones_bf])
    P.op('pool', lambda e: e.iota(trif[:], pattern=[[1, 128]], base=0, channel_multiplier=-1,
                                  allow_small_or_imprecise_dtypes=True), writes=[trif])
    P.op('dve', lambda e: e.tensor_single_scalar(trif[:], trif[:], 0.0, op=ALU.is_ge), reads=[trif], writes=[trif])
    P.op('dve', lambda e: e.tensor_copy(trib[:], trif[:]), reads=[trif], writes=[trib])
    P.op('pool', lambda e: e.iota(iota16[:], pattern=[[1, 16]], base=0, channel_multiplier=0,
                                  allow_small_or_imprecise_dtypes=True), writes=[iota16])
    P.op('pool', lambda e: e.iota(thr16[:], pattern=[[16, 16]], base=16, channel_multiplier=0,
                                  allow_small_or_imprecise_dtypes=True), writes=[thr16])
    C.update(identf=identf, ident=ident, trif=trif, trib=trib, onesf=onesf, ones_bf=ones_bf, iota16=iota16, thr16=thr16)
    return C


def phase_a(P, nc, C, H, xT_h, xg_h, xg_res, mixsrc_h, mixsrc_res, nq_tiles):
    ident, trif, trib, onesf = C['ident'], C['trif'], C['trib'], C['onesf']
    D = 1024
    ntok = nq_tiles * 256
    nblk = ntok // 128
    spt = P.sb([128, 16], F32, "spt")
    lamq = P.sb([128, 256], F32, "lamq")
    gda = P.sb([128, 128], F32, "gda")
    gml = P.sb([128, 128], F32, "gml")
    P.dma('sp', spt[:], H['sp'].ap(), writes=[spt])
    P.dma('sp', lamq[:], H['lamqk'].ap(), writes=[lamq])
    P.dma('sp', gda[:], H['gda'].ap(), writes=[gda])
    P.dma('sp', gml[:], H['gml'].ap(), writes=[gml])
    lt = P.sb([128, 128], F32, "lt")
    lsum = P.sb([128, 2], F32, "lsum")
    neglam = P.sb([128, 1], F32, "neglam")
    P.op('dve', lambda e: e.tensor_tensor(lt[:].rearrange("p (a d) -> p a d", a=2),
                                          AP(lamq, 0, [[256, 128], [128, 2], [1, 64]]),
                                          AP(lamq, 64, [[256, 128], [128, 2], [1, 64]]), op=ALU.mult),
         reads=[lamq], writes=[lt])
    P.op('dve', lambda e: e.tensor_reduce(lsum[:], lt[:].rearrange("p (a d) -> p a d", a=2), axis=AX.X, op=ALU.add),
         reads=[lt], writes=[lsum])
    P.op('act', lambda e: e.activation(lsum[:], lsum[:], AF.Exp), reads=[lsum], writes=[lsum])
    P.op('dve', lambda e: e.tensor_tensor(neglam[:], lsum[:, 1:2], lsum[:, 0:1], op=ALU.subtract), reads=[lsum], writes=[neglam])
    P.op('dve', lambda e: e.tensor_tensor(neglam[:], neglam[:], spt[:, 12:13], op=ALU.subtract), reads=[neglam, spt], writes=[neglam])
    P.op('dve', lambda e: e.tensor_scalar(gda[:], gda[:], spt[:, 13:14], None, op0=ALU.mult), reads=[gda, spt], writes=[gda])

    wab = P.sb([128, 8, NWA], BF16, "wab")
    stg = P.sb([128, 1024], F32, "stg")
    wap = H['wa'].ap().rearrange("(c p) n -> p c n", p=128)
    for kc in range(8):
        P.dma('sp', stg[:, 0:NWA], wap[:, kc, :], writes=[stg])
        P.op('dve', lambda e, kc=kc: e.tensor_copy(wab[:, kc, :], stg[:, 0:NWA]), reads=[stg], writes=[wab])

    daqT = [P.sb([64, S], BF16, f"daqT{c}") for c in range(2)]
    dakT = [P.sb([64, S], BF16, f"dakT{c}") for c in range(2)]
    Vaug = P.sb([128, 64, 129], BF16, "Vaug")
    mlqT = P.sb([128, S], BF16, "mlqT")
    mlkT = P.sb([128, S], BF16, "mlkT")
    mlV = P.sb([128, 64, 129], BF16, "mlV")
    sigo = P.sb([128, 64, 128], BF16, "sigo")
    gates = P.sb([128, 64, 2], F32, "gates")
    P.op('dve', lambda e: e.memset(Vaug[:, :, 128:129], 1.0), writes=[Vaug])
    P.op('dve', lambda e: e.memset(mlV[:, :, 128:129], 1.0), writes=[mlV])

    banks = [P.ps([128, 512], F32, f"bank{i}") for i in range(8)]

    xs = P.sb([128, 8, 256], F32, "xs")
    xb = P.sb([128, 8, 256], BF16, "xb")
    pre = [P.sb([128, 3 + 256], F32, f"pre{i}") for i in range(2)]
    cacc = P.sb([128, 256], F32, "cacc")
    ctmp = P.sb([128, 256], F32, "ctmp")
    xT_v = xT_h.ap().rearrange("(c p) t -> p c t", p=128) if xg_h is None else None
    for i in range(2):
        P.op('dve', lambda e, i=i: e.memset(pre[i][:, 0:3], 0.0), writes=[pre[i]])
    for ti in range(nq_tiles):
        t0 = ti * 256
        if xg_h is None:
            P.dma('sp', xs[:], xT_v[:, :, t0:t0 + 256], writes=[xs])
            P.op('pool', lambda e: e.tensor_copy(xb[:], xs[:]), reads=[xs], writes=[xb])
        else:
            jr, tl = t0 // 2048, t0 % 2048
            for c4 in range(4):
                P.dma('sp', xb[:, 2 * c4:2 * c4 + 2, :],
                      xg_h[c4].ap()[jr * 256:(jr + 1) * 256, tl:tl + 256].rearrange("(k p) t -> p k t", p=128),
                      reads=[xg_res], writes=[xb])
        for g in range(4):
            pb = banks[g // 2]
            col0 = (g // 2) * 128 + (g % 2) * 64
            for kc in range(8):
                P.op('pe', lambda e, g=g, kc=kc, pb=pb, col0=col0: e.matmul(
                    pb[0:64, (g % 2) * 256:(g % 2) * 256 + 256], wab[:, kc, col0:col0 + 64], xb[:, kc, :],
                    start=(kc == 0), stop=(kc == 7)), reads=[wab, xb], writes=[pb])
        for c in range(2):
            P.op('act', lambda e, t0=t0, c=c: e.copy(daqT[c][:, t0:t0 + 256], banks[0][0:64, c * 256:c * 256 + 256]),
                 excl=[banks[0]], writes=[daqT[c]])
            P.op('dve', lambda e, t0=t0, c=c: e.tensor_copy(dakT[c][:, t0:t0 + 256], banks[1][0:64, c * 256:c * 256 + 256]),
                 excl=[banks[1]], writes=[dakT[c]])
        for g in range(2):
            col0 = 384 + g * 128
            for kc in range(8):
                P.op('pe', lambda e, g=g, kc=kc, col0=col0: e.matmul(
                    banks[2][:, g * 256:g * 256 + 256], wab[:, kc, col0:col0 + 128], xb[:, kc, :],
                    start=(kc == 0), stop=(kc == 7)), reads=[wab, xb], writes=[banks[2]])
        for g in range(2):
            pr = pre[g]
            if ti > 0:
                P.op('dve', lambda e, pr=pr: e.tensor_copy(pr[:, 0:3], pr[:, 256:259]), reads=[pr], writes=[pr])
            P.op('act', lambda e, g=g, pr=pr: e.copy(pr[:, 3:259], banks[2][:, g * 256:g * 256 + 256]),
                 excl=[banks[2]], writes=[pr])
            P.op('dve', lambda e, g=g, pr=pr: e.tensor_scalar(cacc[:], pr[:, 3:259], spt[:, g * 4 + 3:g * 4 + 4], spt[:, 8 + g:9 + g],
                                                              op0=ALU.mult, op1=ALU.add), reads=[pr, spt], writes=[cacc])
            for j in range(3):
                P.op('dve', lambda e, g=g, pr=pr, j=j: e.scalar_tensor_tensor(
                    out=cacc[:], in0=pr[:, j:j + 256], scalar=spt[:, g * 4 + j:g * 4 + j + 1], in1=cacc[:],
                    op0=ALU.mult, op1=ALU.add), reads=[pr, spt, cacc], writes=[cacc])
            if g == 0:
                P.op('act', lambda e, t0=t0: e.activation(mlqT[:, t0:t0 + 256], cacc[:], AF.Silu), reads=[cacc], writes=[mlqT])
            else:
                P.op('act', lambda e: e.activation(ctmp[:], cacc[:], AF.Silu), reads=[cacc], writes=[ctmp])
                P.op('dve', lambda e, t0=t0: e.tensor_scalar(mlkT[:, t0:t0 + 256], ctmp[:], 128.0 ** -0.5, None, op0=ALU.mult),
                     reads=[ctmp], writes=[mlkT])
        for sub in range(2):
            blk = ti * 2 + sub
            pb = banks[3 + sub]
            for gi, col0 in enumerate((256, 640, 768)):
                for kc in range(8):
                    P.op('pe', lambda e, sub=sub, gi=gi, col0=col0, kc=kc, pb=pb: e.matmul(
                        pb[:, gi * 128:(gi + 1) * 128], xb[:, kc, sub * 128:(sub + 1) * 128], wab[:, kc, col0:col0 + 128],
                        start=(kc == 0), stop=(kc == 7)), reads=[wab, xb], writes=[pb])
            for kc in range(8):
                P.op('pe', lambda e, sub=sub, kc=kc, blk=blk: e.matmul(
                    banks[5][:, blk * 2:blk * 2 + 2], xb[:, kc, sub * 128:(sub + 1) * 128], wab[:, kc, 896:898],
                    start=(kc == 0), stop=(kc == 7)), reads=[wab, xb], writes=[banks[5]])
            P.op('act', lambda e, blk=blk, pb=pb: e.copy(Vaug[:, blk, 0:128], pb[:, 0:128]), excl=[pb], writes=[Vaug])
            P.op('dve', lambda e, blk=blk, pb=pb: e.tensor_copy(mlV[:, blk, 0:128], pb[:, 128:256]), excl=[pb], writes=[mlV])
            P.op('act', lambda e, blk=blk, pb=pb: e.activation(sigo[:, blk, :], pb[:, 256:384], AF.Sigmoid), excl=[pb], writes=[sigo])
    P.op('dve', lambda e: e.tensor_copy(gates[:, 0:nblk, :], banks[5][:, 0:nblk * 2].rearrange("p (b g) -> p b g", g=2)),
         excl=[banks[5]], writes=[gates])

    Eb = [P.sb([128, 2, 256], BF16, f"Eb{i}") for i in range(2)]
    o_da = P.sb([128, 128], F32, "o_da")
    o_out = [P.sb([128, 128], BF16, f"o_out{i}") for i in range(2)]
    oT_sb = [P.sb([128, 128], BF16, f"oT_sb{i}") for i in range(2)]
    pT6 = banks[6].t.bitcast(BF16)
    pT7 = banks[7].t.bitcast(BF16)
    junk = P.sb([128, 128], F32, "junk")
    rz = P.sb([128, 2], F32, "rz")
    ss = P.sb([128, 1], F32, "ss")
    step = 0
    nout = 0
    for qt in range(nq_tiles):
        q0 = qt * 256
        nj = 2 * qt + 2
        for j in range(nj):
            pS = banks[step % 2]
            E = Eb[step % 2]
            step += 1
            subs = (0, 1) if j < nj - 1 else (1,)
            qa, qb = (0, 256) if j < nj - 1 else (128, 256)
            for c in range(2):
                P.op('pe', lambda e, c=c, j=j, pS=pS, qa=qa, qb=qb, q0=q0: e.matmul(
                    pS[:, c * 256 + qa:c * 256 + qb], dakT[c][:, j * 128:(j + 1) * 128], daqT[c][:, q0 + qa:q0 + qb],
                    start=True, stop=True), reads=[dakT[c], daqT[c]], writes=[pS])
            P.op('act', lambda e, E=E, pS=pS, qa=qa, qb=qb: e.activation(
                E[:, :, qa:qb], pS[:, :].rearrange("p (c q) -> p c q", c=2)[:, :, qa:qb], AF.Exp, scale=0.125),
                 excl=[pS], writes=[E])
            if j >= nj - 2:
                sm = 0 if j == nj - 2 else 1
                P.op('pool', lambda e, E=E, sm=sm: e.tensor_tensor(
                    E[:, :, sm * 128:(sm + 1) * 128], E[:, :, sm * 128:(sm + 1) * 128],
                    AP(trib, 0, [[128, 128], [0, 2], [1, 128]]), op=ALU.mult), reads=[E, trib], writes=[E])
            for s_ in subs:
                last_j = nj - 2 if s_ == 0 else nj - 1
                for c in range(2):
                    pacc = banks[2 + s_ * 2 + c]
                    P.op('pe', lambda e, E=E, s_=s_, c=c, j=j, pacc=pacc, last_j=last_j: e.matmul(
                        pacc[:, 0:129], E[:, c, s_ * 128:(s_ + 1) * 128], Vaug[:, j, :],
                        start=(j == 0), stop=(j == last_j)), reads=[E, Vaug], writes=[pacc])
        for s_ in range(2):
            p0, p1 = banks[2 + s_ * 2], banks[2 + s_ * 2 + 1]
            oo = o_out[nout % 2]
            nout += 1
            P.op('dve', lambda e, p0=p0: e.reciprocal(rz[:, 0:1], p0[:, 128:129]), excl=[p0], writes=[rz])
            P.op('dve', lambda e, p1=p1: e.reciprocal(rz[:, 1:2], p1[:, 128:129]), excl=[p1], writes=[rz])
            P.op('dve', lambda e: e.tensor_tensor(rz[:, 1:2], rz[:, 1:2], neglam[:], op=ALU.mult), reads=[rz, neglam], writes=[rz])
            P.op('dve', lambda e, p0=p0: e.tensor_scalar(o_da[:], p0[:, 0:128], rz[:, 0:1], None, op0=ALU.mult),
                 reads=[rz], excl=[p0], writes=[o_da])
            P.op('dve', lambda e, p1=p1: e.scalar_tensor_tensor(out=o_da[:], in0=p1[:, 0:128], scalar=rz[:, 1:2], in1=o_da[:],
                                                                op0=ALU.mult, op1=ALU.add), reads=[rz, o_da], excl=[p1], writes=[o_da])
            P.op('dve', lambda e: e.scalar_tensor_tensor(out=junk[:], in0=o_da[:], scalar=1.0, in1=o_da[:], op0=ALU.mult, op1=ALU.mult,
                                                         accum_out=ss[:]), reads=[o_da], writes=[junk, ss])
            P.op('dve', lambda e: e.tensor_scalar(ss[:], ss[:], 1.0 / 128.0, LN_EPS, op0=ALU.mult, op1=ALU.add), reads=[ss], writes=[ss])
            P.op('act', lambda e: e.activation(ss[:], ss[:], AF.Sqrt), reads=[ss], writes=[ss])
            P.op('dve', lambda e: e.reciprocal(ss[:], ss[:]), reads=[ss], writes=[ss])
            P.op('dve', lambda e, oo=oo: e.scalar_tensor_tensor(out=oo[:], in0=o_da[:], scalar=ss[:, 0:1], in1=gda[:], op0=ALU.mult, op1=ALU.mult),
                 reads=[o_da, ss, gda], writes=[oo])
            g_blk = (q0 + s_ * 128) // 128
            oT_ = oT_sb[nout % 2]
            P.op('pe', lambda e, oo=oo: e.transpose(bass.AP(pT6, 0, [[1024, 128], [1, 128]]), oo[:], ident[:]),
                 reads=[oo, ident], writes=[banks[6]])
            P.op('act', lambda e, oT_=oT_: e.copy(oT_[:], bass.AP(pT6, 0, [[1024, 128], [1, 128]])), excl=[banks[6]], writes=[oT_])
            jj, qq, ww = g_blk // 16, (g_blk % 16) // 4, g_blk % 4
            P.dma('sp', mixsrc_h[jj].ap()[(qq * 2) * 128:(qq * 2 + 1) * 128, ww * 128:(ww + 1) * 128], oT_[:], reads=[oT_], writes=[mixsrc_res])

    nch = nblk
    lf = P.sb([128, 64], F32, "lf")
    bcs = P.sb([128, 64], F32, "bcs")
    ek = P.sb([128, 64], F32, "ek")
    eb = P.sb([128, 64], F32, "eb")
    ebL = P.sb([128, 64], F32, "ebL")
    nfb = P.sb([128, 1], F32, "nfb")
    P.op('dve', lambda e: e.tensor_scalar(nfb[:], spt[:, 11:12], -1.0, None, op0=ALU.mult), reads=[spt], writes=[nfb])
    P.op('act', lambda e: e.activation(lf[:, 0:nch], gates[:, 0:nch, 1], AF.Exp, bias=nfb[:, 0:1], scale=-1.0), reads=[gates, nfb], writes=[lf])
    P.op('dve', lambda e: e.tensor_scalar(lf[:, 0:nch], lf[:, 0:nch], 1.0, None, op0=ALU.add), reads=[lf], writes=[lf])
    P.op('act', lambda e: e.activation(lf[:, 0:nch], lf[:, 0:nch], AF.Ln), reads=[lf], writes=[lf])
    P.op('dve', lambda e: e.tensor_scalar(lf[:, 0:nch], lf[:, 0:nch], -1.0, None, op0=ALU.mult), reads=[lf], writes=[lf])
    P.op('pe', lambda e: e.matmul(banks[0][:, 0:nch], trif[:], lf[:, 0:nch], start=True, stop=True), reads=[trif, lf], writes=[banks[0]])
    P.op('pe', lambda e: e.matmul(banks[1][:, 0:nch], onesf[:], lf[:, 0:nch], start=True, stop=True), reads=[onesf, lf], writes=[banks[1]])
    P.op('dve', lambda e: e.tensor_copy(bcs[:, 0:nch], banks[0][:, 0:nch]), excl=[banks[0]], writes=[bcs])
    P.op('act', lambda e: e.activation(eb[:, 0:nch], bcs[:, 0:nch], AF.Exp), reads=[bcs], writes=[eb])
    P.op('act', lambda e: e.activation(ebL[:, 0:nch], banks[1][:, 0:nch], AF.Exp), excl=[banks[1]], writes=[ebL])
    P.op('dve', lambda e: e.tensor_tensor(ek[:, 0:nch], gates[:, 0:nch, 0], bcs[:, 0:nch], op=ALU.subtract), reads=[gates, bcs], writes=[ek])
    P.op('act', lambda e: e.activation(ek[:, 0:nch], ek[:, 0:nch], AF.Exp, bias=spt[:, 10:11], scale=1.0), reads=[ek, spt], writes=[ek])

    Dst = [P.sb([128, 129], F32, f"Dst{i}") for i in range(2)]
    Cb = [P.sb([128, 129], BF16, f"Cb{i}") for i in range(2)]
    ktok = [P.sb([128, 128], BF16, f"ktok{i}") for i in range(2)]
    vp = [P.sb([128, 129], BF16, f"vp{i}") for i in range(2)]
    ATb = [P.sb([128, 128], BF16, f"ATb{i}") for i in range(2)]
    hh = P.sb([128, 128], F32, "hh")
    hsm = P.sb([128, 4], F32, "hsm")
    st6 = P.sb([128, 6], F32, "st6")
    mvv = P.sb([128, 2], F32, "mvv")
    rstd = P.sb([128, 1], F32, "rstd")
    ho = [P.sb([128, 128], BF16, f"ho{i}") for i in range(2)]
    pKt = banks[2].t.bitcast(BF16)
    for c in range(nch):
        sl = slice(c * 128, (c + 1) * 128)
        kt, v_, at = ktok[c % 2], vp[c % 2], ATb[c % 2]
        P.op('pe', lambda e, sl=sl: e.transpose(bass.AP(pKt, 0, [[1024, 128], [1, 128]]), mlkT[:, sl], ident[:]),
             reads=[mlkT, ident], writes=[banks[2]])
        P.op('act', lambda e, kt=kt: e.copy(kt[:], bass.AP(pKt, 0, [[1024, 128], [1, 128]])), excl=[banks[2]], writes=[kt])
        P.op('pool', lambda e, c=c, v_=v_: e.tensor_scalar(v_[:], mlV[:, c, :], ek[:, c:c + 1], None, op0=ALU.mult),
             reads=[mlV, ek], writes=[v_])
        P.op('pe', lambda e, kt=kt, v_=v_: e.matmul(banks[3][:, 0:129], kt[:], v_[:], start=True, stop=True),
             reads=[kt, v_], writes=[banks[3]])
        P.op('pe', lambda e, sl=sl: e.matmul(banks[4][:, 0:128], mlkT[:, sl], mlqT[:, sl], start=True, stop=True),
             reads=[mlkT, mlqT], writes=[banks[4]])
        P.op('dve', lambda e, c=c, at=at: e.scalar_tensor_tensor(out=at[:], in0=banks[4][:, 0:128], scalar=ek[:, c:c + 1], in1=trif[:],
                                                                 op0=ALU.mult, op1=ALU.mult), reads=[ek, trif], excl=[banks[4]], writes=[at])
        pH = banks[5 + c % 2]
        if c > 0:
            P.op('pe', lambda e, sl=sl, c=c, pH=pH: e.matmul(pH[:, 0:129], mlqT[:, sl], Cb[(c - 1) % 2][:], start=True, stop=False),
                 reads=[mlqT, Cb[(c - 1) % 2]], writes=[pH])
        P.op('pe', lambda e, at=at, c=c, pH=pH: e.matmul(pH[:, 0:129], at[:], mlV[:, c, :], start=(c == 0), stop=True),
             reads=[at, mlV], writes=[pH])
        Dc = Dst[c % 2]
        if c == 0:
            P.op('dve', lambda e, Dc=Dc: e.tensor_copy(Dc[:], banks[3][:, 0:129]), excl=[banks[3]], writes=[Dc])
        else:
            Dp = Dst[(c - 1) % 2]
            P.op('dve', lambda e, Dc=Dc, Dp=Dp, c=c: e.scalar_tensor_tensor(out=Dc[:], in0=Dp[:], scalar=ebL[:, c - 1:c], in1=banks[3][:, 0:129],
                                                                          op0=ALU.mult, op1=ALU.add), reads=[Dp, ebL], excl=[banks[3]], writes=[Dc])
        P.op('act', lambda e, Dc=Dc, c=c: e.activation(Cb[c % 2][:], Dc[:], AF.Identity, scale=ebL[:, c:c + 1]), reads=[Dc, ebL], writes=[Cb[c % 2]])
        P.op('dve', lambda e, c=c, pH=pH: e.tensor_tensor(hsm[:, 0:1], pH[:, 128:129], eb[:, c:c + 1], op=ALU.mult), reads=[eb], excl=[pH], writes=[hsm])
        P.op('dve', lambda e: e.tensor_scalar(hsm[:, 1:2], hsm[:, 0:1], -1.0, 1.0, op0=ALU.mult, op1=ALU.max), reads=[hsm], writes=[hsm])
        P.op('dve', lambda e: e.tensor_scalar(hsm[:, 2:3], hsm[:, 0:1], 1.0, None, op0=ALU.max), reads=[hsm], writes=[hsm])
        P.op('dve', lambda e: e.tensor_tensor(hsm[:, 1:2], hsm[:, 1:2], hsm[:, 2:3], op=ALU.max), reads=[hsm], writes=[hsm])
        P.op('dve', lambda e: e.reciprocal(hsm[:, 2:3], hsm[:, 1:2]), reads=[hsm], writes=[hsm])
        P.op('dve', lambda e, c=c: e.tensor_tensor(hsm[:, 3:4], hsm[:, 2:3], eb[:, c:c + 1], op=ALU.mult), reads=[hsm, eb], writes=[hsm])
        P.op('dve', lambda e, pH=pH: e.tensor_scalar(hh[:], pH[:, 0:128], hsm[:, 3:4], None, op0=ALU.mult), reads=[hsm], excl=[pH], writes=[hh])
        P.op('dve', lambda e: e.bn_stats(st6[:], hh[:]), reads=[hh], writes=[st6])
        P.op('dve', lambda e: e.bn_aggr(mvv[:], st6[:]), reads=[st6], writes=[mvv])
        P.op('dve', lambda e: e.tensor_scalar(rstd[:], mvv[:, 1:2], LN_EPS, None, op0=ALU.add), reads=[mvv], writes=[rstd])
        P.op('act', lambda e: e.activation(rstd[:], rstd[:], AF.Sqrt), reads=[rstd], writes=[rstd])
        P.op('dve', lambda e: e.reciprocal(rstd[:], rstd[:]), reads=[rstd], writes=[rstd])
        P.op('dve', lambda e: e.tensor_scalar(hh[:], hh[:], mvv[:, 0:1], rstd[:, 0:1], op0=ALU.subtract, op1=ALU.mult),
             reads=[hh, mvv, rstd], writes=[hh])
        P.op('pool', lambda e: e.tensor_tensor(hh[:], hh[:], gml[:], op=ALU.mult), reads=[hh, gml], writes=[hh])
        hoo = ho[c % 2]
        P.op('pool', lambda e, c=c, hoo=hoo: e.tensor_tensor(hoo[:], hh[:], sigo[:, c, :], op=ALU.mult), reads=[hh, sigo], writes=[hoo])
        hT_ = oT_sb[c % 2]
        P.op('pe', lambda e, hoo=hoo: e.transpose(bass.AP(pT7, 0, [[1024, 128], [1, 128]]), hoo[:], ident[:]),
             reads=[hoo, ident], writes=[banks[7]])
        P.op('act', lambda e, hT_=hT_: e.copy(hT_[:], bass.AP(pT7, 0, [[1024, 128], [1, 128]])), excl=[banks[7]], writes=[hT_])
        jj, qq, ww = c // 16, (c % 16) // 4, c % 4
        P.dma('sp', mixsrc_h[jj].ap()[(qq * 2 + 1) * 128:(qq * 2 + 2) * 128, ww * 128:(ww + 1) * 128], hT_[:], reads=[hT_], writes=[mixsrc_res])


def phase_b(P, nc, C, H, xin_h, xin_res, mixg_h, mixg_res, midx_h, xout_h, xout_res, xsrc_h, xsrc_res, ntiles):
    ident, ones_bf, iota16, thr16 = C['ident'], C['ones_bf'], C['iota16'], C['thr16']
    D = 1024
    lnp = P.sb([128, 6 * D], F32, "lnp")
    P.dma('sp', lnp[:], bass.AP(H['lnp'], 0, [[0, 128], [1, 6 * D]]), writes=[lnp])

    r = P.sb([128, D], F32, "r")
    stg = [r] * 2
    stg_i = [0]

    def load_w(handle, n, name, dst=None):
        wb = dst if dst is not None else P.sb([128, 8, n], BF16, name)
        wap = handle.ap().rearrange("(c p) n -> p c n", p=128)
        for kc in range(8):
            for pc in range(n // 1024):
                s = stg[stg_i[0] % 2]
                stg_i[0] += 1
                P.dma('sp', s[:, :], wap[:, kc, pc * 1024:(pc + 1) * 1024], writes=[s])
                if stg_i[0] % 2 == 0:
                    P.op('act', lambda e, s=s, kc=kc, pc=pc: e.copy(wb[:, kc, pc * 1024:(pc + 1) * 1024], s[:, :]), reads=[s], writes=[wb])
                else:
                    P.op('dve', lambda e, s=s, kc=kc, pc=pc: e.tensor_copy(wb[:, kc, pc * 1024:(pc + 1) * 1024], s[:, :]), reads=[s], writes=[wb])
        return wb

    wout_b = load_w(H['wout'], 1024, "wout")
    wq_b = load_w(H['wq'], 1024, "wq")
    wo_b = load_w(H['wo'], 1024, "wo")
    wpq_b = load_w(H['wpq'], 2048, "wpq")

    pA = P.ps([128, 2048], F32, "pA")
    pB = P.ps([128, 1024], F32, "pB")
    pC = P.ps([128, 512], F32, "pC")
    pT = P.ps([128, 8, 128], BF16, "pT")

    KT_b = P.sb([128, 8, 256], BF16, "KT")
    V_b = P.sb([128, 2, 1024], BF16, "V")
    skT_b = P.sb([128, 2, 128], BF16, "skT")
    NROW = 4
    gbuf = P.sb([128, 2 * NROW, 1024], F32, "gbuf")
    gb_bf = gbuf.t.bitcast(BF16)

    def wkv_ap(kc, c0, c1):
        return bass.AP(gb_bf, kc * 2048 + c0, [[2 * NROW * 1024 * 2, 128], [1, c1 - c0]])

    wkvap = H['wkv'].ap().rearrange("(c p) n -> p c n", p=128)
    for kc in range(8):
        for pc in range(2):
            s = stg[stg_i[0] % 2]
            stg_i[0] += 1
            P.dma('sp', s[:, :], wkvap[:, kc, pc * 1024:(pc + 1) * 1024], writes=[s])
            P.op('dve', lambda e, s=s, kc=kc, pc=pc: e.tensor_copy(wkv_ap(kc, pc * 1024, (pc + 1) * 1024), s[:, :]), reads=[s], writes=[gbuf])
    sc = P.sb([128, 16, 128], F32, "sc")
    w8bf = sc.t.bitcast(BF16)
    memT_b = sc

    def memT_ap(kc, m0, m1):
        return bass.AP(w8bf, kc * 256 + m0, [[4096, 128], [1, m1 - m0]])

    def pqT_ap(hc):
        return bass.AP(w8bf, hc * 128, [[4096, 128], [1, 128]])
    mT_ap = H['memT'].ap().rearrange("(c p) n -> p c n", p=128)
    for kc in range(8):
        s = stg[stg_i[0] % 2]
        stg_i[0] += 1
        P.dma('sp', s[:, 0:256], mT_ap[:, kc, :], writes=[s])
        P.op('dve', lambda e, s=s, kc=kc: e.tensor_copy(memT_ap(kc, 0, 256), s[:, 0:256]), reads=[s], writes=[memT_b])
    s = stg[stg_i[0] % 2]
    stg_i[0] += 1
    P.dma('sp', s[:, 0:256].rearrange("p (c n) -> p c n", c=2), H['skT'].ap().rearrange("c d n -> d c n"), writes=[s])
    P.op('dve', lambda e, s=s: e.tensor_copy(skT_b[:], s[:, 0:256].rearrange("p (c n) -> p c n", c=2)), reads=[s], writes=[skT_b])

    for j in range(8):
        for kc in range(8):
            P.op('pe', lambda e, j=j, kc=kc: e.matmul(pB[:, 0:256], wkv_ap(kc, j * 128, (j + 1) * 128), memT_ap(kc, 0, 256),
                                                      start=(kc == 0), stop=(kc == 7)),
                 reads=[gbuf, memT_b], writes=[pB])
        P.op('dve', lambda e, j=j: e.tensor_copy(KT_b[:, j, :], pB[:, 0:256]), excl=[pB], writes=[KT_b])
    for mc in range(2):
        for half in range(2):
            for kc in range(8):
                P.op('pe', lambda e, mc=mc, half=half, kc=kc: e.matmul(
                    pB[:, 0:512], memT_ap(kc, mc * 128, (mc + 1) * 128),
                    wkv_ap(kc, 1024 + half * 512, 1024 + (half + 1) * 512), start=(kc == 0), stop=(kc == 7)),
                     reads=[gbuf, memT_b], writes=[pB])
            P.op('dve', lambda e, mc=mc, half=half: e.tensor_copy(V_b[:, mc, half * 512:(half + 1) * 512], pB[:, 0:512]),
                 excl=[pB], writes=[V_b])

    xt = P.sb([128, D], F32, "xt")
    mixb = P.sb([128, 8, 512], BF16, "mixb")
    x1 = P.sb([128, D], F32, "x1")
    x2 = P.sb([128, D], F32, "x2")
    xb = P.sb([128, D], BF16, "xb")
    xT = P.sb([128, 8, 128], BF16, "xT")
    qT = P.sb([128, 8, 128], BF16, "qT")
    E_b = P.sb([128, 8, 128], BF16, "E")
    rz = P.sb([128, 4, 128], F32, "rz")
    oT = qT
    top = P.sb([128, 16, 16], F32, "top")
    idxu = P.sb([128, 16, 16], U32, "idxu")
    idxf = P.sb([128, 16, 16], F32, "idxf")
    cs = P.sb([128, 8, 16], F32, "cs")
    ciu = P.sb([128, 8, 16], U32, "ciu")
    cif = P.sb([128, 128], F32, "cif")
    big = sc
    big2 = sc
    pqT = sc
    mixf = r
    x3 = r
    junk = r
    cand = sc
    k0f = P.sb([128, 128], F32, "k0f")
    k1f = P.sb([128, 128], F32, "k1f")
    i0s = P.sb([128, 128], F32, "i0s")
    i1s = P.sb([128, 128], F32, "i1s")
    ef = P.sb([128, 128], F32, "ef")
    eu = P.sb([128, 128], U32, "eu")
    gex = P.sb([128, 8, 16], F32, "gex")
    gz = P.sb([128, 8], F32, "gz")
    aact = P.sb([128, 128], F32, "aact")
    wgt = P.sb([128, 128], F32, "wgt")
    acc = x1
    st = P.sb([128, 12], F32, "st")
    mv = P.sb([128, 2], F32, "mv")
    rstd = P.sb([128, 1], F32, "rstd")
    grow = [Res(f"grow{i}") for i in range(16)]

    def layernorm(src, dst, li):
        for c in range(2):
            P.op('dve', lambda e, c=c: e.bn_stats(st[:, c * 6:(c + 1) * 6], src[:, c * 512:(c + 1) * 512]),
                 reads=[src], writes=[st])
        P.op('dve', lambda e: e.bn_aggr(mv[:], st[:]), reads=[st], writes=[mv])
        P.op('dve', lambda e: e.tensor_scalar(rstd[:], mv[:, 1:2], LN_EPS, None, op0=ALU.add), reads=[mv], writes=[rstd])
        P.op('act', lambda e: e.activation(rstd[:], rstd[:], AF.Sqrt), reads=[rstd], writes=[rstd])
        P.op('dve', lambda e: e.reciprocal(rstd[:], rstd[:]), reads=[rstd], writes=[rstd])
        P.op('dve', lambda e: e.tensor_scalar(dst[:], src[:], mv[:, 0:1], rstd[:, 0:1], op0=ALU.subtract, op1=ALU.mult),
             reads=[src, mv, rstd], writes=[dst])
        P.op('dve', lambda e: e.tensor_tensor(dst[:], dst[:], lnp[:, (2 * li) * D:(2 * li + 1) * D], op=ALU.mult),
             reads=[dst, lnp], writes=[dst])
        P.op('dve', lambda e: e.tensor_tensor(dst[:], dst[:], lnp[:, (2 * li + 1) * D:(2 * li + 2) * D], op=ALU.add),
             reads=[dst, lnp], writes=[dst])

    def to_T(src):
        P.op('act', lambda e: e.copy(xb[:], src[:]), reads=[src], writes=[xb])
        for c in range(8):
            P.op('pe', lambda e, c=c: e.transpose(pT[:, c, :], xb[:, c * 128:(c + 1) * 128], ident[:]),
                 reads=[xb, ident], writes=[pT])
        P.op('dve', lambda e: e.tensor_copy(xT[:], pT[:]), excl=[pT], writes=[xT])

    def linear(lhsT_tile, w_b, n, pdst, off=0):
        for half in range(n // 512):
            for kc in range(8):
                P.op('pe', lambda e, half=half, kc=kc: e.matmul(
                    pdst[:, half * 512:(half + 1) * 512], lhsT_tile[:, kc, off:off + 128], w_b[:, kc, half * 512:(half + 1) * 512],
                    start=(kc == 0), stop=(kc == 7)), reads=[lhsT_tile, w_b], writes=[pdst])

    midx = P.sb([128, 32], U32, "midx")
    P.dma('sp', midx[:], midx_h.ap(), writes=[midx])

    def brow(rr):
        return bass.AP(gb_bf, rr * 1024, [[16384, 128], [1, 1024]])

    P.barrier()
    cst = [Res(f"cst{k}") for k in range(4)]
    stb = [r, x1, x2, xt]
    for k in range(128):
        tab = H['u'] if k < 64 else H['v']
        tabb = H['ub'] if k < 64 else H['vb']
        row0 = (k % 64) * 256
        sf = gbuf[:, 2 * (k % 4):2 * (k % 4) + 2, :]
        tb = stb[k % 4]
        sbv = bass.AP(tb.t.bitcast(BF16), 0, [[2048, 128], [1024, 2], [1, 1024]])
        P.dma('sp', sf, tab.ap()[row0:row0 + 256, :].rearrange("(a p) n -> p a n", p=128), writes=[cst[k % 4]])
        if k % 2 == 0:
            P.op('dve', lambda e, sf=sf, sbv=sbv: e.tensor_copy(sbv, sf), reads=[cst[k % 4]], writes=[tb])
        else:
            P.op('act', lambda e, sf=sf, sbv=sbv: e.copy(sbv, sf), reads=[cst[k % 4]], writes=[tb])
        P.dma('act', tabb.ap()[row0:row0 + 256, :].rearrange("(a p) n -> p a n", p=128), sbv, reads=[tb], writes=[Res("cvout")])
    P.barrier()
    for i in range(ntiles):
        t0 = i * 128
        P.dma('sp', xt[:], xin_h.ap()[t0:t0 + 128, :], reads=[xin_res], writes=[xt])
        if i % 4 == 0:
            for kc in range(8):
                P.dma('pool', None, None, reads=[mixg_res, midx], writes=[mixb],
                      fn=lambda e, kc=kc, i=i: e.indirect_dma_start(
                          out=mixb[:, kc, :], out_offset=None, in_=mixg_h.ap(),
                          in_offset=bass.IndirectOffsetOnAxis(ap=midx[:, (i // 4) * 8 + kc:(i // 4) * 8 + kc + 1], axis=0)))
        linear(mixb, wout_b, 1024, pB, off=(i % 4) * 128)
        P.op('dve', lambda e: e.scalar_tensor_tensor(out=r[:], in0=xt[:], scalar=ALPHA, in1=pB[:], op0=ALU.mult, op1=ALU.add),
             reads=[xt], excl=[pB], writes=[r])
        layernorm(r, x1, 0)
        to_T(x1)
        for j in range(8):
            for kc in range(8):
                P.op('pe', lambda e, j=j, kc=kc: e.matmul(pA[:, j * 128:(j + 1) * 128], wq_b[:, kc, j * 128:(j + 1) * 128],
                                                          xT[:, kc, :], start=(kc == 0), stop=(kc == 7)),
                     reads=[wq_b, xT], writes=[pA])
        P.op('act', lambda e: e.copy(qT[:], pA[:, 0:1024].rearrange("p (j t) -> p j t", j=8)), excl=[pA], writes=[qT])
        for h in range(4):
            for mc in range(2):
                for dc in range(2):
                    P.op('pe', lambda e, h=h, mc=mc, dc=dc: e.matmul(
                        pB[:, (h * 2 + mc) * 128:(h * 2 + mc + 1) * 128],
                        KT_b[:, h * 2 + dc, mc * 128:(mc + 1) * 128], qT[:, h * 2 + dc, :],
                        start=(dc == 0), stop=(dc == 1)), reads=[KT_b, qT], writes=[pB])
        P.op('act', lambda e: e.activation(E_b[:], pB[:].rearrange("p (j t) -> p j t", j=8), AF.Exp, scale=1.0 / 16.0),
             excl=[pB], writes=[E_b])
        for h in range(4):
            for mc in range(2):
                P.op('pe', lambda e, h=h, mc=mc: e.matmul(pC[:, h * 128:(h + 1) * 128], ones_bf[:], E_b[:, h * 2 + mc, :],
                                                          start=(mc == 0), stop=(mc == 1)),
                     reads=[ones_bf, E_b], writes=[pC])
        P.op('dve', lambda e: e.reciprocal(rz[:], pC[:].rearrange("p (h t) -> p h t", h=4)), excl=[pC], writes=[rz])
        for h in range(4):
            for dc in range(2):
                for mc in range(2):
                    P.op('pe', lambda e, h=h, dc=dc, mc=mc: e.matmul(
                        pA[:, 1024 + (h * 2 + dc) * 128:1024 + (h * 2 + dc + 1) * 128],
                        V_b[:, mc, h * 256 + dc * 128:h * 256 + (dc + 1) * 128], E_b[:, h * 2 + mc, :],
                        start=(mc == 0), stop=(mc == 1)), reads=[V_b, E_b], writes=[pA])
        for j in range(8):
            P.op('dve', lambda e, j=j: e.tensor_tensor(oT[:, j, :], pA[:, 1024 + j * 128:1024 + (j + 1) * 128], rz[:, j // 2, :],
                                                       op=ALU.mult), reads=[rz], excl=[pA], writes=[oT])
        linear(oT, wo_b, 1024, pB)
        P.op('dve', lambda e: e.scalar_tensor_tensor(out=r[:], in0=x1[:], scalar=ALPHA, in1=pB[:], op0=ALU.mult, op1=ALU.add),
             reads=[x1], excl=[pB], writes=[r])
        layernorm(r, x2, 1)
        to_T(x2)
        for hc in range(16):
            for kc in range(8):
                P.op('pe', lambda e, hc=hc, kc=kc: e.matmul(pA[:, hc * 128:(hc + 1) * 128], wpq_b[:, kc, hc * 128:(hc + 1) * 128],
                                                            xT[:, kc, :], start=(kc == 0), stop=(kc == 7)),
                     reads=[wpq_b, xT], writes=[pA])
        P.op('act', lambda e: e.copy(bass.AP(w8bf, 0, [[4096, 128], [1, 2048]]), pA[:]), excl=[pA], writes=[pqT])
        for hc in range(16):
            P.op('pe', lambda e, hc=hc: e.matmul(pA[:, hc * 128:(hc + 1) * 128], pqT_ap(hc), skT_b[:, hc % 2, :],
                                                 start=True, stop=True), reads=[pqT, skT_b], writes=[pA])
        P.op('act', lambda e: e.copy(sc[:], pA[:].rearrange("p (j t) -> p j t", j=16)), excl=[pA], writes=[sc])
        for hc in range(16):
            P.op('dve', lambda e, hc=hc: e.max(top[:, hc, 0:8], sc[:, hc, :]), reads=[sc], writes=[top])
            P.op('dve', lambda e, hc=hc: e.max_index(idxu[:, hc, 0:8], top[:, hc, 0:8], sc[:, hc, :]), reads=[sc, top], writes=[idxu])
            P.op('dve', lambda e, hc=hc: e.match_replace(sc[:, hc, :], top[:, hc, 0:8], sc[:, hc, :], -1e30),
                 reads=[sc, top], writes=[sc])
            P.op('dve', lambda e, hc=hc: e.max(top[:, hc, 8:16], sc[:, hc, :]), reads=[sc], writes=[top])
            P.op('dve', lambda e, hc=hc: e.max_index(idxu[:, hc, 8:16], top[:, hc, 8:16], sc[:, hc, :]), reads=[sc, top], writes=[idxu])
        P.op('dve', lambda e: e.tensor_copy(idxf[:], idxu[:]), reads=[idxu], writes=[idxf])
        P.op('dve', lambda e: e.tensor_tensor(AP(cand, 0, [[2048, 128], [256, 8], [16, 16], [1, 16]]),
                                              AP(top, 0, [[256, 128], [32, 8], [1, 16], [0, 16]]),
                                              AP(top, 16, [[256, 128], [32, 8], [0, 16], [1, 16]]), op=ALU.add),
             reads=[top], writes=[cand])
        for h in range(8):
            P.op('dve', lambda e, h=h: e.max(cs[:, h, 0:8], sc.t.rearrange('p a b -> p (a b)')[:, h * 256:(h + 1) * 256]), reads=[cand], writes=[cs])
            P.op('dve', lambda e, h=h: e.max_index(ciu[:, h, 0:8], cs[:, h, 0:8], sc.t.rearrange('p a b -> p (a b)')[:, h * 256:(h + 1) * 256]), reads=[cand, cs], writes=[ciu])
            P.op('dve', lambda e, h=h: e.match_replace(sc.t.rearrange('p a b -> p (a b)')[:, h * 256:(h + 1) * 256], cs[:, h, 0:8], sc.t.rearrange('p a b -> p (a b)')[:, h * 256:(h + 1) * 256], -1e30),
                 reads=[cand, cs], writes=[cand])
            P.op('dve', lambda e, h=h: e.max(cs[:, h, 8:16], sc.t.rearrange('p a b -> p (a b)')[:, h * 256:(h + 1) * 256]), reads=[cand], writes=[cs])
            P.op('dve', lambda e, h=h: e.max_index(ciu[:, h, 8:16], cs[:, h, 8:16], sc.t.rearrange('p a b -> p (a b)')[:, h * 256:(h + 1) * 256]), reads=[cand, cs], writes=[ciu])
        P.op('dve', lambda e: e.tensor_copy(cif[:], ciu[:].rearrange("p h k -> p (h k)")), reads=[ciu], writes=[cif])
        P.op('dve', lambda e: e.tensor_tensor(AP(big, 0, [[2048, 128], [16, 128], [1, 16]]),
                                              AP(cif, 0, [[128, 128], [1, 128], [0, 16]]),
                                              AP(thr16, 0, [[16, 128], [0, 128], [1, 16]]), op=ALU.is_ge),
             reads=[cif, thr16], writes=[big])
        P.op('dve', lambda e: e.tensor_reduce(k0f[:], AP(big, 0, [[2048, 128], [16, 128], [1, 16]]), axis=AX.X, op=ALU.add),
             reads=[big], writes=[k0f])
        P.op('dve', lambda e: e.scalar_tensor_tensor(out=k1f[:], in0=k0f[:], scalar=-16.0, in1=cif[:], op0=ALU.mult, op1=ALU.add),
             reads=[k0f, cif], writes=[k1f])
        for (kf, c, dst) in ((k0f, 0, i0s), (k1f, 1, i1s)):
            P.op('dve', lambda e, kf=kf: e.tensor_tensor(AP(big, 0, [[2048, 128], [16, 128], [1, 16]]),
                                                         AP(kf, 0, [[128, 128], [1, 128], [0, 16]]),
                                                         AP(iota16, 0, [[16, 128], [0, 128], [1, 16]]), op=ALU.is_equal),
                 reads=[kf, iota16], writes=[big])
            P.op('dve', lambda e, c=c: e.tensor_tensor(AP(big2, 0, [[2048, 128], [256, 8], [16, 16], [1, 16]]),
                                                       AP(big, 0, [[2048, 128], [256, 8], [16, 16], [1, 16]]),
                                                       AP(idxf, c * 16, [[256, 128], [32, 8], [0, 16], [1, 16]]), op=ALU.mult),
                 reads=[big, idxf], writes=[big2])
            P.op('dve', lambda e, dst=dst: e.tensor_reduce(dst[:], AP(big2, 0, [[2048, 128], [16, 128], [1, 16]]), axis=AX.X, op=ALU.add),
                 reads=[big2], writes=[dst])
        P.op('dve', lambda e: e.scalar_tensor_tensor(out=ef[:], in0=i0s[:], scalar=128.0, in1=i1s[:], op0=ALU.mult, op1=ALU.add),
             reads=[i0s, i1s], writes=[ef])
        P.op('dve', lambda e: e.tensor_copy(eu[:], ef[:]), reads=[ef], writes=[eu])
        P.op('dve', lambda e: e.tensor_tensor(gex[:], cs[:], AP(cs, 0, [[128, 128], [16, 8], [0, 16]]), op=ALU.subtract),
             reads=[cs], writes=[gex])
        P.op('act', lambda e: e.activation(gex[:], gex[:], AF.Exp), reads=[gex], writes=[gex])
        P.op('dve', lambda e: e.tensor_reduce(gz[:], gex[:], axis=AX.X, op=ALU.add), reads=[gex], writes=[gz])
        P.op('dve', lambda e: e.reciprocal(gz[:], gz[:]), reads=[gz], writes=[gz])
        P.op('dve', lambda e: e.tensor_tensor(gex[:], gex[:], AP(gz, 0, [[8, 128], [1, 8], [0, 16]]), op=ALU.mult),
             reads=[gex, gz], writes=[gex])
        for hk in range(128):
            rr = hk % 8
            P.dma('pool', None, None, reads=[eu], writes=[grow[rr]],
                  fn=lambda e, hk=hk, rr=rr: e.indirect_dma_start(
                      out=brow(rr), out_offset=None, in_=H['ub'].ap(),
                      in_offset=bass.IndirectOffsetOnAxis(ap=eu[:, hk:hk + 1], axis=0)))
            P.op('dve', lambda e, hk=hk, rr=rr: e.scalar_tensor_tensor(
                out=junk[:], in0=brow(rr), scalar=1.0, in1=x2[:], op0=ALU.mult, op1=ALU.mult,
                accum_out=aact[:, hk:hk + 1]), reads=[grow[rr], x2], writes=[junk, aact])
        P.op('act', lambda e: e.activation(aact[:], aact[:], AF.Gelu), reads=[aact], writes=[aact])
        P.op('dve', lambda e: e.tensor_tensor(wgt[:], aact[:], gex[:].rearrange("p h k -> p (h k)"), op=ALU.mult),
             reads=[aact, gex], writes=[wgt])
        P.op('act', lambda e: e.mul(acc[:], x2[:], ALPHA), reads=[x2], writes=[acc])
        for hk in range(128):
            rr = 8 + hk % 8
            P.dma('pool', None, None, reads=[eu], writes=[grow[rr]],
                  fn=lambda e, hk=hk, rr=rr: e.indirect_dma_start(
                      out=brow(rr), out_offset=None, in_=H['vb'].ap(),
                      in_offset=bass.IndirectOffsetOnAxis(ap=eu[:, hk:hk + 1], axis=0)))
            P.op('dve', lambda e, hk=hk, rr=rr: e.scalar_tensor_tensor(
                out=acc[:], in0=brow(rr), scalar=wgt[:, hk:hk + 1], in1=acc[:], op0=ALU.mult, op1=ALU.add),
                 reads=[grow[rr], wgt, acc], writes=[acc])
        layernorm(acc, x3, 2)
        P.dma('sp', xout_h.ap()[t0:t0 + 128, :], x3[:], reads=[x3], writes=[xout_res])
        if xsrc_h is not None:
            to_T(x3)
            for c4 in range(4):
                P.dma('sp', xsrc_h[c4].ap().rearrange("(k p) t -> p k t", p=128)[:, :, t0:t0 + 128], xT[:, 2 * c4:2 * c4 + 2, :],
                      reads=[xT], writes=[xsrc_res])

def build_fused(depth=4, nq_tiles=32, ntiles=16):
    nc = bass.Bass("TRN2", target_bir_lowering=False)
    D = 1024
    xT0_h = nc.dram_tensor("xT0", [D, S], F32, kind="ExternalInput")
    x0_h = nc.dram_tensor("x0", [NTOK, D], F32, kind="ExternalInput")
    memT_h = nc.dram_tensor("memT", [D, 256], F32, kind="ExternalInput")
    midx_h = nc.dram_tensor("midx", [128, 32], U32, kind="ExternalInput")
    ub_h = nc.dram_tensor("ub_scr", [16384, D], BF16)
    vb_h = nc.dram_tensor("vb_scr", [16384, D], BF16)
    HA, HB = [], []
    for l in range(depth):
        HA.append({"wa": nc.dram_tensor(f"wa{l}", [D, NWA], F32, kind="ExternalInput"),
                   "sp": nc.dram_tensor(f"sp{l}", [128, 16], F32, kind="ExternalInput"),
                   "lamqk": nc.dram_tensor(f"lamqk{l}", [128, 256], F32, kind="ExternalInput"),
                   "gda": nc.dram_tensor(f"gda{l}", [128, 128], F32, kind="ExternalInput"),
                   "gml": nc.dram_tensor(f"gml{l}", [128, 128], F32, kind="ExternalInput")})
        HB.append({"wout": nc.dram_tensor(f"wout{l}", [D, D], F32, kind="ExternalInput"),
                   "wq": nc.dram_tensor(f"wq{l}", [D, D], F32, kind="ExternalInput"),
                   "wkv": nc.dram_tensor(f"wkv{l}", [D, 2 * D], F32, kind="ExternalInput"),
                   "wo": nc.dram_tensor(f"wo{l}", [D, D], F32, kind="ExternalInput"),
                   "wpq": nc.dram_tensor(f"wpq{l}", [D, 2 * D], F32, kind="ExternalInput"),
                   "skT": nc.dram_tensor(f"skT{l}", [2, 128, 128], F32, kind="ExternalInput"),
                   "lnp": nc.dram_tensor(f"lnp{l}", [6, D], F32, kind="ExternalInput"),
                   "u": nc.dram_tensor(f"u{l}", [16384, D], F32, kind="ExternalInput"),
                   "v": nc.dram_tensor(f"v{l}", [16384, D], F32, kind="ExternalInput"),
                   "memT": memT_h, "ub": ub_h, "vb": vb_h})
    out_h = nc.dram_tensor("out", [NTOK, D], F32, kind="ExternalOutput")
    mixsrc = [[nc.dram_tensor(f"mixsrc{l}_{c}", [1024, 512], BF16) for c in range(4)] for l in range(depth)]
    mixgc = [[nc.dram_tensor(f"mixgc{l}_{c}", [4096, 512], BF16) for c in range(4)] for l in range(depth)]
    mixg = [nc.dram_tensor(f"mixg{l}", [16384, 512], BF16) for l in range(depth)]
    xsrc = [[nc.dram_tensor(f"xsrc{l}_{c}", [256, NTOK], BF16) for c in range(4)] for l in range(depth - 1)]
    xg = [None] + [[nc.dram_tensor(f"xg{l}_{c}", [1024, NTOK], BF16) for c in range(4)] for l in range(1, depth)]
    xres = [None] + [nc.dram_tensor(f"xres{l}", [NTOK, D], F32) for l in range(1, depth)]
    GROUPS = [[0, 1, 2, 3], [4, 5, 6, 7]]

    P = Prog(nc, n_dma_sems={'sp': 8, 'act': 4, 'pool': 20})
    C = make_consts(P)
    out_res = Res("out")
    xg_res = [Res(f"xg{l}") for l in range(depth)]
    xres_res = [Res(f"xres{l}") for l in range(depth)]
    for l in range(depth):
        mixsrc_res, mixg_res, xsrc_res = Res("mixsrc"), Res("mixg"), Res("xsrc")
        with P.scope():
            phase_a(P, nc, C, HA[l], xT0_h, xg[l], xg_res[l], mixsrc[l], mixsrc_res, nq_tiles)
        for c in range(4):
            gc_res = Res("mixgc")
            P.cc("AllGather", GROUPS, mixsrc[l][c].ap(), mixgc[l][c].ap(), reads=[mixsrc_res], writes=[gc_res], inc=1)
            P.dma('sp', mixg[l].ap()[c * 4096:(c + 1) * 4096, :].rearrange("(a b) n -> a (b n)", a=128),
                  mixgc[l][c].ap().rearrange("(a b) n -> a (b n)", a=128), reads=[gc_res], writes=[mixg_res])
        last = (l == depth - 1)
        with P.scope():
            phase_b(P, nc, C, HB[l], x0_h if l == 0 else xres[l], xres_res[l], mixg[l], mixg_res, midx_h,
                    out_h if last else xres[l + 1], out_res if last else xres_res[l + 1],
                    None if last else xsrc[l], xsrc_res, ntiles)
        if not last:
            for c in range(4):
                P.cc("AllGather", GROUPS, xsrc[l][c].ap(), xg[l + 1][c].ap(), reads=[xsrc_res], writes=[xg_res[l + 1]], inc=1)
    P.finish([out_res])
    P.emit()
    P.close()
    return nc


DEPTH = 4
_NC_CACHE = {}


def _head_inputs(w_in_l, conv_w_l, conv_b_l, i_bias_l, f_bias_l, lam_qk_l, da_g_l, ml_g_l, l, h):
    r = np.arange(h * 128, (h + 1) * 128)
    cols = np.concatenate([r, 512 + r, 1024 + r, 1536 + r, 2048 + r, 2560 + r, 3072 + r, [3584 + h], [3588 + h]])
    wa = np.ascontiguousarray(w_in_l[:, cols])
    lam_init = 0.8 - 0.6 * math.exp(-0.3 * l)
    sp = np.zeros((128, 16), np.float32)
    for g in range(2):
        ch = g * 512 + r
        for j in range(4):
            sp[:, g * 4 + j] = conv_w_l[j, ch]
        sp[:, 8 + g] = conv_b_l[ch]
    sp[:, 10] = i_bias_l[h]
    sp[:, 11] = f_bias_l[h]
    sp[:, 12] = lam_init
    sp[:, 13] = 1.0 - lam_init
    lamqk = np.ascontiguousarray(np.broadcast_to(lam_qk_l.reshape(1, 256), (128, 256)))
    gda = np.ascontiguousarray(np.broadcast_to(da_g_l[None, :], (128, 128)))
    gml = np.ascontiguousarray(np.broadcast_to(ml_g_l[None, r], (128, 128)))
    return {f"wa{l}": wa, f"sp{l}": sp, f"lamqk{l}": lamqk, f"gda{l}": gda, f"gml{l}": gml}


def make_in_maps(depth, x, mem, w_in, i_bias, f_bias, conv_w, conv_b, lam_qk, da_norm_g, ml_norm_g, w_out,
                 ln1_g, ln1_b, wq_mem, wkv_mem, wo_mem, ln2_g, ln2_b, w_pq, sub_keys, u_tab, v_tab, ln3_g, ln3_b):
    f = lambda a: np.asarray(a, dtype=np.float32)
    x = f(x); mem = f(mem)
    B = x.shape[0]
    shared = {}
    for l in range(depth):
        shared[f"wout{l}"] = f(w_out[l]); shared[f"wq{l}"] = f(wq_mem[l]); shared[f"wkv{l}"] = f(wkv_mem[l])
        shared[f"wo{l}"] = f(wo_mem[l]); shared[f"wpq{l}"] = f(w_pq[l])
        shared[f"skT{l}"] = np.ascontiguousarray(f(sub_keys[l]).transpose(0, 2, 1))
        shared[f"lnp{l}"] = np.ascontiguousarray(np.stack([f(ln1_g[l]), f(ln1_b[l]), f(ln2_g[l]), f(ln2_b[l]), f(ln3_g[l]), f(ln3_b[l])]))
        shared[f"u{l}"] = f(u_tab[l]); shared[f"v{l}"] = f(v_tab[l])
    in_maps = []
    for b in range(B):
        xT = np.ascontiguousarray(x[b].T)
        memT = np.ascontiguousarray(mem[b].T)
        for r_ in range(4):
            m = dict(shared)
            m["xT0"] = xT
            m["x0"] = np.ascontiguousarray(x[b, r_ * 2048:(r_ + 1) * 2048])
            m["memT"] = memT
            p = np.arange(128, dtype=np.int64)[:, None, None]
            qr = np.arange(4, dtype=np.int64)[None, :, None]
            kc = np.arange(8, dtype=np.int64)[None, None, :]
            midx = ((((r_ * 4 + (kc % 4)) * 4 + qr) * 2 + (kc // 4)) * 128) + p
            m["midx"] = np.ascontiguousarray(midx.reshape(128, 32).astype(np.uint32))
            for l in range(depth):
                m.update(_head_inputs(f(w_in[l]), f(conv_w[l]), f(conv_b[l]), f(i_bias[l]), f(f_bias[l]), f(lam_qk[l]),
                                      f(da_norm_g[l]), f(ml_norm_g[l]), l, r_))
            in_maps.append(m)
    return in_maps


def kernel(x, mem, w_in, i_bias, f_bias, conv_w, conv_b, lam_qk, da_norm_g, ml_norm_g, w_out,
           ln1_g, ln1_b, wq_mem, wkv_mem, wo_mem, ln2_g, ln2_b, w_pq, sub_keys, u_tab, v_tab,
           ln3_g, ln3_b):
    if "f" not in _NC_CACHE:
        _NC_CACHE["f"] = build_fused(DEPTH)
    in_maps = make_in_maps(DEPTH, x, mem, w_in, i_bias, f_bias, conv_w, conv_b, lam_qk, da_norm_g, ml_norm_g, w_out,
                           ln1_g, ln1_b, wq_mem, wkv_mem, wo_mem, ln2_g, ln2_b, w_pq, sub_keys, u_tab, v_tab, ln3_g, ln3_b)
    res = run_bass_kernel_spmd(_NC_CACHE["f"], in_maps, core_ids=list(range(8))).results
    B, S_, D = np.asarray(x).shape
    out = np.empty((B, S_, D), np.float32)
    for b in range(B):
        for r_ in range(4):
            out[b, r_ * 2048:(r_ + 1) * 2048] = res[b * 4 + r_]["out"]
    return out
```

```python
import math
import contextlib
import numpy as np
import ml_dtypes
import concourse.bass as bass
import concourse.mybir as mybir
from concourse.bass_utils import run_bass_kernel_spmd

F32 = mybir.dt.float32
BF16 = mybir.dt.bfloat16
I32 = mybir.dt.int32
U32 = mybir.dt.uint32
AF = mybir.ActivationFunctionType
ALU = mybir.AluOpType
AX = mybir.AxisListType

SEM_LIMIT = 12000


class Res:
    __slots__ = ("w", "r", "name")

    def __init__(self, name=""):
        self.w = None
        self.r = []
        self.name = name


class Tile:
    def __init__(self, t, res=None, name=""):
        self.t = t
        self.res = res if res is not None else Res(name)

    def __getitem__(self, k):
        return self.t[k]


def _res(x):
    return x.res if isinstance(x, Tile) else x


class Prog:
    ENGS = ("pe", "act", "dve", "pool", "sp")

    def __init__(self, nc, n_dma_sems=6):
        self.nc = nc
        self.es = contextlib.ExitStack()
        self.cur_es = self.es
        self.nsem = 0
        self.streams = {e: [] for e in self.ENGS}
        self.cur_sem = {}
        self.cur_val = {}
        for e in self.ENGS:
            self._new_eng_sem(e)
        self.known = {e: {} for e in self.ENGS}
        self.dsem = {}
        self.nds = n_dma_sems
        self.dma_rr = {e: 0 for e in self.ENGS}
        self.ntile = 0
        self.last_tok = {}

    def sem(self, name):
        self.nsem += 1
        return self.es.enter_context(self.nc.semaphore(f"{name}_{self.nsem}"))

    def _new_eng_sem(self, e):
        self.nsem = getattr(self, "nsem", 0)
        self.cur_sem[e] = self.sem("c" + e)
        self.cur_val[e] = 0

    def sb(self, shape, dt, name=None):
        self.ntile += 1
        nm = f"{name or 't'}_{self.ntile}"
        t = self.cur_es.enter_context(self.nc.sbuf_tensor(nm, list(shape), dt))
        return Tile(t, name=nm)

    def ps(self, shape, dt=F32, name=None):
        self.ntile += 1
        nm = f"{name or 'p'}_{self.ntile}"
        t = self.cur_es.enter_context(self.nc.psum_tensor(nm, list(shape), dt))
        return Tile(t, name=nm)

    def _collect(self, eng, reads, writes, excl=()):
        toks = []
        for r in reads:
            r = _res(r)
            if r.w is not None:
                toks.append(r.w)
        for w in list(writes) + list(excl):
            w = _res(w)
            if w.w is not None:
                toks.append(w.w)
            toks.extend(w.r)
        waits = []
        kn = self.known[eng]
        best = {}
        for (s, v, src) in toks:
            if src == "pe" and eng == "pe":
                continue
            if kn.get(s, 0) >= v:
                continue
            if best.get(s, (None, 0))[1] < v:
                best[s] = (s, v)
        for s, (sh, v) in best.items():
            kn[s] = v
            waits.append((sh, v))
        return waits

    def _commit(self, tok, reads, writes, excl=()):
        for r in reads:
            _res(r).r.append(tok)
        for w in list(writes) + list(excl):
            w = _res(w)
            w.w = tok
            w.r = []

    def op(self, eng, fn, reads=(), writes=(), excl=()):
        waits = self._collect(eng, reads, writes, excl)
        if self.cur_val[eng] >= SEM_LIMIT:
            self._new_eng_sem(eng)
        s = self.cur_sem[eng]
        self.cur_val[eng] += 1
        v = self.cur_val[eng]
        tok = (s, v, eng)
        self.last_tok[eng] = tok
        self._commit(tok, reads, writes, excl)
        self.streams[eng].append((waits, fn, s, 1))
        return tok

    def dma(self, q, out, in_, reads=(), writes=(), fn=None, inc=16):
        key = q
        nds = self.nds[q] if isinstance(self.nds, dict) else self.nds
        if key not in self.dsem:
            self.dsem[key] = [[self.sem("d" + q), 0] for _ in range(nds)]
        k = self.dma_rr[q]
        self.dma_rr[q] = (k + 1) % nds
        slot = self.dsem[key][k]
        waits = self._collect(q, reads, writes)
        kn = self.known[q]
        if slot[1] > 0 and kn.get(slot[0], 0) < slot[1]:
            waits.append((slot[0], slot[1]))
            kn[slot[0]] = slot[1]
        if slot[1] + inc > SEM_LIMIT:
            slot[0] = self.sem("d" + q)
            slot[1] = 0
        slot[1] += inc
        tok = (slot[0], slot[1], "dma")
        self._commit(tok, reads, writes)
        if fn is None:
            fn = lambda e, o=out, i=in_: e.dma_start(out=o, in_=i)
        self.streams[q].append((waits, fn, slot[0], inc))
        return tok


    def cc(self, kind, groups, in_ap, out_ap, reads=(), writes=(), inc=16):
        fn = lambda e: e.collective_compute(kind, ALU.bypass, replica_groups=groups, ins=[in_ap], outs=[out_ap])
        self._cc_inc = inc
        return self.dma('pool', None, None, reads=reads, writes=writes, fn=fn, inc=inc)


    @contextlib.contextmanager
    def scope(self):
        prev = self.cur_es
        es = contextlib.ExitStack()
        self.cur_es = es
        try:
            yield
            self.barrier()
            self.emit()
        finally:
            self.cur_es = prev
            es.close()

    def barrier(self):
        toks = []
        for e in self.ENGS:
            if e in self.last_tok:
                toks.append(self.last_tok[e])
        for q, slots in self.dsem.items():
            for (sh, v) in slots:
                if v > 0:
                    toks.append((sh, v, "dma"))
        for e in self.ENGS:
            kn = self.known[e]
            waits = []
            for (sh, v, src) in toks:
                if src == e and e == "pe":
                    continue
                if kn.get(sh, 0) < v:
                    kn[sh] = v
                    waits.append((sh, v))
            if waits:
                self.streams[e].append((waits, None, None, 0))

    def finish(self, all_res):
        waits = self._collect("sp", all_res, [])
        kn = self.known["sp"]
        for q, slots in self.dsem.items():
            for (sh, v) in slots:
                if v > 0 and kn.get(sh, 0) < v:
                    waits.append((sh, v))
                    kn[sh] = v
        self.streams["sp"].append((waits, None, None, 0))

    def emit(self):
        nc = self.nc
        streams = self.streams

        def run(engname, eng):
            for (waits, fn, s, inc) in streams[engname]:
                for (sh, v) in waits:
                    eng.wait_ge(sh, v)
                if fn is not None:
                    ins = fn(eng)
                    ins.then_inc(s, inc)

        with nc.Block() as block:
            @block.tensor
            def _(e):
                run("pe", e)

            @block.scalar
            def _(e):
                run("act", e)

            @block.vector
            def _(e):
                run("dve", e)

            @block.gpsimd
            def _(e):
                run("pool", e)

            @block.sync
            def _(e):
                run("sp", e)
        self.streams = {e: [] for e in self.ENGS}

    def close(self):
        self.es.close()


LN_EPS = 1e-5
ALPHA = 8.0 ** 0.25
S = 8192
NWA = 898
NTOK = 2048


def AP(t, off, dims):
    return bass.AP(t.t if isinstance(t, Tile) else t, off, dims)


def make_consts(P):
    C = {}
    identf = P.sb([128, 128], F32, "identf")
    ident = P.sb([128, 128], BF16, "ident")
    trif = P.sb([128, 128], F32, "trif")
    trib = P.sb([128, 128], BF16, "trib")
    onesf = P.sb([128, 128], F32, "onesf")
    ones_bf = P.sb([128, 128], BF16, "ones")
    iota16 = P.sb([128, 16], F32, "iota16")
    thr16 = P.sb([128, 16], F32, "thr16")
    P.op('dve', lambda e: e.memset(identf[:], 0.0), writes=[identf])
    P.op('pool', lambda e: e.affine_select(out=identf[:], in_=identf[:], pattern=[[-1, 128]],
                                           compare_op=ALU.not_equal, fill=1.0, base=0, channel_multiplier=1),
         reads=[identf], writes=[identf])
    P.op('dve', lambda e: e.tensor_copy(ident[:], identf[:]), reads=[identf], writes=[ident])
    P.op('dve', lambda e: e.memset(onesf[:], 1.0), writes=[onesf])
    P.op('dve', lambda e: e.memset(ones_bf[:], 1.0), writes=[ones_bf])
    P.op('pool', lambda e: e.iota(trif[:], pattern=[[1, 128]], base=0, channel_multiplier=-1,
                                  allow_small_or_imprecise_dtypes=True), writes=[trif])
    P.op('dve', lambda e: e.tensor_single_scalar(trif[:], trif[:], 0.0, op=ALU.is_ge), reads=[trif], writes=[trif])
    P.op('dve', lambda e: e.tensor_copy(trib[:], trif[:]), reads=[trif], writes=[trib])
    P.op('pool', lambda e: e.iota(iota16[:], pattern=[[1, 16]], base=0, channel_multiplier=0,
                                  allow_small_or_imprecise_dtypes=True), writes=[iota16])
    P.op('pool', lambda e: e.iota(thr16[:], pattern=[[16, 16]], base=16, channel_multiplier=0,
                                  allow_small_or_imprecise_dtypes=True), writes=[thr16])
    C.update(identf=identf, ident=ident, trif=trif, trib=trib, onesf=onesf, ones_bf=ones_bf, iota16=iota16, thr16=thr16)
    return C


def phase_a(P, nc, C, H, xT_h, xg_h, xg_res, mixsrc_h, mixsrc_res, nq_tiles):
    ident, trif, trib, onesf = C['ident'], C['trif'], C['trib'], C['onesf']
    D = 1024
    ntok = nq_tiles * 256
    nblk = ntok // 128
    spt = P.sb([128, 16], F32, "spt")
    lamq = P.sb([128, 256], F32, "lamq")
    gda = P.sb([128, 128], F32, "gda")
    gml = P.sb([128, 128], F32, "gml")
    P.dma('sp', spt[:], H['sp'].ap(), writes=[spt])
    P.dma('sp', lamq[:], H['lamqk'].ap(), writes=[lamq])
    P.dma('sp', gda[:], H['gda'].ap(), writes=[gda])
    P.dma('sp', gml[:], H['gml'].ap(), writes=[gml])
    lt = P.sb([128, 128], F32, "lt")
    lsum = P.sb([128, 2], F32, "lsum")
    neglam = P.sb([128, 1], F32, "neglam")
    P.op('dve', lambda e: e.tensor_tensor(lt[:].rearrange("p (a d) -> p a d", a=2),
                                          AP(lamq, 0, [[256, 128], [128, 2], [1, 64]]),
                                          AP(lamq, 64, [[256, 128], [128, 2], [1, 64]]), op=ALU.mult),
         reads=[lamq], writes=[lt])
    P.op('dve', lambda e: e.tensor_reduce(lsum[:], lt[:].rearrange("p (a d) -> p a d", a=2), axis=AX.X, op=ALU.add),
         reads=[lt], writes=[lsum])
    P.op('act', lambda e: e.activation(lsum[:], lsum[:], AF.Exp), reads=[lsum], writes=[lsum])
    P.op('dve', lambda e: e.tensor_tensor(neglam[:], lsum[:, 1:2], lsum[:, 0:1], op=ALU.subtract), reads=[lsum], writes=[neglam])
    P.op('dve', lambda e: e.tensor_tensor(neglam[:], neglam[:], spt[:, 12:13], op=ALU.subtract), reads=[neglam, spt], writes=[neglam])
    P.op('dve', lambda e: e.tensor_scalar(gda[:], gda[:], spt[:, 13:14], None, op0=ALU.mult), reads=[gda, spt], writes=[gda])

    wab = P.sb([128, 8, NWA], BF16, "wab")
    stg = P.sb([128, 1024], F32, "stg")
    wap = H['wa'].ap().rearrange("(c p) n -> p c n", p=128)
    for kc in range(8):
        P.dma('sp', stg[:, 0:NWA], wap[:, kc, :], writes=[stg])
        P.op('dve', lambda e, kc=kc: e.tensor_copy(wab[:, kc, :], stg[:, 0:NWA]), reads=[stg], writes=[wab])

    daqT = [P.sb([64, S], BF16, f"daqT{c}") for c in range(2)]
    dakT = [P.sb([64, S], BF16, f"dakT{c}") for c in range(2)]
    Vaug = P.sb([128, 64, 129], BF16, "Vaug")
    mlqT = P.sb([128, S], BF16, "mlqT")
    mlkT = P.sb([128, S], BF16, "mlkT")
    mlV = P.sb([128, 64, 129], BF16, "mlV")
    sigo = P.sb([128, 64, 128], BF16, "sigo")
    gates = P.sb([128, 64, 2], F32, "gates")
    P.op('dve', lambda e: e.memset(Vaug[:, :, 128:129], 1.0), writes=[Vaug])
    P.op('dve', lambda e: e.memset(mlV[:, :, 128:129], 1.0), writes=[mlV])

    banks = [P.ps([128, 512], F32, f"bank{i}") for i in range(8)]

    xs = P.sb([128, 8, 256], F32, "xs")
    xb = P.sb([128, 8, 256], BF16, "xb")
    pre = [P.sb([128, 3 + 256], F32, f"pre{i}") for i in range(2)]
    cacc = P.sb([128, 256], F32, "cacc")
    ctmp = P.sb([128, 256], F32, "ctmp")
    xT_v = xT_h.ap().rearrange("(c p) t -> p c t", p=128) if xg_h is None else None
    for i in range(2):
        P.op('dve', lambda e, i=i: e.memset(pre[i][:, 0:3], 0.0), writes=[pre[i]])
    for ti in range(nq_tiles):
        t0 = ti * 256
        if xg_h is None:
            P.dma('sp', xs[:], xT_v[:, :, t0:t0 + 256], writes=[xs])
            P.op('pool', lambda e: e.tensor_copy(xb[:], xs[:]), reads=[xs], writes=[xb])
        else:
            jr, tl = t0 // 2048, t0 % 2048
            for c4 in range(4):
                P.dma('sp', xb[:, 2 * c4:2 * c4 + 2, :],
                      xg_h[c4].ap()[jr * 256:(jr + 1) * 256, tl:tl + 256].rearrange("(k p) t -> p k t", p=128),
                      reads=[xg_res], writes=[xb])
        for g in range(4):
            pb = banks[g // 2]
            col0 = (g // 2) * 128 + (g % 2) * 64
            for kc in range(8):
                P.op('pe', lambda e, g=g, kc=kc, pb=pb, col0=col0: e.matmul(
                    pb[0:64, (g % 2) * 256:(g % 2) * 256 + 256], wab[:, kc, col0:col0 + 64], xb[:, kc, :],
                    start=(kc == 0), stop=(kc == 7)), reads=[wab, xb], writes=[pb])
        for c in range(2):
            P.op('act', lambda e, t0=t0, c=c: e.copy(daqT[c][:, t0:t0 + 256], banks[0][0:64, c * 256:c * 256 + 256]),
                 excl=[banks[0]], writes=[daqT[c]])
            P.op('dve', lambda e, t0=t0, c=c: e.tensor_copy(dakT[c][:, t0:t0 + 256], banks[1][0:64, c * 256:c * 256 + 256]),
                 excl=[banks[1]], writes=[dakT[c]])
        for g in range(2):
            col0 = 384 + g * 128
            for kc in range(8):
                P.op('pe', lambda e, g=g, kc=kc, col0=col0: e.matmul(
                    banks[2][:, g * 256:g * 256 + 256], wab[:, kc, col0:col0 + 128], xb[:, kc, :],
                    start=(kc == 0), stop=(kc == 7)), reads=[wab, xb], writes=[banks[2]])
        for g in range(2):
            pr = pre[g]
            if ti > 0:
                P.op('dve', lambda e, pr=pr: e.tensor_copy(pr[:, 0:3], pr[:, 256:259]), reads=[pr], writes=[pr])
            P.op('act', lambda e, g=g, pr=pr: e.copy(pr[:, 3:259], banks[2][:, g * 256:g * 256 + 256]),
                 excl=[banks[2]], writes=[pr])
            P.op('dve', lambda e, g=g, pr=pr: e.tensor_scalar(cacc[:], pr[:, 3:259], spt[:, g * 4 + 3:g * 4 + 4], spt[:, 8 + g:9 + g],
                                                              op0=ALU.mult, op1=ALU.add), reads=[pr, spt], writes=[cacc])
            for j in range(3):
                P.op('dve', lambda e, g=g, pr=pr, j=j: e.scalar_tensor_tensor(
                    out=cacc[:], in0=pr[:, j:j + 256], scalar=spt[:, g * 4 + j:g * 4 + j + 1], in1=cacc[:],
                    op0=ALU.mult, op1=ALU.add), reads=[pr, spt, cacc], writes=[cacc])
            if g == 0:
                P.op('act', lambda e, t0=t0: e.activation(mlqT[:, t0:t0 + 256], cacc[:], AF.Silu), reads=[cacc], writes=[mlqT])
            else:
                P.op('act', lambda e: e.activation(ctmp[:], cacc[:], AF.Silu), reads=[cacc], writes=[ctmp])
                P.op('dve', lambda e, t0=t0: e.tensor_scalar(mlkT[:, t0:t0 + 256], ctmp[:], 128.0 ** -0.5, None, op0=ALU.mult),
                     reads=[ctmp], writes=[mlkT])
        for sub in range(2):
            blk = ti * 2 + sub
            pb = banks[3 + sub]
            for gi, col0 in enumerate((256, 640, 768)):
                for kc in range(8):
                    P.op('pe', lambda e, sub=sub, gi=gi, col0=col0, kc=kc, pb=pb: e.matmul(
                        pb[:, gi * 128:(gi + 1) * 128], xb[:, kc, sub * 128:(sub + 1) * 128], wab[:, kc, col0:col0 + 128],
                        start=(kc == 0), stop=(kc == 7)), reads=[wab, xb], writes=[pb])
            for kc in range(8):
                P.op('pe', lambda e, sub=sub, kc=kc, blk=blk: e.matmul(
                    banks[5][:, blk * 2:blk * 2 + 2], xb[:, kc, sub * 128:(sub + 1) * 128], wab[:, kc, 896:898],
                    start=(kc == 0), stop=(kc == 7)), reads=[wab, xb], writes=[banks[5]])
            P.op('act', lambda e, blk=blk, pb=pb: e.copy(Vaug[:, blk, 0:128], pb[:, 0:128]), excl=[pb], writes=[Vaug])
            P.op('dve', lambda e, blk=blk, pb=pb: e.tensor_copy(mlV[:, blk, 0:128], pb[:, 128:256]), excl=[pb], writes=[mlV])
            P.op('act', lambda e, blk=blk, pb=pb: e.activation(sigo[:, blk, :], pb[:, 256:384], AF.Sigmoid), excl=[pb], writes=[sigo])
    P.op('dve', lambda e: e.tensor_copy(gates[:, 0:nblk, :], banks[5][:, 0:nblk * 2].rearrange("p (b g) -> p b g", g=2)),
         excl=[banks[5]], writes=[gates])

    Eb = [P.sb([128, 2, 256], BF16, f"Eb{i}") for i in range(2)]
    o_da = P.sb([128, 128], F32, "o_da")
    o_out = [P.sb([128, 128], BF16, f"o_out{i}") for i in range(2)]
    oT_sb = [P.sb([128, 128], BF16, f"oT_sb{i}") for i in range(2)]
    pT6 = banks[6].t.bitcast(BF16)
    pT7 = banks[7].t.bitcast(BF16)
    junk = P.sb([128, 128], F32, "junk")
    rz = P.sb([128, 2], F32, "rz")
    ss = P.sb([128, 1], F32, "ss")
    step = 0
    nout = 0
    for qt in range(nq_tiles):
        q0 = qt * 256
        nj = 2 * qt + 2
        for j in range(nj):
            pS = banks[step % 2]
            E = Eb[step % 2]
            step += 1
            subs = (0, 1) if j < nj - 1 else (1,)
            qa, qb = (0, 256) if j < nj - 1 else (128, 256)
            for c in range(2):
                P.op('pe', lambda e, c=c, j=j, pS=pS, qa=qa, qb=qb, q0=q0: e.matmul(
                    pS[:, c * 256 + qa:c * 256 + qb], dakT[c][:, j * 128:(j + 1) * 128], daqT[c][:, q0 + qa:q0 + qb],
                    start=True, stop=True), reads=[dakT[c], daqT[c]], writes=[pS])
            P.op('act', lambda e, E=E, pS=pS, qa=qa, qb=qb: e.activation(
                E[:, :, qa:qb], pS[:, :].rearrange("p (c q) -> p c q", c=2)[:, :, qa:qb], AF.Exp, scale=0.125),
                 excl=[pS], writes=[E])
            if j >= nj - 2:
                sm = 0 if j == nj - 2 else 1
                P.op('pool', lambda e, E=E, sm=sm: e.tensor_tensor(
                    E[:, :, sm * 128:(sm + 1) * 128], E[:, :, sm * 128:(sm + 1) * 128],
                    AP(trib, 0, [[128, 128], [0, 2], [1, 128]]), op=ALU.mult), reads=[E, trib], writes=[E])
            for s_ in subs:
                last_j = nj - 2 if s_ == 0 else nj - 1
                for c in range(2):
                    pacc = banks[2 + s_ * 2 + c]
                    P.op('pe', lambda e, E=E, s_=s_, c=c, j=j, pacc=pacc, last_j=last_j: e.matmul(
                        pacc[:, 0:129], E[:, c, s_ * 128:(s_ + 1) * 128], Vaug[:, j, :],
                        start=(j == 0), stop=(j == last_j)), reads=[E, Vaug], writes=[pacc])
        for s_ in range(2):
            p0, p1 = banks[2 + s_ * 2], banks[2 + s_ * 2 + 1]
            oo = o_out[nout % 2]
            nout += 1
            P.op('dve', lambda e, p0=p0: e.reciprocal(rz[:, 0:1], p0[:, 128:129]), excl=[p0], writes=[rz])
            P.op('dve', lambda e, p1=p1: e.reciprocal(rz[:, 1:2], p1[:, 128:129]), excl=[p1], writes=[rz])
            P.op('dve', lambda e: e.tensor_tensor(rz[:, 1:2], rz[:, 1:2], neglam[:], op=ALU.mult), reads=[rz, neglam], writes=[rz])
            P.op('dve', lambda e, p0=p0: e.tensor_scalar(o_da[:], p0[:, 0:128], rz[:, 0:1], None, op0=ALU.mult),
                 reads=[rz], excl=[p0], writes=[o_da])
            P.op('dve', lambda e, p1=p1: e.scalar_tensor_tensor(out=o_da[:], in0=p1[:, 0:128], scalar=rz[:, 1:2], in1=o_da[:],
                                                                op0=ALU.mult, op1=ALU.add), reads=[rz, o_da], excl=[p1], writes=[o_da])
            P.op('dve', lambda e: e.scalar_tensor_tensor(out=junk[:], in0=o_da[:], scalar=1.0, in1=o_da[:], op0=ALU.mult, op1=ALU.mult,
                                                         accum_out=ss[:]), reads=[o_da], writes=[junk, ss])
            P.op('dve', lambda e: e.tensor_scalar(ss[:], ss[:], 1.0 / 128.0, LN_EPS, op0=ALU.mult, op1=ALU.add), reads=[ss], writes=[ss])
            P.op('act', lambda e: e.activation(ss[:], ss[:], AF.Sqrt), reads=[ss], writes=[ss])
            P.op('dve', lambda e: e.reciprocal(ss[:], ss[:]), reads=[ss], writes=[ss])
            P.op('dve', lambda e, oo=oo: e.scalar_tensor_tensor(out=oo[:], in0=o_da[:], scalar=ss[:, 0:1], in1=gda[:], op0=ALU.mult, op1=ALU.mult),
                 reads=[o_da, ss, gda], writes=[oo])
            g_blk = (q0 + s_ * 128) // 128
            oT_ = oT_sb[nout % 2]
            P.op('pe', lambda e, oo=oo: e.transpose(bass.AP(pT6, 0, [[1024, 128], [1, 128]]), oo[:], ident[:]),
                 reads=[oo, ident], writes=[banks[6]])
            P.op('act', lambda e, oT_=oT_: e.copy(oT_[:], bass.AP(pT6, 0, [[1024, 128], [1, 128]])), excl=[banks[6]], writes=[oT_])
            jj, qq, ww = g_blk // 16, (g_blk % 16) // 4, g_blk % 4
            P.dma('sp', mixsrc_h[jj].ap()[(qq * 2) * 128:(qq * 2 + 1) * 128, ww * 128:(ww + 1) * 128], oT_[:], reads=[oT_], writes=[mixsrc_res])

    nch = nblk
    lf = P.sb([128, 64], F32, "lf")
    bcs = P.sb([128, 64], F32, "bcs")
    ek = P.sb([128, 64], F32, "ek")
    eb = P.sb([128, 64], F32, "eb")
    ebL = P.sb([128, 64], F32, "ebL")
    nfb = P.sb([128, 1], F32, "nfb")
    P.op('dve', lambda e: e.tensor_scalar(nfb[:], spt[:, 11:12], -1.0, None, op0=ALU.mult), reads=[spt], writes=[nfb])
    P.op('act', lambda e: e.activation(lf[:, 0:nch], gates[:, 0:nch, 1], AF.Exp, bias=nfb[:, 0:1], scale=-1.0), reads=[gates, nfb], writes=[lf])
    P.op('dve', lambda e: e.tensor_scalar(lf[:, 0:nch], lf[:, 0:nch], 1.0, None, op0=ALU.add), reads=[lf], writes=[lf])
    P.op('act', lambda e: e.activation(lf[:, 0:nch], lf[:, 0:nch], AF.Ln), reads=[lf], writes=[lf])
    P.op('dve', lambda e: e.tensor_scalar(lf[:, 0:nch], lf[:, 0:nch], -1.0, None, op0=ALU.mult), reads=[lf], writes=[lf])
    P.op('pe', lambda e: e.matmul(banks[0][:, 0:nch], trif[:], lf[:, 0:nch], start=True, stop=True), reads=[trif, lf], writes=[banks[0]])
    P.op('pe', lambda e: e.matmul(banks[1][:, 0:nch], onesf[:], lf[:, 0:nch], start=True, stop=True), reads=[onesf, lf], writes=[banks[1]])
    P.op('dve', lambda e: e.tensor_copy(bcs[:, 0:nch], banks[0][:, 0:nch]), excl=[banks[0]], writes=[bcs])
    P.op('act', lambda e: e.activation(eb[:, 0:nch], bcs[:, 0:nch], AF.Exp), reads=[bcs], writes=[eb])
    P.op('act', lambda e: e.activation(ebL[:, 0:nch], banks[1][:, 0:nch], AF.Exp), excl=[banks[1]], writes=[ebL])
    P.op('dve', lambda e: e.tensor_tensor(ek[:, 0:nch], gates[:, 0:nch, 0], bcs[:, 0:nch], op=ALU.subtract), reads=[gates, bcs], writes=[ek])
    P.op('act', lambda e: e.activation(ek[:, 0:nch], ek[:, 0:nch], AF.Exp, bias=spt[:, 10:11], scale=1.0), reads=[ek, spt], writes=[ek])

    Dst = [P.sb([128, 129], F32, f"Dst{i}") for i in range(2)]
    Cb = [P.sb([128, 129], BF16, f"Cb{i}") for i in range(2)]
    ktok = [P.sb([128, 128], BF16, f"ktok{i}") for i in range(2)]
    vp = [P.sb([128, 129], BF16, f"vp{i}") for i in range(2)]
    ATb = [P.sb([128, 128], BF16, f"ATb{i}") for i in range(2)]
    hh = P.sb([128, 128], F32, "hh")
    hsm = P.sb([128, 4], F32, "hsm")
    st6 = P.sb([128, 6], F32, "st6")
    mvv = P.sb([128, 2], F32, "mvv")
    rstd = P.sb([128, 1], F32, "rstd")
    ho = [P.sb([128, 128], BF16, f"ho{i}") for i in range(2)]
    pKt = banks[2].t.bitcast(BF16)
    for c in range(nch):
        sl = slice(c * 128, (c + 1) * 128)
        kt, v_, at = ktok[c % 2], vp[c % 2], ATb[c % 2]
        P.op('pe', lambda e, sl=sl: e.transpose(bass.AP(pKt, 0, [[1024, 128], [1, 128]]), mlkT[:, sl], ident[:]),
             reads=[mlkT, ident], writes=[banks[2]])
        P.op('act', lambda e, kt=kt: e.copy(kt[:], bass.AP(pKt, 0, [[1024, 128], [1, 128]])), excl=[banks[2]], writes=[kt])
        P.op('pool', lambda e, c=c, v_=v_: e.tensor_scalar(v_[:], mlV[:, c, :], ek[:, c:c + 1], None, op0=ALU.mult),
             reads=[mlV, ek], writes=[v_])
        P.op('pe', lambda e, kt=kt, v_=v_: e.matmul(banks[3][:, 0:129], kt[:], v_[:], start=True, stop=True),
             reads=[kt, v_], writes=[banks[3]])
        P.op('pe', lambda e, sl=sl: e.matmul(banks[4][:, 0:128], mlkT[:, sl], mlqT[:, sl], start=True, stop=True),
             reads=[mlkT, mlqT], writes=[banks[4]])
        P.op('dve', lambda e, c=c, at=at: e.scalar_tensor_tensor(out=at[:], in0=banks[4][:, 0:128], scalar=ek[:, c:c + 1], in1=trif[:],
                                                                 op0=ALU.mult, op1=ALU.mult), reads=[ek, trif], excl=[banks[4]], writes=[at])
        pH = banks[5 + c % 2]
        if c > 0:
            P.op('pe', lambda e, sl=sl, c=c, pH=pH: e.matmul(pH[:, 0:129], mlqT[:, sl], Cb[(c - 1) % 2][:], start=True, stop=False),
                 reads=[mlqT, Cb[(c - 1) % 2]], writes=[pH])
        P.op('pe', lambda e, at=at, c=c, pH=pH: e.matmul(pH[:, 0:129], at[:], mlV[:, c, :], start=(c == 0), stop=True),
             reads=[at, mlV], writes=[pH])
        Dc = Dst[c % 2]
        if c == 0:
            P.op('dve', lambda e, Dc=Dc: e.tensor_copy(Dc[:], banks[3][:, 0:129]), excl=[banks[3]], writes=[Dc])
        else:
            Dp = Dst[(c - 1) % 2]
            P.op('dve', lambda e, Dc=Dc, Dp=Dp, c=c: e.scalar_tensor_tensor(out=Dc[:], in0=Dp[:], scalar=ebL[:, c - 1:c], in1=banks[3][:, 0:129],
                                                                          op0=ALU.mult, op1=ALU.add), reads=[Dp, ebL], excl=[banks[3]], writes=[Dc])
        P.op('act', lambda e, Dc=Dc, c=c: e.activation(Cb[c % 2][:], Dc[:], AF.Identity, scale=ebL[:, c:c + 1]), reads=[Dc, ebL], writes=[Cb[c % 2]])
        P.op('dve', lambda e, c=c, pH=pH: e.tensor_tensor(hsm[:, 0:1], pH[:, 128:129], eb[:, c:c + 1], op=ALU.mult), reads=[eb], excl=[pH], writes=[hsm])
        P.op('dve', lambda e: e.tensor_scalar(hsm[:, 1:2], hsm[:, 0:1], -1.0, 1.0, op0=ALU.mult, op1=ALU.max), reads=[hsm], writes=[hsm])
        P.op('dve', lambda e: e.tensor_scalar(hsm[:, 2:3], hsm[:, 0:1], 1.0, None, op0=ALU.max), reads=[hsm], writes=[hsm])
        P.op('dve', lambda e: e.tensor_tensor(hsm[:, 1:2], hsm[:, 1:2], hsm[:, 2:3], op=ALU.max), reads=[hsm], writes=[hsm])
        P.op('dve', lambda e: e.reciprocal(hsm[:, 2:3], hsm[:, 1:2]), reads=[hsm], writes=[hsm])
        P.op('dve', lambda e, c=c: e.tensor_tensor(hsm[:, 3:4], hsm[:, 2:3], eb[:, c:c + 1], op=ALU.mult), reads=[hsm, eb], writes=[hsm])
        P.op('dve', lambda e, pH=pH: e.tensor_scalar(hh[:], pH[:, 0:128], hsm[:, 3:4], None, op0=ALU.mult), reads=[hsm], excl=[pH], writes=[hh])
        P.op('dve', lambda e: e.bn_stats(st6[:], hh[:]), reads=[hh], writes=[st6])
        P.op('dve', lambda e: e.bn_aggr(mvv[:], st6[:]), reads=[st6], writes=[mvv])
        P.op('dve', lambda e: e.tensor_scalar(rstd[:], mvv[:, 1:2], LN_EPS, None, op0=ALU.add), reads=[mvv], writes=[rstd])
        P.op('act', lambda e: e.activation(rstd[:], rstd[:], AF.Sqrt), reads=[rstd], writes=[rstd])
        P.op('dve', lambda e: e.reciprocal(rstd[:], rstd[:]), reads=[rstd], writes=[rstd])
        P.op('dve', lambda e: e.tensor_scalar(hh[:], hh[:], mvv[:, 0:1], rstd[:, 0:1], op0=ALU.subtract, op1=ALU.mult),
             reads=[hh, mvv, rstd], writes=[hh])
        P.op('pool', lambda e: e.tensor_tensor(hh[:], hh[:], gml[:], op=ALU.mult), reads=[hh, gml], writes=[hh])
        hoo = ho[c % 2]
        P.op('pool', lambda e, c=c, hoo=hoo: e.tensor_tensor(hoo[:], hh[:], sigo[:, c, :], op=ALU.mult), reads=[hh, sigo], writes=[hoo])
        hT_ = oT_sb[c % 2]
        P.op('pe', lambda e, hoo=hoo: e.transpose(bass.AP(pT7, 0, [[1024, 128], [1, 128]]), hoo[:], ident[:]),
             reads=[hoo, ident], writes=[banks[7]])
        P.op('act', lambda e, hT_=hT_: e.copy(hT_[:], bass.AP(pT7, 0, [[1024, 128], [1, 128]])), excl=[banks[7]], writes=[hT_])
        jj, qq, ww = c // 16, (c % 16) // 4, c % 4
        P.dma('sp', mixsrc_h[jj].ap()[(qq * 2 + 1) * 128:(qq * 2 + 2) * 128, ww * 128:(ww + 1) * 128], hT_[:], reads=[hT_], writes=[mixsrc_res])


def phase_b(P, nc, C, H, xin_h, xin_res, mixg_h, mixg_res, midx_h, xout_h, xout_res, xsrc_h, xsrc_res, ntiles):
    ident, ones_bf, iota16, thr16 = C['ident'], C['ones_bf'], C['iota16'], C['thr16']
    D = 1024
    lnp = P.sb([128, 6 * D], F32, "lnp")
    P.dma('sp', lnp[:], bass.AP(H['lnp'], 0, [[0, 128], [1, 6 * D]]), writes=[lnp])

    r = P.sb([128, D], F32, "r")
    stg = [r] * 2
    stg_i = [0]

    def load_w(handle, n, name, dst=None):
        wb = dst if dst is not None else P.sb([128, 8, n], BF16, name)
        wap = handle.ap().rearrange("(c p) n -> p c n", p=128)
        for kc in range(8):
            for pc in range(n // 1024):
                s = stg[stg_i[0] % 2]
                stg_i[0] += 1
                P.dma('sp', s[:, :], wap[:, kc, pc * 1024:(pc + 1) * 1024], writes=[s])
                if stg_i[0] % 2 == 0:
                    P.op('act', lambda e, s=s, kc=kc, pc=pc: e.copy(wb[:, kc, pc * 1024:(pc + 1) * 1024], s[:, :]), reads=[s], writes=[wb])
                else:
                    P.op('dve', lambda e, s=s, kc=kc, pc=pc: e.tensor_copy(wb[:, kc, pc * 1024:(pc + 1) * 1024], s[:, :]), reads=[s], writes=[wb])
        return wb

    wout_b = load_w(H['wout'], 1024, "wout")
    wq_b = load_w(H['wq'], 1024, "wq")
    wo_b = load_w(H['wo'], 1024, "wo")
    wpq_b = load_w(H['wpq'], 2048, "wpq")

    pA = P.ps([128, 2048], F32, "pA")
    pB = P.ps([128, 1024], F32, "pB")
    pC = P.ps([128, 512], F32, "pC")
    pT = P.ps([128, 8, 128], BF16, "pT")

    KT_b = P.sb([128, 8, 256], BF16, "KT")
    V_b = P.sb([128, 2, 1024], BF16, "V")
    skT_b = P.sb([128, 2, 128], BF16, "skT")
    NROW = 4
    gbuf = P.sb([128, 2 * NROW, 1024], F32, "gbuf")
    gb_bf = gbuf.t.bitcast(BF16)

    def wkv_ap(kc, c0, c1):
        return bass.AP(gb_bf, kc * 2048 + c0, [[2 * NROW * 1024 * 2, 128], [1, c1 - c0]])

    wkvap = H['wkv'].ap().rearrange("(c p) n -> p c n", p=128)
    for kc in range(8):
        for pc in range(2):
            s = stg[stg_i[0] % 2]
            stg_i[0] += 1
            P.dma('sp', s[:, :], wkvap[:, kc, pc * 1024:(pc + 1) * 1024], writes=[s])
            P.op('dve', lambda e, s=s, kc=kc, pc=pc: e.tensor_copy(wkv_ap(kc, pc * 1024, (pc + 1) * 1024), s[:, :]), reads=[s], writes=[gbuf])
    sc = P.sb([128, 16, 128], F32, "sc")
    w8bf = sc.t.bitcast(BF16)
    memT_b = sc

    def memT_ap(kc, m0, m1):
        return bass.AP(w8bf, kc * 256 + m0, [[4096, 128], [1, m1 - m0]])

    def pqT_ap(hc):
        return bass.AP(w8bf, hc * 128, [[4096, 128], [1, 128]])
    mT_ap = H['memT'].ap().rearrange("(c p) n -> p c n", p=128)
    for kc in range(8):
        s = stg[stg_i[0] % 2]
        stg_i[0] += 1
        P.dma('sp', s[:, 0:256], mT_ap[:, kc, :], writes=[s])
        P.op('dve', lambda e, s=s, kc=kc: e.tensor_copy(memT_ap(kc, 0, 256), s[:, 0:256]), reads=[s], writes=[memT_b])
    s = stg[stg_i[0] % 2]
    stg_i[0] += 1
    P.dma('sp', s[:, 0:256].rearrange("p (c n) -> p c n", c=2), H['skT'].ap().rearrange("c d n -> d c n"), writes=[s])
    P.op('dve', lambda e, s=s: e.tensor_copy(skT_b[:], s[:, 0:256].rearrange("p (c n) -> p c n", c=2)), reads=[s], writes=[skT_b])

    for j in range(8):
        for kc in range(8):
            P.op('pe', lambda e, j=j, kc=kc: e.matmul(pB[:, 0:256], wkv_ap(kc, j * 128, (j + 1) * 128), memT_ap(kc, 0, 256),
                                                      start=(kc == 0), stop=(kc == 7)),
                 reads=[gbuf, memT_b], writes=[pB])
        P.op('dve', lambda e, j=j: e.tensor_copy(KT_b[:, j, :], pB[:, 0:256]), excl=[pB], writes=[KT_b])
    for mc in range(2):
        for half in range(2):
            for kc in range(8):
                P.op('pe', lambda e, mc=mc, half=half, kc=kc: e.matmul(
                    pB[:, 0:512], memT_ap(kc, mc * 128, (mc + 1) * 128),
                    wkv_ap(kc, 1024 + half * 512, 1024 + (half + 1) * 512), start=(kc == 0), stop=(kc == 7)),
                     reads=[gbuf, memT_b], writes=[pB])
            P.op('dve', lambda e, mc=mc, half=half: e.tensor_copy(V_b[:, mc, half * 512:(half + 1) * 512], pB[:, 0:512]),
                 excl=[pB], writes=[V_b])

    xt = P.sb([128, D], F32, "xt")
    mixb = P.sb([128, 8, 512], BF16, "mixb")
    x1 = P.sb([128, D], F32, "x1")
    x2 = P.sb([128, D], F32, "x2")
    xb = P.sb([128, D], BF16, "xb")
    xT = P.sb([128, 8, 128], BF16, "xT")
    qT = P.sb([128, 8, 128], BF16, "qT")
    E_b = P.sb([128, 8, 128], BF16, "E")
    rz = P.sb([128, 4, 128], F32, "rz")
    oT = qT
    top = P.sb([128, 16, 16], F32, "top")
    idxu = P.sb([128, 16, 16], U32, "idxu")
    idxf = P.sb([128, 16, 16], F32, "idxf")
    cs = P.sb([128, 8, 16], F32, "cs")
    ciu = P.sb([128, 8, 16], U32, "ciu")
    cif = P.sb([128, 128], F32, "cif")
    big = sc
    big2 = sc
    pqT = sc
    mixf = r
    x3 = r
    junk = r
    cand = sc
    k0f = P.sb([128, 128], F32, "k0f")
    k1f = P.sb([128, 128], F32, "k1f")
    i0s = P.sb([128, 128], F32, "i0s")
    i1s = P.sb([128, 128], F32, "i1s")
    ef = P.sb([128, 128], F32, "ef")
    eu = P.sb([128, 128], U32, "eu")
    gex = P.sb([128, 8, 16], F32, "gex")
    gz = P.sb([128, 8], F32, "gz")
    aact = P.sb([128, 128], F32, "aact")
    wgt = P.sb([128, 128], F32, "wgt")
    acc = x1
    st = P.sb([128, 12], F32, "st")
    mv = P.sb([128, 2], F32, "mv")
    rstd = P.sb([128, 1], F32, "rstd")
    grow = [Res(f"grow{i}") for i in range(16)]
    dgs = [P.sb([128, 128], BF16, f"dg{i}") for i in range(4)]

    def layernorm(src, dst, li):
        for c in range(2):
            P.op('dve', lambda e, c=c: e.bn_stats(st[:, c * 6:(c + 1) * 6], src[:, c * 512:(c + 1) * 512]),
                 reads=[src], writes=[st])
        P.op('dve', lambda e: e.bn_aggr(mv[:], st[:]), reads=[st], writes=[mv])
        P.op('dve', lambda e: e.tensor_scalar(rstd[:], mv[:, 1:2], LN_EPS, None, op0=ALU.add), reads=[mv], writes=[rstd])
        P.op('act', lambda e: e.activation(rstd[:], rstd[:], AF.Sqrt), reads=[rstd], writes=[rstd])
        P.op('dve', lambda e: e.reciprocal(rstd[:], rstd[:]), reads=[rstd], writes=[rstd])
        P.op('dve', lambda e: e.tensor_scalar(dst[:], src[:], mv[:, 0:1], rstd[:, 0:1], op0=ALU.subtract, op1=ALU.mult),
             reads=[src, mv, rstd], writes=[dst])
        P.op('dve', lambda e: e.tensor_tensor(dst[:], dst[:], lnp[:, (2 * li) * D:(2 * li + 1) * D], op=ALU.mult),
             reads=[dst, lnp], writes=[dst])
        P.op('dve', lambda e: e.tensor_tensor(dst[:], dst[:], lnp[:, (2 * li + 1) * D:(2 * li + 2) * D], op=ALU.add),
             reads=[dst, lnp], writes=[dst])

    def to_T(src):
        P.op('act', lambda e: e.copy(xb[:], src[:]), reads=[src], writes=[xb])
        for c in range(8):
            P.op('pe', lambda e, c=c: e.transpose(pT[:, c, :], xb[:, c * 128:(c + 1) * 128], ident[:]),
                 reads=[xb, ident], writes=[pT])
        P.op('dve', lambda e: e.tensor_copy(xT[:], pT[:]), excl=[pT], writes=[xT])

    def linear(lhsT_tile, w_b, n, pdst, off=0):
        for half in range(n // 512):
            for kc in range(8):
                P.op('pe', lambda e, half=half, kc=kc: e.matmul(
                    pdst[:, half * 512:(half + 1) * 512], lhsT_tile[:, kc, off:off + 128], w_b[:, kc, half * 512:(half + 1) * 512],
                    start=(kc == 0), stop=(kc == 7)), reads=[lhsT_tile, w_b], writes=[pdst])

    midx = P.sb([128, 32], U32, "midx")
    P.dma('sp', midx[:], midx_h.ap(), writes=[midx])

    def brow(rr):
        return bass.AP(gb_bf, rr * 1024, [[16384, 128], [1, 1024]])

    P.barrier()
    cst = [Res(f"cst{k}") for k in range(4)]
    stb = [r, x1, x2, xt]
    for k in range(128):
        tab = H['u'] if k < 64 else H['v']
        tabb = H['ub'] if k < 64 else H['vb']
        row0 = (k % 64) * 256
        sf = gbuf[:, 2 * (k % 4):2 * (k % 4) + 2, :]
        tb = stb[k % 4]
        sbv = bass.AP(tb.t.bitcast(BF16), 0, [[2048, 128], [1024, 2], [1, 1024]])
        P.dma('sp', sf, tab.ap()[row0:row0 + 256, :].rearrange("(a p) n -> p a n", p=128), writes=[cst[k % 4]])
        if k % 2 == 0:
            P.op('dve', lambda e, sf=sf, sbv=sbv: e.tensor_copy(sbv, sf), reads=[cst[k % 4]], writes=[tb])
        else:
            P.op('act', lambda e, sf=sf, sbv=sbv: e.copy(sbv, sf), reads=[cst[k % 4]], writes=[tb])
        P.dma('pool', tabb.ap()[row0:row0 + 256, :].rearrange("(a p) n -> p a n", p=128), sbv, reads=[tb], writes=[Res("cvout")])
    P.barrier()
    for i in range(ntiles):
        t0 = i * 128
        P.dma('sp', xt[:], xin_h.ap()[t0:t0 + 128, :], reads=[xin_res], writes=[xt])
        if i % 4 == 0:
            for kc in range(8):
                P.dma('pool', None, None, reads=[mixg_res, midx], writes=[mixb],
                      fn=lambda e, kc=kc, i=i: e.indirect_dma_start(
                          out=mixb[:, kc, :], out_offset=None, in_=mixg_h.ap(),
                          in_offset=bass.IndirectOffsetOnAxis(ap=midx[:, (i // 4) * 8 + kc:(i // 4) * 8 + kc + 1], axis=0)))
        linear(mixb, wout_b, 1024, pB, off=(i % 4) * 128)
        P.op('dve', lambda e: e.scalar_tensor_tensor(out=r[:], in0=xt[:], scalar=ALPHA, in1=pB[:], op0=ALU.mult, op1=ALU.add),
             reads=[xt], excl=[pB], writes=[r])
        layernorm(r, x1, 0)
        to_T(x1)
        for j in range(8):
            for kc in range(8):
                P.op('pe', lambda e, j=j, kc=kc: e.matmul(pA[:, j * 128:(j + 1) * 128], wq_b[:, kc, j * 128:(j + 1) * 128],
                                                          xT[:, kc, :], start=(kc == 0), stop=(kc == 7)),
                     reads=[wq_b, xT], writes=[pA])
        P.op('act', lambda e: e.copy(qT[:], pA[:, 0:1024].rearrange("p (j t) -> p j t", j=8)), excl=[pA], writes=[qT])
        for h in range(4):
            for mc in range(2):
                for dc in range(2):
                    P.op('pe', lambda e, h=h, mc=mc, dc=dc: e.matmul(
                        pB[:, (h * 2 + mc) * 128:(h * 2 + mc + 1) * 128],
                        KT_b[:, h * 2 + dc, mc * 128:(mc + 1) * 128], qT[:, h * 2 + dc, :],
                        start=(dc == 0), stop=(dc == 1)), reads=[KT_b, qT], writes=[pB])
        P.op('act', lambda e: e.activation(E_b[:], pB[:].rearrange("p (j t) -> p j t", j=8), AF.Exp, scale=1.0 / 16.0),
             excl=[pB], writes=[E_b])
        for h in range(4):
            for mc in range(2):
                P.op('pe', lambda e, h=h, mc=mc: e.matmul(pC[:, h * 128:(h + 1) * 128], ones_bf[:], E_b[:, h * 2 + mc, :],
                                                          start=(mc == 0), stop=(mc == 1)),
                     reads=[ones_bf, E_b], writes=[pC])
        P.op('dve', lambda e: e.reciprocal(rz[:], pC[:].rearrange("p (h t) -> p h t", h=4)), excl=[pC], writes=[rz])
        for h in range(4):
            for dc in range(2):
                for mc in range(2):
                    P.op('pe', lambda e, h=h, dc=dc, mc=mc: e.matmul(
                        pA[:, 1024 + (h * 2 + dc) * 128:1024 + (h * 2 + dc + 1) * 128],
                        V_b[:, mc, h * 256 + dc * 128:h * 256 + (dc + 1) * 128], E_b[:, h * 2 + mc, :],
                        start=(mc == 0), stop=(mc == 1)), reads=[V_b, E_b], writes=[pA])
        for j in range(8):
            P.op('dve', lambda e, j=j: e.tensor_tensor(oT[:, j, :], pA[:, 1024 + j * 128:1024 + (j + 1) * 128], rz[:, j // 2, :],
                                                       op=ALU.mult), reads=[rz], excl=[pA], writes=[oT])
        linear(oT, wo_b, 1024, pB)
        P.op('dve', lambda e: e.scalar_tensor_tensor(out=r[:], in0=x1[:], scalar=ALPHA, in1=pB[:], op0=ALU.mult, op1=ALU.add),
             reads=[x1], excl=[pB], writes=[r])
        layernorm(r, x2, 1)
        to_T(x2)
        for hc in range(16):
            for kc in range(8):
                P.op('pe', lambda e, hc=hc, kc=kc: e.matmul(pA[:, hc * 128:(hc + 1) * 128], wpq_b[:, kc, hc * 128:(hc + 1) * 128],
                                                            xT[:, kc, :], start=(kc == 0), stop=(kc == 7)),
                     reads=[wpq_b, xT], writes=[pA])
        P.op('act', lambda e: e.copy(bass.AP(w8bf, 0, [[4096, 128], [1, 2048]]), pA[:]), excl=[pA], writes=[pqT])
        for hc in range(16):
            P.op('pe', lambda e, hc=hc: e.matmul(pA[:, hc * 128:(hc + 1) * 128], pqT_ap(hc), skT_b[:, hc % 2, :],
                                                 start=True, stop=True), reads=[pqT, skT_b], writes=[pA])
        P.op('act', lambda e: e.copy(sc[:], pA[:].rearrange("p (j t) -> p j t", j=16)), excl=[pA], writes=[sc])
        for hc in range(16):
            P.op('dve', lambda e, hc=hc: e.max(top[:, hc, 0:8], sc[:, hc, :]), reads=[sc], writes=[top])
            P.op('dve', lambda e, hc=hc: e.max_index(idxu[:, hc, 0:8], top[:, hc, 0:8], sc[:, hc, :]), reads=[sc, top], writes=[idxu])
            P.op('dve', lambda e, hc=hc: e.match_replace(sc[:, hc, :], top[:, hc, 0:8], sc[:, hc, :], -1e30),
                 reads=[sc, top], writes=[sc])
            P.op('dve', lambda e, hc=hc: e.max(top[:, hc, 8:16], sc[:, hc, :]), reads=[sc], writes=[top])
            P.op('dve', lambda e, hc=hc: e.max_index(idxu[:, hc, 8:16], top[:, hc, 8:16], sc[:, hc, :]), reads=[sc, top], writes=[idxu])
        P.op('dve', lambda e: e.tensor_copy(idxf[:], idxu[:]), reads=[idxu], writes=[idxf])
        P.op('dve', lambda e: e.tensor_tensor(AP(cand, 0, [[2048, 128], [256, 8], [16, 16], [1, 16]]),
                                              AP(top, 0, [[256, 128], [32, 8], [1, 16], [0, 16]]),
                                              AP(top, 16, [[256, 128], [32, 8], [0, 16], [1, 16]]), op=ALU.add),
             reads=[top], writes=[cand])
        for h in range(8):
            P.op('dve', lambda e, h=h: e.max(cs[:, h, 0:8], sc.t.rearrange('p a b -> p (a b)')[:, h * 256:(h + 1) * 256]), reads=[cand], writes=[cs])
            P.op('dve', lambda e, h=h: e.max_index(ciu[:, h, 0:8], cs[:, h, 0:8], sc.t.rearrange('p a b -> p (a b)')[:, h * 256:(h + 1) * 256]), reads=[cand, cs], writes=[ciu])
            P.op('dve', lambda e, h=h: e.match_replace(sc.t.rearrange('p a b -> p (a b)')[:, h * 256:(h + 1) * 256], cs[:, h, 0:8], sc.t.rearrange('p a b -> p (a b)')[:, h * 256:(h + 1) * 256], -1e30),
                 reads=[cand, cs], writes=[cand])
            P.op('dve', lambda e, h=h: e.max(cs[:, h, 8:16], sc.t.rearrange('p a b -> p (a b)')[:, h * 256:(h + 1) * 256]), reads=[cand], writes=[cs])
            P.op('dve', lambda e, h=h: e.max_index(ciu[:, h, 8:16], cs[:, h, 8:16], sc.t.rearrange('p a b -> p (a b)')[:, h * 256:(h + 1) * 256]), reads=[cand, cs], writes=[ciu])
        P.op('dve', lambda e: e.tensor_copy(cif[:], ciu[:].rearrange("p h k -> p (h k)")), reads=[ciu], writes=[cif])
        P.op('dve', lambda e: e.tensor_tensor(AP(big, 0, [[2048, 128], [16, 128], [1, 16]]),
                                              AP(cif, 0, [[128, 128], [1, 128], [0, 16]]),
                                              AP(thr16, 0, [[16, 128], [0, 128], [1, 16]]), op=ALU.is_ge),
             reads=[cif, thr16], writes=[big])
        P.op('dve', lambda e: e.tensor_reduce(k0f[:], AP(big, 0, [[2048, 128], [16, 128], [1, 16]]), axis=AX.X, op=ALU.add),
             reads=[big], writes=[k0f])
        P.op('dve', lambda e: e.scalar_tensor_tensor(out=k1f[:], in0=k0f[:], scalar=-16.0, in1=cif[:], op0=ALU.mult, op1=ALU.add),
             reads=[k0f, cif], writes=[k1f])
        for (kf, c, dst) in ((k0f, 0, i0s), (k1f, 1, i1s)):
            P.op('dve', lambda e, kf=kf: e.tensor_tensor(AP(big, 0, [[2048, 128], [16, 128], [1, 16]]),
                                                         AP(kf, 0, [[128, 128], [1, 128], [0, 16]]),
                                                         AP(iota16, 0, [[16, 128], [0, 128], [1, 16]]), op=ALU.is_equal),
                 reads=[kf, iota16], writes=[big])
            P.op('dve', lambda e, c=c: e.tensor_tensor(AP(big2, 0, [[2048, 128], [256, 8], [16, 16], [1, 16]]),
                                                       AP(big, 0, [[2048, 128], [256, 8], [16, 16], [1, 16]]),
                                                       AP(idxf, c * 16, [[256, 128], [32, 8], [0, 16], [1, 16]]), op=ALU.mult),
                 reads=[big, idxf], writes=[big2])
            P.op('dve', lambda e, dst=dst: e.tensor_reduce(dst[:], AP(big2, 0, [[2048, 128], [16, 128], [1, 16]]), axis=AX.X, op=ALU.add),
                 reads=[big2], writes=[dst])
        P.op('dve', lambda e: e.scalar_tensor_tensor(out=ef[:], in0=i0s[:], scalar=128.0, in1=i1s[:], op0=ALU.mult, op1=ALU.add),
             reads=[i0s, i1s], writes=[ef])
        P.op('dve', lambda e: e.tensor_copy(eu[:], ef[:]), reads=[ef], writes=[eu])
        P.op('dve', lambda e: e.tensor_tensor(gex[:], cs[:], AP(cs, 0, [[128, 128], [16, 8], [0, 16]]), op=ALU.subtract),
             reads=[cs], writes=[gex])
        P.op('act', lambda e: e.activation(gex[:], gex[:], AF.Exp), reads=[gex], writes=[gex])
        P.op('dve', lambda e: e.tensor_reduce(gz[:], gex[:], axis=AX.X, op=ALU.add), reads=[gex], writes=[gz])
        P.op('dve', lambda e: e.reciprocal(gz[:], gz[:]), reads=[gz], writes=[gz])
        P.op('dve', lambda e: e.tensor_tensor(gex[:], gex[:], AP(gz, 0, [[8, 128], [1, 8], [0, 16]]), op=ALU.mult),
             reads=[gex, gz], writes=[gex])
        for hk in range(128):
            rr = hk % 8
            P.dma('pool', None, None, reads=[eu], writes=[grow[rr]],
                  fn=lambda e, hk=hk, rr=rr: e.indirect_dma_start(
                      out=brow(rr), out_offset=None, in_=H['ub'].ap(),
                      in_offset=bass.IndirectOffsetOnAxis(ap=eu[:, hk:hk + 1], axis=0)))
            P.op('dve', lambda e, hk=hk, rr=rr: e.scalar_tensor_tensor(
                out=junk[:], in0=brow(rr), scalar=1.0, in1=x2[:], op0=ALU.mult, op1=ALU.mult,
                accum_out=aact[:, hk:hk + 1]), reads=[grow[rr], x2], writes=[junk, aact])
        P.op('act', lambda e: e.activation(aact[:], aact[:], AF.Gelu), reads=[aact], writes=[aact])
        P.op('dve', lambda e: e.tensor_tensor(wgt[:], aact[:], gex[:].rearrange("p h k -> p (h k)"), op=ALU.mult),
             reads=[aact, gex], writes=[wgt])
        for hk in range(128):
            rr = 8 + hk % 8
            P.dma('pool', None, None, reads=[eu], writes=[grow[rr]],
                  fn=lambda e, hk=hk, rr=rr: e.indirect_dma_start(
                      out=brow(rr), out_offset=None, in_=H['vb'].ap(),
                      in_offset=bass.IndirectOffsetOnAxis(ap=eu[:, hk:hk + 1], axis=0)))
            dg = dgs[hk % 4]
            P.op('act', lambda e, hk=hk, dg=dg: e.activation(dg[:], ident[:], AF.Identity, scale=wgt[:, hk:hk + 1]),
                 reads=[ident, wgt], writes=[dg])
            for half in range(2):
                P.op('pe', lambda e, hk=hk, rr=rr, dg=dg, half=half: e.matmul(
                    pB[:, half * 512:(half + 1) * 512], dg[:], bass.AP(gb_bf, rr * 1024 + half * 512, [[16384, 128], [1, 512]]),
                    start=(hk == 0), stop=(hk == 127)), reads=[dg, grow[rr]], writes=[pB])
        P.op('dve', lambda e: e.scalar_tensor_tensor(out=acc[:], in0=x2[:], scalar=ALPHA, in1=pB[:], op0=ALU.mult, op1=ALU.add),
             reads=[x2], excl=[pB], writes=[acc])
        layernorm(acc, x3, 2)
        P.dma('sp', xout_h.ap()[t0:t0 + 128, :], x3[:], reads=[x3], writes=[xout_res])
        if xsrc_h is not None:
            to_T(x3)
            for c4 in range(4):
                P.dma('sp', xsrc_h[c4].ap().rearrange("(k p) t -> p k t", p=128)[:, :, t0:t0 + 128], xT[:, 2 * c4:2 * c4 + 2, :],
                      reads=[xT], writes=[xsrc_res])

def build_fused(depth=4, nq_tiles=32, ntiles=16):
    nc = bass.Bass("TRN2", target_bir_lowering=False)
    D = 1024
    xT0_h = nc.dram_tensor("xT0", [D, S], F32, kind="ExternalInput")
    x0_h = nc.dram_tensor("x0", [NTOK, D], F32, kind="ExternalInput")
    memT_h = nc.dram_tensor("memT", [D, 256], F32, kind="ExternalInput")
    midx_h = nc.dram_tensor("midx", [128, 32], U32, kind="ExternalInput")
    ub_h = nc.dram_tensor("ub_scr", [16384, D], BF16)
    vb_h = nc.dram_tensor("vb_scr", [16384, D], BF16)
    HA, HB = [], []
    for l in range(depth):
        HA.append({"wa": nc.dram_tensor(f"wa{l}", [D, NWA], F32, kind="ExternalInput"),
                   "sp": nc.dram_tensor(f"sp{l}", [128, 16], F32, kind="ExternalInput"),
                   "lamqk": nc.dram_tensor(f"lamqk{l}", [128, 256], F32, kind="ExternalInput"),
                   "gda": nc.dram_tensor(f"gda{l}", [128, 128], F32, kind="ExternalInput"),
                   "gml": nc.dram_tensor(f"gml{l}", [128, 128], F32, kind="ExternalInput")})
        HB.append({"wout": nc.dram_tensor(f"wout{l}", [D, D], F32, kind="ExternalInput"),
                   "wq": nc.dram_tensor(f"wq{l}", [D, D], F32, kind="ExternalInput"),
                   "wkv": nc.dram_tensor(f"wkv{l}", [D, 2 * D], F32, kind="ExternalInput"),
                   "wo": nc.dram_tensor(f"wo{l}", [D, D], F32, kind="ExternalInput"),
                   "wpq": nc.dram_tensor(f"wpq{l}", [D, 2 * D], F32, kind="ExternalInput"),
                   "skT": nc.dram_tensor(f"skT{l}", [2, 128, 128], F32, kind="ExternalInput"),
                   "lnp": nc.dram_tensor(f"lnp{l}", [6, D], F32, kind="ExternalInput"),
                   "u": nc.dram_tensor(f"u{l}", [16384, D], F32, kind="ExternalInput"),
                   "v": nc.dram_tensor(f"v{l}", [16384, D], F32, kind="ExternalInput"),
                   "memT": memT_h, "ub": ub_h, "vb": vb_h})
    out_h = nc.dram_tensor("out", [NTOK, D], F32, kind="ExternalOutput")
    mixsrc = [[nc.dram_tensor(f"mixsrc{l}_{c}", [1024, 512], BF16) for c in range(4)] for l in range(depth)]
    mixgc = [[nc.dram_tensor(f"mixgc{l}_{c}", [4096, 512], BF16) for c in range(4)] for l in range(depth)]
    mixg = [nc.dram_tensor(f"mixg{l}", [16384, 512], BF16) for l in range(depth)]
    xsrc = [[nc.dram_tensor(f"xsrc{l}_{c}", [256, NTOK], BF16) for c in range(4)] for l in range(depth - 1)]
    xg = [None] + [[nc.dram_tensor(f"xg{l}_{c}", [1024, NTOK], BF16) for c in range(4)] for l in range(1, depth)]
    xres = [None] + [nc.dram_tensor(f"xres{l}", [NTOK, D], F32) for l in range(1, depth)]
    GROUPS = [[0, 1, 2, 3], [4, 5, 6, 7]]

    P = Prog(nc, n_dma_sems={'sp': 8, 'act': 4, 'pool': 20})
    C = make_consts(P)
    out_res = Res("out")
    xg_res = [Res(f"xg{l}") for l in range(depth)]
    xres_res = [Res(f"xres{l}") for l in range(depth)]
    for l in range(depth):
        mixsrc_res, mixg_res, xsrc_res = Res("mixsrc"), Res("mixg"), Res("xsrc")
        with P.scope():
            phase_a(P, nc, C, HA[l], xT0_h, xg[l], xg_res[l], mixsrc[l], mixsrc_res, nq_tiles)
        for c in range(4):
            gc_res = Res("mixgc")
            P.cc("AllGather", GROUPS, mixsrc[l][c].ap(), mixgc[l][c].ap(), reads=[mixsrc_res], writes=[gc_res], inc=1)
            P.dma('sp', mixg[l].ap()[c * 4096:(c + 1) * 4096, :].rearrange("(a b) n -> a (b n)", a=128),
                  mixgc[l][c].ap().rearrange("(a b) n -> a (b n)", a=128), reads=[gc_res], writes=[mixg_res])
        last = (l == depth - 1)
        with P.scope():
            phase_b(P, nc, C, HB[l], x0_h if l == 0 else xres[l], xres_res[l], mixg[l], mixg_res, midx_h,
                    out_h if last else xres[l + 1], out_res if last else xres_res[l + 1],
                    None if last else xsrc[l], xsrc_res, ntiles)
        if not last:
            for c in range(4):
                P.cc("AllGather", GROUPS, xsrc[l][c].ap(), xg[l + 1][c].ap(), reads=[xsrc_res], writes=[xg_res[l + 1]], inc=1)
    P.finish([out_res])
    P.emit()
    P.close()
    return nc


DEPTH = 4
_NC_CACHE = {}


def _head_inputs(w_in_l, conv_w_l, conv_b_l, i_bias_l, f_bias_l, lam_qk_l, da_g_l, ml_g_l, l, h):
    r = np.arange(h * 128, (h + 1) * 128)
    cols = np.concatenate([r, 512 + r, 1024 + r, 1536 + r, 2048 + r, 2560 + r, 3072 + r, [3584 + h], [3588 + h]])
    wa = np.ascontiguousarray(w_in_l[:, cols])
    lam_init = 0.8 - 0.6 * math.exp(-0.3 * l)
    sp = np.zeros((128, 16), np.float32)
    for g in range(2):
        ch = g * 512 + r
        for j in range(4):
            sp[:, g * 4 + j] = conv_w_l[j, ch]
        sp[:, 8 + g] = conv_b_l[ch]
    sp[:, 10] = i_bias_l[h]
    sp[:, 11] = f_bias_l[h]
    sp[:, 12] = lam_init
    sp[:, 13] = 1.0 - lam_init
    lamqk = np.ascontiguousarray(np.broadcast_to(lam_qk_l.reshape(1, 256), (128, 256)))
    gda = np.ascontiguousarray(np.broadcast_to(da_g_l[None, :], (128, 128)))
    gml = np.ascontiguousarray(np.broadcast_to(ml_g_l[None, r], (128, 128)))
    return {f"wa{l}": wa, f"sp{l}": sp, f"lamqk{l}": lamqk, f"gda{l}": gda, f"gml{l}": gml}


def make_in_maps(depth, x, mem, w_in, i_bias, f_bias, conv_w, conv_b, lam_qk, da_norm_g, ml_norm_g, w_out,
                 ln1_g, ln1_b, wq_mem, wkv_mem, wo_mem, ln2_g, ln2_b, w_pq, sub_keys, u_tab, v_tab, ln3_g, ln3_b):
    f = lambda a: np.asarray(a, dtype=np.float32)
    x = f(x); mem = f(mem)
    B = x.shape[0]
    shared = {}
    for l in range(depth):
        shared[f"wout{l}"] = f(w_out[l]); shared[f"wq{l}"] = f(wq_mem[l]); shared[f"wkv{l}"] = f(wkv_mem[l])
        shared[f"wo{l}"] = f(wo_mem[l]); shared[f"wpq{l}"] = f(w_pq[l])
        shared[f"skT{l}"] = np.ascontiguousarray(f(sub_keys[l]).transpose(0, 2, 1))
        shared[f"lnp{l}"] = np.ascontiguousarray(np.stack([f(ln1_g[l]), f(ln1_b[l]), f(ln2_g[l]), f(ln2_b[l]), f(ln3_g[l]), f(ln3_b[l])]))
        shared[f"u{l}"] = f(u_tab[l]); shared[f"v{l}"] = f(v_tab[l])
    in_maps = []
    for b in range(B):
        xT = np.ascontiguousarray(x[b].T)
        memT = np.ascontiguousarray(mem[b].T)
        for r_ in range(4):
            m = dict(shared)
            m["xT0"] = xT
            m["x0"] = np.ascontiguousarray(x[b, r_ * 2048:(r_ + 1) * 2048])
            m["memT"] = memT
            p = np.arange(128, dtype=np.int64)[:, None, None]
            qr = np.arange(4, dtype=np.int64)[None, :, None]
            kc = np.arange(8, dtype=np.int64)[None, None, :]
            midx = ((((r_ * 4 + (kc % 4)) * 4 + qr) * 2 + (kc // 4)) * 128) + p
            m["midx"] = np.ascontiguousarray(midx.reshape(128, 32).astype(np.uint32))
            for l in range(depth):
                m.update(_head_inputs(f(w_in[l]), f(conv_w[l]), f(conv_b[l]), f(i_bias[l]), f(f_bias[l]), f(lam_qk[l]),
                                      f(da_norm_g[l]), f(ml_norm_g[l]), l, r_))
            in_maps.append(m)
    return in_maps


def kernel(x, mem, w_in, i_bias, f_bias, conv_w, conv_b, lam_qk, da_norm_g, ml_norm_g, w_out,
           ln1_g, ln1_b, wq_mem, wkv_mem, wo_mem, ln2_g, ln2_b, w_pq, sub_keys, u_tab, v_tab,
           ln3_g, ln3_b):
    if "f" not in _NC_CACHE:
        _NC_CACHE["f"] = build_fused(DEPTH)
    in_maps = make_in_maps(DEPTH, x, mem, w_in, i_bias, f_bias, conv_w, conv_b, lam_qk, da_norm_g, ml_norm_g, w_out,
                           ln1_g, ln1_b, wq_mem, wkv_mem, wo_mem, ln2_g, ln2_b, w_pq, sub_keys, u_tab, v_tab, ln3_g, ln3_b)
    res = run_bass_kernel_spmd(_NC_CACHE["f"], in_maps, core_ids=list(range(8))).results
    B, S_, D = np.asarray(x).shape
    out = np.empty((B, S_, D), np.float32)
    for b in range(B):
        for r_ in range(4):
            out[b, r_ * 2048:(r_ + 1) * 2048] = res[b * 4 + r_]["out"]
    return out
```

```python
import math
import contextlib
import numpy as np
import ml_dtypes
import concourse.bass as bass
import concourse.mybir as mybir
from concourse.bass_utils import run_bass_kernel_spmd

F32 = mybir.dt.float32
BF16 = mybir.dt.bfloat16
I32 = mybir.dt.int32
U32 = mybir.dt.uint32
AF = mybir.ActivationFunctionType
ALU = mybir.AluOpType
AX = mybir.AxisListType

SEM_LIMIT = 12000


class Res:
    __slots__ = ("w", "r", "name")

    def __init__(self, name=""):
        self.w = None
        self.r = []
        self.name = name


class Tile:
    def __init__(self, t, res=None, name=""):
        self.t = t
        self.res = res if res is not None else Res(name)

    def __getitem__(self, k):
        return self.t[k]


def _res(x):
    return x.res if isinstance(x, Tile) else x


class Prog:
    ENGS = ("pe", "act", "dve", "pool", "sp")

    def __init__(self, nc, n_dma_sems=6):
        self.nc = nc
        self.es = contextlib.ExitStack()
        self.cur_es = self.es
        self.nsem = 0
        self.streams = {e: [] for e in self.ENGS}
        self.cur_sem = {}
        self.cur_val = {}
        for e in self.ENGS:
            self._new_eng_sem(e)
        self.known = {e: {} for e in self.ENGS}
        self.dsem = {}
        self.nds = n_dma_sems
        self.dma_rr = {e: 0 for e in self.ENGS}
        self.ntile = 0
        self.last_tok = {}

    def sem(self, name):
        self.nsem += 1
        return self.es.enter_context(self.nc.semaphore(f"{name}_{self.nsem}"))

    def _new_eng_sem(self, e):
        self.nsem = getattr(self, "nsem", 0)
        self.cur_sem[e] = self.sem("c" + e)
        self.cur_val[e] = 0

    def sb(self, shape, dt, name=None):
        self.ntile += 1
        nm = f"{name or 't'}_{self.ntile}"
        t = self.cur_es.enter_context(self.nc.sbuf_tensor(nm, list(shape), dt))
        return Tile(t, name=nm)

    def ps(self, shape, dt=F32, name=None):
        self.ntile += 1
        nm = f"{name or 'p'}_{self.ntile}"
        t = self.cur_es.enter_context(self.nc.psum_tensor(nm, list(shape), dt))
        return Tile(t, name=nm)

    def _collect(self, eng, reads, writes, excl=()):
        toks = []
        for r in reads:
            r = _res(r)
            if r.w is not None:
                toks.append(r.w)
        for w in list(writes) + list(excl):
            w = _res(w)
            if w.w is not None:
                toks.append(w.w)
            toks.extend(w.r)
        waits = []
        kn = self.known[eng]
        best = {}
        for (s, v, src) in toks:
            if src == "pe" and eng == "pe":
                continue
            if kn.get(s, 0) >= v:
                continue
            if best.get(s, (None, 0))[1] < v:
                best[s] = (s, v)
        for s, (sh, v) in best.items():
            kn[s] = v
            waits.append((sh, v))
        return waits

    def _commit(self, tok, reads, writes, excl=()):
        for r in reads:
            _res(r).r.append(tok)
        for w in list(writes) + list(excl):
            w = _res(w)
            w.w = tok
            w.r = []

    def op(self, eng, fn, reads=(), writes=(), excl=()):
        waits = self._collect(eng, reads, writes, excl)
        if self.cur_val[eng] >= SEM_LIMIT:
            self._new_eng_sem(eng)
        s = self.cur_sem[eng]
        self.cur_val[eng] += 1
        v = self.cur_val[eng]
        tok = (s, v, eng)
        self.last_tok[eng] = tok
        self._commit(tok, reads, writes, excl)
        self.streams[eng].append((waits, fn, s, 1))
        return tok

    def dma(self, q, out, in_, reads=(), writes=(), fn=None, inc=16):
        key = q
        nds = self.nds[q] if isinstance(self.nds, dict) else self.nds
        if key not in self.dsem:
            self.dsem[key] = [[self.sem("d" + q), 0] for _ in range(nds)]
        k = self.dma_rr[q]
        self.dma_rr[q] = (k + 1) % nds
        slot = self.dsem[key][k]
        waits = self._collect(q, reads, writes)
        kn = self.known[q]
        if slot[1] > 0 and kn.get(slot[0], 0) < slot[1]:
            waits.append((slot[0], slot[1]))
            kn[slot[0]] = slot[1]
        if slot[1] + inc > SEM_LIMIT:
            slot[0] = self.sem("d" + q)
            slot[1] = 0
        slot[1] += inc
        tok = (slot[0], slot[1], "dma")
        self._commit(tok, reads, writes)
        if fn is None:
            fn = lambda e, o=out, i=in_: e.dma_start(out=o, in_=i)
        self.streams[q].append((waits, fn, slot[0], inc))
        return tok


    def cc(self, kind, groups, in_ap, out_ap, reads=(), writes=(), inc=16):
        fn = lambda e: e.collective_compute(kind, ALU.bypass, replica_groups=groups, ins=[in_ap], outs=[out_ap])
        self._cc_inc = inc
        return self.dma('pool', None, None, reads=reads, writes=writes, fn=fn, inc=inc)


    @contextlib.contextmanager
    def scope(self):
        prev = self.cur_es
        es = contextlib.ExitStack()
        self.cur_es = es
        try:
            yield
            self.barrier()
            self.emit()
        finally:
            self.cur_es = prev
            es.close()

    def barrier(self):
        toks = []
        for e in self.ENGS:
            if e in self.last_tok:
                toks.append(self.last_tok[e])
        for q, slots in self.dsem.items():
            for (sh, v) in slots:
                if v > 0:
                    toks.append((sh, v, "dma"))
        for e in self.ENGS:
            kn = self.known[e]
            waits = []
            for (sh, v, src) in toks:
                if src == e and e == "pe":
                    continue
                if kn.get(sh, 0) < v:
                    kn[sh] = v
                    waits.append((sh, v))
            if waits:
                self.streams[e].append((waits, None, None, 0))

    def finish(self, all_res):
        waits = self._collect("sp", all_res, [])
        kn = self.known["sp"]
        for q, slots in self.dsem.items():
            for (sh, v) in slots:
                if v > 0 and kn.get(sh, 0) < v:
                    waits.append((sh, v))
                    kn[sh] = v
        self.streams["sp"].append((waits, None, None, 0))

    def emit(self):
        nc = self.nc
        streams = self.streams

        def run(engname, eng):
            for (waits, fn, s, inc) in streams[engname]:
                for (sh, v) in waits:
                    eng.wait_ge(sh, v)
                if fn is not None:
                    ins = fn(eng)
                    ins.then_inc(s, inc)

        with nc.Block() as block:
            @block.tensor
            def _(e):
                run("pe", e)

            @block.scalar
            def _(e):
                run("act", e)

            @block.vector
            def _(e):
                run("dve", e)

            @block.gpsimd
            def _(e):
                run("pool", e)

            @block.sync
            def _(e):
                run("sp", e)
        self.streams = {e: [] for e in self.ENGS}

    def close(self):
        self.es.close()


LN_EPS = 1e-5
ALPHA = 8.0 ** 0.25
S = 8192
NWA = 898
NTOK = 2048


def AP(t, off, dims):
    return bass.AP(t.t if isinstance(t, Tile) else t, off, dims)


def make_consts(P):
    C = {}
    identf = P.sb([128, 128], F32, "identf")
    ident = P.sb([128, 128], BF16, "ident")
    trif = P.sb([128, 128], F32, "trif")
    trib = P.sb([128, 128], BF16, "trib")
    onesf = P.sb([128, 128], F32, "onesf")
    ones_bf = P.sb([128, 128], BF16, "ones")
    iota16 = P.sb([128, 16], F32, "iota16")
    thr16 = P.sb([128, 16], F32, "thr16")
    P.op('dve', lambda e: e.memset(identf[:], 0.0), writes=[identf])
    P.op('pool', lambda e: e.affine_select(out=identf[:], in_=identf[:], pattern=[[-1, 128]],
                                           compare_op=ALU.not_equal, fill=1.0, base=0, channel_multiplier=1),
         reads=[identf], writes=[identf])
    P.op('dve', lambda e: e.tensor_copy(ident[:], identf[:]), reads=[identf], writes=[ident])
    P.op('dve', lambda e: e.memset(onesf[:], 1.0), writes=[onesf])
    P.op('dve', lambda e: e.memset(ones_bf[:], 1.0), writes=[ones_bf])
    P.op('pool', lambda e: e.iota(trif[:], pattern=[[1, 128]], base=0, channel_multiplier=-1,
                                  allow_small_or_imprecise_dtypes=True), writes=[trif])
    P.op('dve', lambda e: e.tensor_single_scalar(trif[:], trif[:], 0.0, op=ALU.is_ge), reads=[trif], writes=[trif])
    P.op('dve', lambda e: e.tensor_copy(trib[:], trif[:]), reads=[trif], writes=[trib])
    P.op('pool', lambda e: e.iota(iota16[:], pattern=[[1, 16]], base=0, channel_multiplier=0,
                                  allow_small_or_imprecise_dtypes=True), writes=[iota16])
    P.op('pool', lambda e: e.iota(thr16[:], pattern=[[16, 16]], base=16, channel_multiplier=0,
                                  allow_small_or_imprecise_dtypes=True), writes=[thr16])
    C.update(identf=identf, ident=ident, trif=trif, trib=trib, onesf=onesf, ones_bf=ones_bf, iota16=iota16, thr16=thr16)
    return C


def phase_a(P, nc, C, H, xT_h, xg_h, xg_res, mixsrc_h, mixsrc_res, nq_tiles):
    ident, trif, trib, onesf = C['ident'], C['trif'], C['trib'], C['onesf']
    D = 1024
    ntok = nq_tiles * 256
    nblk = ntok // 128
    spt = P.sb([128, 16], F32, "spt")
    lamq = P.sb([128, 256], F32, "lamq")
    gda = P.sb([128, 128], F32, "gda")
    gml = P.sb([128, 128], F32, "gml")
    P.dma('sp', spt[:], H['sp'].ap(), writes=[spt])
    P.dma('sp', lamq[:], H['lamqk'].ap(), writes=[lamq])
    P.dma('sp', gda[:], H['gda'].ap(), writes=[gda])
    P.dma('sp', gml[:], H['gml'].ap(), writes=[gml])
    lt = P.sb([128, 128], F32, "lt")
    lsum = P.sb([128, 2], F32, "lsum")
    neglam = P.sb([128, 1], F32, "neglam")
    P.op('dve', lambda e: e.tensor_tensor(lt[:].rearrange("p (a d) -> p a d", a=2),
                                          AP(lamq, 0, [[256, 128], [128, 2], [1, 64]]),
                                          AP(lamq, 64, [[256, 128], [128, 2], [1, 64]]), op=ALU.mult),
         reads=[lamq], writes=[lt])
    P.op('dve', lambda e: e.tensor_reduce(lsum[:], lt[:].rearrange("p (a d) -> p a d", a=2), axis=AX.X, op=ALU.add),
         reads=[lt], writes=[lsum])
    P.op('act', lambda e: e.activation(lsum[:], lsum[:], AF.Exp), reads=[lsum], writes=[lsum])
    P.op('dve', lambda e: e.tensor_tensor(neglam[:], lsum[:, 1:2], lsum[:, 0:1], op=ALU.subtract), reads=[lsum], writes=[neglam])
    P.op('dve', lambda e: e.tensor_tensor(neglam[:], neglam[:], spt[:, 12:13], op=ALU.subtract), reads=[neglam, spt], writes=[neglam])
    P.op('dve', lambda e: e.tensor_scalar(gda[:], gda[:], spt[:, 13:14], None, op0=ALU.mult), reads=[gda, spt], writes=[gda])

    wab = P.sb([128, 8, NWA], BF16, "wab")
    stg = P.sb([128, 1024], F32, "stg")
    wap = H['wa'].ap().rearrange("(c p) n -> p c n", p=128)
    for kc in range(8):
        P.dma('sp', stg[:, 0:NWA], wap[:, kc, :], writes=[stg])
        P.op('dve', lambda e, kc=kc: e.tensor_copy(wab[:, kc, :], stg[:, 0:NWA]), reads=[stg], writes=[wab])

    daqT = [P.sb([64, S], BF16, f"daqT{c}") for c in range(2)]
    dakT = [P.sb([64, S], BF16, f"dakT{c}") for c in range(2)]
    Vaug = P.sb([128, 64, 129], BF16, "Vaug")
    mlqT = P.sb([128, S], BF16, "mlqT")
    mlkT = P.sb([128, S], BF16, "mlkT")
    mlV = P.sb([128, 64, 129], BF16, "mlV")
    sigo = P.sb([128, 64, 128], BF16, "sigo")
    gates = P.sb([128, 64, 2], F32, "gates")
    P.op('dve', lambda e: e.memset(Vaug[:, :, 128:129], 1.0), writes=[Vaug])
    P.op('dve', lambda e: e.memset(mlV[:, :, 128:129], 1.0), writes=[mlV])

    banks = [P.ps([128, 512], F32, f"bank{i}") for i in range(8)]

    xs = P.sb([128, 8, 256], F32, "xs")
    xb = P.sb([128, 8, 256], BF16, "xb")
    pre = [P.sb([128, 3 + 256], F32, f"pre{i}") for i in range(2)]
    cacc = P.sb([128, 256], F32, "cacc")
    ctmp = P.sb([128, 256], F32, "ctmp")
    xT_v = xT_h.ap().rearrange("(c p) t -> p c t", p=128) if xg_h is None else None
    for i in range(2):
        P.op('dve', lambda e, i=i: e.memset(pre[i][:, 0:3], 0.0), writes=[pre[i]])
    for ti in range(nq_tiles):
        t0 = ti * 256
        if xg_h is None:
            P.dma('sp', xs[:], xT_v[:, :, t0:t0 + 256], writes=[xs])
            P.op('pool', lambda e: e.tensor_copy(xb[:], xs[:]), reads=[xs], writes=[xb])
        else:
            jr, tl = t0 // 2048, t0 % 2048
            for c4 in range(4):
                P.dma('sp', xb[:, 2 * c4:2 * c4 + 2, :],
                      xg_h[c4].ap()[jr * 256:(jr + 1) * 256, tl:tl + 256].rearrange("(k p) t -> p k t", p=128),
                      reads=[xg_res], writes=[xb])
        for g in range(4):
            pb = banks[g // 2]
            col0 = (g // 2) * 128 + (g % 2) * 64
            for kc in range(8):
                P.op('pe', lambda e, g=g, kc=kc, pb=pb, col0=col0: e.matmul(
                    pb[0:64, (g % 2) * 256:(g % 2) * 256 + 256], wab[:, kc, col0:col0 + 64], xb[:, kc, :],
                    start=(kc == 0), stop=(kc == 7)), reads=[wab, xb], writes=[pb])
        for c in range(2):
            P.op('act', lambda e, t0=t0, c=c: e.copy(daqT[c][:, t0:t0 + 256], banks[0][0:64, c * 256:c * 256 + 256]),
                 excl=[banks[0]], writes=[daqT[c]])
            P.op('dve', lambda e, t0=t0, c=c: e.tensor_copy(dakT[c][:, t0:t0 + 256], banks[1][0:64, c * 256:c * 256 + 256]),
                 excl=[banks[1]], writes=[dakT[c]])
        for g in range(2):
            col0 = 384 + g * 128
            for kc in range(8):
                P.op('pe', lambda e, g=g, kc=kc, col0=col0: e.matmul(
                    banks[2][:, g * 256:g * 256 + 256], wab[:, kc, col0:col0 + 128], xb[:, kc, :],
                    start=(kc == 0), stop=(kc == 7)), reads=[wab, xb], writes=[banks[2]])
        for g in range(2):
            pr = pre[g]
            if ti > 0:
                P.op('dve', lambda e, pr=pr: e.tensor_copy(pr[:, 0:3], pr[:, 256:259]), reads=[pr], writes=[pr])
            P.op('act', lambda e, g=g, pr=pr: e.copy(pr[:, 3:259], banks[2][:, g * 256:g * 256 + 256]),
                 excl=[banks[2]], writes=[pr])
            P.op('dve', lambda e, g=g, pr=pr: e.tensor_scalar(cacc[:], pr[:, 3:259], spt[:, g * 4 + 3:g * 4 + 4], spt[:, 8 + g:9 + g],
                                                              op0=ALU.mult, op1=ALU.add), reads=[pr, spt], writes=[cacc])
            for j in range(3):
                P.op('dve', lambda e, g=g, pr=pr, j=j: e.scalar_tensor_tensor(
                    out=cacc[:], in0=pr[:, j:j + 256], scalar=spt[:, g * 4 + j:g * 4 + j + 1], in1=cacc[:],
                    op0=ALU.mult, op1=ALU.add), reads=[pr, spt, cacc], writes=[cacc])
            if g == 0:
                P.op('act', lambda e, t0=t0: e.activation(mlqT[:, t0:t0 + 256], cacc[:], AF.Silu), reads=[cacc], writes=[mlqT])
            else:
                P.op('act', lambda e: e.activation(ctmp[:], cacc[:], AF.Silu), reads=[cacc], writes=[ctmp])
                P.op('dve', lambda e, t0=t0: e.tensor_scalar(mlkT[:, t0:t0 + 256], ctmp[:], 128.0 ** -0.5, None, op0=ALU.mult),
                     reads=[ctmp], writes=[mlkT])
        for sub in range(2):
            blk = ti * 2 + sub
            pb = banks[3 + sub]
            for gi, col0 in enumerate((256, 640, 768)):
                for kc in range(8):
                    P.op('pe', lambda e, sub=sub, gi=gi, col0=col0, kc=kc, pb=pb: e.matmul(
                        pb[:, gi * 128:(gi + 1) * 128], xb[:, kc, sub * 128:(sub + 1) * 128], wab[:, kc, col0:col0 + 128],
                        start=(kc == 0), stop=(kc == 7)), reads=[wab, xb], writes=[pb])
            for kc in range(8):
                P.op('pe', lambda e, sub=sub, kc=kc, blk=blk: e.matmul(
                    banks[5][:, blk * 2:blk * 2 + 2], xb[:, kc, sub * 128:(sub + 1) * 128], wab[:, kc, 896:898],
                    start=(kc == 0), stop=(kc == 7)), reads=[wab, xb], writes=[banks[5]])
            P.op('act', lambda e, blk=blk, pb=pb: e.copy(Vaug[:, blk, 0:128], pb[:, 0:128]), excl=[pb], writes=[Vaug])
            P.op('dve', lambda e, blk=blk, pb=pb: e.tensor_copy(mlV[:, blk, 0:128], pb[:, 128:256]), excl=[pb], writes=[mlV])
            P.op('act', lambda e, blk=blk, pb=pb: e.activation(sigo[:, blk, :], pb[:, 256:384], AF.Sigmoid), excl=[pb], writes=[sigo])
    P.op('dve', lambda e: e.tensor_copy(gates[:, 0:nblk, :], banks[5][:, 0:nblk * 2].rearrange("p (b g) -> p b g", g=2)),
         excl=[banks[5]], writes=[gates])

    Eb = [P.sb([128, 2, 256], BF16, f"Eb{i}") for i in range(2)]
    o_da = P.sb([128, 128], F32, "o_da")
    o_out = [P.sb([128, 128], BF16, f"o_out{i}") for i in range(2)]
    oT_sb = [P.sb([128, 128], BF16, f"oT_sb{i}") for i in range(2)]
    pT6 = banks[6].t.bitcast(BF16)
    pT7 = banks[7].t.bitcast(BF16)
    junk = P.sb([128, 128], F32, "junk")
    rz = P.sb([128, 2], F32, "rz")
    ss = P.sb([128, 1], F32, "ss")
    step = 0
    nout = 0
    for qt in range(nq_tiles):
        q0 = qt * 256
        nj = 2 * qt + 2
        for j in range(nj):
            pS = banks[step % 2]
            E = Eb[step % 2]
            step += 1
            subs = (0, 1) if j < nj - 1 else (1,)
            qa, qb = (0, 256) if j < nj - 1 else (128, 256)
            for c in range(2):
                P.op('pe', lambda e, c=c, j=j, pS=pS, qa=qa, qb=qb, q0=q0: e.matmul(
                    pS[:, c * 256 + qa:c * 256 + qb], dakT[c][:, j * 128:(j + 1) * 128], daqT[c][:, q0 + qa:q0 + qb],
                    start=True, stop=True), reads=[dakT[c], daqT[c]], writes=[pS])
            P.op('act', lambda e, E=E, pS=pS, qa=qa, qb=qb: e.activation(
                E[:, :, qa:qb], pS[:, :].rearrange("p (c q) -> p c q", c=2)[:, :, qa:qb], AF.Exp, scale=0.125),
                 excl=[pS], writes=[E])
            if j >= nj - 2:
                sm = 0 if j == nj - 2 else 1
                P.op('pool', lambda e, E=E, sm=sm: e.tensor_tensor(
                    E[:, :, sm * 128:(sm + 1) * 128], E[:, :, sm * 128:(sm + 1) * 128],
                    AP(trib, 0, [[128, 128], [0, 2], [1, 128]]), op=ALU.mult), reads=[E, trib], writes=[E])
            for s_ in subs:
                last_j = nj - 2 if s_ == 0 else nj - 1
                for c in range(2):
                    pacc = banks[2 + s_ * 2 + c]
                    P.op('pe', lambda e, E=E, s_=s_, c=c, j=j, pacc=pacc, last_j=last_j: e.matmul(
                        pacc[:, 0:129], E[:, c, s_ * 128:(s_ + 1) * 128], Vaug[:, j, :],
                        start=(j == 0), stop=(j == last_j)), reads=[E, Vaug], writes=[pacc])
        for s_ in range(2):
            p0, p1 = banks[2 + s_ * 2], banks[2 + s_ * 2 + 1]
            oo = o_out[nout % 2]
            nout += 1
            P.op('dve', lambda e, p0=p0: e.reciprocal(rz[:, 0:1], p0[:, 128:129]), excl=[p0], writes=[rz])
            P.op('dve', lambda e, p1=p1: e.reciprocal(rz[:, 1:2], p1[:, 128:129]), excl=[p1], writes=[rz])
            P.op('dve', lambda e: e.tensor_tensor(rz[:, 1:2], rz[:, 1:2], neglam[:], op=ALU.mult), reads=[rz, neglam], writes=[rz])
            P.op('dve', lambda e, p0=p0: e.tensor_scalar(o_da[:], p0[:, 0:128], rz[:, 0:1], None, op0=ALU.mult),
                 reads=[rz], excl=[p0], writes=[o_da])
            P.op('dve', lambda e, p1=p1: e.scalar_tensor_tensor(out=o_da[:], in0=p1[:, 0:128], scalar=rz[:, 1:2], in1=o_da[:],
                                                                op0=ALU.mult, op1=ALU.add), reads=[rz, o_da], excl=[p1], writes=[o_da])
            P.op('dve', lambda e: e.scalar_tensor_tensor(out=junk[:], in0=o_da[:], scalar=1.0, in1=o_da[:], op0=ALU.mult, op1=ALU.mult,
                                                         accum_out=ss[:]), reads=[o_da], writes=[junk, ss])
            P.op('dve', lambda e: e.tensor_scalar(ss[:], ss[:], 1.0 / 128.0, LN_EPS, op0=ALU.mult, op1=ALU.add), reads=[ss], writes=[ss])
            P.op('act', lambda e: e.activation(ss[:], ss[:], AF.Sqrt), reads=[ss], writes=[ss])
            P.op('dve', lambda e: e.reciprocal(ss[:], ss[:]), reads=[ss], writes=[ss])
            P.op('dve', lambda e, oo=oo: e.scalar_tensor_tensor(out=oo[:], in0=o_da[:], scalar=ss[:, 0:1], in1=gda[:], op0=ALU.mult, op1=ALU.mult),
                 reads=[o_da, ss, gda], writes=[oo])
            g_blk = (q0 + s_ * 128) // 128
            oT_ = oT_sb[nout % 2]
            P.op('pe', lambda e, oo=oo: e.transpose(bass.AP(pT6, 0, [[1024, 128], [1, 128]]), oo[:], ident[:]),
                 reads=[oo, ident], writes=[banks[6]])
            P.op('act', lambda e, oT_=oT_: e.copy(oT_[:], bass.AP(pT6, 0, [[1024, 128], [1, 128]])), excl=[banks[6]], writes=[oT_])
            jj, qq, ww = g_blk // 16, (g_blk % 16) // 4, g_blk % 4
            P.dma('sp', mixsrc_h[jj].ap()[(qq * 2) * 128:(qq * 2 + 1) * 128, ww * 128:(ww + 1) * 128], oT_[:], reads=[oT_], writes=[mixsrc_res])

    nch = nblk
    lf = P.sb([128, 64], F32, "lf")
    bcs = P.sb([128, 64], F32, "bcs")
    ek = P.sb([128, 64], F32, "ek")
    eb = P.sb([128, 64], F32, "eb")
    ebL = P.sb([128, 64], F32, "ebL")
    nfb = P.sb([128, 1], F32, "nfb")
    P.op('dve', lambda e: e.tensor_scalar(nfb[:], spt[:, 11:12], -1.0, None, op0=ALU.mult), reads=[spt], writes=[nfb])
    P.op('act', lambda e: e.activation(lf[:, 0:nch], gates[:, 0:nch, 1], AF.Exp, bias=nfb[:, 0:1], scale=-1.0), reads=[gates, nfb], writes=[lf])
    P.op('dve', lambda e: e.tensor_scalar(lf[:, 0:nch], lf[:, 0:nch], 1.0, None, op0=ALU.add), reads=[lf], writes=[lf])
    P.op('act', lambda e: e.activation(lf[:, 0:nch], lf[:, 0:nch], AF.Ln), reads=[lf], writes=[lf])
    P.op('dve', lambda e: e.tensor_scalar(lf[:, 0:nch], lf[:, 0:nch], -1.0, None, op0=ALU.mult), reads=[lf], writes=[lf])
    P.op('pe', lambda e: e.matmul(banks[0][:, 0:nch], trif[:], lf[:, 0:nch], start=True, stop=True), reads=[trif, lf], writes=[banks[0]])
    P.op('pe', lambda e: e.matmul(banks[1][:, 0:nch], onesf[:], lf[:, 0:nch], start=True, stop=True), reads=[onesf, lf], writes=[banks[1]])
    P.op('dve', lambda e: e.tensor_copy(bcs[:, 0:nch], banks[0][:, 0:nch]), excl=[banks[0]], writes=[bcs])
    P.op('act', lambda e: e.activation(eb[:, 0:nch], bcs[:, 0:nch], AF.Exp), reads=[bcs], writes=[eb])
    P.op('act', lambda e: e.activation(ebL[:, 0:nch], banks[1][:, 0:nch], AF.Exp), excl=[banks[1]], writes=[ebL])
    P.op('dve', lambda e: e.tensor_tensor(ek[:, 0:nch], gates[:, 0:nch, 0], bcs[:, 0:nch], op=ALU.subtract), reads=[gates, bcs], writes=[ek])
    P.op('act', lambda e: e.activation(ek[:, 0:nch], ek[:, 0:nch], AF.Exp, bias=spt[:, 10:11], scale=1.0), reads=[ek, spt], writes=[ek])

    Dst = [P.sb([128, 129], F32, f"Dst{i}") for i in range(2)]
    Cb = [P.sb([128, 129], BF16, f"Cb{i}") for i in range(2)]
    ktok = [P.sb([128, 128], BF16, f"ktok{i}") for i in range(2)]
    vp = [P.sb([128, 129], BF16, f"vp{i}") for i in range(2)]
    ATb = [P.sb([128, 128], BF16, f"ATb{i}") for i in range(2)]
    hh = P.sb([128, 128], F32, "hh")
    hsm = P.sb([128, 4], F32, "hsm")
    st6 = P.sb([128, 6], F32, "st6")
    mvv = P.sb([128, 2], F32, "mvv")
    rstd = P.sb([128, 1], F32, "rstd")
    ho = [P.sb([128, 128], BF16, f"ho{i}") for i in range(2)]
    pKt = banks[2].t.bitcast(BF16)
    for c in range(nch):
        sl = slice(c * 128, (c + 1) * 128)
        kt, v_, at = ktok[c % 2], vp[c % 2], ATb[c % 2]
        P.op('pe', lambda e, sl=sl: e.transpose(bass.AP(pKt, 0, [[1024, 128], [1, 128]]), mlkT[:, sl], ident[:]),
             reads=[mlkT, ident], writes=[banks[2]])
        P.op('act', lambda e, kt=kt: e.copy(kt[:], bass.AP(pKt, 0, [[1024, 128], [1, 128]])), excl=[banks[2]], writes=[kt])
        P.op('pool', lambda e, c=c, v_=v_: e.tensor_scalar(v_[:], mlV[:, c, :], ek[:, c:c + 1], None, op0=ALU.mult),
             reads=[mlV, ek], writes=[v_])
        P.op('pe', lambda e, kt=kt, v_=v_: e.matmul(banks[3][:, 0:129], kt[:], v_[:], start=True, stop=True),
             reads=[kt, v_], writes=[banks[3]])
        P.op('pe', lambda e, sl=sl: e.matmul(banks[4][:, 0:128], mlkT[:, sl], mlqT[:, sl], start=True, stop=True),
             reads=[mlkT, mlqT], writes=[banks[4]])
        P.op('dve', lambda e, c=c, at=at: e.scalar_tensor_tensor(out=at[:], in0=banks[4][:, 0:128], scalar=ek[:, c:c + 1], in1=trif[:],
                                                                 op0=ALU.mult, op1=ALU.mult), reads=[ek, trif], excl=[banks[4]], writes=[at])
        pH = banks[5 + c % 2]
        if c > 0:
            P.op('pe', lambda e, sl=sl, c=c, pH=pH: e.matmul(pH[:, 0:129], mlqT[:, sl], Cb[(c - 1) % 2][:], start=True, stop=False),
                 reads=[mlqT, Cb[(c - 1) % 2]], writes=[pH])
        P.op('pe', lambda e, at=at, c=c, pH=pH: e.matmul(pH[:, 0:129], at[:], mlV[:, c, :], start=(c == 0), stop=True),
             reads=[at, mlV], writes=[pH])
        Dc = Dst[c % 2]
        if c == 0:
            P.op('dve', lambda e, Dc=Dc: e.tensor_copy(Dc[:], banks[3][:, 0:129]), excl=[banks[3]], writes=[Dc])
        else:
            Dp = Dst[(c - 1) % 2]
            P.op('dve', lambda e, Dc=Dc, Dp=Dp, c=c: e.scalar_tensor_tensor(out=Dc[:], in0=Dp[:], scalar=ebL[:, c - 1:c], in1=banks[3][:, 0:129],
                                                                          op0=ALU.mult, op1=ALU.add), reads=[Dp, ebL], excl=[banks[3]], writes=[Dc])
        P.op('act', lambda e, Dc=Dc, c=c: e.activation(Cb[c % 2][:], Dc[:], AF.Identity, scale=ebL[:, c:c + 1]), reads=[Dc, ebL], writes=[Cb[c % 2]])
        P.op('dve', lambda e, c=c, pH=pH: e.tensor_tensor(hsm[:, 0:1], pH[:, 128:129], eb[:, c:c + 1], op=ALU.mult), reads=[eb], excl=[pH], writes=[hsm])
        P.op('dve', lambda e: e.tensor_scalar(hsm[:, 1:2], hsm[:, 0:1], -1.0, 1.0, op0=ALU.mult, op1=ALU.max), reads=[hsm], writes=[hsm])
        P.op('dve', lambda e: e.tensor_scalar(hsm[:, 2:3], hsm[:, 0:1], 1.0, None, op0=ALU.max), reads=[hsm], writes=[hsm])
        P.op('dve', lambda e: e.tensor_tensor(hsm[:, 1:2], hsm[:, 1:2], hsm[:, 2:3], op=ALU.max), reads=[hsm], writes=[hsm])
        P.op('dve', lambda e: e.reciprocal(hsm[:, 2:3], hsm[:, 1:2]), reads=[hsm], writes=[hsm])
        P.op('dve', lambda e, c=c: e.tensor_tensor(hsm[:, 3:4], hsm[:, 2:3], eb[:, c:c + 1], op=ALU.mult), reads=[hsm, eb], writes=[hsm])
        P.op('dve', lambda e, pH=pH: e.tensor_scalar(hh[:], pH[:, 0:128], hsm[:, 3:4], None, op0=ALU.mult), reads=[hsm], excl=[pH], writes=[hh])
        P.op('dve', lambda e: e.bn_stats(st6[:], hh[:]), reads=[hh], writes=[st6])
        P.op('dve', lambda e: e.bn_aggr(mvv[:], st6[:]), reads=[st6], writes=[mvv])
        P.op('dve', lambda e: e.tensor_scalar(rstd[:], mvv[:, 1:2], LN_EPS, None, op0=ALU.add), reads=[mvv], writes=[rstd])
        P.op('act', lambda e: e.activation(rstd[:], rstd[:], AF.Sqrt), reads=[rstd], writes=[rstd])
        P.op('dve', lambda e: e.reciprocal(rstd[:], rstd[:]), reads=[rstd], writes=[rstd])
        P.op('dve', lambda e: e.tensor_scalar(hh[:], hh[:], mvv[:, 0:1], rstd[:, 0:1], op0=ALU.subtract, op1=ALU.mult),
             reads=[hh, mvv, rstd], writes=[hh])
        P.op('pool', lambda e: e.tensor_tensor(hh[:], hh[:], gml[:], op=ALU.mult), reads=[hh, gml], writes=[hh])
        hoo = ho[c % 2]
        P.op('pool', lambda e, c=c, hoo=hoo: e.tensor_tensor(hoo[:], hh[:], sigo[:, c, :], op=ALU.mult), reads=[hh, sigo], writes=[hoo])
        hT_ = oT_sb[c % 2]
        P.op('pe', lambda e, hoo=hoo: e.transpose(bass.AP(pT7, 0, [[1024, 128], [1, 128]]), hoo[:], ident[:]),
             reads=[hoo, ident], writes=[banks[7]])
        P.op('act', lambda e, hT_=hT_: e.copy(hT_[:], bass.AP(pT7, 0, [[1024, 128], [1, 128]])), excl=[banks[7]], writes=[hT_])
        jj, qq, ww = c // 16, (c % 16) // 4, c % 4
        P.dma('sp', mixsrc_h[jj].ap()[(qq * 2 + 1) * 128:(qq * 2 + 2) * 128, ww * 128:(ww + 1) * 128], hT_[:], reads=[hT_], writes=[mixsrc_res])


def phase_b(P, nc, C, H, xin_h, xin_res, mixg_h, mixg_res, midx_h, xout_h, xout_res, xsrc_h, xsrc_res, ntiles):
    ident, ones_bf, iota16, thr16 = C['ident'], C['ones_bf'], C['iota16'], C['thr16']
    D = 1024
    lnp = P.sb([128, 6 * D], F32, "lnp")
    P.dma('sp', lnp[:], bass.AP(H['lnp'], 0, [[0, 128], [1, 6 * D]]), writes=[lnp])

    r = P.sb([128, D], F32, "r")
    stg = [r] * 2
    stg_i = [0]

    def load_w(handle, n, name, dst=None):
        wb = dst if dst is not None else P.sb([128, 8, n], BF16, name)
        wap = handle.ap().rearrange("(c p) n -> p c n", p=128)
        for kc in range(8):
            for pc in range(n // 1024):
                s = stg[stg_i[0] % 2]
                stg_i[0] += 1
                P.dma('sp', s[:, :], wap[:, kc, pc * 1024:(pc + 1) * 1024], writes=[s])
                if stg_i[0] % 2 == 0:
                    P.op('act', lambda e, s=s, kc=kc, pc=pc: e.copy(wb[:, kc, pc * 1024:(pc + 1) * 1024], s[:, :]), reads=[s], writes=[wb])
                else:
                    P.op('dve', lambda e, s=s, kc=kc, pc=pc: e.tensor_copy(wb[:, kc, pc * 1024:(pc + 1) * 1024], s[:, :]), reads=[s], writes=[wb])
        return wb

    wout_b = load_w(H['wout'], 1024, "wout")
    wq_b = load_w(H['wq'], 1024, "wq")
    wo_b = load_w(H['wo'], 1024, "wo")
    wpq_b = load_w(H['wpq'], 2048, "wpq")

    pA = P.ps([128, 2048], F32, "pA")
    pB = P.ps([128, 1024], F32, "pB")
    pC = P.ps([128, 512], F32, "pC")
    pT = P.ps([128, 8, 128], BF16, "pT")

    KT_b = P.sb([128, 8, 256], BF16, "KT")
    V_b = P.sb([128, 2, 1024], BF16, "V")
    skT_b = P.sb([128, 2, 128], BF16, "skT")
    NROW = 4
    gbuf = P.sb([128, 2 * NROW, 1024], F32, "gbuf")
    gb_bf = gbuf.t.bitcast(BF16)

    def wkv_ap(kc, c0, c1):
        return bass.AP(gb_bf, kc * 2048 + c0, [[2 * NROW * 1024 * 2, 128], [1, c1 - c0]])

    wkvap = H['wkv'].ap().rearrange("(c p) n -> p c n", p=128)
    for kc in range(8):
        for pc in range(2):
            s = stg[stg_i[0] % 2]
            stg_i[0] += 1
            P.dma('sp', s[:, :], wkvap[:, kc, pc * 1024:(pc + 1) * 1024], writes=[s])
            P.op('dve', lambda e, s=s, kc=kc, pc=pc: e.tensor_copy(wkv_ap(kc, pc * 1024, (pc + 1) * 1024), s[:, :]), reads=[s], writes=[gbuf])
    sc = P.sb([128, 16, 128], F32, "sc")
    w8bf = sc.t.bitcast(BF16)
    memT_b = sc

    def memT_ap(kc, m0, m1):
        return bass.AP(w8bf, kc * 256 + m0, [[4096, 128], [1, m1 - m0]])

    def pqT_ap(hc):
        return bass.AP(w8bf, hc * 128, [[4096, 128], [1, 128]])
    mT_ap = H['memT'].ap().rearrange("(c p) n -> p c n", p=128)
    for kc in range(8):
        s = stg[stg_i[0] % 2]
        stg_i[0] += 1
        P.dma('sp', s[:, 0:256], mT_ap[:, kc, :], writes=[s])
        P.op('dve', lambda e, s=s, kc=kc: e.tensor_copy(memT_ap(kc, 0, 256), s[:, 0:256]), reads=[s], writes=[memT_b])
    s = stg[stg_i[0] % 2]
    stg_i[0] += 1
    P.dma('sp', s[:, 0:256].rearrange("p (c n) -> p c n", c=2), H['skT'].ap().rearrange("c d n -> d c n"), writes=[s])
    P.op('dve', lambda e, s=s: e.tensor_copy(skT_b[:], s[:, 0:256].rearrange("p (c n) -> p c n", c=2)), reads=[s], writes=[skT_b])

    for j in range(8):
        for kc in range(8):
            P.op('pe', lambda e, j=j, kc=kc: e.matmul(pB[:, 0:256], wkv_ap(kc, j * 128, (j + 1) * 128), memT_ap(kc, 0, 256),
                                                      start=(kc == 0), stop=(kc == 7)),
                 reads=[gbuf, memT_b], writes=[pB])
        P.op('dve', lambda e, j=j: e.tensor_copy(KT_b[:, j, :], pB[:, 0:256]), excl=[pB], writes=[KT_b])
    for mc in range(2):
        for half in range(2):
            for kc in range(8):
                P.op('pe', lambda e, mc=mc, half=half, kc=kc: e.matmul(
                    pB[:, 0:512], memT_ap(kc, mc * 128, (mc + 1) * 128),
                    wkv_ap(kc, 1024 + half * 512, 1024 + (half + 1) * 512), start=(kc == 0), stop=(kc == 7)),
                     reads=[gbuf, memT_b], writes=[pB])
            P.op('dve', lambda e, mc=mc, half=half: e.tensor_copy(V_b[:, mc, half * 512:(half + 1) * 512], pB[:, 0:512]),
                 excl=[pB], writes=[V_b])

    xt = P.sb([128, D], F32, "xt")
    mixb = P.sb([128, 8, 512], BF16, "mixb")
    x1 = P.sb([128, D], F32, "x1")
    x2 = P.sb([128, D], F32, "x2")
    xb = P.sb([128, D], BF16, "xb")
    xT = P.sb([128, 8, 128], BF16, "xT")
    qT = P.sb([128, 8, 128], BF16, "qT")
    E_b = P.sb([128, 8, 128], BF16, "E")
    rz = P.sb([128, 4, 128], F32, "rz")
    oT = qT
    top = P.sb([128, 16, 16], F32, "top")
    idxu = P.sb([128, 16, 16], U32, "idxu")
    idxf = P.sb([128, 16, 16], F32, "idxf")
    cs = P.sb([128, 8, 16], F32, "cs")
    ciu = P.sb([128, 8, 16], U32, "ciu")
    cif = P.sb([128, 128], F32, "cif")
    big = sc
    big2 = sc
    pqT = sc
    mixf = r
    x3 = r
    junk = r
    cand = sc
    k0f = P.sb([128, 128], F32, "k0f")
    k1f = P.sb([128, 128], F32, "k1f")
    i0s = P.sb([128, 128], F32, "i0s")
    i1s = P.sb([128, 128], F32, "i1s")
    ef = P.sb([128, 128], F32, "ef")
    eu = P.sb([128, 128], U32, "eu")
    gex = P.sb([128, 8, 16], F32, "gex")
    gz = P.sb([128, 8], F32, "gz")
    aact = P.sb([128, 128], F32, "aact")
    wgt = P.sb([128, 128], F32, "wgt")
    acc = x1
    st = P.sb([128, 12], F32, "st")
    mv = P.sb([128, 2], F32, "mv")
    rstd = P.sb([128, 1], F32, "rstd")
    grow = [Res(f"grow{i}") for i in range(16)]
    dgs = [P.sb([128, 128], BF16, f"dg{i}") for i in range(4)]
    gel = P.sb([128, 128], F32, "gel")
    acol = [Res() for _ in range(128)]
    gcol = [Res() for _ in range(128)]
    wcol = [Res() for _ in range(128)]

    def layernorm(src, dst, li):
        for c in range(2):
            P.op('dve', lambda e, c=c: e.bn_stats(st[:, c * 6:(c + 1) * 6], src[:, c * 512:(c + 1) * 512]),
                 reads=[src], writes=[st])
        P.op('dve', lambda e: e.bn_aggr(mv[:], st[:]), reads=[st], writes=[mv])
        P.op('dve', lambda e: e.tensor_scalar(rstd[:], mv[:, 1:2], LN_EPS, None, op0=ALU.add), reads=[mv], writes=[rstd])
        P.op('act', lambda e: e.activation(rstd[:], rstd[:], AF.Sqrt), reads=[rstd], writes=[rstd])
        P.op('dve', lambda e: e.reciprocal(rstd[:], rstd[:]), reads=[rstd], writes=[rstd])
        P.op('dve', lambda e: e.tensor_scalar(dst[:], src[:], mv[:, 0:1], rstd[:, 0:1], op0=ALU.subtract, op1=ALU.mult),
             reads=[src, mv, rstd], writes=[dst])
        P.op('dve', lambda e: e.tensor_tensor(dst[:], dst[:], lnp[:, (2 * li) * D:(2 * li + 1) * D], op=ALU.mult),
             reads=[dst, lnp], writes=[dst])
        P.op('dve', lambda e: e.tensor_tensor(dst[:], dst[:], lnp[:, (2 * li + 1) * D:(2 * li + 2) * D], op=ALU.add),
             reads=[dst, lnp], writes=[dst])

    def to_T(src):
        P.op('act', lambda e: e.copy(xb[:], src[:]), reads=[src], writes=[xb])
        for c in range(8):
            P.op('pe', lambda e, c=c: e.transpose(pT[:, c, :], xb[:, c * 128:(c + 1) * 128], ident[:]),
                 reads=[xb, ident], writes=[pT])
        P.op('dve', lambda e: e.tensor_copy(xT[:], pT[:]), excl=[pT], writes=[xT])

    def linear(lhsT_tile, w_b, n, pdst, off=0):
        for half in range(n // 512):
            for kc in range(8):
                P.op('pe', lambda e, half=half, kc=kc: e.matmul(
                    pdst[:, half * 512:(half + 1) * 512], lhsT_tile[:, kc, off:off + 128], w_b[:, kc, half * 512:(half + 1) * 512],
                    start=(kc == 0), stop=(kc == 7)), reads=[lhsT_tile, w_b], writes=[pdst])

    midx = P.sb([128, 32], U32, "midx")
    P.dma('sp', midx[:], midx_h.ap(), writes=[midx])

    def brow(rr):
        return bass.AP(gb_bf, rr * 1024, [[16384, 128], [1, 1024]])

    P.barrier()
    cst = [Res(f"cst{k}") for k in range(4)]
    stb = [r, x1, x2, xt]
    for k in range(128):
        tab = H['u'] if k < 64 else H['v']
        tsel = 0 if k < 64 else 1
        row0 = (k % 64) * 256
        sf = gbuf[:, 2 * (k % 4):2 * (k % 4) + 2, :]
        tb = stb[k % 4]
        sbv = bass.AP(tb.t.bitcast(BF16), 0, [[2048, 128], [1024, 2], [1, 1024]])
        P.dma('sp', sf, tab.ap()[row0:row0 + 256, :].rearrange("(a p) n -> p a n", p=128), writes=[cst[k % 4]])
        if k % 2 == 0:
            P.op('dve', lambda e, sf=sf, sbv=sbv: e.tensor_copy(sbv, sf), reads=[cst[k % 4]], writes=[tb])
        else:
            P.op('act', lambda e, sf=sf, sbv=sbv: e.copy(sbv, sf), reads=[cst[k % 4]], writes=[tb])
        P.dma('pool', H['uvb'].ap()[row0:row0 + 256, tsel * 1024:(tsel + 1) * 1024].rearrange("(a p) n -> p a n", p=128), sbv, reads=[tb], writes=[Res("cvout")])
    P.barrier()
    for i in range(ntiles):
        t0 = i * 128
        P.dma('sp', xt[:], xin_h.ap()[t0:t0 + 128, :], reads=[xin_res], writes=[xt])
        if i % 4 == 0:
            for kc in range(8):
                P.dma('pool', None, None, reads=[mixg_res, midx], writes=[mixb],
                      fn=lambda e, kc=kc, i=i: e.indirect_dma_start(
                          out=mixb[:, kc, :], out_offset=None, in_=mixg_h.ap(),
                          in_offset=bass.IndirectOffsetOnAxis(ap=midx[:, (i // 4) * 8 + kc:(i // 4) * 8 + kc + 1], axis=0)))
        linear(mixb, wout_b, 1024, pB, off=(i % 4) * 128)
        P.op('dve', lambda e: e.scalar_tensor_tensor(out=r[:], in0=xt[:], scalar=ALPHA, in1=pB[:], op0=ALU.mult, op1=ALU.add),
             reads=[xt], excl=[pB], writes=[r])
        layernorm(r, x1, 0)
        to_T(x1)
        for j in range(8):
            for kc in range(8):
                P.op('pe', lambda e, j=j, kc=kc: e.matmul(pA[:, j * 128:(j + 1) * 128], wq_b[:, kc, j * 128:(j + 1) * 128],
                                                          xT[:, kc, :], start=(kc == 0), stop=(kc == 7)),
                     reads=[wq_b, xT], writes=[pA])
        P.op('act', lambda e: e.copy(qT[:], pA[:, 0:1024].rearrange("p (j t) -> p j t", j=8)), excl=[pA], writes=[qT])
        for h in range(4):
            for mc in range(2):
                for dc in range(2):
                    P.op('pe', lambda e, h=h, mc=mc, dc=dc: e.matmul(
                        pB[:, (h * 2 + mc) * 128:(h * 2 + mc + 1) * 128],
                        KT_b[:, h * 2 + dc, mc * 128:(mc + 1) * 128], qT[:, h * 2 + dc, :],
                        start=(dc == 0), stop=(dc == 1)), reads=[KT_b, qT], writes=[pB])
        P.op('act', lambda e: e.activation(E_b[:], pB[:].rearrange("p (j t) -> p j t", j=8), AF.Exp, scale=1.0 / 16.0),
             excl=[pB], writes=[E_b])
        for h in range(4):
            for mc in range(2):
                P.op('pe', lambda e, h=h, mc=mc: e.matmul(pC[:, h * 128:(h + 1) * 128], ones_bf[:], E_b[:, h * 2 + mc, :],
                                                          start=(mc == 0), stop=(mc == 1)),
                     reads=[ones_bf, E_b], writes=[pC])
        P.op('dve', lambda e: e.reciprocal(rz[:], pC[:].rearrange("p (h t) -> p h t", h=4)), excl=[pC], writes=[rz])
        for h in range(4):
            for dc in range(2):
                for mc in range(2):
                    P.op('pe', lambda e, h=h, dc=dc, mc=mc: e.matmul(
                        pA[:, 1024 + (h * 2 + dc) * 128:1024 + (h * 2 + dc + 1) * 128],
                        V_b[:, mc, h * 256 + dc * 128:h * 256 + (dc + 1) * 128], E_b[:, h * 2 + mc, :],
                        start=(mc == 0), stop=(mc == 1)), reads=[V_b, E_b], writes=[pA])
        for j in range(8):
            P.op('dve', lambda e, j=j: e.tensor_tensor(oT[:, j, :], pA[:, 1024 + j * 128:1024 + (j + 1) * 128], rz[:, j // 2, :],
                                                       op=ALU.mult), reads=[rz], excl=[pA], writes=[oT])
        linear(oT, wo_b, 1024, pB)
        P.op('dve', lambda e: e.scalar_tensor_tensor(out=r[:], in0=x1[:], scalar=ALPHA, in1=pB[:], op0=ALU.mult, op1=ALU.add),
             reads=[x1], excl=[pB], writes=[r])
        layernorm(r, x2, 1)
        to_T(x2)
        for hc in range(16):
            for kc in range(8):
                P.op('pe', lambda e, hc=hc, kc=kc: e.matmul(pA[:, hc * 128:(hc + 1) * 128], wpq_b[:, kc, hc * 128:(hc + 1) * 128],
                                                            xT[:, kc, :], start=(kc == 0), stop=(kc == 7)),
                     reads=[wpq_b, xT], writes=[pA])
        P.op('act', lambda e: e.copy(bass.AP(w8bf, 0, [[4096, 128], [1, 2048]]), pA[:]), excl=[pA], writes=[pqT])
        for hc in range(16):
            P.op('pe', lambda e, hc=hc: e.matmul(pA[:, hc * 128:(hc + 1) * 128], pqT_ap(hc), skT_b[:, hc % 2, :],
                                                 start=True, stop=True), reads=[pqT, skT_b], writes=[pA])
        P.op('act', lambda e: e.copy(sc[:], pA[:].rearrange("p (j t) -> p j t", j=16)), excl=[pA], writes=[sc])
        for hc in range(16):
            P.op('dve', lambda e, hc=hc: e.max(top[:, hc, 0:8], sc[:, hc, :]), reads=[sc], writes=[top])
            P.op('dve', lambda e, hc=hc: e.max_index(idxu[:, hc, 0:8], top[:, hc, 0:8], sc[:, hc, :]), reads=[sc, top], writes=[idxu])
            P.op('dve', lambda e, hc=hc: e.match_replace(sc[:, hc, :], top[:, hc, 0:8], sc[:, hc, :], -1e30),
                 reads=[sc, top], writes=[sc])
            P.op('dve', lambda e, hc=hc: e.max(top[:, hc, 8:16], sc[:, hc, :]), reads=[sc], writes=[top])
            P.op('dve', lambda e, hc=hc: e.max_index(idxu[:, hc, 8:16], top[:, hc, 8:16], sc[:, hc, :]), reads=[sc, top], writes=[idxu])
        P.op('dve', lambda e: e.tensor_copy(idxf[:], idxu[:]), reads=[idxu], writes=[idxf])
        P.op('dve', lambda e: e.tensor_tensor(AP(cand, 0, [[2048, 128], [256, 8], [16, 16], [1, 16]]),
                                              AP(top, 0, [[256, 128], [32, 8], [1, 16], [0, 16]]),
                                              AP(top, 16, [[256, 128], [32, 8], [0, 16], [1, 16]]), op=ALU.add),
             reads=[top], writes=[cand])
        for h in range(8):
            P.op('dve', lambda e, h=h: e.max(cs[:, h, 0:8], sc.t.rearrange('p a b -> p (a b)')[:, h * 256:(h + 1) * 256]), reads=[cand], writes=[cs])
            P.op('dve', lambda e, h=h: e.max_index(ciu[:, h, 0:8], cs[:, h, 0:8], sc.t.rearrange('p a b -> p (a b)')[:, h * 256:(h + 1) * 256]), reads=[cand, cs], writes=[ciu])
            P.op('dve', lambda e, h=h: e.match_replace(sc.t.rearrange('p a b -> p (a b)')[:, h * 256:(h + 1) * 256], cs[:, h, 0:8], sc.t.rearrange('p a b -> p (a b)')[:, h * 256:(h + 1) * 256], -1e30),
                 reads=[cand, cs], writes=[cand])
            P.op('dve', lambda e, h=h: e.max(cs[:, h, 8:16], sc.t.rearrange('p a b -> p (a b)')[:, h * 256:(h + 1) * 256]), reads=[cand], writes=[cs])
            P.op('dve', lambda e, h=h: e.max_index(ciu[:, h, 8:16], cs[:, h, 8:16], sc.t.rearrange('p a b -> p (a b)')[:, h * 256:(h + 1) * 256]), reads=[cand, cs], writes=[ciu])
        P.op('dve', lambda e: e.tensor_copy(cif[:], ciu[:].rearrange("p h k -> p (h k)")), reads=[ciu], writes=[cif])
        P.op('dve', lambda e: e.tensor_tensor(AP(big, 0, [[2048, 128], [16, 128], [1, 16]]),
                                              AP(cif, 0, [[128, 128], [1, 128], [0, 16]]),
                                              AP(thr16, 0, [[16, 128], [0, 128], [1, 16]]), op=ALU.is_ge),
             reads=[cif, thr16], writes=[big])
        P.op('dve', lambda e: e.tensor_reduce(k0f[:], AP(big, 0, [[2048, 128], [16, 128], [1, 16]]), axis=AX.X, op=ALU.add),
             reads=[big], writes=[k0f])
        P.op('dve', lambda e: e.scalar_tensor_tensor(out=k1f[:], in0=k0f[:], scalar=-16.0, in1=cif[:], op0=ALU.mult, op1=ALU.add),
             reads=[k0f, cif], writes=[k1f])
        for (kf, c, dst) in ((k0f, 0, i0s), (k1f, 1, i1s)):
            P.op('dve', lambda e, kf=kf: e.tensor_tensor(AP(big, 0, [[2048, 128], [16, 128], [1, 16]]),
                                                         AP(kf, 0, [[128, 128], [1, 128], [0, 16]]),
                                                         AP(iota16, 0, [[16, 128], [0, 128], [1, 16]]), op=ALU.is_equal),
                 reads=[kf, iota16], writes=[big])
            P.op('dve', lambda e, c=c: e.tensor_tensor(AP(big2, 0, [[2048, 128], [256, 8], [16, 16], [1, 16]]),
                                                       AP(big, 0, [[2048, 128], [256, 8], [16, 16], [1, 16]]),
                                                       AP(idxf, c * 16, [[256, 128], [32, 8], [0, 16], [1, 16]]), op=ALU.mult),
                 reads=[big, idxf], writes=[big2])
            P.op('dve', lambda e, dst=dst: e.tensor_reduce(dst[:], AP(big2, 0, [[2048, 128], [16, 128], [1, 16]]), axis=AX.X, op=ALU.add),
                 reads=[big2], writes=[dst])
        P.op('dve', lambda e: e.scalar_tensor_tensor(out=ef[:], in0=i0s[:], scalar=128.0, in1=i1s[:], op0=ALU.mult, op1=ALU.add),
             reads=[i0s, i1s], writes=[ef])
        P.op('dve', lambda e: e.tensor_copy(eu[:], ef[:]), reads=[ef], writes=[eu])
        P.op('dve', lambda e: e.tensor_tensor(gex[:], cs[:], AP(cs, 0, [[128, 128], [16, 8], [0, 16]]), op=ALU.subtract),
             reads=[cs], writes=[gex])
        P.op('act', lambda e: e.activation(gex[:], gex[:], AF.Exp), reads=[gex], writes=[gex])
        P.op('dve', lambda e: e.tensor_reduce(gz[:], gex[:], axis=AX.X, op=ALU.add), reads=[gex], writes=[gz])
        P.op('dve', lambda e: e.reciprocal(gz[:], gz[:]), reads=[gz], writes=[gz])
        P.op('dve', lambda e: e.tensor_tensor(gex[:], gex[:], AP(gz, 0, [[8, 128], [1, 8], [0, 16]]), op=ALU.mult),
             reads=[gex, gz], writes=[gex])
        gexf = gex.t.rearrange("p h k -> p (h k)")
        for s_ in range(128 + 3):
            if s_ < 128:
                hk, rr = s_, s_ % 8
                P.dma('pool', None, None, reads=[eu], writes=[grow[rr]],
                      fn=lambda e, hk=hk, rr=rr: e.indirect_dma_start(
                          out=bass.AP(gb_bf, rr * 2048, [[16384, 128], [1, 2048]]), out_offset=None, in_=H['uvb'].ap(),
                          in_offset=bass.IndirectOffsetOnAxis(ap=eu[:, hk:hk + 1], axis=0)))
                P.op('dve', lambda e, hk=hk, rr=rr: e.scalar_tensor_tensor(
                    out=junk[:], in0=bass.AP(gb_bf, rr * 2048, [[16384, 128], [1, 1024]]), scalar=1.0, in1=x2[:],
                    op0=ALU.mult, op1=ALU.mult, accum_out=aact[:, hk:hk + 1]), reads=[grow[rr], x2], writes=[junk, acol[hk]])
            if 0 <= s_ - 1 < 128:
                hk = s_ - 1
                P.op('act', lambda e, hk=hk: e.activation(gel[:, hk:hk + 1], aact[:, hk:hk + 1], AF.Gelu),
                     reads=[acol[hk]], writes=[gcol[hk]])
            if 0 <= s_ - 2 < 128:
                hk = s_ - 2
                P.op('dve', lambda e, hk=hk: e.tensor_tensor(wgt[:, hk:hk + 1], gel[:, hk:hk + 1], gexf[:, hk:hk + 1], op=ALU.mult),
                     reads=[gcol[hk], gex], writes=[wcol[hk]])
            if 0 <= s_ - 3 < 128:
                hk = s_ - 3
                rr = hk % 8
                dg = dgs[hk % 4]
                P.op('act', lambda e, hk=hk, dg=dg: e.activation(dg[:], ident[:], AF.Identity, scale=wgt[:, hk:hk + 1]),
                     reads=[ident, wcol[hk]], writes=[dg])
                for half in range(2):
                    P.op('pe', lambda e, hk=hk, rr=rr, dg=dg, half=half: e.matmul(
                        pB[:, half * 512:(half + 1) * 512], dg[:],
                        bass.AP(gb_bf, rr * 2048 + 1024 + half * 512, [[16384, 128], [1, 512]]),
                        start=(hk == 0), stop=(hk == 127)), reads=[dg, grow[rr]], writes=[pB])
        P.op('dve', lambda e: e.scalar_tensor_tensor(out=acc[:], in0=x2[:], scalar=ALPHA, in1=pB[:], op0=ALU.mult, op1=ALU.add),
             reads=[x2], excl=[pB], writes=[acc])
        layernorm(acc, x3, 2)
        P.dma('sp', xout_h.ap()[t0:t0 + 128, :], x3[:], reads=[x3], writes=[xout_res])
        if xsrc_h is not None:
            to_T(x3)
            for c4 in range(4):
                P.dma('sp', xsrc_h[c4].ap().rearrange("(k p) t -> p k t", p=128)[:, :, t0:t0 + 128], xT[:, 2 * c4:2 * c4 + 2, :],
                      reads=[xT], writes=[xsrc_res])

def build_fused(depth=4, nq_tiles=32, ntiles=16):
    nc = bass.Bass("TRN2", target_bir_lowering=False)
    D = 1024
    xT0_h = nc.dram_tensor("xT0", [D, S], F32, kind="ExternalInput")
    x0_h = nc.dram_tensor("x0", [NTOK, D], F32, kind="ExternalInput")
    memT_h = nc.dram_tensor("memT", [D, 256], F32, kind="ExternalInput")
    midx_h = nc.dram_tensor("midx", [128, 32], U32, kind="ExternalInput")
    uvb_h = nc.dram_tensor("uvb_scr", [16384, 2 * D], BF16)
    HA, HB = [], []
    for l in range(depth):
        HA.append({"wa": nc.dram_tensor(f"wa{l}", [D, NWA], F32, kind="ExternalInput"),
                   "sp": nc.dram_tensor(f"sp{l}", [128, 16], F32, kind="ExternalInput"),
                   "lamqk": nc.dram_tensor(f"lamqk{l}", [128, 256], F32, kind="ExternalInput"),
                   "gda": nc.dram_tensor(f"gda{l}", [128, 128], F32, kind="ExternalInput"),
                   "gml": nc.dram_tensor(f"gml{l}", [128, 128], F32, kind="ExternalInput")})
        HB.append({"wout": nc.dram_tensor(f"wout{l}", [D, D], F32, kind="ExternalInput"),
                   "wq": nc.dram_tensor(f"wq{l}", [D, D], F32, kind="ExternalInput"),
                   "wkv": nc.dram_tensor(f"wkv{l}", [D, 2 * D], F32, kind="ExternalInput"),
                   "wo": nc.dram_tensor(f"wo{l}", [D, D], F32, kind="ExternalInput"),
                   "wpq": nc.dram_tensor(f"wpq{l}", [D, 2 * D], F32, kind="ExternalInput"),
                   "skT": nc.dram_tensor(f"skT{l}", [2, 128, 128], F32, kind="ExternalInput"),
                   "lnp": nc.dram_tensor(f"lnp{l}", [6, D], F32, kind="ExternalInput"),
                   "u": nc.dram_tensor(f"u{l}", [16384, D], F32, kind="ExternalInput"),
                   "v": nc.dram_tensor(f"v{l}", [16384, D], F32, kind="ExternalInput"),
                   "memT": memT_h, "uvb": uvb_h})
    out_h = nc.dram_tensor("out", [NTOK, D], F32, kind="ExternalOutput")
    mixsrc = [[nc.dram_tensor(f"mixsrc{l}_{c}", [1024, 512], BF16) for c in range(4)] for l in range(depth)]
    mixgc = [[nc.dram_tensor(f"mixgc{l}_{c}", [4096, 512], BF16) for c in range(4)] for l in range(depth)]
    mixg = [nc.dram_tensor(f"mixg{l}", [16384, 512], BF16) for l in range(depth)]
    xsrc = [[nc.dram_tensor(f"xsrc{l}_{c}", [256, NTOK], BF16) for c in range(4)] for l in range(depth - 1)]
    xg = [None] + [[nc.dram_tensor(f"xg{l}_{c}", [1024, NTOK], BF16) for c in range(4)] for l in range(1, depth)]
    xres = [None] + [nc.dram_tensor(f"xres{l}", [NTOK, D], F32) for l in range(1, depth)]
    GROUPS = [[0, 1, 2, 3], [4, 5, 6, 7]]

    P = Prog(nc, n_dma_sems={'sp': 8, 'act': 4, 'pool': 20})
    C = make_consts(P)
    out_res = Res("out")
    xg_res = [Res(f"xg{l}") for l in range(depth)]
    xres_res = [Res(f"xres{l}") for l in range(depth)]
    for l in range(depth):
        mixsrc_res, mixg_res, xsrc_res = Res("mixsrc"), Res("mixg"), Res("xsrc")
        with P.scope():
            phase_a(P, nc, C, HA[l], xT0_h, xg[l], xg_res[l], mixsrc[l], mixsrc_res, nq_tiles)
        for c in range(4):
            gc_res = Res("mixgc")
            P.cc("AllGather", GROUPS, mixsrc[l][c].ap(), mixgc[l][c].ap(), reads=[mixsrc_res], writes=[gc_res], inc=1)
            P.dma('sp', mixg[l].ap()[c * 4096:(c + 1) * 4096, :].rearrange("(a b) n -> a (b n)", a=128),
                  mixgc[l][c].ap().rearrange("(a b) n -> a (b n)", a=128), reads=[gc_res], writes=[mixg_res])
        last = (l == depth - 1)
        with P.scope():
            phase_b(P, nc, C, HB[l], x0_h if l == 0 else xres[l], xres_res[l], mixg[l], mixg_res, midx_h,
                    out_h if last else xres[l + 1], out_res if last else xres_res[l + 1],
                    None if last else xsrc[l], xsrc_res, ntiles)
        if not last:
            for c in range(4):
                P.cc("AllGather", GROUPS, xsrc[l][c].ap(), xg[l + 1][c].ap(), reads=[xsrc_res], writes=[xg_res[l + 1]], inc=1)
    P.finish([out_res])
    P.emit()
    P.close()
    return nc


DEPTH = 4
_NC_CACHE = {}


def _head_inputs(w_in_l, conv_w_l, conv_b_l, i_bias_l, f_bias_l, lam_qk_l, da_g_l, ml_g_l, l, h):
    r = np.arange(h * 128, (h + 1) * 128)
    cols = np.concatenate([r, 512 + r, 1024 + r, 1536 + r, 2048 + r, 2560 + r, 3072 + r, [3584 + h], [3588 + h]])
    wa = np.ascontiguousarray(w_in_l[:, cols])
    lam_init = 0.8 - 0.6 * math.exp(-0.3 * l)
    sp = np.zeros((128, 16), np.float32)
    for g in range(2):
        ch = g * 512 + r
        for j in range(4):
            sp[:, g * 4 + j] = conv_w_l[j, ch]
        sp[:, 8 + g] = conv_b_l[ch]
    sp[:, 10] = i_bias_l[h]
    sp[:, 11] = f_bias_l[h]
    sp[:, 12] = lam_init
    sp[:, 13] = 1.0 - lam_init
    lamqk = np.ascontiguousarray(np.broadcast_to(lam_qk_l.reshape(1, 256), (128, 256)))
    gda = np.ascontiguousarray(np.broadcast_to(da_g_l[None, :], (128, 128)))
    gml = np.ascontiguousarray(np.broadcast_to(ml_g_l[None, r], (128, 128)))
    return {f"wa{l}": wa, f"sp{l}": sp, f"lamqk{l}": lamqk, f"gda{l}": gda, f"gml{l}": gml}


def make_in_maps(depth, x, mem, w_in, i_bias, f_bias, conv_w, conv_b, lam_qk, da_norm_g, ml_norm_g, w_out,
                 ln1_g, ln1_b, wq_mem, wkv_mem, wo_mem, ln2_g, ln2_b, w_pq, sub_keys, u_tab, v_tab, ln3_g, ln3_b):
    f = lambda a: np.asarray(a, dtype=np.float32)
    x = f(x); mem = f(mem)
    B = x.shape[0]
    shared = {}
    for l in range(depth):
        shared[f"wout{l}"] = f(w_out[l]); shared[f"wq{l}"] = f(wq_mem[l]); shared[f"wkv{l}"] = f(wkv_mem[l])
        shared[f"wo{l}"] = f(wo_mem[l]); shared[f"wpq{l}"] = f(w_pq[l])
        shared[f"skT{l}"] = np.ascontiguousarray(f(sub_keys[l]).transpose(0, 2, 1))
        shared[f"lnp{l}"] = np.ascontiguousarray(np.stack([f(ln1_g[l]), f(ln1_b[l]), f(ln2_g[l]), f(ln2_b[l]), f(ln3_g[l]), f(ln3_b[l])]))
        shared[f"u{l}"] = f(u_tab[l]); shared[f"v{l}"] = f(v_tab[l])
    in_maps = []
    for b in range(B):
        xT = np.ascontiguousarray(x[b].T)
        memT = np.ascontiguousarray(mem[b].T)
        for r_ in range(4):
            m = dict(shared)
            m["xT0"] = xT
            m["x0"] = np.ascontiguousarray(x[b, r_ * 2048:(r_ + 1) * 2048])
            m["memT"] = memT
            p = np.arange(128, dtype=np.int64)[:, None, None]
            qr = np.arange(4, dtype=np.int64)[None, :, None]
            kc = np.arange(8, dtype=np.int64)[None, None, :]
            midx = ((((r_ * 4 + (kc % 4)) * 4 + qr) * 2 + (kc // 4)) * 128) + p
            m["midx"] = np.ascontiguousarray(midx.reshape(128, 32).astype(np.uint32))
            for l in range(depth):
                m.update(_head_inputs(f(w_in[l]), f(conv_w[l]), f(conv_b[l]), f(i_bias[l]), f(f_bias[l]), f(lam_qk[l]),
                                      f(da_norm_g[l]), f(ml_norm_g[l]), l, r_))
            in_maps.append(m)
    return in_maps


def kernel(x, mem, w_in, i_bias, f_bias, conv_w, conv_b, lam_qk, da_norm_g, ml_norm_g, w_out,
           ln1_g, ln1_b, wq_mem, wkv_mem, wo_mem, ln2_g, ln2_b, w_pq, sub_keys, u_tab, v_tab,
           ln3_g, ln3_b):
    if "f" not in _NC_CACHE:
        _NC_CACHE["f"] = build_fused(DEPTH)
    in_maps = make_in_maps(DEPTH, x, mem, w_in, i_bias, f_bias, conv_w, conv_b, lam_qk, da_norm_g, ml_norm_g, w_out,
                           ln1_g, ln1_b, wq_mem, wkv_mem, wo_mem, ln2_g, ln2_b, w_pq, sub_keys, u_tab, v_tab, ln3_g, ln3_b)
    res = run_bass_kernel_spmd(_NC_CACHE["f"], in_maps, core_ids=list(range(8))).results
    B, S_, D = np.asarray(x).shape
    out = np.empty((B, S_, D), np.float32)
    for b in range(B):
        for r_ in range(4):
            out[b, r_ * 2048:(r_ + 1) * 2048] = res[b * 4 + r_]["out"]
    return out
```

```python
import math
import contextlib
import numpy as np
import ml_dtypes
import concourse.bass as bass
import concourse.mybir as mybir
from concourse.bass_utils import run_bass_kernel_spmd

F32 = mybir.dt.float32
BF16 = mybir.dt.bfloat16
I32 = mybir.dt.int32
U32 = mybir.dt.uint32
AF = mybir.ActivationFunctionType
ALU = mybir.AluOpType
AX = mybir.AxisListType

SEM_LIMIT = 12000


class Res:
    __slots__ = ("w", "r", "name")

    def __init__(self, name=""):
        self.w = None
        self.r = []
        self.name = name


class Tile:
    def __init__(self, t, res=None, name=""):
        self.t = t
        self.res = res if res is not None else Res(name)

    def __getitem__(self, k):
        return self.t[k]


def _res(x):
    return x.res if isinstance(x, Tile) else x


class Prog:
    ENGS = ("pe", "act", "dve", "pool", "sp")

    def __init__(self, nc, n_dma_sems=6):
        self.nc = nc
        self.es = contextlib.ExitStack()
        self.cur_es = self.es
        self.nsem = 0
        self.streams = {e: [] for e in self.ENGS}
        self.cur_sem = {}
        self.cur_val = {}
        for e in self.ENGS:
            self._new_eng_sem(e)
        self.known = {e: {} for e in self.ENGS}
        self.dsem = {}
        self.nds = n_dma_sems
        self.dma_rr = {e: 0 for e in self.ENGS}
        self.ntile = 0
        self.last_tok = {}

    def sem(self, name):
        self.nsem += 1
        return self.es.enter_context(self.nc.semaphore(f"{name}_{self.nsem}"))

    def _new_eng_sem(self, e):
        self.nsem = getattr(self, "nsem", 0)
        self.cur_sem[e] = self.sem("c" + e)
        self.cur_val[e] = 0

    def sb(self, shape, dt, name=None):
        self.ntile += 1
        nm = f"{name or 't'}_{self.ntile}"
        t = self.cur_es.enter_context(self.nc.sbuf_tensor(nm, list(shape), dt))
        return Tile(t, name=nm)

    def ps(self, shape, dt=F32, name=None):
        self.ntile += 1
        nm = f"{name or 'p'}_{self.ntile}"
        t = self.cur_es.enter_context(self.nc.psum_tensor(nm, list(shape), dt))
        return Tile(t, name=nm)

    def _collect(self, eng, reads, writes, excl=()):
        toks = []
        for r in reads:
            r = _res(r)
            if r.w is not None:
                toks.append(r.w)
        for w in list(writes) + list(excl):
            w = _res(w)
            if w.w is not None:
                toks.append(w.w)
            toks.extend(w.r)
        waits = []
        kn = self.known[eng]
        best = {}
        for (s, v, src) in toks:
            if src == "pe" and eng == "pe":
                continue
            if kn.get(s, 0) >= v:
                continue
            if best.get(s, (None, 0))[1] < v:
                best[s] = (s, v)
        for s, (sh, v) in best.items():
            kn[s] = v
            waits.append((sh, v))
        return waits

    def _commit(self, tok, reads, writes, excl=()):
        for r in reads:
            _res(r).r.append(tok)
        for w in list(writes) + list(excl):
            w = _res(w)
            w.w = tok
            w.r = []

    def op(self, eng, fn, reads=(), writes=(), excl=()):
        waits = self._collect(eng, reads, writes, excl)
        if self.cur_val[eng] >= SEM_LIMIT:
            self._new_eng_sem(eng)
        s = self.cur_sem[eng]
        self.cur_val[eng] += 1
        v = self.cur_val[eng]
        tok = (s, v, eng)
        self.last_tok[eng] = tok
        self._commit(tok, reads, writes, excl)
        self.streams[eng].append((waits, fn, s, 1))
        return tok

    def dma(self, q, out, in_, reads=(), writes=(), fn=None, inc=16):
        key = q
        nds = self.nds[q] if isinstance(self.nds, dict) else self.nds
        if key not in self.dsem:
            self.dsem[key] = [[self.sem("d" + q), 0] for _ in range(nds)]
        k = self.dma_rr[q]
        self.dma_rr[q] = (k + 1) % nds
        slot = self.dsem[key][k]
        waits = self._collect(q, reads, writes)
        kn = self.known[q]
        if slot[1] > 0 and kn.get(slot[0], 0) < slot[1]:
            waits.append((slot[0], slot[1]))
            kn[slot[0]] = slot[1]
        if slot[1] + inc > SEM_LIMIT:
            slot[0] = self.sem("d" + q)
            slot[1] = 0
        slot[1] += inc
        tok = (slot[0], slot[1], "dma")
        self._commit(tok, reads, writes)
        if fn is None:
            fn = lambda e, o=out, i=in_: e.dma_start(out=o, in_=i)
        self.streams[q].append((waits, fn, slot[0], inc))
        return tok


    def cc(self, kind, groups, in_ap, out_ap, reads=(), writes=(), inc=16):
        fn = lambda e: e.collective_compute(kind, ALU.bypass, replica_groups=groups, ins=[in_ap], outs=[out_ap])
        self._cc_inc = inc
        return self.dma('pool', None, None, reads=reads, writes=writes, fn=fn, inc=inc)


    @contextlib.contextmanager
    def scope(self):
        prev = self.cur_es
        es = contextlib.ExitStack()
        self.cur_es = es
        try:
            yield
            self.barrier()
            self.emit()
        finally:
            self.cur_es = prev
            es.close()

    def barrier(self):
        toks = []
        for e in self.ENGS:
            if e in self.last_tok:
                toks.append(self.last_tok[e])
        for q, slots in self.dsem.items():
            for (sh, v) in slots:
                if v > 0:
                    toks.append((sh, v, "dma"))
        for e in self.ENGS:
            kn = self.known[e]
            waits = []
            for (sh, v, src) in toks:
                if src == e and e == "pe":
                    continue
                if kn.get(sh, 0) < v:
                    kn[sh] = v
                    waits.append((sh, v))
            if waits:
                self.streams[e].append((waits, None, None, 0))

    def finish(self, all_res):
        waits = self._collect("sp", all_res, [])
        kn = self.known["sp"]
        for q, slots in self.dsem.items():
            for (sh, v) in slots:
                if v > 0 and kn.get(sh, 0) < v:
                    waits.append((sh, v))
                    kn[sh] = v
        self.streams["sp"].append((waits, None, None, 0))

    def emit(self):
        nc = self.nc
        streams = self.streams

        def run(engname, eng):
            for (waits, fn, s, inc) in streams[engname]:
                for (sh, v) in waits:
                    eng.wait_ge(sh, v)
                if fn is not None:
                    ins = fn(eng)
                    ins.then_inc(s, inc)

        with nc.Block() as block:
            @block.tensor
            def _(e):
                run("pe", e)

            @block.scalar
            def _(e):
                run("act", e)

            @block.vector
            def _(e):
                run("dve", e)

            @block.gpsimd
            def _(e):
                run("pool", e)

            @block.sync
            def _(e):
                run("sp", e)
        self.streams = {e: [] for e in self.ENGS}

    def close(self):
        self.es.close()


LN_EPS = 1e-5
ALPHA = 8.0 ** 0.25
S = 8192
NWA = 898
NTOK = 2048


def AP(t, off, dims):
    return bass.AP(t.t if isinstance(t, Tile) else t, off, dims)


def make_consts(P):
    C = {}
    identf = P.sb([128, 128], F32, "identf")
    ident = P.sb([128, 128], BF16, "ident")
    trif = P.sb([128, 128], F32, "trif")
    trib = P.sb([128, 128], BF16, "trib")
    onesf = P.sb([128, 128], F32, "onesf")
    ones_bf = P.sb([128, 128], BF16, "ones")
    iota16 = P.sb([128, 16], F32, "iota16")
    thr16 = P.sb([128, 16], F32, "thr16")
    P.op('dve', lambda e: e.memset(identf[:], 0.0), writes=[identf])
    P.op('pool', lambda e: e.affine_select(out=identf[:], in_=identf[:], pattern=[[-1, 128]],
                                           compare_op=ALU.not_equal, fill=1.0, base=0, channel_multiplier=1),
         reads=[identf], writes=[identf])
    P.op('dve', lambda e: e.tensor_copy(ident[:], identf[:]), reads=[identf], writes=[ident])
    P.op('dve', lambda e: e.memset(onesf[:], 1.0), writes=[onesf])
    P.op('dve', lambda e: e.memset(ones_bf[:], 1.0), writes=[ones_bf])
    P.op('pool', lambda e: e.iota(trif[:], pattern=[[1, 128]], base=0, channel_multiplier=-1,
                                  allow_small_or_imprecise_dtypes=True), writes=[trif])
    P.op('dve', lambda e: e.tensor_single_scalar(trif[:], trif[:], 0.0, op=ALU.is_ge), reads=[trif], writes=[trif])
    P.op('dve', lambda e: e.tensor_copy(trib[:], trif[:]), reads=[trif], writes=[trib])
    P.op('pool', lambda e: e.iota(iota16[:], pattern=[[1, 16]], base=0, channel_multiplier=0,
                                  allow_small_or_imprecise_dtypes=True), writes=[iota16])
    P.op('pool', lambda e: e.iota(thr16[:], pattern=[[16, 16]], base=16, channel_multiplier=0,
                                  allow_small_or_imprecise_dtypes=True), writes=[thr16])
    C.update(identf=identf, ident=ident, trif=trif, trib=trib, onesf=onesf, ones_bf=ones_bf, iota16=iota16, thr16=thr16)
    return C


def phase_a(P, nc, C, H, xT_h, xg_h, xg_res, mixsrc_h, mixsrc_res, nq_tiles):
    ident, trif, trib, onesf = C['ident'], C['trif'], C['trib'], C['onesf']
    D = 1024
    ntok = nq_tiles * 256
    nblk = ntok // 128
    spt = P.sb([128, 16], F32, "spt")
    lamq = P.sb([128, 256], F32, "lamq")
    gda = P.sb([128, 128], F32, "gda")
    gml = P.sb([128, 128], F32, "gml")
    P.dma('sp', spt[:], H['sp'].ap(), writes=[spt])
    P.dma('sp', lamq[:], H['lamqk'].ap(), writes=[lamq])
    P.dma('sp', gda[:], H['gda'].ap(), writes=[gda])
    P.dma('sp', gml[:], H['gml'].ap(), writes=[gml])
    lt = P.sb([128, 128], F32, "lt")
    lsum = P.sb([128, 2], F32, "lsum")
    neglam = P.sb([128, 1], F32, "neglam")
    P.op('dve', lambda e: e.tensor_tensor(lt[:].rearrange("p (a d) -> p a d", a=2),
                                          AP(lamq, 0, [[256, 128], [128, 2], [1, 64]]),
                                          AP(lamq, 64, [[256, 128], [128, 2], [1, 64]]), op=ALU.mult),
         reads=[lamq], writes=[lt])
    P.op('dve', lambda e: e.tensor_reduce(lsum[:], lt[:].rearrange("p (a d) -> p a d", a=2), axis=AX.X, op=ALU.add),
         reads=[lt], writes=[lsum])
    P.op('act', lambda e: e.activation(lsum[:], lsum[:], AF.Exp), reads=[lsum], writes=[lsum])
    P.op('dve', lambda e: e.tensor_tensor(neglam[:], lsum[:, 1:2], lsum[:, 0:1], op=ALU.subtract), reads=[lsum], writes=[neglam])
    P.op('dve', lambda e: e.tensor_tensor(neglam[:], neglam[:], spt[:, 12:13], op=ALU.subtract), reads=[neglam, spt], writes=[neglam])
    P.op('dve', lambda e: e.tensor_scalar(gda[:], gda[:], spt[:, 13:14], None, op0=ALU.mult), reads=[gda, spt], writes=[gda])

    wab = P.sb([128, 8, NWA], BF16, "wab")
    stg = P.sb([128, 1024], F32, "stg")
    wap = H['wa'].ap().rearrange("(c p) n -> p c n", p=128)
    for kc in range(8):
        P.dma('sp', stg[:, 0:NWA], wap[:, kc, :], writes=[stg])
        P.op('dve', lambda e, kc=kc: e.tensor_copy(wab[:, kc, :], stg[:, 0:NWA]), reads=[stg], writes=[wab])

    daqT = [P.sb([64, S], BF16, f"daqT{c}") for c in range(2)]
    dakT = [P.sb([64, S], BF16, f"dakT{c}") for c in range(2)]
    Vaug = P.sb([128, 64, 129], BF16, "Vaug")
    mlqT = P.sb([128, S], BF16, "mlqT")
    mlkT = P.sb([128, S], BF16, "mlkT")
    mlV = P.sb([128, 64, 129], BF16, "mlV")
    sigo = P.sb([128, 64, 128], BF16, "sigo")
    gates = P.sb([128, 64, 2], F32, "gates")
    P.op('dve', lambda e: e.memset(Vaug[:, :, 128:129], 1.0), writes=[Vaug])
    P.op('dve', lambda e: e.memset(mlV[:, :, 128:129], 1.0), writes=[mlV])

    banks = [P.ps([128, 512], F32, f"bank{i}") for i in range(8)]

    xs = P.sb([128, 8, 256], F32, "xs")
    xb = P.sb([128, 8, 256], BF16, "xb")
    pre = [P.sb([128, 3 + 256], F32, f"pre{i}") for i in range(2)]
    cacc = P.sb([128, 256], F32, "cacc")
    ctmp = P.sb([128, 256], F32, "ctmp")
    xT_v = xT_h.ap().rearrange("(c p) t -> p c t", p=128) if xg_h is None else None
    for i in range(2):
        P.op('dve', lambda e, i=i: e.memset(pre[i][:, 0:3], 0.0), writes=[pre[i]])
    for ti in range(nq_tiles):
        t0 = ti * 256
        if xg_h is None:
            P.dma('sp', xs[:], xT_v[:, :, t0:t0 + 256], writes=[xs])
            P.op('pool', lambda e: e.tensor_copy(xb[:], xs[:]), reads=[xs], writes=[xb])
        else:
            jr, tl = t0 // 2048, t0 % 2048
            for c4 in range(4):
                P.dma('sp', xb[:, 2 * c4:2 * c4 + 2, :],
                      xg_h[c4].ap()[jr * 256:(jr + 1) * 256, tl:tl + 256].rearrange("(k p) t -> p k t", p=128),
                      reads=[xg_res], writes=[xb])
        for g in range(4):
            pb = banks[g // 2]
            col0 = (g // 2) * 128 + (g % 2) * 64
            for kc in range(8):
                P.op('pe', lambda e, g=g, kc=kc, pb=pb, col0=col0: e.matmul(
                    pb[0:64, (g % 2) * 256:(g % 2) * 256 + 256], wab[:, kc, col0:col0 + 64], xb[:, kc, :],
                    start=(kc == 0), stop=(kc == 7)), reads=[wab, xb], writes=[pb])
        for c in range(2):
            P.op('act', lambda e, t0=t0, c=c: e.copy(daqT[c][:, t0:t0 + 256], banks[0][0:64, c * 256:c * 256 + 256]),
                 excl=[banks[0]], writes=[daqT[c]])
            P.op('dve', lambda e, t0=t0, c=c: e.tensor_copy(dakT[c][:, t0:t0 + 256], banks[1][0:64, c * 256:c * 256 + 256]),
                 excl=[banks[1]], writes=[dakT[c]])
        for g in range(2):
            col0 = 384 + g * 128
            for kc in range(8):
                P.op('pe', lambda e, g=g, kc=kc, col0=col0: e.matmul(
                    banks[2][:, g * 256:g * 256 + 256], wab[:, kc, col0:col0 + 128], xb[:, kc, :],
                    start=(kc == 0), stop=(kc == 7)), reads=[wab, xb], writes=[banks[2]])
        for g in range(2):
            pr = pre[g]
            if ti > 0:
                P.op('dve', lambda e, pr=pr: e.tensor_copy(pr[:, 0:3], pr[:, 256:259]), reads=[pr], writes=[pr])
            P.op('act', lambda e, g=g, pr=pr: e.copy(pr[:, 3:259], banks[2][:, g * 256:g * 256 + 256]),
                 excl=[banks[2]], writes=[pr])
            P.op('dve', lambda e, g=g, pr=pr: e.tensor_scalar(cacc[:], pr[:, 3:259], spt[:, g * 4 + 3:g * 4 + 4], spt[:, 8 + g:9 + g],
                                                              op0=ALU.mult, op1=ALU.add), reads=[pr, spt], writes=[cacc])
            for j in range(3):
                P.op('dve', lambda e, g=g, pr=pr, j=j: e.scalar_tensor_tensor(
                    out=cacc[:], in0=pr[:, j:j + 256], scalar=spt[:, g * 4 + j:g * 4 + j + 1], in1=cacc[:],
                    op0=ALU.mult, op1=ALU.add), reads=[pr, spt, cacc], writes=[cacc])
            if g == 0:
                P.op('act', lambda e, t0=t0: e.activation(mlqT[:, t0:t0 + 256], cacc[:], AF.Silu), reads=[cacc], writes=[mlqT])
            else:
                P.op('act', lambda e: e.activation(ctmp[:], cacc[:], AF.Silu), reads=[cacc], writes=[ctmp])
                P.op('dve', lambda e, t0=t0: e.tensor_scalar(mlkT[:, t0:t0 + 256], ctmp[:], 128.0 ** -0.5, None, op0=ALU.mult),
                     reads=[ctmp], writes=[mlkT])
        for sub in range(2):
            blk = ti * 2 + sub
            pb = banks[3 + sub]
            for gi, col0 in enumerate((256, 640, 768)):
                for kc in range(8):
                    P.op('pe', lambda e, sub=sub, gi=gi, col0=col0, kc=kc, pb=pb: e.matmul(
                        pb[:, gi * 128:(gi + 1) * 128], xb[:, kc, sub * 128:(sub + 1) * 128], wab[:, kc, col0:col0 + 128],
                        start=(kc == 0), stop=(kc == 7)), reads=[wab, xb], writes=[pb])
            for kc in range(8):
                P.op('pe', lambda e, sub=sub, kc=kc, blk=blk: e.matmul(
                    banks[5][:, blk * 2:blk * 2 + 2], xb[:, kc, sub * 128:(sub + 1) * 128], wab[:, kc, 896:898],
                    start=(kc == 0), stop=(kc == 7)), reads=[wab, xb], writes=[banks[5]])
            P.op('act', lambda e, blk=blk, pb=pb: e.copy(Vaug[:, blk, 0:128], pb[:, 0:128]), excl=[pb], writes=[Vaug])
            P.op('dve', lambda e, blk=blk, pb=pb: e.tensor_copy(mlV[:, blk, 0:128], pb[:, 128:256]), excl=[pb], writes=[mlV])
            P.op('act', lambda e, blk=blk, pb=pb: e.activation(sigo[:, blk, :], pb[:, 256:384], AF.Sigmoid), excl=[pb], writes=[sigo])
    P.op('dve', lambda e: e.tensor_copy(gates[:, 0:nblk, :], banks[5][:, 0:nblk * 2].rearrange("p (b g) -> p b g", g=2)),
         excl=[banks[5]], writes=[gates])

    Eb = [P.sb([128, 2, 256], BF16, f"Eb{i}") for i in range(2)]
    o_da = P.sb([128, 128], F32, "o_da")
    o_out = [P.sb([128, 128], BF16, f"o_out{i}") for i in range(2)]
    oT_sb = [P.sb([128, 128], BF16, f"oT_sb{i}") for i in range(2)]
    pT6 = banks[6].t.bitcast(BF16)
    pT7 = banks[7].t.bitcast(BF16)
    junk = P.sb([128, 128], F32, "junk")
    rz = P.sb([128, 2], F32, "rz")
    ss = P.sb([128, 1], F32, "ss")
    nout_box = [0]
    steps = [(qt, j) for qt in range(nq_tiles) for j in range(2 * qt + 2)]

    def da_front(k):
        qt, j = steps[k]
        q0 = qt * 256
        nj = 2 * qt + 2
        pS = banks[k % 2]
        E = Eb[k % 2]
        qa, qb = (0, 256) if j < nj - 1 else (128, 256)
        for c in range(2):
            P.op('pe', lambda e, c=c, j=j, pS=pS, qa=qa, qb=qb, q0=q0: e.matmul(
                pS[:, c * 256 + qa:c * 256 + qb], dakT[c][:, j * 128:(j + 1) * 128], daqT[c][:, q0 + qa:q0 + qb],
                start=True, stop=True), reads=[dakT[c], daqT[c]], writes=[pS])
        P.op('act', lambda e, E=E, pS=pS, qa=qa, qb=qb: e.activation(
            E[:, :, qa:qb], pS[:, :].rearrange("p (c q) -> p c q", c=2)[:, :, qa:qb], AF.Exp, scale=0.125),
             excl=[pS], writes=[E])
        if j >= nj - 2:
            sm = 0 if j == nj - 2 else 1
            P.op('pool', lambda e, E=E, sm=sm: e.tensor_tensor(
                E[:, :, sm * 128:(sm + 1) * 128], E[:, :, sm * 128:(sm + 1) * 128],
                AP(trib, 0, [[128, 128], [0, 2], [1, 128]]), op=ALU.mult), reads=[E, trib], writes=[E])

    def da_back(k):
        qt, j = steps[k]
        q0 = qt * 256
        nj = 2 * qt + 2
        E = Eb[k % 2]
        subs = (0, 1) if j < nj - 1 else (1,)
        for s_ in subs:
            last_j = nj - 2 if s_ == 0 else nj - 1
            for c in range(2):
                pacc = banks[2 + s_ * 2 + c]
                P.op('pe', lambda e, E=E, s_=s_, c=c, j=j, pacc=pacc, last_j=last_j: e.matmul(
                    pacc[:, 0:129], E[:, c, s_ * 128:(s_ + 1) * 128], Vaug[:, j, :],
                    start=(j == 0), stop=(j == last_j)), reads=[E, Vaug], writes=[pacc])
        if j == nj - 1:
            nout = nout_box[0]
            for s_ in range(2):
                p0, p1 = banks[2 + s_ * 2], banks[2 + s_ * 2 + 1]
                oo = o_out[nout % 2]
                nout += 1
                nout_box[0] = nout
                P.op('dve', lambda e, p0=p0: e.reciprocal(rz[:, 0:1], p0[:, 128:129]), excl=[p0], writes=[rz])
                P.op('dve', lambda e, p1=p1: e.reciprocal(rz[:, 1:2], p1[:, 128:129]), excl=[p1], writes=[rz])
                P.op('dve', lambda e: e.tensor_tensor(rz[:, 1:2], rz[:, 1:2], neglam[:], op=ALU.mult), reads=[rz, neglam], writes=[rz])
                P.op('dve', lambda e, p0=p0: e.tensor_scalar(o_da[:], p0[:, 0:128], rz[:, 0:1], None, op0=ALU.mult),
                     reads=[rz], excl=[p0], writes=[o_da])
                P.op('dve', lambda e, p1=p1: e.scalar_tensor_tensor(out=o_da[:], in0=p1[:, 0:128], scalar=rz[:, 1:2], in1=o_da[:],
                                                                    op0=ALU.mult, op1=ALU.add), reads=[rz, o_da], excl=[p1], writes=[o_da])
                P.op('dve', lambda e: e.scalar_tensor_tensor(out=junk[:], in0=o_da[:], scalar=1.0, in1=o_da[:], op0=ALU.mult, op1=ALU.mult,
                                                             accum_out=ss[:]), reads=[o_da], writes=[junk, ss])
                P.op('dve', lambda e: e.tensor_scalar(ss[:], ss[:], 1.0 / 128.0, LN_EPS, op0=ALU.mult, op1=ALU.add), reads=[ss], writes=[ss])
                P.op('act', lambda e: e.activation(ss[:], ss[:], AF.Sqrt), reads=[ss], writes=[ss])
                P.op('dve', lambda e: e.reciprocal(ss[:], ss[:]), reads=[ss], writes=[ss])
                P.op('dve', lambda e, oo=oo: e.scalar_tensor_tensor(out=oo[:], in0=o_da[:], scalar=ss[:, 0:1], in1=gda[:], op0=ALU.mult, op1=ALU.mult),
                     reads=[o_da, ss, gda], writes=[oo])
                g_blk = (q0 + s_ * 128) // 128
                oT_ = oT_sb[nout % 2]
                P.op('pe', lambda e, oo=oo: e.transpose(bass.AP(pT6, 0, [[1024, 128], [1, 128]]), oo[:], ident[:]),
                     reads=[oo, ident], writes=[banks[6]])
                P.op('act', lambda e, oT_=oT_: e.copy(oT_[:], bass.AP(pT6, 0, [[1024, 128], [1, 128]])), excl=[banks[6]], writes=[oT_])
                jj, qq, ww = g_blk // 16, (g_blk % 16) // 4, g_blk % 4
                P.dma('sp', mixsrc_h[jj].ap()[(qq * 2) * 128:(qq * 2 + 1) * 128, ww * 128:(ww + 1) * 128], oT_[:], reads=[oT_], writes=[mixsrc_res])


    da_front(0)
    for k in range(len(steps)):
        if k + 1 < len(steps):
            da_front(k + 1)
        da_back(k)

    nch = nblk
    lf = P.sb([128, 64], F32, "lf")
    bcs = P.sb([128, 64], F32, "bcs")
    ek = P.sb([128, 64], F32, "ek")
    eb = P.sb([128, 64], F32, "eb")
    ebL = P.sb([128, 64], F32, "ebL")
    nfb = P.sb([128, 1], F32, "nfb")
    P.op('dve', lambda e: e.tensor_scalar(nfb[:], spt[:, 11:12], -1.0, None, op0=ALU.mult), reads=[spt], writes=[nfb])
    P.op('act', lambda e: e.activation(lf[:, 0:nch], gates[:, 0:nch, 1], AF.Exp, bias=nfb[:, 0:1], scale=-1.0), reads=[gates, nfb], writes=[lf])
    P.op('dve', lambda e: e.tensor_scalar(lf[:, 0:nch], lf[:, 0:nch], 1.0, None, op0=ALU.add), reads=[lf], writes=[lf])
    P.op('act', lambda e: e.activation(lf[:, 0:nch], lf[:, 0:nch], AF.Ln), reads=[lf], writes=[lf])
    P.op('dve', lambda e: e.tensor_scalar(lf[:, 0:nch], lf[:, 0:nch], -1.0, None, op0=ALU.mult), reads=[lf], writes=[lf])
    P.op('pe', lambda e: e.matmul(banks[0][:, 0:nch], trif[:], lf[:, 0:nch], start=True, stop=True), reads=[trif, lf], writes=[banks[0]])
    P.op('pe', lambda e: e.matmul(banks[1][:, 0:nch], onesf[:], lf[:, 0:nch], start=True, stop=True), reads=[onesf, lf], writes=[banks[1]])
    P.op('dve', lambda e: e.tensor_copy(bcs[:, 0:nch], banks[0][:, 0:nch]), excl=[banks[0]], writes=[bcs])
    P.op('act', lambda e: e.activation(eb[:, 0:nch], bcs[:, 0:nch], AF.Exp), reads=[bcs], writes=[eb])
    P.op('act', lambda e: e.activation(ebL[:, 0:nch], banks[1][:, 0:nch], AF.Exp), excl=[banks[1]], writes=[ebL])
    P.op('dve', lambda e: e.tensor_tensor(ek[:, 0:nch], gates[:, 0:nch, 0], bcs[:, 0:nch], op=ALU.subtract), reads=[gates, bcs], writes=[ek])
    P.op('act', lambda e: e.activation(ek[:, 0:nch], ek[:, 0:nch], AF.Exp, bias=spt[:, 10:11], scale=1.0), reads=[ek, spt], writes=[ek])

    Dst = [P.sb([128, 129], F32, f"Dst{i}") for i in range(2)]
    Cb = [P.sb([128, 129], BF16, f"Cb{i}") for i in range(2)]
    ktok = [P.sb([128, 128], BF16, f"ktok{i}") for i in range(2)]
    vp = [P.sb([128, 129], BF16, f"vp{i}") for i in range(2)]
    ATb = [P.sb([128, 128], BF16, f"ATb{i}") for i in range(2)]
    hh = P.sb([128, 128], F32, "hh")
    hsm = P.sb([128, 4], F32, "hsm")
    st6 = P.sb([128, 6], F32, "st6")
    mvv = P.sb([128, 2], F32, "mvv")
    rstd = P.sb([128, 1], F32, "rstd")
    ho = [P.sb([128, 128], BF16, f"ho{i}") for i in range(2)]
    pKt = banks[2].t.bitcast(BF16)
    for c in range(nch):
        sl = slice(c * 128, (c + 1) * 128)
        kt, v_, at = ktok[c % 2], vp[c % 2], ATb[c % 2]
        P.op('pe', lambda e, sl=sl: e.transpose(bass.AP(pKt, 0, [[1024, 128], [1, 128]]), mlkT[:, sl], ident[:]),
             reads=[mlkT, ident], writes=[banks[2]])
        P.op('act', lambda e, kt=kt: e.copy(kt[:], bass.AP(pKt, 0, [[1024, 128], [1, 128]])), excl=[banks[2]], writes=[kt])
        P.op('pool', lambda e, c=c, v_=v_: e.tensor_scalar(v_[:], mlV[:, c, :], ek[:, c:c + 1], None, op0=ALU.mult),
             reads=[mlV, ek], writes=[v_])
        P.op('pe', lambda e, kt=kt, v_=v_: e.matmul(banks[3][:, 0:129], kt[:], v_[:], start=True, stop=True),
             reads=[kt, v_], writes=[banks[3]])
        P.op('pe', lambda e, sl=sl: e.matmul(banks[4][:, 0:128], mlkT[:, sl], mlqT[:, sl], start=True, stop=True),
             reads=[mlkT, mlqT], writes=[banks[4]])
        P.op('dve', lambda e, c=c, at=at: e.scalar_tensor_tensor(out=at[:], in0=banks[4][:, 0:128], scalar=ek[:, c:c + 1], in1=trif[:],
                                                                 op0=ALU.mult, op1=ALU.mult), reads=[ek, trif], excl=[banks[4]], writes=[at])
        pH = banks[5 + c % 2]
        if c > 0:
            P.op('pe', lambda e, sl=sl, c=c, pH=pH: e.matmul(pH[:, 0:129], mlqT[:, sl], Cb[(c - 1) % 2][:], start=True, stop=False),
                 reads=[mlqT, Cb[(c - 1) % 2]], writes=[pH])
        P.op('pe', lambda e, at=at, c=c, pH=pH: e.matmul(pH[:, 0:129], at[:], mlV[:, c, :], start=(c == 0), stop=True),
             reads=[at, mlV], writes=[pH])
        Dc = Dst[c % 2]
        if c == 0:
            P.op('dve', lambda e, Dc=Dc: e.tensor_copy(Dc[:], banks[3][:, 0:129]), excl=[banks[3]], writes=[Dc])
        else:
            Dp = Dst[(c - 1) % 2]
            P.op('dve', lambda e, Dc=Dc, Dp=Dp, c=c: e.scalar_tensor_tensor(out=Dc[:], in0=Dp[:], scalar=ebL[:, c - 1:c], in1=banks[3][:, 0:129],
                                                                          op0=ALU.mult, op1=ALU.add), reads=[Dp, ebL], excl=[banks[3]], writes=[Dc])
        P.op('act', lambda e, Dc=Dc, c=c: e.activation(Cb[c % 2][:], Dc[:], AF.Identity, scale=ebL[:, c:c + 1]), reads=[Dc, ebL], writes=[Cb[c % 2]])
        P.op('dve', lambda e, c=c, pH=pH: e.tensor_tensor(hsm[:, 0:1], pH[:, 128:129], eb[:, c:c + 1], op=ALU.mult), reads=[eb], excl=[pH], writes=[hsm])
        P.op('dve', lambda e: e.tensor_scalar(hsm[:, 1:2], hsm[:, 0:1], -1.0, 1.0, op0=ALU.mult, op1=ALU.max), reads=[hsm], writes=[hsm])
        P.op('dve', lambda e: e.tensor_scalar(hsm[:, 2:3], hsm[:, 0:1], 1.0, None, op0=ALU.max), reads=[hsm], writes=[hsm])
        P.op('dve', lambda e: e.tensor_tensor(hsm[:, 1:2], hsm[:, 1:2], hsm[:, 2:3], op=ALU.max), reads=[hsm], writes=[hsm])
        P.op('dve', lambda e: e.reciprocal(hsm[:, 2:3], hsm[:, 1:2]), reads=[hsm], writes=[hsm])
        P.op('dve', lambda e, c=c: e.tensor_tensor(hsm[:, 3:4], hsm[:, 2:3], eb[:, c:c + 1], op=ALU.mult), reads=[hsm, eb], writes=[hsm])
        P.op('dve', lambda e, pH=pH: e.tensor_scalar(hh[:], pH[:, 0:128], hsm[:, 3:4], None, op0=ALU.mult), reads=[hsm], excl=[pH], writes=[hh])
        P.op('dve', lambda e: e.bn_stats(st6[:], hh[:]), reads=[hh], writes=[st6])
        P.op('dve', lambda e: e.bn_aggr(mvv[:], st6[:]), reads=[st6], writes=[mvv])
        P.op('dve', lambda e: e.tensor_scalar(rstd[:], mvv[:, 1:2], LN_EPS, None, op0=ALU.add), reads=[mvv], writes=[rstd])
        P.op('act', lambda e: e.activation(rstd[:], rstd[:], AF.Sqrt), reads=[rstd], writes=[rstd])
        P.op('dve', lambda e: e.reciprocal(rstd[:], rstd[:]), reads=[rstd], writes=[rstd])
        P.op('dve', lambda e: e.tensor_scalar(hh[:], hh[:], mvv[:, 0:1], rstd[:, 0:1], op0=ALU.subtract, op1=ALU.mult),
             reads=[hh, mvv, rstd], writes=[hh])
        P.op('pool', lambda e: e.tensor_tensor(hh[:], hh[:], gml[:], op=ALU.mult), reads=[hh, gml], writes=[hh])
        hoo = ho[c % 2]
        P.op('pool', lambda e, c=c, hoo=hoo: e.tensor_tensor(hoo[:], hh[:], sigo[:, c, :], op=ALU.mult), reads=[hh, sigo], writes=[hoo])
        hT_ = oT_sb[c % 2]
        P.op('pe', lambda e, hoo=hoo: e.transpose(bass.AP(pT7, 0, [[1024, 128], [1, 128]]), hoo[:], ident[:]),
             reads=[hoo, ident], writes=[banks[7]])
        P.op('act', lambda e, hT_=hT_: e.copy(hT_[:], bass.AP(pT7, 0, [[1024, 128], [1, 128]])), excl=[banks[7]], writes=[hT_])
        jj, qq, ww = c // 16, (c % 16) // 4, c % 4
        P.dma('sp', mixsrc_h[jj].ap()[(qq * 2 + 1) * 128:(qq * 2 + 2) * 128, ww * 128:(ww + 1) * 128], hT_[:], reads=[hT_], writes=[mixsrc_res])


def phase_b(P, nc, C, H, xin_h, xin_res, mixg_h, mixg_res, midx_h, xout_h, xout_res, xsrc_h, xsrc_res, ntiles):
    ident, ones_bf, iota16, thr16 = C['ident'], C['ones_bf'], C['iota16'], C['thr16']
    D = 1024
    lnp = P.sb([128, 6 * D], F32, "lnp")
    P.dma('sp', lnp[:], bass.AP(H['lnp'], 0, [[0, 128], [1, 6 * D]]), writes=[lnp])

    r = P.sb([128, D], F32, "r")
    stg = [r] * 2
    stg_i = [0]

    def load_w(handle, n, name, dst=None):
        wb = dst if dst is not None else P.sb([128, 8, n], BF16, name)
        wap = handle.ap().rearrange("(c p) n -> p c n", p=128)
        for kc in range(8):
            for pc in range(n // 1024):
                s = stg[stg_i[0] % 2]
                stg_i[0] += 1
                P.dma('sp', s[:, :], wap[:, kc, pc * 1024:(pc + 1) * 1024], writes=[s])
                if stg_i[0] % 2 == 0:
                    P.op('act', lambda e, s=s, kc=kc, pc=pc: e.copy(wb[:, kc, pc * 1024:(pc + 1) * 1024], s[:, :]), reads=[s], writes=[wb])
                else:
                    P.op('dve', lambda e, s=s, kc=kc, pc=pc: e.tensor_copy(wb[:, kc, pc * 1024:(pc + 1) * 1024], s[:, :]), reads=[s], writes=[wb])
        return wb

    wout_b = load_w(H['wout'], 1024, "wout")
    wq_b = load_w(H['wq'], 1024, "wq")
    wo_b = load_w(H['wo'], 1024, "wo")
    wpq_b = load_w(H['wpq'], 2048, "wpq")

    pA = P.ps([128, 2048], F32, "pA")
    pB = P.ps([128, 1024], F32, "pB")
    pC = P.ps([128, 512], F32, "pC")
    pT = P.ps([128, 8, 128], BF16, "pT")

    KT_b = P.sb([128, 8, 256], BF16, "KT")
    V_b = P.sb([128, 2, 1024], BF16, "V")
    skT_b = P.sb([128, 2, 128], BF16, "skT")
    NROW = 4
    gbuf = P.sb([128, 2 * NROW, 1024], F32, "gbuf")
    gb_bf = gbuf.t.bitcast(BF16)

    def wkv_ap(kc, c0, c1):
        return bass.AP(gb_bf, kc * 2048 + c0, [[2 * NROW * 1024 * 2, 128], [1, c1 - c0]])

    wkvap = H['wkv'].ap().rearrange("(c p) n -> p c n", p=128)
    for kc in range(8):
        for pc in range(2):
            s = stg[stg_i[0] % 2]
            stg_i[0] += 1
            P.dma('sp', s[:, :], wkvap[:, kc, pc * 1024:(pc + 1) * 1024], writes=[s])
            P.op('dve', lambda e, s=s, kc=kc, pc=pc: e.tensor_copy(wkv_ap(kc, pc * 1024, (pc + 1) * 1024), s[:, :]), reads=[s], writes=[gbuf])
    sc = P.sb([128, 16, 128], F32, "sc")
    w8bf = sc.t.bitcast(BF16)
    memT_b = sc

    def memT_ap(kc, m0, m1):
        return bass.AP(w8bf, kc * 256 + m0, [[4096, 128], [1, m1 - m0]])

    def pqT_ap(hc):
        return bass.AP(w8bf, hc * 128, [[4096, 128], [1, 128]])
    mT_ap = H['memT'].ap().rearrange("(c p) n -> p c n", p=128)
    for kc in range(8):
        s = stg[stg_i[0] % 2]
        stg_i[0] += 1
        P.dma('sp', s[:, 0:256], mT_ap[:, kc, :], writes=[s])
        P.op('dve', lambda e, s=s, kc=kc: e.tensor_copy(memT_ap(kc, 0, 256), s[:, 0:256]), reads=[s], writes=[memT_b])
    s = stg[stg_i[0] % 2]
    stg_i[0] += 1
    P.dma('sp', s[:, 0:256].rearrange("p (c n) -> p c n", c=2), H['skT'].ap().rearrange("c d n -> d c n"), writes=[s])
    P.op('dve', lambda e, s=s: e.tensor_copy(skT_b[:], s[:, 0:256].rearrange("p (c n) -> p c n", c=2)), reads=[s], writes=[skT_b])

    for j in range(8):
        for kc in range(8):
            P.op('pe', lambda e, j=j, kc=kc: e.matmul(pB[:, 0:256], wkv_ap(kc, j * 128, (j + 1) * 128), memT_ap(kc, 0, 256),
                                                      start=(kc == 0), stop=(kc == 7)),
                 reads=[gbuf, memT_b], writes=[pB])
        P.op('dve', lambda e, j=j: e.tensor_copy(KT_b[:, j, :], pB[:, 0:256]), excl=[pB], writes=[KT_b])
    for mc in range(2):
        for half in range(2):
            for kc in range(8):
                P.op('pe', lambda e, mc=mc, half=half, kc=kc: e.matmul(
                    pB[:, 0:512], memT_ap(kc, mc * 128, (mc + 1) * 128),
                    wkv_ap(kc, 1024 + half * 512, 1024 + (half + 1) * 512), start=(kc == 0), stop=(kc == 7)),
                     reads=[gbuf, memT_b], writes=[pB])
            P.op('dve', lambda e, mc=mc, half=half: e.tensor_copy(V_b[:, mc, half * 512:(half + 1) * 512], pB[:, 0:512]),
                 excl=[pB], writes=[V_b])

    xt = P.sb([128, D], F32, "xt")
    mixb = P.sb([128, 8, 512], BF16, "mixb")
    x1 = P.sb([128, D], F32, "x1")
    x2 = P.sb([128, D], F32, "x2")
    xb = P.sb([128, D], BF16, "xb")
    xT = P.sb([128, 8, 128], BF16, "xT")
    qT = P.sb([128, 8, 128], BF16, "qT")
    E_b = P.sb([128, 8, 128], BF16, "E")
    rz = P.sb([128, 4, 128], F32, "rz")
    oT = qT
    top = P.sb([128, 16, 16], F32, "top")
    idxu = P.sb([128, 16, 16], U32, "idxu")
    idxf = P.sb([128, 16, 16], F32, "idxf")
    cs = P.sb([128, 8, 16], F32, "cs")
    ciu = P.sb([128, 8, 16], U32, "ciu")
    cif = P.sb([128, 128], F32, "cif")
    big = sc
    big2 = sc
    pqT = sc
    mixf = r
    x3 = r
    junk = r
    cand = sc
    k0f = P.sb([128, 128], F32, "k0f")
    k1f = P.sb([128, 128], F32, "k1f")
    i0s = P.sb([128, 128], F32, "i0s")
    i1s = P.sb([128, 128], F32, "i1s")
    ef = P.sb([128, 128], F32, "ef")
    eu = P.sb([128, 128], U32, "eu")
    gex = P.sb([128, 8, 16], F32, "gex")
    gz = P.sb([128, 8], F32, "gz")
    aact = P.sb([128, 128], F32, "aact")
    wgt = P.sb([128, 128], F32, "wgt")
    acc = x1
    st = P.sb([128, 12], F32, "st")
    mv = P.sb([128, 2], F32, "mv")
    rstd = P.sb([128, 1], F32, "rstd")
    grow = [Res(f"grow{i}") for i in range(16)]
    dgs = [P.sb([128, 128], BF16, f"dg{i}") for i in range(4)]
    gel = P.sb([128, 128], F32, "gel")
    acol = [Res() for _ in range(128)]
    gcol = [Res() for _ in range(128)]
    wcol = [Res() for _ in range(128)]

    def layernorm(src, dst, li):
        for c in range(2):
            P.op('dve', lambda e, c=c: e.bn_stats(st[:, c * 6:(c + 1) * 6], src[:, c * 512:(c + 1) * 512]),
                 reads=[src], writes=[st])
        P.op('dve', lambda e: e.bn_aggr(mv[:], st[:]), reads=[st], writes=[mv])
        P.op('dve', lambda e: e.tensor_scalar(rstd[:], mv[:, 1:2], LN_EPS, None, op0=ALU.add), reads=[mv], writes=[rstd])
        P.op('act', lambda e: e.activation(rstd[:], rstd[:], AF.Sqrt), reads=[rstd], writes=[rstd])
        P.op('dve', lambda e: e.reciprocal(rstd[:], rstd[:]), reads=[rstd], writes=[rstd])
        P.op('dve', lambda e: e.tensor_scalar(dst[:], src[:], mv[:, 0:1], rstd[:, 0:1], op0=ALU.subtract, op1=ALU.mult),
             reads=[src, mv, rstd], writes=[dst])
        P.op('dve', lambda e: e.tensor_tensor(dst[:], dst[:], lnp[:, (2 * li) * D:(2 * li + 1) * D], op=ALU.mult),
             reads=[dst, lnp], writes=[dst])
        P.op('dve', lambda e: e.tensor_tensor(dst[:], dst[:], lnp[:, (2 * li + 1) * D:(2 * li + 2) * D], op=ALU.add),
             reads=[dst, lnp], writes=[dst])

    def to_T(src):
        P.op('act', lambda e: e.copy(xb[:], src[:]), reads=[src], writes=[xb])
        for c in range(8):
            P.op('pe', lambda e, c=c: e.transpose(pT[:, c, :], xb[:, c * 128:(c + 1) * 128], ident[:]),
                 reads=[xb, ident], writes=[pT])
        P.op('dve', lambda e: e.tensor_copy(xT[:], pT[:]), excl=[pT], writes=[xT])

    def linear(lhsT_tile, w_b, n, pdst, off=0):
        for half in range(n // 512):
            for kc in range(8):
                P.op('pe', lambda e, half=half, kc=kc: e.matmul(
                    pdst[:, half * 512:(half + 1) * 512], lhsT_tile[:, kc, off:off + 128], w_b[:, kc, half * 512:(half + 1) * 512],
                    start=(kc == 0), stop=(kc == 7)), reads=[lhsT_tile, w_b], writes=[pdst])

    midx = P.sb([128, 32], U32, "midx")
    P.dma('sp', midx[:], midx_h.ap(), writes=[midx])

    def brow(rr):
        return bass.AP(gb_bf, rr * 1024, [[16384, 128], [1, 1024]])

    P.barrier()
    cst = [Res(f"cst{k}") for k in range(4)]
    stb = [r, x1, x2, xt]
    for k in range(128):
        tab = H['u'] if k < 64 else H['v']
        tsel = 0 if k < 64 else 1
        row0 = (k % 64) * 256
        sf = gbuf[:, 2 * (k % 4):2 * (k % 4) + 2, :]
        tb = stb[k % 4]
        sbv = bass.AP(tb.t.bitcast(BF16), 0, [[2048, 128], [1024, 2], [1, 1024]])
        P.dma('sp', sf, tab.ap()[row0:row0 + 256, :].rearrange("(a p) n -> p a n", p=128), writes=[cst[k % 4]])
        if k % 2 == 0:
            P.op('dve', lambda e, sf=sf, sbv=sbv: e.tensor_copy(sbv, sf), reads=[cst[k % 4]], writes=[tb])
        else:
            P.op('act', lambda e, sf=sf, sbv=sbv: e.copy(sbv, sf), reads=[cst[k % 4]], writes=[tb])
        P.dma('pool', H['uvb'].ap()[row0:row0 + 256, tsel * 1024:(tsel + 1) * 1024].rearrange("(a p) n -> p a n", p=128), sbv, reads=[tb], writes=[Res("cvout")])
    P.barrier()
    for i in range(ntiles):
        t0 = i * 128
        P.dma('sp', xt[:], xin_h.ap()[t0:t0 + 128, :], reads=[xin_res], writes=[xt])
        if i % 4 == 0:
            for kc in range(8):
                P.dma('pool', None, None, reads=[mixg_res, midx], writes=[mixb],
                      fn=lambda e, kc=kc, i=i: e.indirect_dma_start(
                          out=mixb[:, kc, :], out_offset=None, in_=mixg_h.ap(),
                          in_offset=bass.IndirectOffsetOnAxis(ap=midx[:, (i // 4) * 8 + kc:(i // 4) * 8 + kc + 1], axis=0)))
        linear(mixb, wout_b, 1024, pB, off=(i % 4) * 128)
        P.op('dve', lambda e: e.scalar_tensor_tensor(out=r[:], in0=xt[:], scalar=ALPHA, in1=pB[:], op0=ALU.mult, op1=ALU.add),
             reads=[xt], excl=[pB], writes=[r])
        layernorm(r, x1, 0)
        to_T(x1)
        for j in range(8):
            for kc in range(8):
                P.op('pe', lambda e, j=j, kc=kc: e.matmul(pA[:, j * 128:(j + 1) * 128], wq_b[:, kc, j * 128:(j + 1) * 128],
                                                          xT[:, kc, :], start=(kc == 0), stop=(kc == 7)),
                     reads=[wq_b, xT], writes=[pA])
        P.op('act', lambda e: e.copy(qT[:], pA[:, 0:1024].rearrange("p (j t) -> p j t", j=8)), excl=[pA], writes=[qT])
        for h in range(4):
            for mc in range(2):
                for dc in range(2):
                    P.op('pe', lambda e, h=h, mc=mc, dc=dc: e.matmul(
                        pB[:, (h * 2 + mc) * 128:(h * 2 + mc + 1) * 128],
                        KT_b[:, h * 2 + dc, mc * 128:(mc + 1) * 128], qT[:, h * 2 + dc, :],
                        start=(dc == 0), stop=(dc == 1)), reads=[KT_b, qT], writes=[pB])
        P.op('act', lambda e: e.activation(E_b[:], pB[:].rearrange("p (j t) -> p j t", j=8), AF.Exp, scale=1.0 / 16.0),
             excl=[pB], writes=[E_b])
        for h in range(4):
            for mc in range(2):
                P.op('pe', lambda e, h=h, mc=mc: e.matmul(pC[:, h * 128:(h + 1) * 128], ones_bf[:], E_b[:, h * 2 + mc, :],
                                                          start=(mc == 0), stop=(mc == 1)),
                     reads=[ones_bf, E_b], writes=[pC])
        P.op('dve', lambda e: e.reciprocal(rz[:], pC[:].rearrange("p (h t) -> p h t", h=4)), excl=[pC], writes=[rz])
        for h in range(4):
            for dc in range(2):
                for mc in range(2):
                    P.op('pe', lambda e, h=h, dc=dc, mc=mc: e.matmul(
                        pA[:, 1024 + (h * 2 + dc) * 128:1024 + (h * 2 + dc + 1) * 128],
                        V_b[:, mc, h * 256 + dc * 128:h * 256 + (dc + 1) * 128], E_b[:, h * 2 + mc, :],
                        start=(mc == 0), stop=(mc == 1)), reads=[V_b, E_b], writes=[pA])
        for j in range(8):
            P.op('dve', lambda e, j=j: e.tensor_tensor(oT[:, j, :], pA[:, 1024 + j * 128:1024 + (j + 1) * 128], rz[:, j // 2, :],
                                                       op=ALU.mult), reads=[rz], excl=[pA], writes=[oT])
        linear(oT, wo_b, 1024, pB)
        P.op('dve', lambda e: e.scalar_tensor_tensor(out=r[:], in0=x1[:], scalar=ALPHA, in1=pB[:], op0=ALU.mult, op1=ALU.add),
             reads=[x1], excl=[pB], writes=[r])
        layernorm(r, x2, 1)
        to_T(x2)
        for hc in range(16):
            for kc in range(8):
                P.op('pe', lambda e, hc=hc, kc=kc: e.matmul(pA[:, hc * 128:(hc + 1) * 128], wpq_b[:, kc, hc * 128:(hc + 1) * 128],
                                                            xT[:, kc, :], start=(kc == 0), stop=(kc == 7)),
                     reads=[wpq_b, xT], writes=[pA])
        P.op('act', lambda e: e.copy(bass.AP(w8bf, 0, [[4096, 128], [1, 2048]]), pA[:]), excl=[pA], writes=[pqT])
        for hc in range(16):
            P.op('pe', lambda e, hc=hc: e.matmul(pA[:, hc * 128:(hc + 1) * 128], pqT_ap(hc), skT_b[:, hc % 2, :],
                                                 start=True, stop=True), reads=[pqT, skT_b], writes=[pA])
        P.op('act', lambda e: e.copy(sc[:], pA[:].rearrange("p (j t) -> p j t", j=16)), excl=[pA], writes=[sc])
        for hc in range(16):
            P.op('dve', lambda e, hc=hc: e.max(top[:, hc, 0:8], sc[:, hc, :]), reads=[sc], writes=[top])
            P.op('dve', lambda e, hc=hc: e.max_index(idxu[:, hc, 0:8], top[:, hc, 0:8], sc[:, hc, :]), reads=[sc, top], writes=[idxu])
            P.op('dve', lambda e, hc=hc: e.match_replace(sc[:, hc, :], top[:, hc, 0:8], sc[:, hc, :], -1e30),
                 reads=[sc, top], writes=[sc])
            P.op('dve', lambda e, hc=hc: e.max(top[:, hc, 8:16], sc[:, hc, :]), reads=[sc], writes=[top])
            P.op('dve', lambda e, hc=hc: e.max_index(idxu[:, hc, 8:16], top[:, hc, 8:16], sc[:, hc, :]), reads=[sc, top], writes=[idxu])
        P.op('dve', lambda e: e.tensor_copy(idxf[:], idxu[:]), reads=[idxu], writes=[idxf])
        P.op('dve', lambda e: e.tensor_tensor(AP(cand, 0, [[2048, 128], [256, 8], [16, 16], [1, 16]]),
                                              AP(top, 0, [[256, 128], [32, 8], [1, 16], [0, 16]]),
                                              AP(top, 16, [[256, 128], [32, 8], [0, 16], [1, 16]]), op=ALU.add),
             reads=[top], writes=[cand])
        for h in range(8):
            P.op('dve', lambda e, h=h: e.max(cs[:, h, 0:8], sc.t.rearrange('p a b -> p (a b)')[:, h * 256:(h + 1) * 256]), reads=[cand], writes=[cs])
            P.op('dve', lambda e, h=h: e.max_index(ciu[:, h, 0:8], cs[:, h, 0:8], sc.t.rearrange('p a b -> p (a b)')[:, h * 256:(h + 1) * 256]), reads=[cand, cs], writes=[ciu])
            P.op('dve', lambda e, h=h: e.match_replace(sc.t.rearrange('p a b -> p (a b)')[:, h * 256:(h + 1) * 256], cs[:, h, 0:8], sc.t.rearrange('p a b -> p (a b)')[:, h * 256:(h + 1) * 256], -1e30),
                 reads=[cand, cs], writes=[cand])
            P.op('dve', lambda e, h=h: e.max(cs[:, h, 8:16], sc.t.rearrange('p a b -> p (a b)')[:, h * 256:(h + 1) * 256]), reads=[cand], writes=[cs])
            P.op('dve', lambda e, h=h: e.max_index(ciu[:, h, 8:16], cs[:, h, 8:16], sc.t.rearrange('p a b -> p (a b)')[:, h * 256:(h + 1) * 256]), reads=[cand, cs], writes=[ciu])
        P.op('dve', lambda e: e.tensor_copy(cif[:], ciu[:].rearrange("p h k -> p (h k)")), reads=[ciu], writes=[cif])
        P.op('dve', lambda e: e.tensor_tensor(AP(big, 0, [[2048, 128], [16, 128], [1, 16]]),
                                              AP(cif, 0, [[128, 128], [1, 128], [0, 16]]),
                                              AP(thr16, 0, [[16, 128], [0, 128], [1, 16]]), op=ALU.is_ge),
             reads=[cif, thr16], writes=[big])
        P.op('dve', lambda e: e.tensor_reduce(k0f[:], AP(big, 0, [[2048, 128], [16, 128], [1, 16]]), axis=AX.X, op=ALU.add),
             reads=[big], writes=[k0f])
        P.op('dve', lambda e: e.scalar_tensor_tensor(out=k1f[:], in0=k0f[:], scalar=-16.0, in1=cif[:], op0=ALU.mult, op1=ALU.add),
             reads=[k0f, cif], writes=[k1f])
        for (kf, c, dst) in ((k0f, 0, i0s), (k1f, 1, i1s)):
            P.op('dve', lambda e, kf=kf: e.tensor_tensor(AP(big, 0, [[2048, 128], [16, 128], [1, 16]]),
                                                         AP(kf, 0, [[128, 128], [1, 128], [0, 16]]),
                                                         AP(iota16, 0, [[16, 128], [0, 128], [1, 16]]), op=ALU.is_equal),
                 reads=[kf, iota16], writes=[big])
            P.op('dve', lambda e, c=c: e.tensor_tensor(AP(big2, 0, [[2048, 128], [256, 8], [16, 16], [1, 16]]),
                                                       AP(big, 0, [[2048, 128], [256, 8], [16, 16], [1, 16]]),
                                                       AP(idxf, c * 16, [[256, 128], [32, 8], [0, 16], [1, 16]]), op=ALU.mult),
                 reads=[big, idxf], writes=[big2])
            P.op('dve', lambda e, dst=dst: e.tensor_reduce(dst[:], AP(big2, 0, [[2048, 128], [16, 128], [1, 16]]), axis=AX.X, op=ALU.add),
                 reads=[big2], writes=[dst])
        P.op('dve', lambda e: e.scalar_tensor_tensor(out=ef[:], in0=i0s[:], scalar=128.0, in1=i1s[:], op0=ALU.mult, op1=ALU.add),
             reads=[i0s, i1s], writes=[ef])
        P.op('dve', lambda e: e.tensor_copy(eu[:], ef[:]), reads=[ef], writes=[eu])
        P.op('dve', lambda e: e.tensor_tensor(gex[:], cs[:], AP(cs, 0, [[128, 128], [16, 8], [0, 16]]), op=ALU.subtract),
             reads=[cs], writes=[gex])
        P.op('act', lambda e: e.activation(gex[:], gex[:], AF.Exp), reads=[gex], writes=[gex])
        P.op('dve', lambda e: e.tensor_reduce(gz[:], gex[:], axis=AX.X, op=ALU.add), reads=[gex], writes=[gz])
        P.op('dve', lambda e: e.reciprocal(gz[:], gz[:]), reads=[gz], writes=[gz])
        P.op('dve', lambda e: e.tensor_tensor(gex[:], gex[:], AP(gz, 0, [[8, 128], [1, 8], [0, 16]]), op=ALU.mult),
             reads=[gex, gz], writes=[gex])
        gexf = gex.t.rearrange("p h k -> p (h k)")
        for s_ in range(128 + 3):
            if s_ < 128:
                hk, rr = s_, s_ % 8
                P.dma('pool', None, None, reads=[eu], writes=[grow[rr]],
                      fn=lambda e, hk=hk, rr=rr: e.indirect_dma_start(
                          out=bass.AP(gb_bf, rr * 2048, [[16384, 128], [1, 2048]]), out_offset=None, in_=H['uvb'].ap(),
                          in_offset=bass.IndirectOffsetOnAxis(ap=eu[:, hk:hk + 1], axis=0)))
                P.op('dve', lambda e, hk=hk, rr=rr: e.scalar_tensor_tensor(
                    out=junk[:], in0=bass.AP(gb_bf, rr * 2048, [[16384, 128], [1, 1024]]), scalar=1.0, in1=x2[:],
                    op0=ALU.mult, op1=ALU.mult, accum_out=aact[:, hk:hk + 1]), reads=[grow[rr], x2], writes=[junk, acol[hk]])
            if 0 <= s_ - 1 < 128:
                hk = s_ - 1
                P.op('act', lambda e, hk=hk: e.activation(gel[:, hk:hk + 1], aact[:, hk:hk + 1], AF.Gelu),
                     reads=[acol[hk]], writes=[gcol[hk]])
            if 0 <= s_ - 2 < 128:
                hk = s_ - 2
                P.op('dve', lambda e, hk=hk: e.tensor_tensor(wgt[:, hk:hk + 1], gel[:, hk:hk + 1], gexf[:, hk:hk + 1], op=ALU.mult),
                     reads=[gcol[hk], gex], writes=[wcol[hk]])
            if 0 <= s_ - 3 < 128:
                hk = s_ - 3
                rr = hk % 8
                dg = dgs[hk % 4]
                P.op('act', lambda e, hk=hk, dg=dg: e.activation(dg[:], ident[:], AF.Identity, scale=wgt[:, hk:hk + 1]),
                     reads=[ident, wcol[hk]], writes=[dg])
                for half in range(2):
                    P.op('pe', lambda e, hk=hk, rr=rr, dg=dg, half=half: e.matmul(
                        pB[:, half * 512:(half + 1) * 512], dg[:],
                        bass.AP(gb_bf, rr * 2048 + 1024 + half * 512, [[16384, 128], [1, 512]]),
                        start=(hk == 0), stop=(hk == 127)), reads=[dg, grow[rr]], writes=[pB])
        P.op('dve', lambda e: e.scalar_tensor_tensor(out=acc[:], in0=x2[:], scalar=ALPHA, in1=pB[:], op0=ALU.mult, op1=ALU.add),
             reads=[x2], excl=[pB], writes=[acc])
        layernorm(acc, x3, 2)
        P.dma('sp', xout_h.ap()[t0:t0 + 128, :], x3[:], reads=[x3], writes=[xout_res])
        if xsrc_h is not None:
            to_T(x3)
            for c4 in range(4):
                P.dma('sp', xsrc_h[c4].ap().rearrange("(k p) t -> p k t", p=128)[:, :, t0:t0 + 128], xT[:, 2 * c4:2 * c4 + 2, :],
                      reads=[xT], writes=[xsrc_res])

def build_fused(depth=4, nq_tiles=32, ntiles=16):
    nc = bass.Bass("TRN2", target_bir_lowering=False)
    D = 1024
    xT0_h = nc.dram_tensor("xT0", [D, S], F32, kind="ExternalInput")
    x0_h = nc.dram_tensor("x0", [NTOK, D], F32, kind="ExternalInput")
    memT_h = nc.dram_tensor("memT", [D, 256], F32, kind="ExternalInput")
    midx_h = nc.dram_tensor("midx", [128, 32], U32, kind="ExternalInput")
    uvb_h = nc.dram_tensor("uvb_scr", [16384, 2 * D], BF16)
    HA, HB = [], []
    for l in range(depth):
        HA.append({"wa": nc.dram_tensor(f"wa{l}", [D, NWA], F32, kind="ExternalInput"),
                   "sp": nc.dram_tensor(f"sp{l}", [128, 16], F32, kind="ExternalInput"),
                   "lamqk": nc.dram_tensor(f"lamqk{l}", [128, 256], F32, kind="ExternalInput"),
                   "gda": nc.dram_tensor(f"gda{l}", [128, 128], F32, kind="ExternalInput"),
                   "gml": nc.dram_tensor(f"gml{l}", [128, 128], F32, kind="ExternalInput")})
        HB.append({"wout": nc.dram_tensor(f"wout{l}", [D, D], F32, kind="ExternalInput"),
                   "wq": nc.dram_tensor(f"wq{l}", [D, D], F32, kind="ExternalInput"),
                   "wkv": nc.dram_tensor(f"wkv{l}", [D, 2 * D], F32, kind="ExternalInput"),
                   "wo": nc.dram_tensor(f"wo{l}", [D, D], F32, kind="ExternalInput"),
                   "wpq": nc.dram_tensor(f"wpq{l}", [D, 2 * D], F32, kind="ExternalInput"),
                   "skT": nc.dram_tensor(f"skT{l}", [2, 128, 128], F32, kind="ExternalInput"),
                   "lnp": nc.dram_tensor(f"lnp{l}", [6, D], F32, kind="ExternalInput"),
                   "u": nc.dram_tensor(f"u{l}", [16384, D], F32, kind="ExternalInput"),
                   "v": nc.dram_tensor(f"v{l}", [16384, D], F32, kind="ExternalInput"),
                   "memT": memT_h, "uvb": uvb_h})
    out_h = nc.dram_tensor("out", [NTOK, D], F32, kind="ExternalOutput")
    mixsrc = [[nc.dram_tensor(f"mixsrc{l}_{c}", [1024, 512], BF16) for c in range(4)] for l in range(depth)]
    mixgc = [[nc.dram_tensor(f"mixgc{l}_{c}", [4096, 512], BF16) for c in range(4)] for l in range(depth)]
    mixg = [nc.dram_tensor(f"mixg{l}", [16384, 512], BF16) for l in range(depth)]
    xsrc = [[nc.dram_tensor(f"xsrc{l}_{c}", [256, NTOK], BF16) for c in range(4)] for l in range(depth - 1)]
    xg = [None] + [[nc.dram_tensor(f"xg{l}_{c}", [1024, NTOK], BF16) for c in range(4)] for l in range(1, depth)]
    xres = [None] + [nc.dram_tensor(f"xres{l}", [NTOK, D], F32) for l in range(1, depth)]
    GROUPS = [[0, 1, 2, 3], [4, 5, 6, 7]]

    P = Prog(nc, n_dma_sems={'sp': 8, 'act': 4, 'pool': 20})
    C = make_consts(P)
    out_res = Res("out")
    xg_res = [Res(f"xg{l}") for l in range(depth)]
    xres_res = [Res(f"xres{l}") for l in range(depth)]
    for l in range(depth):
        mixsrc_res, mixg_res, xsrc_res = Res("mixsrc"), Res("mixg"), Res("xsrc")
        with P.scope():
            phase_a(P, nc, C, HA[l], xT0_h, xg[l], xg_res[l], mixsrc[l], mixsrc_res, nq_tiles)
        for c in range(4):
            gc_res = Res("mixgc")
            P.cc("AllGather", GROUPS, mixsrc[l][c].ap(), mixgc[l][c].ap(), reads=[mixsrc_res], writes=[gc_res], inc=1)
            P.dma('sp', mixg[l].ap()[c * 4096:(c + 1) * 4096, :].rearrange("(a b) n -> a (b n)", a=128),
                  mixgc[l][c].ap().rearrange("(a b) n -> a (b n)", a=128), reads=[gc_res], writes=[mixg_res])
        last = (l == depth - 1)
        with P.scope():
            phase_b(P, nc, C, HB[l], x0_h if l == 0 else xres[l], xres_res[l], mixg[l], mixg_res, midx_h,
                    out_h if last else xres[l + 1], out_res if last else xres_res[l + 1],
                    None if last else xsrc[l], xsrc_res, ntiles)
        if not last:
            for c in range(4):
                P.cc("AllGather", GROUPS, xsrc[l][c].ap(), xg[l + 1][c].ap(), reads=[xsrc_res], writes=[xg_res[l + 1]], inc=1)
    P.finish([out_res])
    P.emit()
    P.close()
    return nc


DEPTH = 4
_NC_CACHE = {}


def _head_inputs(w_in_l, conv_w_l, conv_b_l, i_bias_l, f_bias_l, lam_qk_l, da_g_l, ml_g_l, l, h):
    r = np.arange(h * 128, (h + 1) * 128)
    cols = np.concatenate([r, 512 + r, 1024 + r, 1536 + r, 2048 + r, 2560 + r, 3072 + r, [3584 + h], [3588 + h]])
    wa = np.ascontiguousarray(w_in_l[:, cols])
    lam_init = 0.8 - 0.6 * math.exp(-0.3 * l)
    sp = np.zeros((128, 16), np.float32)
    for g in range(2):
        ch = g * 512 + r
        for j in range(4):
            sp[:, g * 4 + j] = conv_w_l[j, ch]
        sp[:, 8 + g] = conv_b_l[ch]
    sp[:, 10] = i_bias_l[h]
    sp[:, 11] = f_bias_l[h]
    sp[:, 12] = lam_init
    sp[:, 13] = 1.0 - lam_init
    lamqk = np.ascontiguousarray(np.broadcast_to(lam_qk_l.reshape(1, 256), (128, 256)))
    gda = np.ascontiguousarray(np.broadcast_to(da_g_l[None, :], (128, 128)))
    gml = np.ascontiguousarray(np.broadcast_to(ml_g_l[None, r], (128, 128)))
    return {f"wa{l}": wa, f"sp{l}": sp, f"lamqk{l}": lamqk, f"gda{l}": gda, f"gml{l}": gml}


def make_in_maps(depth, x, mem, w_in, i_bias, f_bias, conv_w, conv_b, lam_qk, da_norm_g, ml_norm_g, w_out,
                 ln1_g, ln1_b, wq_mem, wkv_mem, wo_mem, ln2_g, ln2_b, w_pq, sub_keys, u_tab, v_tab, ln3_g, ln3_b):
    f = lambda a: np.asarray(a, dtype=np.float32)
    x = f(x); mem = f(mem)
    B = x.shape[0]
    shared = {}
    for l in range(depth):
        shared[f"wout{l}"] = f(w_out[l]); shared[f"wq{l}"] = f(wq_mem[l]); shared[f"wkv{l}"] = f(wkv_mem[l])
        shared[f"wo{l}"] = f(wo_mem[l]); shared[f"wpq{l}"] = f(w_pq[l])
        shared[f"skT{l}"] = np.ascontiguousarray(f(sub_keys[l]).transpose(0, 2, 1))
        shared[f"lnp{l}"] = np.ascontiguousarray(np.stack([f(ln1_g[l]), f(ln1_b[l]), f(ln2_g[l]), f(ln2_b[l]), f(ln3_g[l]), f(ln3_b[l])]))
        shared[f"u{l}"] = f(u_tab[l]); shared[f"v{l}"] = f(v_tab[l])
    in_maps = []
    for b in range(B):
        xT = np.ascontiguousarray(x[b].T)
        memT = np.ascontiguousarray(mem[b].T)
        for r_ in range(4):
            m = dict(shared)
            m["xT0"] = xT
            m["x0"] = np.ascontiguousarray(x[b, r_ * 2048:(r_ + 1) * 2048])
            m["memT"] = memT
            p = np.arange(128, dtype=np.int64)[:, None, None]
            qr = np.arange(4, dtype=np.int64)[None, :, None]
            kc = np.arange(8, dtype=np.int64)[None, None, :]
            midx = ((((r_ * 4 + (kc % 4)) * 4 + qr) * 2 + (kc // 4)) * 128) + p
            m["midx"] = np.ascontiguousarray(midx.reshape(128, 32).astype(np.uint32))
            for l in range(depth):
                m.update(_head_inputs(f(w_in[l]), f(conv_w[l]), f(conv_b[l]), f(i_bias[l]), f(f_bias[l]), f(lam_qk[l]),
                                      f(da_norm_g[l]), f(ml_norm_g[l]), l, r_))
            in_maps.append(m)
    return in_maps


def kernel(x, mem, w_in, i_bias, f_bias, conv_w, conv_b, lam_qk, da_norm_g, ml_norm_g, w_out,
           ln1_g, ln1_b, wq_mem, wkv_mem, wo_mem, ln2_g, ln2_b, w_pq, sub_keys, u_tab, v_tab,
           ln3_g, ln3_b):
    if "f" not in _NC_CACHE:
        _NC_CACHE["f"] = build_fused(DEPTH)
    in_maps = make_in_maps(DEPTH, x, mem, w_in, i_bias, f_bias, conv_w, conv_b, lam_qk, da_norm_g, ml_norm_g, w_out,
                           ln1_g, ln1_b, wq_mem, wkv_mem, wo_mem, ln2_g, ln2_b, w_pq, sub_keys, u_tab, v_tab, ln3_g, ln3_b)
    res = run_bass_kernel_spmd(_NC_CACHE["f"], in_maps, core_ids=list(range(8))).results
    B, S_, D = np.asarray(x).shape
    out = np.empty((B, S_, D), np.float32)
    for b in range(B):
        for r_ in range(4):
            out[b, r_ * 2048:(r_ + 1) * 2048] = res[b * 4 + r_]["out"]
    return out
```

```python
import math
import contextlib
import numpy as np
import ml_dtypes
import concourse.bass as bass
import concourse.mybir as mybir
from concourse.bass_utils import run_bass_kernel_spmd

F32 = mybir.dt.float32
BF16 = mybir.dt.bfloat16
I32 = mybir.dt.int32
U32 = mybir.dt.uint32
AF = mybir.ActivationFunctionType
ALU = mybir.AluOpType
AX = mybir.AxisListType

SEM_LIMIT = 12000


class Res:
    __slots__ = ("w", "r", "name")

    def __init__(self, name=""):
        self.w = None
        self.r = []
        self.name = name


class Tile:
    def __init__(self, t, res=None, name=""):
        self.t = t
        self.res = res if res is not None else Res(name)

    def __getitem__(self, k):
        return self.t[k]


def _res(x):
    return x.res if isinstance(x, Tile) else x


class Prog:
    ENGS = ("pe", "act", "dve", "pool", "sp")

    def __init__(self, nc, n_dma_sems=6):
        self.nc = nc
        self.es = contextlib.ExitStack()
        self.cur_es = self.es
        self.nsem = 0
        self.streams = {e: [] for e in self.ENGS}
        self.cur_sem = {}
        self.cur_val = {}
        for e in self.ENGS:
            self._new_eng_sem(e)
        self.known = {e: {} for e in self.ENGS}
        self.dsem = {}
        self.nds = n_dma_sems
        self.dma_rr = {e: 0 for e in self.ENGS}
        self.ntile = 0
        self.last_tok = {}

    def sem(self, name):
        self.nsem += 1
        return self.es.enter_context(self.nc.semaphore(f"{name}_{self.nsem}"))

    def _new_eng_sem(self, e):
        self.nsem = getattr(self, "nsem", 0)
        self.cur_sem[e] = self.sem("c" + e)
        self.cur_val[e] = 0

    def sb(self, shape, dt, name=None):
        self.ntile += 1
        nm = f"{name or 't'}_{self.ntile}"
        t = self.cur_es.enter_context(self.nc.sbuf_tensor(nm, list(shape), dt))
        return Tile(t, name=nm)

    def ps(self, shape, dt=F32, name=None):
        self.ntile += 1
        nm = f"{name or 'p'}_{self.ntile}"
        t = self.cur_es.enter_context(self.nc.psum_tensor(nm, list(shape), dt))
        return Tile(t, name=nm)

    def _collect(self, eng, reads, writes, excl=()):
        toks = []
        for r in reads:
            r = _res(r)
            if r.w is not None:
                toks.append(r.w)
        for w in list(writes) + list(excl):
            w = _res(w)
            if w.w is not None:
                toks.append(w.w)
            toks.extend(w.r)
        waits = []
        kn = self.known[eng]
        best = {}
        for (s, v, src) in toks:
            if src == "pe" and eng == "pe":
                continue
            if kn.get(s, 0) >= v:
                continue
            if best.get(s, (None, 0))[1] < v:
                best[s] = (s, v)
        for s, (sh, v) in best.items():
            kn[s] = v
            waits.append((sh, v))
        return waits

    def _commit(self, tok, reads, writes, excl=()):
        for r in reads:
            _res(r).r.append(tok)
        for w in list(writes) + list(excl):
            w = _res(w)
            w.w = tok
            w.r = []

    def op(self, eng, fn, reads=(), writes=(), excl=()):
        waits = self._collect(eng, reads, writes, excl)
        if self.cur_val[eng] >= SEM_LIMIT:
            self._new_eng_sem(eng)
        s = self.cur_sem[eng]
        self.cur_val[eng] += 1
        v = self.cur_val[eng]
        tok = (s, v, eng)
        self.last_tok[eng] = tok
        self._commit(tok, reads, writes, excl)
        self.streams[eng].append((waits, fn, s, 1))
        return tok

    def dma(self, q, out, in_, reads=(), writes=(), fn=None, inc=16):
        key = q
        nds = self.nds[q] if isinstance(self.nds, dict) else self.nds
        if key not in self.dsem:
            self.dsem[key] = [[self.sem("d" + q), 0] for _ in range(nds)]
        k = self.dma_rr[q]
        self.dma_rr[q] = (k + 1) % nds
        slot = self.dsem[key][k]
        waits = self._collect(q, reads, writes)
        kn = self.known[q]
        if slot[1] > 0 and kn.get(slot[0], 0) < slot[1]:
            waits.append((slot[0], slot[1]))
            kn[slot[0]] = slot[1]
        if slot[1] + inc > SEM_LIMIT:
            slot[0] = self.sem("d" + q)
            slot[1] = 0
        slot[1] += inc
        tok = (slot[0], slot[1], "dma")
        self._commit(tok, reads, writes)
        if fn is None:
            fn = lambda e, o=out, i=in_: e.dma_start(out=o, in_=i)
        self.streams[q].append((waits, fn, slot[0], inc))
        return tok


    def cc(self, kind, groups, in_ap, out_ap, reads=(), writes=(), inc=16):
        fn = lambda e: e.collective_compute(kind, ALU.bypass, replica_groups=groups, ins=[in_ap], outs=[out_ap])
        self._cc_inc = inc
        return self.dma('pool', None, None, reads=reads, writes=writes, fn=fn, inc=inc)


    @contextlib.contextmanager
    def scope(self):
        prev = self.cur_es
        es = contextlib.ExitStack()
        self.cur_es = es
        try:
            yield
            self.barrier()
            self.emit()
        finally:
            self.cur_es = prev
            es.close()

    def barrier(self):
        toks = []
        for e in self.ENGS:
            if e in self.last_tok:
                toks.append(self.last_tok[e])
        for q, slots in self.dsem.items():
            for (sh, v) in slots:
                if v > 0:
                    toks.append((sh, v, "dma"))
        for e in self.ENGS:
            kn = self.known[e]
            waits = []
            for (sh, v, src) in toks:
                if src == e and e == "pe":
                    continue
                if kn.get(sh, 0) < v:
                    kn[sh] = v
                    waits.append((sh, v))
            if waits:
                self.streams[e].append((waits, None, None, 0))

    def finish(self, all_res):
        waits = self._collect("sp", all_res, [])
        kn = self.known["sp"]
        for q, slots in self.dsem.items():
            for (sh, v) in slots:
                if v > 0 and kn.get(sh, 0) < v:
                    waits.append((sh, v))
                    kn[sh] = v
        self.streams["sp"].append((waits, None, None, 0))

    def emit(self):
        nc = self.nc
        streams = self.streams

        def run(engname, eng):
            for (waits, fn, s, inc) in streams[engname]:
                for (sh, v) in waits:
                    eng.wait_ge(sh, v)
                if fn is not None:
                    ins = fn(eng)
                    ins.then_inc(s, inc)

        with nc.Block() as block:
            @block.tensor
            def _(e):
                run("pe", e)

            @block.scalar
            def _(e):
                run("act", e)

            @block.vector
            def _(e):
                run("dve", e)

            @block.gpsimd
            def _(e):
                run("pool", e)

            @block.sync
            def _(e):
                run("sp", e)
        self.streams = {e: [] for e in self.ENGS}

    def close(self):
        self.es.close()


LN_EPS = 1e-5
ALPHA = 8.0 ** 0.25
S = 8192
NWA = 898
NTOK = 2048


def AP(t, off, dims):
    return bass.AP(t.t if isinstance(t, Tile) else t, off, dims)


def make_consts(P):
    C = {}
    identf = P.sb([128, 128], F32, "identf")
    ident = P.sb([128, 128], BF16, "ident")
    trif = P.sb([128, 128], F32, "trif")
    trib = P.sb([128, 128], BF16, "trib")
    onesf = P.sb([128, 128], F32, "onesf")
    ones_bf = P.sb([128, 128], BF16, "ones")
    iota16 = P.sb([128, 16], F32, "iota16")
    thr16 = P.sb([128, 16], F32, "thr16")
    P.op('dve', lambda e: e.memset(identf[:], 0.0), writes=[identf])
    P.op('pool', lambda e: e.affine_select(out=identf[:], in_=identf[:], pattern=[[-1, 128]],
                                           compare_op=ALU.not_equal, fill=1.0, base=0, channel_multiplier=1),
         reads=[identf], writes=[identf])
    P.op('dve', lambda e: e.tensor_copy(ident[:], identf[:]), reads=[identf], writes=[ident])
    P.op('dve', lambda e: e.memset(onesf[:], 1.0), writes=[onesf])
    P.op('dve', lambda e: e.memset(ones_bf[:], 1.0), writes=[ones_bf])
    P.op('pool', lambda e: e.iota(trif[:], pattern=[[1, 128]], base=0, channel_multiplier=-1,
                                  allow_small_or_imprecise_dtypes=True), writes=[trif])
    P.op('dve', lambda e: e.tensor_single_scalar(trif[:], trif[:], 0.0, op=ALU.is_ge), reads=[trif], writes=[trif])
    P.op('dve', lambda e: e.tensor_copy(trib[:], trif[:]), reads=[trif], writes=[trib])
    P.op('pool', lambda e: e.iota(iota16[:], pattern=[[1, 16]], base=0, channel_multiplier=0,
                                  allow_small_or_imprecise_dtypes=True), writes=[iota16])
    P.op('pool', lambda e: e.iota(thr16[:], pattern=[[16, 16]], base=16, channel_multiplier=0,
                                  allow_small_or_imprecise_dtypes=True), writes=[thr16])
    C.update(identf=identf, ident=ident, trif=trif, trib=trib, onesf=onesf, ones_bf=ones_bf, iota16=iota16, thr16=thr16)
    return C


def phase_a(P, nc, C, H, xT_h, xg_h, xg_res, mixsrc_h, mixsrc_res, nq_tiles):
    ident, trif, trib, onesf = C['ident'], C['trif'], C['trib'], C['onesf']
    D = 1024
    ntok = nq_tiles * 256
    nblk = ntok // 128
    spt = P.sb([128, 16], F32, "spt")
    lamq = P.sb([128, 256], F32, "lamq")
    gda = P.sb([128, 128], F32, "gda")
    gml = P.sb([128, 128], F32, "gml")
    P.dma('sp', spt[:], H['sp'].ap(), writes=[spt])
    P.dma('sp', lamq[:], H['lamqk'].ap(), writes=[lamq])
    P.dma('sp', gda[:], H['gda'].ap(), writes=[gda])
    P.dma('sp', gml[:], H['gml'].ap(), writes=[gml])
    lt = P.sb([128, 128], F32, "lt")
    lsum = P.sb([128, 2], F32, "lsum")
    neglam = P.sb([128, 1], F32, "neglam")
    P.op('dve', lambda e: e.tensor_tensor(lt[:].rearrange("p (a d) -> p a d", a=2),
                                          AP(lamq, 0, [[256, 128], [128, 2], [1, 64]]),
                                          AP(lamq, 64, [[256, 128], [128, 2], [1, 64]]), op=ALU.mult),
         reads=[lamq], writes=[lt])
    P.op('dve', lambda e: e.tensor_reduce(lsum[:], lt[:].rearrange("p (a d) -> p a d", a=2), axis=AX.X, op=ALU.add),
         reads=[lt], writes=[lsum])
    P.op('act', lambda e: e.activation(lsum[:], lsum[:], AF.Exp), reads=[lsum], writes=[lsum])
    P.op('dve', lambda e: e.tensor_tensor(neglam[:], lsum[:, 1:2], lsum[:, 0:1], op=ALU.subtract), reads=[lsum], writes=[neglam])
    P.op('dve', lambda e: e.tensor_tensor(neglam[:], neglam[:], spt[:, 12:13], op=ALU.subtract), reads=[neglam, spt], writes=[neglam])
    P.op('dve', lambda e: e.tensor_scalar(gda[:], gda[:], spt[:, 13:14], None, op0=ALU.mult), reads=[gda, spt], writes=[gda])

    wab = P.sb([128, 8, NWA], BF16, "wab")
    stg = P.sb([128, 1024], F32, "stg")
    wap = H['wa'].ap().rearrange("(c p) n -> p c n", p=128)
    for kc in range(8):
        P.dma('sp', stg[:, 0:NWA], wap[:, kc, :], writes=[stg])
        P.op('dve', lambda e, kc=kc: e.tensor_copy(wab[:, kc, :], stg[:, 0:NWA]), reads=[stg], writes=[wab])

    daqT = [P.sb([64, S], BF16, f"daqT{c}") for c in range(2)]
    dakT = [P.sb([64, S], BF16, f"dakT{c}") for c in range(2)]
    Vaug = P.sb([128, 64, 129], BF16, "Vaug")
    mlqT = P.sb([128, S], BF16, "mlqT")
    mlkT = P.sb([128, S], BF16, "mlkT")
    mlV = P.sb([128, 64, 129], BF16, "mlV")
    sigo = P.sb([128, 64, 128], BF16, "sigo")
    gates = P.sb([128, 64, 2], F32, "gates")
    P.op('dve', lambda e: e.memset(Vaug[:, :, 128:129], 1.0), writes=[Vaug])
    P.op('dve', lambda e: e.memset(mlV[:, :, 128:129], 1.0), writes=[mlV])

    banks = [P.ps([128, 512], F32, f"bank{i}") for i in range(8)]

    xs = P.sb([128, 8, 256], F32, "xs")
    xb = P.sb([128, 8, 256], BF16, "xb")
    pre = [P.sb([128, 3 + 256], F32, f"pre{i}") for i in range(2)]
    cacc = P.sb([128, 256], F32, "cacc")
    ctmp = P.sb([128, 256], F32, "ctmp")
    xT_v = xT_h.ap().rearrange("(c p) t -> p c t", p=128) if xg_h is None else None
    for i in range(2):
        P.op('dve', lambda e, i=i: e.memset(pre[i][:, 0:3], 0.0), writes=[pre[i]])
    for ti in range(nq_tiles):
        t0 = ti * 256
        if xg_h is None:
            P.dma('sp', xs[:], xT_v[:, :, t0:t0 + 256], writes=[xs])
            P.op('pool', lambda e: e.tensor_copy(xb[:], xs[:]), reads=[xs], writes=[xb])
        else:
            jr, tl = t0 // 2048, t0 % 2048
            for c4 in range(4):
                P.dma('sp', xb[:, 2 * c4:2 * c4 + 2, :],
                      xg_h[c4].ap()[jr * 256:(jr + 1) * 256, tl:tl + 256].rearrange("(k p) t -> p k t", p=128),
                      reads=[xg_res], writes=[xb])
        for g in range(4):
            pb = banks[g // 2]
            col0 = (g // 2) * 128 + (g % 2) * 64
            for kc in range(8):
                P.op('pe', lambda e, g=g, kc=kc, pb=pb, col0=col0: e.matmul(
                    pb[0:64, (g % 2) * 256:(g % 2) * 256 + 256], wab[:, kc, col0:col0 + 64], xb[:, kc, :],
                    start=(kc == 0), stop=(kc == 7)), reads=[wab, xb], writes=[pb])
        for c in range(2):
            P.op('act', lambda e, t0=t0, c=c: e.copy(daqT[c][:, t0:t0 + 256], banks[0][0:64, c * 256:c * 256 + 256]),
                 excl=[banks[0]], writes=[daqT[c]])
            P.op('dve', lambda e, t0=t0, c=c: e.tensor_copy(dakT[c][:, t0:t0 + 256], banks[1][0:64, c * 256:c * 256 + 256]),
                 excl=[banks[1]], writes=[dakT[c]])
        for g in range(2):
            col0 = 384 + g * 128
            for kc in range(8):
                P.op('pe', lambda e, g=g, kc=kc, col0=col0: e.matmul(
                    banks[2][:, g * 256:g * 256 + 256], wab[:, kc, col0:col0 + 128], xb[:, kc, :],
                    start=(kc == 0), stop=(kc == 7)), reads=[wab, xb], writes=[banks[2]])
        for g in range(2):
            pr = pre[g]
            if ti > 0:
                P.op('dve', lambda e, pr=pr: e.tensor_copy(pr[:, 0:3], pr[:, 256:259]), reads=[pr], writes=[pr])
            P.op('act', lambda e, g=g, pr=pr: e.copy(pr[:, 3:259], banks[2][:, g * 256:g * 256 + 256]),
                 excl=[banks[2]], writes=[pr])
            P.op('dve', lambda e, g=g, pr=pr: e.tensor_scalar(cacc[:], pr[:, 3:259], spt[:, g * 4 + 3:g * 4 + 4], spt[:, 8 + g:9 + g],
                                                              op0=ALU.mult, op1=ALU.add), reads=[pr, spt], writes=[cacc])
            for j in range(3):
                P.op('dve', lambda e, g=g, pr=pr, j=j: e.scalar_tensor_tensor(
                    out=cacc[:], in0=pr[:, j:j + 256], scalar=spt[:, g * 4 + j:g * 4 + j + 1], in1=cacc[:],
                    op0=ALU.mult, op1=ALU.add), reads=[pr, spt, cacc], writes=[cacc])
            if g == 0:
                P.op('act', lambda e, t0=t0: e.activation(mlqT[:, t0:t0 + 256], cacc[:], AF.Silu), reads=[cacc], writes=[mlqT])
            else:
                P.op('act', lambda e: e.activation(ctmp[:], cacc[:], AF.Silu), reads=[cacc], writes=[ctmp])
                P.op('dve', lambda e, t0=t0: e.tensor_scalar(mlkT[:, t0:t0 + 256], ctmp[:], 128.0 ** -0.5, None, op0=ALU.mult),
                     reads=[ctmp], writes=[mlkT])
        for sub in range(2):
            blk = ti * 2 + sub
            pb = banks[3 + sub]
            for gi, col0 in enumerate((256, 640, 768)):
                for kc in range(8):
                    P.op('pe', lambda e, sub=sub, gi=gi, col0=col0, kc=kc, pb=pb: e.matmul(
                        pb[:, gi * 128:(gi + 1) * 128], xb[:, kc, sub * 128:(sub + 1) * 128], wab[:, kc, col0:col0 + 128],
                        start=(kc == 0), stop=(kc == 7)), reads=[wab, xb], writes=[pb])
            for kc in range(8):
                P.op('pe', lambda e, sub=sub, kc=kc, blk=blk: e.matmul(
                    banks[5][:, blk * 2:blk * 2 + 2], xb[:, kc, sub * 128:(sub + 1) * 128], wab[:, kc, 896:898],
                    start=(kc == 0), stop=(kc == 7)), reads=[wab, xb], writes=[banks[5]])
            P.op('act', lambda e, blk=blk, pb=pb: e.copy(Vaug[:, blk, 0:128], pb[:, 0:128]), excl=[pb], writes=[Vaug])
            P.op('dve', lambda e, blk=blk, pb=pb: e.tensor_copy(mlV[:, blk, 0:128], pb[:, 128:256]), excl=[pb], writes=[mlV])
            P.op('act', lambda e, blk=blk, pb=pb: e.activation(sigo[:, blk, :], pb[:, 256:384], AF.Sigmoid), excl=[pb], writes=[sigo])
    P.op('dve', lambda e: e.tensor_copy(gates[:, 0:nblk, :], banks[5][:, 0:nblk * 2].rearrange("p (b g) -> p b g", g=2)),
         excl=[banks[5]], writes=[gates])

    Eb = [P.sb([128, 2, 256], BF16, f"Eb{i}") for i in range(2)]
    o_da = P.sb([128, 128], F32, "o_da")
    o_out = [P.sb([128, 128], BF16, f"o_out{i}") for i in range(2)]
    oT_sb = [P.sb([128, 128], BF16, f"oT_sb{i}") for i in range(2)]
    pT6 = banks[6].t.bitcast(BF16)
    pT7 = banks[7].t.bitcast(BF16)
    junk = P.sb([128, 128], F32, "junk")
    rz = P.sb([128, 2], F32, "rz")
    ss = P.sb([128, 1], F32, "ss")
    nout_box = [0]
    steps = [(qt, j) for qt in range(nq_tiles) for j in range(2 * qt + 2)]
    cvf = [P.sb([128, 1024], F32, f"cvf{i}") for i in range(2)]
    cvb = [P.sb([128, 1024], BF16, f"cvb{i}") for i in range(2)]
    NCV = 256
    cv_next = [0]

    def cv_src(n):
        tab = H['u'] if n < 128 else H['v']
        row0 = (n % 128) * 128
        return tab.ap()[row0:row0 + 128, :]

    def cv_step():
        n = cv_next[0]
        if n > NCV:
            return
        cv_next[0] += 1
        if n < NCV:
            P.dma('pool', cvf[n % 2][:], cv_src(n), writes=[cvf[n % 2]])
        m = n - 1
        if m >= 0:
            P.op('pool', lambda e, m=m: e.tensor_copy(cvb[m % 2][:], cvf[m % 2][:]), reads=[cvf[m % 2]], writes=[cvb[m % 2]])
            row0 = (m % 128) * 128
            tsel = 0 if m < 128 else 1
            P.dma('pool', H['uvb'].ap()[row0:row0 + 128, tsel * 1024:(tsel + 1) * 1024], cvb[m % 2][:], reads=[cvb[m % 2]],
                  writes=[Res("cvout")])

    def da_front(k):
        qt, j = steps[k]
        q0 = qt * 256
        nj = 2 * qt + 2
        pS = banks[k % 2]
        E = Eb[k % 2]
        qa, qb = (0, 256) if j < nj - 1 else (128, 256)
        for c in range(2):
            P.op('pe', lambda e, c=c, j=j, pS=pS, qa=qa, qb=qb, q0=q0: e.matmul(
                pS[:, c * 256 + qa:c * 256 + qb], dakT[c][:, j * 128:(j + 1) * 128], daqT[c][:, q0 + qa:q0 + qb],
                start=True, stop=True), reads=[dakT[c], daqT[c]], writes=[pS])
        P.op('act', lambda e, E=E, pS=pS, qa=qa, qb=qb: e.activation(
            E[:, :, qa:qb], pS[:, :].rearrange("p (c q) -> p c q", c=2)[:, :, qa:qb], AF.Exp, scale=0.125),
             excl=[pS], writes=[E])
        if j >= nj - 2:
            sm = 0 if j == nj - 2 else 1
            P.op('pool', lambda e, E=E, sm=sm: e.tensor_tensor(
                E[:, :, sm * 128:(sm + 1) * 128], E[:, :, sm * 128:(sm + 1) * 128],
                AP(trib, 0, [[128, 128], [0, 2], [1, 128]]), op=ALU.mult), reads=[E, trib], writes=[E])

    def da_back(k):
        qt, j = steps[k]
        q0 = qt * 256
        nj = 2 * qt + 2
        E = Eb[k % 2]
        subs = (0, 1) if j < nj - 1 else (1,)
        for s_ in subs:
            last_j = nj - 2 if s_ == 0 else nj - 1
            for c in range(2):
                pacc = banks[2 + s_ * 2 + c]
                P.op('pe', lambda e, E=E, s_=s_, c=c, j=j, pacc=pacc, last_j=last_j: e.matmul(
                    pacc[:, 0:129], E[:, c, s_ * 128:(s_ + 1) * 128], Vaug[:, j, :],
                    start=(j == 0), stop=(j == last_j)), reads=[E, Vaug], writes=[pacc])
        if j == nj - 1:
            nout = nout_box[0]
            for s_ in range(2):
                p0, p1 = banks[2 + s_ * 2], banks[2 + s_ * 2 + 1]
                oo = o_out[nout % 2]
                nout += 1
                nout_box[0] = nout
                P.op('dve', lambda e, p0=p0: e.reciprocal(rz[:, 0:1], p0[:, 128:129]), excl=[p0], writes=[rz])
                P.op('dve', lambda e, p1=p1: e.reciprocal(rz[:, 1:2], p1[:, 128:129]), excl=[p1], writes=[rz])
                P.op('dve', lambda e: e.tensor_tensor(rz[:, 1:2], rz[:, 1:2], neglam[:], op=ALU.mult), reads=[rz, neglam], writes=[rz])
                P.op('dve', lambda e, p0=p0: e.tensor_scalar(o_da[:], p0[:, 0:128], rz[:, 0:1], None, op0=ALU.mult),
                     reads=[rz], excl=[p0], writes=[o_da])
                P.op('dve', lambda e, p1=p1: e.scalar_tensor_tensor(out=o_da[:], in0=p1[:, 0:128], scalar=rz[:, 1:2], in1=o_da[:],
                                                                    op0=ALU.mult, op1=ALU.add), reads=[rz, o_da], excl=[p1], writes=[o_da])
                P.op('dve', lambda e: e.scalar_tensor_tensor(out=junk[:], in0=o_da[:], scalar=1.0, in1=o_da[:], op0=ALU.mult, op1=ALU.mult,
                                                             accum_out=ss[:]), reads=[o_da], writes=[junk, ss])
                P.op('dve', lambda e: e.tensor_scalar(ss[:], ss[:], 1.0 / 128.0, LN_EPS, op0=ALU.mult, op1=ALU.add), reads=[ss], writes=[ss])
                P.op('act', lambda e: e.activation(ss[:], ss[:], AF.Sqrt), reads=[ss], writes=[ss])
                P.op('dve', lambda e: e.reciprocal(ss[:], ss[:]), reads=[ss], writes=[ss])
                P.op('dve', lambda e, oo=oo: e.scalar_tensor_tensor(out=oo[:], in0=o_da[:], scalar=ss[:, 0:1], in1=gda[:], op0=ALU.mult, op1=ALU.mult),
                     reads=[o_da, ss, gda], writes=[oo])
                g_blk = (q0 + s_ * 128) // 128
                oT_ = oT_sb[nout % 2]
                P.op('pe', lambda e, oo=oo: e.transpose(bass.AP(pT6, 0, [[1024, 128], [1, 128]]), oo[:], ident[:]),
                     reads=[oo, ident], writes=[banks[6]])
                P.op('act', lambda e, oT_=oT_: e.copy(oT_[:], bass.AP(pT6, 0, [[1024, 128], [1, 128]])), excl=[banks[6]], writes=[oT_])
                jj, qq, ww = g_blk // 16, (g_blk % 16) // 4, g_blk % 4
                P.dma('sp', mixsrc_h[jj].ap()[(qq * 2) * 128:(qq * 2 + 1) * 128, ww * 128:(ww + 1) * 128], oT_[:], reads=[oT_], writes=[mixsrc_res])


    da_front(0)
    for k in range(len(steps)):
        if k + 1 < len(steps):
            da_front(k + 1)
        da_back(k)
        if k % 4 == 0:
            cv_step()
    while cv_next[0] <= NCV:
        cv_step()

    nch = nblk
    lf = P.sb([128, 64], F32, "lf")
    bcs = P.sb([128, 64], F32, "bcs")
    ek = P.sb([128, 64], F32, "ek")
    eb = P.sb([128, 64], F32, "eb")
    ebL = P.sb([128, 64], F32, "ebL")
    nfb = P.sb([128, 1], F32, "nfb")
    P.op('dve', lambda e: e.tensor_scalar(nfb[:], spt[:, 11:12], -1.0, None, op0=ALU.mult), reads=[spt], writes=[nfb])
    P.op('act', lambda e: e.activation(lf[:, 0:nch], gates[:, 0:nch, 1], AF.Exp, bias=nfb[:, 0:1], scale=-1.0), reads=[gates, nfb], writes=[lf])
    P.op('dve', lambda e: e.tensor_scalar(lf[:, 0:nch], lf[:, 0:nch], 1.0, None, op0=ALU.add), reads=[lf], writes=[lf])
    P.op('act', lambda e: e.activation(lf[:, 0:nch], lf[:, 0:nch], AF.Ln), reads=[lf], writes=[lf])
    P.op('dve', lambda e: e.tensor_scalar(lf[:, 0:nch], lf[:, 0:nch], -1.0, None, op0=ALU.mult), reads=[lf], writes=[lf])
    P.op('pe', lambda e: e.matmul(banks[0][:, 0:nch], trif[:], lf[:, 0:nch], start=True, stop=True), reads=[trif, lf], writes=[banks[0]])
    P.op('pe', lambda e: e.matmul(banks[1][:, 0:nch], onesf[:], lf[:, 0:nch], start=True, stop=True), reads=[onesf, lf], writes=[banks[1]])
    P.op('dve', lambda e: e.tensor_copy(bcs[:, 0:nch], banks[0][:, 0:nch]), excl=[banks[0]], writes=[bcs])
    P.op('act', lambda e: e.activation(eb[:, 0:nch], bcs[:, 0:nch], AF.Exp), reads=[bcs], writes=[eb])
    P.op('act', lambda e: e.activation(ebL[:, 0:nch], banks[1][:, 0:nch], AF.Exp), excl=[banks[1]], writes=[ebL])
    P.op('dve', lambda e: e.tensor_tensor(ek[:, 0:nch], gates[:, 0:nch, 0], bcs[:, 0:nch], op=ALU.subtract), reads=[gates, bcs], writes=[ek])
    P.op('act', lambda e: e.activation(ek[:, 0:nch], ek[:, 0:nch], AF.Exp, bias=spt[:, 10:11], scale=1.0), reads=[ek, spt], writes=[ek])

    Dst = [P.sb([128, 129], F32, f"Dst{i}") for i in range(2)]
    Cb = [P.sb([128, 129], BF16, f"Cb{i}") for i in range(2)]
    ktok = [P.sb([128, 128], BF16, f"ktok{i}") for i in range(2)]
    vp = [P.sb([128, 129], BF16, f"vp{i}") for i in range(2)]
    ATb = [P.sb([128, 128], BF16, f"ATb{i}") for i in range(2)]
    hh = P.sb([128, 128], F32, "hh")
    hsm = P.sb([128, 4], F32, "hsm")
    st6 = P.sb([128, 6], F32, "st6")
    mvv = P.sb([128, 2], F32, "mvv")
    rstd = P.sb([128, 1], F32, "rstd")
    ho = [P.sb([128, 128], BF16, f"ho{i}") for i in range(2)]
    pKt = banks[2].t.bitcast(BF16)
    for c in range(nch):
        sl = slice(c * 128, (c + 1) * 128)
        kt, v_, at = ktok[c % 2], vp[c % 2], ATb[c % 2]
        P.op('pe', lambda e, sl=sl: e.transpose(bass.AP(pKt, 0, [[1024, 128], [1, 128]]), mlkT[:, sl], ident[:]),
             reads=[mlkT, ident], writes=[banks[2]])
        P.op('act', lambda e, kt=kt: e.copy(kt[:], bass.AP(pKt, 0, [[1024, 128], [1, 128]])), excl=[banks[2]], writes=[kt])
        P.op('pool', lambda e, c=c, v_=v_: e.tensor_scalar(v_[:], mlV[:, c, :], ek[:, c:c + 1], None, op0=ALU.mult),
             reads=[mlV, ek], writes=[v_])
        P.op('pe', lambda e, kt=kt, v_=v_: e.matmul(banks[3][:, 0:129], kt[:], v_[:], start=True, stop=True),
             reads=[kt, v_], writes=[banks[3]])
        P.op('pe', lambda e, sl=sl: e.matmul(banks[4][:, 0:128], mlkT[:, sl], mlqT[:, sl], start=True, stop=True),
             reads=[mlkT, mlqT], writes=[banks[4]])
        P.op('dve', lambda e, c=c, at=at: e.scalar_tensor_tensor(out=at[:], in0=banks[4][:, 0:128], scalar=ek[:, c:c + 1], in1=trif[:],
                                                                 op0=ALU.mult, op1=ALU.mult), reads=[ek, trif], excl=[banks[4]], writes=[at])
        pH = banks[5 + c % 2]
        if c > 0:
            P.op('pe', lambda e, sl=sl, c=c, pH=pH: e.matmul(pH[:, 0:129], mlqT[:, sl], Cb[(c - 1) % 2][:], start=True, stop=False),
                 reads=[mlqT, Cb[(c - 1) % 2]], writes=[pH])
        P.op('pe', lambda e, at=at, c=c, pH=pH: e.matmul(pH[:, 0:129], at[:], mlV[:, c, :], start=(c == 0), stop=True),
             reads=[at, mlV], writes=[pH])
        Dc = Dst[c % 2]
        if c == 0:
            P.op('dve', lambda e, Dc=Dc: e.tensor_copy(Dc[:], banks[3][:, 0:129]), excl=[banks[3]], writes=[Dc])
        else:
            Dp = Dst[(c - 1) % 2]
            P.op('dve', lambda e, Dc=Dc, Dp=Dp, c=c: e.scalar_tensor_tensor(out=Dc[:], in0=Dp[:], scalar=ebL[:, c - 1:c], in1=banks[3][:, 0:129],
                                                                          op0=ALU.mult, op1=ALU.add), reads=[Dp, ebL], excl=[banks[3]], writes=[Dc])
        P.op('act', lambda e, Dc=Dc, c=c: e.activation(Cb[c % 2][:], Dc[:], AF.Identity, scale=ebL[:, c:c + 1]), reads=[Dc, ebL], writes=[Cb[c % 2]])
        P.op('dve', lambda e, c=c, pH=pH: e.tensor_tensor(hsm[:, 0:1], pH[:, 128:129], eb[:, c:c + 1], op=ALU.mult), reads=[eb], excl=[pH], writes=[hsm])
        P.op('dve', lambda e: e.tensor_scalar(hsm[:, 1:2], hsm[:, 0:1], -1.0, 1.0, op0=ALU.mult, op1=ALU.max), reads=[hsm], writes=[hsm])
        P.op('dve', lambda e: e.tensor_scalar(hsm[:, 2:3], hsm[:, 0:1], 1.0, None, op0=ALU.max), reads=[hsm], writes=[hsm])
        P.op('dve', lambda e: e.tensor_tensor(hsm[:, 1:2], hsm[:, 1:2], hsm[:, 2:3], op=ALU.max), reads=[hsm], writes=[hsm])
        P.op('dve', lambda e: e.reciprocal(hsm[:, 2:3], hsm[:, 1:2]), reads=[hsm], writes=[hsm])
        P.op('dve', lambda e, c=c: e.tensor_tensor(hsm[:, 3:4], hsm[:, 2:3], eb[:, c:c + 1], op=ALU.mult), reads=[hsm, eb], writes=[hsm])
        P.op('dve', lambda e, pH=pH: e.tensor_scalar(hh[:], pH[:, 0:128], hsm[:, 3:4], None, op0=ALU.mult), reads=[hsm], excl=[pH], writes=[hh])
        P.op('dve', lambda e: e.bn_stats(st6[:], hh[:]), reads=[hh], writes=[st6])
        P.op('dve', lambda e: e.bn_aggr(mvv[:], st6[:]), reads=[st6], writes=[mvv])
        P.op('dve', lambda e: e.tensor_scalar(rstd[:], mvv[:, 1:2], LN_EPS, None, op0=ALU.add), reads=[mvv], writes=[rstd])
        P.op('act', lambda e: e.activation(rstd[:], rstd[:], AF.Sqrt), reads=[rstd], writes=[rstd])
        P.op('dve', lambda e: e.reciprocal(rstd[:], rstd[:]), reads=[rstd], writes=[rstd])
        P.op('dve', lambda e: e.tensor_scalar(hh[:], hh[:], mvv[:, 0:1], rstd[:, 0:1], op0=ALU.subtract, op1=ALU.mult),
             reads=[hh, mvv, rstd], writes=[hh])
        P.op('pool', lambda e: e.tensor_tensor(hh[:], hh[:], gml[:], op=ALU.mult), reads=[hh, gml], writes=[hh])
        hoo = ho[c % 2]
        P.op('pool', lambda e, c=c, hoo=hoo: e.tensor_tensor(hoo[:], hh[:], sigo[:, c, :], op=ALU.mult), reads=[hh, sigo], writes=[hoo])
        hT_ = oT_sb[c % 2]
        P.op('pe', lambda e, hoo=hoo: e.transpose(bass.AP(pT7, 0, [[1024, 128], [1, 128]]), hoo[:], ident[:]),
             reads=[hoo, ident], writes=[banks[7]])
        P.op('act', lambda e, hT_=hT_: e.copy(hT_[:], bass.AP(pT7, 0, [[1024, 128], [1, 128]])), excl=[banks[7]], writes=[hT_])
        jj, qq, ww = c // 16, (c % 16) // 4, c % 4
        P.dma('sp', mixsrc_h[jj].ap()[(qq * 2 + 1) * 128:(qq * 2 + 2) * 128, ww * 128:(ww + 1) * 128], hT_[:], reads=[hT_], writes=[mixsrc_res])


def phase_b(P, nc, C, H, xin_h, xin_res, mixg_h, mixg_res, midx_h, xout_h, xout_res, xsrc_h, xsrc_res, ntiles):
    ident, ones_bf, iota16, thr16 = C['ident'], C['ones_bf'], C['iota16'], C['thr16']
    D = 1024
    lnp = P.sb([128, 6 * D], F32, "lnp")
    P.dma('sp', lnp[:], bass.AP(H['lnp'], 0, [[0, 128], [1, 6 * D]]), writes=[lnp])

    r = P.sb([128, D], F32, "r")
    xt = P.sb([128, D], F32, "xt")
    x1 = P.sb([128, D], F32, "x1")
    x2 = P.sb([128, D], F32, "x2")
    stg = [r, xt, x1, x2]
    stg_i = [0]

    def load_w(handle, n, name, dst=None):
        wb = dst if dst is not None else P.sb([128, 8, n], BF16, name)
        wap = handle.ap().rearrange("(c p) n -> p c n", p=128)
        for kc in range(8):
            for pc in range(n // 1024):
                s = stg[stg_i[0] % 4]
                stg_i[0] += 1
                P.dma('sp', s[:, :], wap[:, kc, pc * 1024:(pc + 1) * 1024], writes=[s])
                if stg_i[0] % 2 == 0:
                    P.op('act', lambda e, s=s, kc=kc, pc=pc: e.copy(wb[:, kc, pc * 1024:(pc + 1) * 1024], s[:, :]), reads=[s], writes=[wb])
                else:
                    P.op('dve', lambda e, s=s, kc=kc, pc=pc: e.tensor_copy(wb[:, kc, pc * 1024:(pc + 1) * 1024], s[:, :]), reads=[s], writes=[wb])
        return wb

    wout_b = load_w(H['wout'], 1024, "wout")
    wq_b = load_w(H['wq'], 1024, "wq")
    wo_b = load_w(H['wo'], 1024, "wo")
    wpq_b = load_w(H['wpq'], 2048, "wpq")

    pA = P.ps([128, 2048], F32, "pA")
    pB = P.ps([128, 1024], F32, "pB")
    pC = P.ps([128, 512], F32, "pC")
    pT = P.ps([128, 8, 128], BF16, "pT")

    KT_b = P.sb([128, 8, 256], BF16, "KT")
    V_b = P.sb([128, 2, 1024], BF16, "V")
    skT_b = P.sb([128, 2, 128], BF16, "skT")
    NROW = 4
    gbuf = P.sb([128, 2 * NROW, 1024], F32, "gbuf")
    gb_bf = gbuf.t.bitcast(BF16)

    def wkv_ap(kc, c0, c1):
        return bass.AP(gb_bf, kc * 2048 + c0, [[2 * NROW * 1024 * 2, 128], [1, c1 - c0]])

    wkvap = H['wkv'].ap().rearrange("(c p) n -> p c n", p=128)
    for kc in range(8):
        for pc in range(2):
            s = stg[stg_i[0] % 4]
            stg_i[0] += 1
            P.dma('sp', s[:, :], wkvap[:, kc, pc * 1024:(pc + 1) * 1024], writes=[s])
            P.op('dve', lambda e, s=s, kc=kc, pc=pc: e.tensor_copy(wkv_ap(kc, pc * 1024, (pc + 1) * 1024), s[:, :]), reads=[s], writes=[gbuf])
    sc = P.sb([128, 16, 128], F32, "sc")
    w8bf = sc.t.bitcast(BF16)
    memT_b = sc

    def memT_ap(kc, m0, m1):
        return bass.AP(w8bf, kc * 256 + m0, [[4096, 128], [1, m1 - m0]])

    def pqT_ap(hc):
        return bass.AP(w8bf, hc * 128, [[4096, 128], [1, 128]])
    mT_ap = H['memT'].ap().rearrange("(c p) n -> p c n", p=128)
    for kc in range(8):
        s = stg[stg_i[0] % 4]
        stg_i[0] += 1
        P.dma('sp', s[:, 0:256], mT_ap[:, kc, :], writes=[s])
        P.op('dve', lambda e, s=s, kc=kc: e.tensor_copy(memT_ap(kc, 0, 256), s[:, 0:256]), reads=[s], writes=[memT_b])
    s = stg[stg_i[0] % 4]
    stg_i[0] += 1
    P.dma('sp', s[:, 0:256].rearrange("p (c n) -> p c n", c=2), H['skT'].ap().rearrange("c d n -> d c n"), writes=[s])
    P.op('dve', lambda e, s=s: e.tensor_copy(skT_b[:], s[:, 0:256].rearrange("p (c n) -> p c n", c=2)), reads=[s], writes=[skT_b])

    for j in range(8):
        for kc in range(8):
            P.op('pe', lambda e, j=j, kc=kc: e.matmul(pB[:, 0:256], wkv_ap(kc, j * 128, (j + 1) * 128), memT_ap(kc, 0, 256),
                                                      start=(kc == 0), stop=(kc == 7)),
                 reads=[gbuf, memT_b], writes=[pB])
        P.op('dve', lambda e, j=j: e.tensor_copy(KT_b[:, j, :], pB[:, 0:256]), excl=[pB], writes=[KT_b])
    for mc in range(2):
        for half in range(2):
            for kc in range(8):
                P.op('pe', lambda e, mc=mc, half=half, kc=kc: e.matmul(
                    pB[:, 0:512], memT_ap(kc, mc * 128, (mc + 1) * 128),
                    wkv_ap(kc, 1024 + half * 512, 1024 + (half + 1) * 512), start=(kc == 0), stop=(kc == 7)),
                     reads=[gbuf, memT_b], writes=[pB])
            P.op('dve', lambda e, mc=mc, half=half: e.tensor_copy(V_b[:, mc, half * 512:(half + 1) * 512], pB[:, 0:512]),
                 excl=[pB], writes=[V_b])

    mixb = P.sb([128, 8, 512], BF16, "mixb")
    xb = P.sb([128, D], BF16, "xb")
    xT = P.sb([128, 8, 128], BF16, "xT")
    qT = P.sb([128, 8, 128], BF16, "qT")
    E_b = P.sb([128, 8, 128], BF16, "E")
    rz = P.sb([128, 4, 128], F32, "rz")
    oT = qT
    top = P.sb([128, 16, 16], F32, "top")
    idxu = P.sb([128, 16, 16], U32, "idxu")
    idxf = P.sb([128, 16, 16], F32, "idxf")
    cs = P.sb([128, 8, 16], F32, "cs")
    ciu = P.sb([128, 8, 16], U32, "ciu")
    cif = P.sb([128, 128], F32, "cif")
    big = sc
    big2 = sc
    pqT = sc
    mixf = r
    x3 = r
    junk = r
    cand = sc
    k0f = P.sb([128, 128], F32, "k0f")
    k1f = P.sb([128, 128], F32, "k1f")
    i0s = P.sb([128, 128], F32, "i0s")
    i1s = P.sb([128, 128], F32, "i1s")
    ef = P.sb([128, 128], F32, "ef")
    eu = P.sb([128, 128], U32, "eu")
    gex = P.sb([128, 8, 16], F32, "gex")
    gz = P.sb([128, 8], F32, "gz")
    aact = P.sb([128, 128], F32, "aact")
    wgt = P.sb([128, 128], F32, "wgt")
    acc = x1
    st = P.sb([128, 12], F32, "st")
    mv = P.sb([128, 2], F32, "mv")
    rstd = P.sb([128, 1], F32, "rstd")
    grow = [Res(f"grow{i}") for i in range(16)]
    dgs = [P.sb([128, 128], BF16, f"dg{i}") for i in range(4)]
    gel = P.sb([128, 128], F32, "gel")
    acol = [Res() for _ in range(128)]
    gcol = [Res() for _ in range(128)]
    wcol = [Res() for _ in range(128)]

    def layernorm(src, dst, li):
        for c in range(2):
            P.op('dve', lambda e, c=c: e.bn_stats(st[:, c * 6:(c + 1) * 6], src[:, c * 512:(c + 1) * 512]),
                 reads=[src], writes=[st])
        P.op('dve', lambda e: e.bn_aggr(mv[:], st[:]), reads=[st], writes=[mv])
        P.op('dve', lambda e: e.tensor_scalar(rstd[:], mv[:, 1:2], LN_EPS, None, op0=ALU.add), reads=[mv], writes=[rstd])
        P.op('act', lambda e: e.activation(rstd[:], rstd[:], AF.Sqrt), reads=[rstd], writes=[rstd])
        P.op('dve', lambda e: e.reciprocal(rstd[:], rstd[:]), reads=[rstd], writes=[rstd])
        P.op('dve', lambda e: e.tensor_scalar(dst[:], src[:], mv[:, 0:1], rstd[:, 0:1], op0=ALU.subtract, op1=ALU.mult),
             reads=[src, mv, rstd], writes=[dst])
        P.op('dve', lambda e: e.tensor_tensor(dst[:], dst[:], lnp[:, (2 * li) * D:(2 * li + 1) * D], op=ALU.mult),
             reads=[dst, lnp], writes=[dst])
        P.op('dve', lambda e: e.tensor_tensor(dst[:], dst[:], lnp[:, (2 * li + 1) * D:(2 * li + 2) * D], op=ALU.add),
             reads=[dst, lnp], writes=[dst])

    def to_T(src):
        P.op('act', lambda e: e.copy(xb[:], src[:]), reads=[src], writes=[xb])
        for c in range(8):
            P.op('pe', lambda e, c=c: e.transpose(pT[:, c, :], xb[:, c * 128:(c + 1) * 128], ident[:]),
                 reads=[xb, ident], writes=[pT])
        P.op('dve', lambda e: e.tensor_copy(xT[:], pT[:]), excl=[pT], writes=[xT])

    def linear(lhsT_tile, w_b, n, pdst, off=0):
        for half in range(n // 512):
            for kc in range(8):
                P.op('pe', lambda e, half=half, kc=kc: e.matmul(
                    pdst[:, half * 512:(half + 1) * 512], lhsT_tile[:, kc, off:off + 128], w_b[:, kc, half * 512:(half + 1) * 512],
                    start=(kc == 0), stop=(kc == 7)), reads=[lhsT_tile, w_b], writes=[pdst])

    midx = P.sb([128, 32], U32, "midx")
    P.dma('sp', midx[:], midx_h.ap(), writes=[midx])

    def brow(rr):
        return bass.AP(gb_bf, rr * 1024, [[16384, 128], [1, 1024]])

    P.barrier()
    for i in range(ntiles):
        t0 = i * 128
        P.dma('sp', xt[:], xin_h.ap()[t0:t0 + 128, :], reads=[xin_res], writes=[xt])
        if i % 4 == 0:
            for kc in range(8):
                P.dma('pool', None, None, reads=[mixg_res, midx], writes=[mixb],
                      fn=lambda e, kc=kc, i=i: e.indirect_dma_start(
                          out=mixb[:, kc, :], out_offset=None, in_=mixg_h.ap(),
                          in_offset=bass.IndirectOffsetOnAxis(ap=midx[:, (i // 4) * 8 + kc:(i // 4) * 8 + kc + 1], axis=0)))
        linear(mixb, wout_b, 1024, pB, off=(i % 4) * 128)
        P.op('dve', lambda e: e.scalar_tensor_tensor(out=r[:], in0=xt[:], scalar=ALPHA, in1=pB[:], op0=ALU.mult, op1=ALU.add),
             reads=[xt], excl=[pB], writes=[r])
        layernorm(r, x1, 0)
        to_T(x1)
        for j in range(8):
            for kc in range(8):
                P.op('pe', lambda e, j=j, kc=kc: e.matmul(pA[:, j * 128:(j + 1) * 128], wq_b[:, kc, j * 128:(j + 1) * 128],
                                                          xT[:, kc, :], start=(kc == 0), stop=(kc == 7)),
                     reads=[wq_b, xT], writes=[pA])
        P.op('act', lambda e: e.copy(qT[:], pA[:, 0:1024].rearrange("p (j t) -> p j t", j=8)), excl=[pA], writes=[qT])
        for h in range(4):
            for mc in range(2):
                for dc in range(2):
                    P.op('pe', lambda e, h=h, mc=mc, dc=dc: e.matmul(
                        pB[:, (h * 2 + mc) * 128:(h * 2 + mc + 1) * 128],
                        KT_b[:, h * 2 + dc, mc * 128:(mc + 1) * 128], qT[:, h * 2 + dc, :],
                        start=(dc == 0), stop=(dc == 1)), reads=[KT_b, qT], writes=[pB])
        P.op('act', lambda e: e.activation(E_b[:], pB[:].rearrange("p (j t) -> p j t", j=8), AF.Exp, scale=1.0 / 16.0),
             excl=[pB], writes=[E_b])
        for h in range(4):
            for mc in range(2):
                P.op('pe', lambda e, h=h, mc=mc: e.matmul(pC[:, h * 128:(h + 1) * 128], ones_bf[:], E_b[:, h * 2 + mc, :],
                                                          start=(mc == 0), stop=(mc == 1)),
                     reads=[ones_bf, E_b], writes=[pC])
        P.op('dve', lambda e: e.reciprocal(rz[:], pC[:].rearrange("p (h t) -> p h t", h=4)), excl=[pC], writes=[rz])
        for h in range(4):
            for dc in range(2):
                for mc in range(2):
                    P.op('pe', lambda e, h=h, dc=dc, mc=mc: e.matmul(
                        pA[:, 1024 + (h * 2 + dc) * 128:1024 + (h * 2 + dc + 1) * 128],
                        V_b[:, mc, h * 256 + dc * 128:h * 256 + (dc + 1) * 128], E_b[:, h * 2 + mc, :],
                        start=(mc == 0), stop=(mc == 1)), reads=[V_b, E_b], writes=[pA])
        for j in range(8):
            P.op('dve', lambda e, j=j: e.tensor_tensor(oT[:, j, :], pA[:, 1024 + j * 128:1024 + (j + 1) * 128], rz[:, j // 2, :],
                                                       op=ALU.mult), reads=[rz], excl=[pA], writes=[oT])
        linear(oT, wo_b, 1024, pB)
        P.op('dve', lambda e: e.scalar_tensor_tensor(out=r[:], in0=x1[:], scalar=ALPHA, in1=pB[:], op0=ALU.mult, op1=ALU.add),
             reads=[x1], excl=[pB], writes=[r])
        layernorm(r, x2, 1)
        to_T(x2)
        for hc in range(16):
            for kc in range(8):
                P.op('pe', lambda e, hc=hc, kc=kc: e.matmul(pA[:, hc * 128:(hc + 1) * 128], wpq_b[:, kc, hc * 128:(hc + 1) * 128],
                                                            xT[:, kc, :], start=(kc == 0), stop=(kc == 7)),
                     reads=[wpq_b, xT], writes=[pA])
        P.op('act', lambda e: e.copy(bass.AP(w8bf, 0, [[4096, 128], [1, 2048]]), pA[:]), excl=[pA], writes=[pqT])
        for hc in range(16):
            P.op('pe', lambda e, hc=hc: e.matmul(pA[:, hc * 128:(hc + 1) * 128], pqT_ap(hc), skT_b[:, hc % 2, :],
                                                 start=True, stop=True), reads=[pqT, skT_b], writes=[pA])
        P.op('act', lambda e: e.copy(sc[:], pA[:].rearrange("p (j t) -> p j t", j=16)), excl=[pA], writes=[sc])
        for hc in range(16):
            P.op('dve', lambda e, hc=hc: e.max(top[:, hc, 0:8], sc[:, hc, :]), reads=[sc], writes=[top])
            P.op('dve', lambda e, hc=hc: e.max_index(idxu[:, hc, 0:8], top[:, hc, 0:8], sc[:, hc, :]), reads=[sc, top], writes=[idxu])
            P.op('dve', lambda e, hc=hc: e.match_replace(sc[:, hc, :], top[:, hc, 0:8], sc[:, hc, :], -1e30),
                 reads=[sc, top], writes=[sc])
            P.op('dve', lambda e, hc=hc: e.max(top[:, hc, 8:16], sc[:, hc, :]), reads=[sc], writes=[top])
            P.op('dve', lambda e, hc=hc: e.max_index(idxu[:, hc, 8:16], top[:, hc, 8:16], sc[:, hc, :]), reads=[sc, top], writes=[idxu])
        P.op('dve', lambda e: e.tensor_copy(idxf[:], idxu[:]), reads=[idxu], writes=[idxf])
        P.op('dve', lambda e: e.tensor_tensor(AP(cand, 0, [[2048, 128], [256, 8], [16, 16], [1, 16]]),
                                              AP(top, 0, [[256, 128], [32, 8], [1, 16], [0, 16]]),
                                              AP(top, 16, [[256, 128], [32, 8], [0, 16], [1, 16]]), op=ALU.add),
             reads=[top], writes=[cand])
        for h in range(8):
            P.op('dve', lambda e, h=h: e.max(cs[:, h, 0:8], sc.t.rearrange('p a b -> p (a b)')[:, h * 256:(h + 1) * 256]), reads=[cand], writes=[cs])
            P.op('dve', lambda e, h=h: e.max_index(ciu[:, h, 0:8], cs[:, h, 0:8], sc.t.rearrange('p a b -> p (a b)')[:, h * 256:(h + 1) * 256]), reads=[cand, cs], writes=[ciu])
            P.op('dve', lambda e, h=h: e.match_replace(sc.t.rearrange('p a b -> p (a b)')[:, h * 256:(h + 1) * 256], cs[:, h, 0:8], sc.t.rearrange('p a b -> p (a b)')[:, h * 256:(h + 1) * 256], -1e30),
                 reads=[cand, cs], writes=[cand])
            P.op('dve', lambda e, h=h: e.max(cs[:, h, 8:16], sc.t.rearrange('p a b -> p (a b)')[:, h * 256:(h + 1) * 256]), reads=[cand], writes=[cs])
            P.op('dve', lambda e, h=h: e.max_index(ciu[:, h, 8:16], cs[:, h, 8:16], sc.t.rearrange('p a b -> p (a b)')[:, h * 256:(h + 1) * 256]), reads=[cand, cs], writes=[ciu])
        P.op('dve', lambda e: e.tensor_copy(cif[:], ciu[:].rearrange("p h k -> p (h k)")), reads=[ciu], writes=[cif])
        P.op('dve', lambda e: e.tensor_tensor(AP(big, 0, [[2048, 128], [16, 128], [1, 16]]),
                                              AP(cif, 0, [[128, 128], [1, 128], [0, 16]]),
                                              AP(thr16, 0, [[16, 128], [0, 128], [1, 16]]), op=ALU.is_ge),
             reads=[cif, thr16], writes=[big])
        P.op('dve', lambda e: e.tensor_reduce(k0f[:], AP(big, 0, [[2048, 128], [16, 128], [1, 16]]), axis=AX.X, op=ALU.add),
             reads=[big], writes=[k0f])
        P.op('dve', lambda e: e.scalar_tensor_tensor(out=k1f[:], in0=k0f[:], scalar=-16.0, in1=cif[:], op0=ALU.mult, op1=ALU.add),
             reads=[k0f, cif], writes=[k1f])
        for (kf, c, dst) in ((k0f, 0, i0s), (k1f, 1, i1s)):
            P.op('dve', lambda e, kf=kf: e.tensor_tensor(AP(big, 0, [[2048, 128], [16, 128], [1, 16]]),
                                                         AP(kf, 0, [[128, 128], [1, 128], [0, 16]]),
                                                         AP(iota16, 0, [[16, 128], [0, 128], [1, 16]]), op=ALU.is_equal),
                 reads=[kf, iota16], writes=[big])
            P.op('dve', lambda e, c=c: e.tensor_tensor(AP(big2, 0, [[2048, 128], [256, 8], [16, 16], [1, 16]]),
                                                       AP(big, 0, [[2048, 128], [256, 8], [16, 16], [1, 16]]),
                                                       AP(idxf, c * 16, [[256, 128], [32, 8], [0, 16], [1, 16]]), op=ALU.mult),
                 reads=[big, idxf], writes=[big2])
            P.op('dve', lambda e, dst=dst: e.tensor_reduce(dst[:], AP(big2, 0, [[2048, 128], [16, 128], [1, 16]]), axis=AX.X, op=ALU.add),
                 reads=[big2], writes=[dst])
        P.op('dve', lambda e: e.scalar_tensor_tensor(out=ef[:], in0=i0s[:], scalar=128.0, in1=i1s[:], op0=ALU.mult, op1=ALU.add),
             reads=[i0s, i1s], writes=[ef])
        P.op('dve', lambda e: e.tensor_copy(eu[:], ef[:]), reads=[ef], writes=[eu])
        P.op('dve', lambda e: e.tensor_tensor(gex[:], cs[:], AP(cs, 0, [[128, 128], [16, 8], [0, 16]]), op=ALU.subtract),
             reads=[cs], writes=[gex])
        P.op('act', lambda e: e.activation(gex[:], gex[:], AF.Exp), reads=[gex], writes=[gex])
        P.op('dve', lambda e: e.tensor_reduce(gz[:], gex[:], axis=AX.X, op=ALU.add), reads=[gex], writes=[gz])
        P.op('dve', lambda e: e.reciprocal(gz[:], gz[:]), reads=[gz], writes=[gz])
        P.op('dve', lambda e: e.tensor_tensor(gex[:], gex[:], AP(gz, 0, [[8, 128], [1, 8], [0, 16]]), op=ALU.mult),
             reads=[gex, gz], writes=[gex])
        gexf = gex.t.rearrange("p h k -> p (h k)")

        def rowap(rr, off, n):
            if rr < 8:
                return bass.AP(gb_bf, rr * 2048 + off, [[16384, 128], [1, n]])
            return bass.AP(w8bf, (rr - 8) * 2048 + off, [[4096, 128], [1, n]])

        def rowdep(rr):
            return [grow[rr]] if rr < 8 else [grow[rr], sc]
        for s_ in range(128 + 3):
            if s_ < 128:
                hk, rr = s_, s_ % 10
                P.dma('pool', None, None, reads=[eu] + ([sc] if rr >= 8 else []), writes=[grow[rr]],
                      fn=lambda e, hk=hk, rr=rr: e.indirect_dma_start(
                          out=rowap(rr, 0, 2048), out_offset=None, in_=H['uvb'].ap(),
                          in_offset=bass.IndirectOffsetOnAxis(ap=eu[:, hk:hk + 1], axis=0)))
                P.op('dve', lambda e, hk=hk, rr=rr: e.scalar_tensor_tensor(
                    out=junk[:], in0=rowap(rr, 0, 1024), scalar=1.0, in1=x2[:],
                    op0=ALU.mult, op1=ALU.mult, accum_out=aact[:, hk:hk + 1]), reads=rowdep(rr) + [x2], writes=[junk, acol[hk]])
            if 0 <= s_ - 1 < 128:
                hk = s_ - 1
                P.op('act', lambda e, hk=hk: e.activation(gel[:, hk:hk + 1], aact[:, hk:hk + 1], AF.Gelu),
                     reads=[acol[hk]], writes=[gcol[hk]])
            if 0 <= s_ - 2 < 128:
                hk = s_ - 2
                P.op('dve', lambda e, hk=hk: e.tensor_tensor(wgt[:, hk:hk + 1], gel[:, hk:hk + 1], gexf[:, hk:hk + 1], op=ALU.mult),
                     reads=[gcol[hk], gex], writes=[wcol[hk]])
            if 0 <= s_ - 3 < 128:
                hk = s_ - 3
                rr = hk % 10
                dg = dgs[hk % 4]
                P.op('act', lambda e, hk=hk, dg=dg: e.activation(dg[:], ident[:], AF.Identity, scale=wgt[:, hk:hk + 1]),
                     reads=[ident, wcol[hk]], writes=[dg])
                for half in range(2):
                    P.op('pe', lambda e, hk=hk, rr=rr, dg=dg, half=half: e.matmul(
                        pB[:, half * 512:(half + 1) * 512], dg[:],
                        rowap(rr, 1024 + half * 512, 512),
                        start=(hk == 0), stop=(hk == 127)), reads=[dg] + rowdep(rr), writes=[pB])
        P.op('dve', lambda e: e.scalar_tensor_tensor(out=acc[:], in0=x2[:], scalar=ALPHA, in1=pB[:], op0=ALU.mult, op1=ALU.add),
             reads=[x2], excl=[pB], writes=[acc])
        layernorm(acc, x3, 2)
        P.dma('sp', xout_h.ap()[t0:t0 + 128, :], x3[:], reads=[x3], writes=[xout_res])
        if xsrc_h is not None:
            to_T(x3)
            for c4 in range(4):
                P.dma('sp', xsrc_h[c4].ap().rearrange("(k p) t -> p k t", p=128)[:, :, t0:t0 + 128], xT[:, 2 * c4:2 * c4 + 2, :],
                      reads=[xT], writes=[xsrc_res])

def build_fused(depth=4, nq_tiles=32, ntiles=16):
    nc = bass.Bass("TRN2", target_bir_lowering=False)
    D = 1024
    xT0_h = nc.dram_tensor("xT0", [D, S], F32, kind="ExternalInput")
    x0_h = nc.dram_tensor("x0", [NTOK, D], F32, kind="ExternalInput")
    memT_h = nc.dram_tensor("memT", [D, 256], F32, kind="ExternalInput")
    midx_h = nc.dram_tensor("midx", [128, 32], U32, kind="ExternalInput")
    uvb_h = nc.dram_tensor("uvb_scr", [16384, 2 * D], BF16)
    HA, HB = [], []
    for l in range(depth):
        HA.append({"wa": nc.dram_tensor(f"wa{l}", [D, NWA], F32, kind="ExternalInput"),
                   "sp": nc.dram_tensor(f"sp{l}", [128, 16], F32, kind="ExternalInput"),
                   "lamqk": nc.dram_tensor(f"lamqk{l}", [128, 256], F32, kind="ExternalInput"),
                   "gda": nc.dram_tensor(f"gda{l}", [128, 128], F32, kind="ExternalInput"),
                   "gml": nc.dram_tensor(f"gml{l}", [128, 128], F32, kind="ExternalInput")})
        HB.append({"wout": nc.dram_tensor(f"wout{l}", [D, D], F32, kind="ExternalInput"),
                   "wq": nc.dram_tensor(f"wq{l}", [D, D], F32, kind="ExternalInput"),
                   "wkv": nc.dram_tensor(f"wkv{l}", [D, 2 * D], F32, kind="ExternalInput"),
                   "wo": nc.dram_tensor(f"wo{l}", [D, D], F32, kind="ExternalInput"),
                   "wpq": nc.dram_tensor(f"wpq{l}", [D, 2 * D], F32, kind="ExternalInput"),
                   "skT": nc.dram_tensor(f"skT{l}", [2, 128, 128], F32, kind="ExternalInput"),
                   "lnp": nc.dram_tensor(f"lnp{l}", [6, D], F32, kind="ExternalInput"),
                   "u": nc.dram_tensor(f"u{l}", [16384, D], F32, kind="ExternalInput"),
                   "v": nc.dram_tensor(f"v{l}", [16384, D], F32, kind="ExternalInput"),
                   "memT": memT_h, "uvb": uvb_h})
        HA[l]["u"], HA[l]["v"], HA[l]["uvb"] = HB[l]["u"], HB[l]["v"], uvb_h
    out_h = nc.dram_tensor("out", [NTOK, D], F32, kind="ExternalOutput")
    mixsrc = [[nc.dram_tensor(f"mixsrc{l}_{c}", [1024, 512], BF16) for c in range(4)] for l in range(depth)]
    mixgc = [[nc.dram_tensor(f"mixgc{l}_{c}", [4096, 512], BF16) for c in range(4)] for l in range(depth)]
    mixg = [nc.dram_tensor(f"mixg{l}", [16384, 512], BF16) for l in range(depth)]
    xsrc = [[nc.dram_tensor(f"xsrc{l}_{c}", [256, NTOK], BF16) for c in range(4)] for l in range(depth - 1)]
    xg = [None] + [[nc.dram_tensor(f"xg{l}_{c}", [1024, NTOK], BF16) for c in range(4)] for l in range(1, depth)]
    xres = [None] + [nc.dram_tensor(f"xres{l}", [NTOK, D], F32) for l in range(1, depth)]
    GROUPS = [[0, 1, 2, 3], [4, 5, 6, 7]]

    P = Prog(nc, n_dma_sems={'sp': 8, 'act': 4, 'pool': 20})
    C = make_consts(P)
    out_res = Res("out")
    xg_res = [Res(f"xg{l}") for l in range(depth)]
    xres_res = [Res(f"xres{l}") for l in range(depth)]
    for l in range(depth):
        mixsrc_res, mixg_res, xsrc_res = Res("mixsrc"), Res("mixg"), Res("xsrc")
        with P.scope():
            phase_a(P, nc, C, HA[l], xT0_h, xg[l], xg_res[l], mixsrc[l], mixsrc_res, nq_tiles)
        for c in range(4):
            gc_res = Res("mixgc")
            P.cc("AllGather", GROUPS, mixsrc[l][c].ap(), mixgc[l][c].ap(), reads=[mixsrc_res], writes=[gc_res], inc=1)
            P.dma('sp', mixg[l].ap()[c * 4096:(c + 1) * 4096, :].rearrange("(a b) n -> a (b n)", a=128),
                  mixgc[l][c].ap().rearrange("(a b) n -> a (b n)", a=128), reads=[gc_res], writes=[mixg_res])
        last = (l == depth - 1)
        with P.scope():
            phase_b(P, nc, C, HB[l], x0_h if l == 0 else xres[l], xres_res[l], mixg[l], mixg_res, midx_h,
                    out_h if last else xres[l + 1], out_res if last else xres_res[l + 1],
                    None if last else xsrc[l], xsrc_res, ntiles)
        if not last:
            for c in range(4):
                P.cc("AllGather", GROUPS, xsrc[l][c].ap(), xg[l + 1][c].ap(), reads=[xsrc_res], writes=[xg_res[l + 1]], inc=1)
    P.finish([out_res])
    P.emit()
    P.close()
    return nc


DEPTH = 4
_NC_CACHE = {}


def _head_inputs(w_in_l, conv_w_l, conv_b_l, i_bias_l, f_bias_l, lam_qk_l, da_g_l, ml_g_l, l, h):
    r = np.arange(h * 128, (h + 1) * 128)
    cols = np.concatenate([r, 512 + r, 1024 + r, 1536 + r, 2048 + r, 2560 + r, 3072 + r, [3584 + h], [3588 + h]])
    wa = np.ascontiguousarray(w_in_l[:, cols])
    lam_init = 0.8 - 0.6 * math.exp(-0.3 * l)
    sp = np.zeros((128, 16), np.float32)
    for g in range(2):
        ch = g * 512 + r
        for j in range(4):
            sp[:, g * 4 + j] = conv_w_l[j, ch]
        sp[:, 8 + g] = conv_b_l[ch]
    sp[:, 10] = i_bias_l[h]
    sp[:, 11] = f_bias_l[h]
    sp[:, 12] = lam_init
    sp[:, 13] = 1.0 - lam_init
    lamqk = np.ascontiguousarray(np.broadcast_to(lam_qk_l.reshape(1, 256), (128, 256)))
    gda = np.ascontiguousarray(np.broadcast_to(da_g_l[None, :], (128, 128)))
    gml = np.ascontiguousarray(np.broadcast_to(ml_g_l[None, r], (128, 128)))
    return {f"wa{l}": wa, f"sp{l}": sp, f"lamqk{l}": lamqk, f"gda{l}": gda, f"gml{l}": gml}


def make_in_maps(depth, x, mem, w_in, i_bias, f_bias, conv_w, conv_b, lam_qk, da_norm_g, ml_norm_g, w_out,
                 ln1_g, ln1_b, wq_mem, wkv_mem, wo_mem, ln2_g, ln2_b, w_pq, sub_keys, u_tab, v_tab, ln3_g, ln3_b):
    f = lambda a: np.asarray(a, dtype=np.float32)
    x = f(x); mem = f(mem)
    B = x.shape[0]
    shared = {}
    for l in range(depth):
        shared[f"wout{l}"] = f(w_out[l]); shared[f"wq{l}"] = f(wq_mem[l]); shared[f"wkv{l}"] = f(wkv_mem[l])
        shared[f"wo{l}"] = f(wo_mem[l]); shared[f"wpq{l}"] = f(w_pq[l])
        shared[f"skT{l}"] = np.ascontiguousarray(f(sub_keys[l]).transpose(0, 2, 1))
        shared[f"lnp{l}"] = np.ascontiguousarray(np.stack([f(ln1_g[l]), f(ln1_b[l]), f(ln2_g[l]), f(ln2_b[l]), f(ln3_g[l]), f(ln3_b[l])]))
        shared[f"u{l}"] = f(u_tab[l]); shared[f"v{l}"] = f(v_tab[l])
    in_maps = []
    for b in range(B):
        xT = np.ascontiguousarray(x[b].T)
        memT = np.ascontiguousarray(mem[b].T)
        for r_ in range(4):
            m = dict(shared)
            m["xT0"] = xT
            m["x0"] = np.ascontiguousarray(x[b, r_ * 2048:(r_ + 1) * 2048])
            m["memT"] = memT
            p = np.arange(128, dtype=np.int64)[:, None, None]
            qr = np.arange(4, dtype=np.int64)[None, :, None]
            kc = np.arange(8, dtype=np.int64)[None, None, :]
            midx = ((((r_ * 4 + (kc % 4)) * 4 + qr) * 2 + (kc // 4)) * 128) + p
            m["midx"] = np.ascontiguousarray(midx.reshape(128, 32).astype(np.uint32))
            for l in range(depth):
                m.update(_head_inputs(f(w_in[l]), f(conv_w[l]), f(conv_b[l]), f(i_bias[l]), f(f_bias[l]), f(lam_qk[l]),
                                      f(da_norm_g[l]), f(ml_norm_g[l]), l, r_))
            in_maps.append(m)
    return in_maps


def kernel(x, mem, w_in, i_bias, f_bias, conv_w, conv_b, lam_qk, da_norm_g, ml_norm_g, w_out,
           ln1_g, ln1_b, wq_mem, wkv_mem, wo_mem, ln2_g, ln2_b, w_pq, sub_keys, u_tab, v_tab,
           ln3_g, ln3_b):
    if "f" not in _NC_CACHE:
        _NC_CACHE["f"] = build_fused(DEPTH)
    in_maps = make_in_maps(DEPTH, x, mem, w_in, i_bias, f_bias, conv_w, conv_b, lam_qk, da_norm_g, ml_norm_g, w_out,
                           ln1_g, ln1_b, wq_mem, wkv_mem, wo_mem, ln2_g, ln2_b, w_pq, sub_keys, u_tab, v_tab, ln3_g, ln3_b)
    res = run_bass_kernel_spmd(_NC_CACHE["f"], in_maps, core_ids=list(range(8))).results
    B, S_, D = np.asarray(x).shape
    out = np.empty((B, S_, D), np.float32)
    for b in range(B):
        for r_ in range(4):
            out[b, r_ * 2048:(r_ + 1) * 2048] = res[b * 4 + r_]["out"]
    return out
```

```python
import math
import contextlib
import numpy as np
import ml_dtypes
import concourse.bass as bass
import concourse.mybir as mybir
from concourse.bass_utils import run_bass_kernel_spmd

F32 = mybir.dt.float32
BF16 = mybir.dt.bfloat16
I32 = mybir.dt.int32
U32 = mybir.dt.uint32
AF = mybir.ActivationFunctionType
ALU = mybir.AluOpType
AX = mybir.AxisListType

SEM_LIMIT = 15000


class Res:
    __slots__ = ("w", "r", "name")

    def __init__(self, name=""):
        self.w = None
        self.r = []
        self.name = name


class Tile:
    def __init__(self, t, res=None, name=""):
        self.t = t
        self.res = res if res is not None else Res(name)

    def __getitem__(self, k):
        return self.t[k]


def _res(x):
    return x.res if isinstance(x, Tile) else x


class Prog:
    ENGS = ("pe", "act", "dve", "pool", "sp")

    def __init__(self, nc, n_dma_sems=6):
        self.nc = nc
        self.es = contextlib.ExitStack()
        self.cur_es = self.es
        self.nsem = 0
        self.streams = {e: [] for e in self.ENGS}
        self.cur_sem = {}
        self.cur_val = {}
        for e in self.ENGS:
            self._new_eng_sem(e)
        self.known = {e: {} for e in self.ENGS}
        self.dsem = {}
        self.nds = n_dma_sems
        self.dma_rr = {e: 0 for e in self.ENGS}
        self.ntile = 0
        self.last_tok = {}

    def sem(self, name):
        self.nsem += 1
        return self.es.enter_context(self.nc.semaphore(f"{name}_{self.nsem}"))

    def _new_eng_sem(self, e):
        self.nsem = getattr(self, "nsem", 0)
        self.cur_sem[e] = self.sem("c" + e)
        self.cur_val[e] = 0

    def sb(self, shape, dt, name=None):
        self.ntile += 1
        nm = f"{name or 't'}_{self.ntile}"
        t = self.cur_es.enter_context(self.nc.sbuf_tensor(nm, list(shape), dt))
        return Tile(t, name=nm)

    def ps(self, shape, dt=F32, name=None):
        self.ntile += 1
        nm = f"{name or 'p'}_{self.ntile}"
        t = self.cur_es.enter_context(self.nc.psum_tensor(nm, list(shape), dt))
        return Tile(t, name=nm)

    def _collect(self, eng, reads, writes, excl=()):
        toks = []
        for r in reads:
            r = _res(r)
            if r.w is not None:
                toks.append(r.w)
        for w in list(writes) + list(excl):
            w = _res(w)
            if w.w is not None:
                toks.append(w.w)
            toks.extend(w.r)
        waits = []
        kn = self.known[eng]
        best = {}
        for (s, v, src) in toks:
            if src == "pe" and eng == "pe":
                continue
            if kn.get(s, 0) >= v:
                continue
            if best.get(s, (None, 0))[1] < v:
                best[s] = (s, v)
        for s, (sh, v) in best.items():
            kn[s] = v
            waits.append((sh, v))
        return waits

    def _commit(self, tok, reads, writes, excl=()):
        for r in reads:
            _res(r).r.append(tok)
        for w in list(writes) + list(excl):
            w = _res(w)
            w.w = tok
            w.r = []

    def op(self, eng, fn, reads=(), writes=(), excl=()):
        waits = self._collect(eng, reads, writes, excl)
        if self.cur_val[eng] >= SEM_LIMIT:
            self._new_eng_sem(eng)
        s = self.cur_sem[eng]
        self.cur_val[eng] += 1
        v = self.cur_val[eng]
        tok = (s, v, eng)
        self.last_tok[eng] = tok
        self._commit(tok, reads, writes, excl)
        self.streams[eng].append((waits, fn, s, 1))
        return tok

    def dma(self, q, out, in_, reads=(), writes=(), fn=None, inc=16):
        key = q
        nds = self.nds[q] if isinstance(self.nds, dict) else self.nds
        if key not in self.dsem:
            self.dsem[key] = [[self.sem("d" + q), 0] for _ in range(nds)]
        k = self.dma_rr[q]
        self.dma_rr[q] = (k + 1) % nds
        slot = self.dsem[key][k]
        waits = self._collect(q, reads, writes)
        kn = self.known[q]
        if slot[1] > 0 and kn.get(slot[0], 0) < slot[1]:
            waits.append((slot[0], slot[1]))
            kn[slot[0]] = slot[1]
        if slot[1] + inc > SEM_LIMIT:
            slot[0] = self.sem("d" + q)
            slot[1] = 0
        slot[1] += inc
        tok = (slot[0], slot[1], "dma")
        self._commit(tok, reads, writes)
        if fn is None:
            fn = lambda e, o=out, i=in_: e.dma_start(out=o, in_=i)
        self.streams[q].append((waits, fn, slot[0], inc))
        return tok


    def cc(self, kind, groups, in_ap, out_ap, reads=(), writes=(), inc=16):
        fn = lambda e: e.collective_compute(kind, ALU.bypass, replica_groups=groups, ins=[in_ap], outs=[out_ap])
        self._cc_inc = inc
        return self.dma('pool', None, None, reads=reads, writes=writes, fn=fn, inc=inc)


    @contextlib.contextmanager
    def scope(self):
        prev = self.cur_es
        es = contextlib.ExitStack()
        self.cur_es = es
        try:
            yield
            self.barrier()
            self.emit()
        finally:
            self.cur_es = prev
            es.close()

    def barrier(self):
        toks = []
        for e in self.ENGS:
            if e in self.last_tok:
                toks.append(self.last_tok[e])
        for q, slots in self.dsem.items():
            for (sh, v) in slots:
                if v > 0:
                    toks.append((sh, v, "dma"))
        for e in self.ENGS:
            kn = self.known[e]
            waits = []
            for (sh, v, src) in toks:
                if src == e and e == "pe":
                    continue
                if kn.get(sh, 0) < v:
                    kn[sh] = v
                    waits.append((sh, v))
            if waits:
                self.streams[e].append((waits, None, None, 0))

    def finish(self, all_res):
        waits = self._collect("sp", all_res, [])
        kn = self.known["sp"]
        for q, slots in self.dsem.items():
            for (sh, v) in slots:
                if v > 0 and kn.get(sh, 0) < v:
                    waits.append((sh, v))
                    kn[sh] = v
        self.streams["sp"].append((waits, None, None, 0))

    def emit(self):
        nc = self.nc
        streams = self.streams

        def run(engname, eng):
            for (waits, fn, s, inc) in streams[engname]:
                for (sh, v) in waits:
                    eng.wait_ge(sh, v)
                if fn is not None:
                    ins = fn(eng)
                    ins.then_inc(s, inc)

        with nc.Block() as block:
            @block.tensor
            def _(e):
                run("pe", e)

            @block.scalar
            def _(e):
                run("act", e)

            @block.vector
            def _(e):
                run("dve", e)

            @block.gpsimd
            def _(e):
                run("pool", e)

            @block.sync
            def _(e):
                run("sp", e)
        self.streams = {e: [] for e in self.ENGS}

    def close(self):
        self.es.close()


LN_EPS = 1e-5
ALPHA = 8.0 ** 0.25
S = 8192
NWA = 898
NTOK = 2048


def AP(t, off, dims):
    return bass.AP(t.t if isinstance(t, Tile) else t, off, dims)


def make_consts(P):
    C = {}
    identf = P.sb([128, 128], F32, "identf")
    ident = P.sb([128, 128], BF16, "ident")
    trif = P.sb([128, 128], F32, "trif")
    trib = P.sb([128, 128], BF16, "trib")
    onesf = P.sb([128, 128], F32, "onesf")
    ones_bf = P.sb([128, 128], BF16, "ones")
    iota16 = P.sb([128, 16], F32, "iota16")
    thr16 = P.sb([128, 16], F32, "thr16")
    P.op('dve', lambda e: e.memset(identf[:], 0.0), writes=[identf])
    P.op('pool', lambda e: e.affine_select(out=identf[:], in_=identf[:], pattern=[[-1, 128]],
                                           compare_op=ALU.not_equal, fill=1.0, base=0, channel_multiplier=1),
         reads=[identf], writes=[identf])
    P.op('dve', lambda e: e.tensor_copy(ident[:], identf[:]), reads=[identf], writes=[ident])
    P.op('dve', lambda e: e.memset(onesf[:], 1.0), writes=[onesf])
    P.op('dve', lambda e: e.memset(ones_bf[:], 1.0), writes=[ones_bf])
    P.op('pool', lambda e: e.iota(trif[:], pattern=[[1, 128]], base=0, channel_multiplier=-1,
                                  allow_small_or_imprecise_dtypes=True), writes=[trif])
    P.op('dve', lambda e: e.tensor_single_scalar(trif[:], trif[:], 0.0, op=ALU.is_ge), reads=[trif], writes=[trif])
    P.op('dve', lambda e: e.tensor_copy(trib[:], trif[:]), reads=[trif], writes=[trib])
    P.op('pool', lambda e: e.iota(iota16[:], pattern=[[1, 16]], base=0, channel_multiplier=0,
                                  allow_small_or_imprecise_dtypes=True), writes=[iota16])
    P.op('pool', lambda e: e.iota(thr16[:], pattern=[[16, 16]], base=16, channel_multiplier=0,
                                  allow_small_or_imprecise_dtypes=True), writes=[thr16])
    C.update(identf=identf, ident=ident, trif=trif, trib=trib, onesf=onesf, ones_bf=ones_bf, iota16=iota16, thr16=thr16)
    return C


def phase_a(P, nc, C, H, xT_h, xg_h, xg_res, mixsrc_h, mixsrc_res, nq_tiles):
    ident, trif, trib, onesf = C['ident'], C['trif'], C['trib'], C['onesf']
    D = 1024
    ntok = nq_tiles * 256
    nblk = ntok // 128
    spt = P.sb([128, 16], F32, "spt")
    lamq = P.sb([128, 256], F32, "lamq")
    gda = P.sb([128, 128], F32, "gda")
    gml = P.sb([128, 128], F32, "gml")
    P.dma('sp', spt[:], H['sp'].ap(), writes=[spt])
    P.dma('sp', lamq[:], H['lamqk'].ap(), writes=[lamq])
    P.dma('sp', gda[:], H['gda'].ap(), writes=[gda])
    P.dma('sp', gml[:], H['gml'].ap(), writes=[gml])
    lt = P.sb([128, 128], F32, "lt")
    lsum = P.sb([128, 2], F32, "lsum")
    neglam = P.sb([128, 1], F32, "neglam")
    P.op('dve', lambda e: e.tensor_tensor(lt[:].rearrange("p (a d) -> p a d", a=2),
                                          AP(lamq, 0, [[256, 128], [128, 2], [1, 64]]),
                                          AP(lamq, 64, [[256, 128], [128, 2], [1, 64]]), op=ALU.mult),
         reads=[lamq], writes=[lt])
    P.op('dve', lambda e: e.tensor_reduce(lsum[:], lt[:].rearrange("p (a d) -> p a d", a=2), axis=AX.X, op=ALU.add),
         reads=[lt], writes=[lsum])
    P.op('act', lambda e: e.activation(lsum[:], lsum[:], AF.Exp), reads=[lsum], writes=[lsum])
    P.op('dve', lambda e: e.tensor_tensor(neglam[:], lsum[:, 1:2], lsum[:, 0:1], op=ALU.subtract), reads=[lsum], writes=[neglam])
    P.op('dve', lambda e: e.tensor_tensor(neglam[:], neglam[:], spt[:, 12:13], op=ALU.subtract), reads=[neglam, spt], writes=[neglam])
    P.op('dve', lambda e: e.tensor_scalar(gda[:], gda[:], spt[:, 13:14], None, op0=ALU.mult), reads=[gda, spt], writes=[gda])

    wab = P.sb([128, 8, NWA], BF16, "wab")
    stg = P.sb([128, 1024], F32, "stg")
    wap = H['wa'].ap().rearrange("(c p) n -> p c n", p=128)
    for kc in range(8):
        P.dma('sp', stg[:, 0:NWA], wap[:, kc, :], writes=[stg])
        P.op('dve', lambda e, kc=kc: e.tensor_copy(wab[:, kc, :], stg[:, 0:NWA]), reads=[stg], writes=[wab])

    daqT = [P.sb([64, S], BF16, f"daqT{c}") for c in range(2)]
    dakT = [P.sb([64, S], BF16, f"dakT{c}") for c in range(2)]
    Vaug = P.sb([128, 64, 129], BF16, "Vaug")
    mlqT = P.sb([128, S], BF16, "mlqT")
    mlkT = P.sb([128, S], BF16, "mlkT")
    mlV = P.sb([128, 64, 129], BF16, "mlV")
    sigo = P.sb([128, 64, 128], BF16, "sigo")
    gates = P.sb([128, 64, 2], F32, "gates")
    P.op('dve', lambda e: e.memset(Vaug[:, :, 128:129], 1.0), writes=[Vaug])
    P.op('dve', lambda e: e.memset(mlV[:, :, 128:129], 1.0), writes=[mlV])

    banks = [P.ps([128, 512], F32, f"bank{i}") for i in range(8)]

    xs = P.sb([128, 8, 256], F32, "xs")
    xb = P.sb([128, 8, 256], BF16, "xb")
    pre = [P.sb([128, 3 + 256], F32, f"pre{i}") for i in range(2)]
    cacc = P.sb([128, 256], F32, "cacc")
    ctmp = P.sb([128, 256], F32, "ctmp")
    xT_v = xT_h.ap().rearrange("(c p) t -> p c t", p=128) if xg_h is None else None
    for i in range(2):
        P.op('dve', lambda e, i=i: e.memset(pre[i][:, 0:3], 0.0), writes=[pre[i]])
    for ti in range(nq_tiles):
        t0 = ti * 256
        if xg_h is None:
            P.dma('sp', xs[:], xT_v[:, :, t0:t0 + 256], writes=[xs])
            P.op('pool', lambda e: e.tensor_copy(xb[:], xs[:]), reads=[xs], writes=[xb])
        else:
            jr, tl = t0 // 2048, t0 % 2048
            for c4 in range(4):
                P.dma('sp', xb[:, 2 * c4:2 * c4 + 2, :],
                      xg_h[c4].ap()[jr * 256:(jr + 1) * 256, tl:tl + 256].rearrange("(k p) t -> p k t", p=128),
                      reads=[xg_res], writes=[xb])
        for g in range(4):
            pb = banks[g // 2]
            col0 = (g // 2) * 128 + (g % 2) * 64
            for kc in range(8):
                P.op('pe', lambda e, g=g, kc=kc, pb=pb, col0=col0: e.matmul(
                    pb[0:64, (g % 2) * 256:(g % 2) * 256 + 256], wab[:, kc, col0:col0 + 64], xb[:, kc, :],
                    start=(kc == 0), stop=(kc == 7)), reads=[wab, xb], writes=[pb])
        for c in range(2):
            P.op('act', lambda e, t0=t0, c=c: e.copy(daqT[c][:, t0:t0 + 256], banks[0][0:64, c * 256:c * 256 + 256]),
                 excl=[banks[0]], writes=[daqT[c]])
            P.op('dve', lambda e, t0=t0, c=c: e.tensor_copy(dakT[c][:, t0:t0 + 256], banks[1][0:64, c * 256:c * 256 + 256]),
                 excl=[banks[1]], writes=[dakT[c]])
        for g in range(2):
            col0 = 384 + g * 128
            for kc in range(8):
                P.op('pe', lambda e, g=g, kc=kc, col0=col0: e.matmul(
                    banks[2][:, g * 256:g * 256 + 256], wab[:, kc, col0:col0 + 128], xb[:, kc, :],
                    start=(kc == 0), stop=(kc == 7)), reads=[wab, xb], writes=[banks[2]])
        for g in range(2):
            pr = pre[g]
            if ti > 0:
                P.op('dve', lambda e, pr=pr: e.tensor_copy(pr[:, 0:3], pr[:, 256:259]), reads=[pr], writes=[pr])
            P.op('act', lambda e, g=g, pr=pr: e.copy(pr[:, 3:259], banks[2][:, g * 256:g * 256 + 256]),
                 excl=[banks[2]], writes=[pr])
            P.op('dve', lambda e, g=g, pr=pr: e.tensor_scalar(cacc[:], pr[:, 3:259], spt[:, g * 4 + 3:g * 4 + 4], spt[:, 8 + g:9 + g],
                                                              op0=ALU.mult, op1=ALU.add), reads=[pr, spt], writes=[cacc])
            for j in range(3):
                P.op('dve', lambda e, g=g, pr=pr, j=j: e.scalar_tensor_tensor(
                    out=cacc[:], in0=pr[:, j:j + 256], scalar=spt[:, g * 4 + j:g * 4 + j + 1], in1=cacc[:],
                    op0=ALU.mult, op1=ALU.add), reads=[pr, spt, cacc], writes=[cacc])
            if g == 0:
                P.op('act', lambda e, t0=t0: e.activation(mlqT[:, t0:t0 + 256], cacc[:], AF.Silu), reads=[cacc], writes=[mlqT])
            else:
                P.op('act', lambda e: e.activation(ctmp[:], cacc[:], AF.Silu), reads=[cacc], writes=[ctmp])
                P.op('dve', lambda e, t0=t0: e.tensor_scalar(mlkT[:, t0:t0 + 256], ctmp[:], 128.0 ** -0.5, None, op0=ALU.mult),
                     reads=[ctmp], writes=[mlkT])
        for sub in range(2):
            blk = ti * 2 + sub
            pb = banks[3 + sub]
            for gi, col0 in enumerate((256, 640, 768)):
                for kc in range(8):
                    P.op('pe', lambda e, sub=sub, gi=gi, col0=col0, kc=kc, pb=pb: e.matmul(
                        pb[:, gi * 128:(gi + 1) * 128], xb[:, kc, sub * 128:(sub + 1) * 128], wab[:, kc, col0:col0 + 128],
                        start=(kc == 0), stop=(kc == 7)), reads=[wab, xb], writes=[pb])
            for kc in range(8):
                P.op('pe', lambda e, sub=sub, kc=kc, blk=blk: e.matmul(
                    banks[5][:, blk * 2:blk * 2 + 2], xb[:, kc, sub * 128:(sub + 1) * 128], wab[:, kc, 896:898],
                    start=(kc == 0), stop=(kc == 7)), reads=[wab, xb], writes=[banks[5]])
            P.op('act', lambda e, blk=blk, pb=pb: e.copy(Vaug[:, blk, 0:128], pb[:, 0:128]), excl=[pb], writes=[Vaug])
            P.op('dve', lambda e, blk=blk, pb=pb: e.tensor_copy(mlV[:, blk, 0:128], pb[:, 128:256]), excl=[pb], writes=[mlV])
            P.op('act', lambda e, blk=blk, pb=pb: e.activation(sigo[:, blk, :], pb[:, 256:384], AF.Sigmoid), excl=[pb], writes=[sigo])
    P.op('dve', lambda e: e.tensor_copy(gates[:, 0:nblk, :], banks[5][:, 0:nblk * 2].rearrange("p (b g) -> p b g", g=2)),
         excl=[banks[5]], writes=[gates])

    Eb = [P.sb([128, 2, 256], BF16, f"Eb{i}") for i in range(2)]
    o_da = P.sb([128, 128], F32, "o_da")
    o_out = [P.sb([128, 128], BF16, f"o_out{i}") for i in range(2)]
    oT_sb = [P.sb([128, 128], BF16, f"oT_sb{i}") for i in range(2)]
    pT6 = banks[6].t.bitcast(BF16)
    pT7 = banks[7].t.bitcast(BF16)
    junk = P.sb([128, 128], F32, "junk")
    rz = P.sb([128, 2], F32, "rz")
    ss = P.sb([128, 1], F32, "ss")
    nout_box = [0]
    steps = [(qt, j) for qt in range(nq_tiles) for j in range(2 * qt + 2)]
    cvf = [P.sb([128, 1024], F32, f"cvf{i}") for i in range(2)]
    cvb = [P.sb([128, 1024], BF16, f"cvb{i}") for i in range(2)]
    NCV = 256
    cv_next = [0]

    def cv_src(n):
        tab = H['u'] if n < 128 else H['v']
        row0 = (n % 128) * 128
        return tab.ap()[row0:row0 + 128, :]

    def cv_step():
        n = cv_next[0]
        if n > NCV + 1:
            return
        cv_next[0] += 1
        if n < NCV:
            P.dma('pool', cvf[n % 2][:], cv_src(n), writes=[cvf[n % 2]])
        m = n - 1
        if 0 <= m < NCV:
            P.op('dve', lambda e, m=m: e.tensor_copy(cvb[m % 2][:], cvf[m % 2][:]), reads=[cvf[m % 2]], writes=[cvb[m % 2]])
            row0 = (m % 128) * 128
            tsel = 0 if m < 128 else 1
            P.dma('sp', H['uvb'].ap()[row0:row0 + 128, tsel * 1024:(tsel + 1) * 1024], cvb[m % 2][:], reads=[cvb[m % 2]],
                  writes=[Res("cvout")])

    def da_front(k):
        qt, j = steps[k]
        q0 = qt * 256
        nj = 2 * qt + 2
        pS = banks[k % 2]
        E = Eb[k % 2]
        qa, qb = (0, 256) if j < nj - 1 else (128, 256)
        for c in range(2):
            P.op('pe', lambda e, c=c, j=j, pS=pS, qa=qa, qb=qb, q0=q0: e.matmul(
                pS[:, c * 256 + qa:c * 256 + qb], dakT[c][:, j * 128:(j + 1) * 128], daqT[c][:, q0 + qa:q0 + qb],
                start=True, stop=True), reads=[dakT[c], daqT[c]], writes=[pS])
        P.op('act', lambda e, E=E, pS=pS, qa=qa, qb=qb: e.activation(
            E[:, :, qa:qb], pS[:, :].rearrange("p (c q) -> p c q", c=2)[:, :, qa:qb], AF.Exp, scale=0.125),
             excl=[pS], writes=[E])
        if j >= nj - 2:
            sm = 0 if j == nj - 2 else 1
            P.op('pool', lambda e, E=E, sm=sm: e.tensor_tensor(
                E[:, :, sm * 128:(sm + 1) * 128], E[:, :, sm * 128:(sm + 1) * 128],
                AP(trib, 0, [[128, 128], [0, 2], [1, 128]]), op=ALU.mult), reads=[E, trib], writes=[E])

    def da_back(k):
        qt, j = steps[k]
        q0 = qt * 256
        nj = 2 * qt + 2
        E = Eb[k % 2]
        subs = (0, 1) if j < nj - 1 else (1,)
        for s_ in subs:
            last_j = nj - 2 if s_ == 0 else nj - 1
            for c in range(2):
                pacc = banks[2 + s_ * 2 + c]
                P.op('pe', lambda e, E=E, s_=s_, c=c, j=j, pacc=pacc, last_j=last_j: e.matmul(
                    pacc[:, 0:129], E[:, c, s_ * 128:(s_ + 1) * 128], Vaug[:, j, :],
                    start=(j == 0), stop=(j == last_j)), reads=[E, Vaug], writes=[pacc])
        if j == nj - 1:
            nout = nout_box[0]
            for s_ in range(2):
                p0, p1 = banks[2 + s_ * 2], banks[2 + s_ * 2 + 1]
                oo = o_out[nout % 2]
                nout += 1
                nout_box[0] = nout
                P.op('dve', lambda e, p0=p0: e.reciprocal(rz[:, 0:1], p0[:, 128:129]), excl=[p0], writes=[rz])
                P.op('dve', lambda e, p1=p1: e.reciprocal(rz[:, 1:2], p1[:, 128:129]), excl=[p1], writes=[rz])
                P.op('dve', lambda e: e.tensor_tensor(rz[:, 1:2], rz[:, 1:2], neglam[:], op=ALU.mult), reads=[rz, neglam], writes=[rz])
                P.op('dve', lambda e, p0=p0: e.tensor_scalar(o_da[:], p0[:, 0:128], rz[:, 0:1], None, op0=ALU.mult),
                     reads=[rz], excl=[p0], writes=[o_da])
                P.op('dve', lambda e, p1=p1: e.scalar_tensor_tensor(out=o_da[:], in0=p1[:, 0:128], scalar=rz[:, 1:2], in1=o_da[:],
                                                                    op0=ALU.mult, op1=ALU.add), reads=[rz, o_da], excl=[p1], writes=[o_da])
                P.op('dve', lambda e: e.scalar_tensor_tensor(out=junk[:], in0=o_da[:], scalar=1.0, in1=o_da[:], op0=ALU.mult, op1=ALU.mult,
                                                             accum_out=ss[:]), reads=[o_da], writes=[junk, ss])
                P.op('dve', lambda e: e.tensor_scalar(ss[:], ss[:], 1.0 / 128.0, LN_EPS, op0=ALU.mult, op1=ALU.add), reads=[ss], writes=[ss])
                P.op('act', lambda e: e.activation(ss[:], ss[:], AF.Sqrt), reads=[ss], writes=[ss])
                P.op('dve', lambda e: e.reciprocal(ss[:], ss[:]), reads=[ss], writes=[ss])
                P.op('dve', lambda e, oo=oo: e.scalar_tensor_tensor(out=oo[:], in0=o_da[:], scalar=ss[:, 0:1], in1=gda[:], op0=ALU.mult, op1=ALU.mult),
                     reads=[o_da, ss, gda], writes=[oo])
                g_blk = (q0 + s_ * 128) // 128
                oT_ = oT_sb[nout % 2]
                P.op('pe', lambda e, oo=oo: e.transpose(bass.AP(pT6, 0, [[1024, 128], [1, 128]]), oo[:], ident[:]),
                     reads=[oo, ident], writes=[banks[6]])
                P.op('act', lambda e, oT_=oT_: e.copy(oT_[:], bass.AP(pT6, 0, [[1024, 128], [1, 128]])), excl=[banks[6]], writes=[oT_])
                jj, qq, ww = g_blk // 16, (g_blk % 16) // 4, g_blk % 4
                P.dma('sp', mixsrc_h[jj].ap()[(qq * 2) * 128:(qq * 2 + 1) * 128, ww * 128:(ww + 1) * 128], oT_[:], reads=[oT_], writes=[mixsrc_res])


    da_front(0)
    for k in range(len(steps)):
        if k + 1 < len(steps):
            da_front(k + 1)
        da_back(k)
        if k % 4 == 0:
            cv_step()
    while cv_next[0] <= NCV + 1:
        cv_step()

    nch = nblk
    lf = P.sb([128, 64], F32, "lf")
    bcs = P.sb([128, 64], F32, "bcs")
    ek = P.sb([128, 64], F32, "ek")
    eb = P.sb([128, 64], F32, "eb")
    ebL = P.sb([128, 64], F32, "ebL")
    nfb = P.sb([128, 1], F32, "nfb")
    P.op('dve', lambda e: e.tensor_scalar(nfb[:], spt[:, 11:12], -1.0, None, op0=ALU.mult), reads=[spt], writes=[nfb])
    P.op('act', lambda e: e.activation(lf[:, 0:nch], gates[:, 0:nch, 1], AF.Exp, bias=nfb[:, 0:1], scale=-1.0), reads=[gates, nfb], writes=[lf])
    P.op('dve', lambda e: e.tensor_scalar(lf[:, 0:nch], lf[:, 0:nch], 1.0, None, op0=ALU.add), reads=[lf], writes=[lf])
    P.op('act', lambda e: e.activation(lf[:, 0:nch], lf[:, 0:nch], AF.Ln), reads=[lf], writes=[lf])
    P.op('dve', lambda e: e.tensor_scalar(lf[:, 0:nch], lf[:, 0:nch], -1.0, None, op0=ALU.mult), reads=[lf], writes=[lf])
    P.op('pe', lambda e: e.matmul(banks[0][:, 0:nch], trif[:], lf[:, 0:nch], start=True, stop=True), reads=[trif, lf], writes=[banks[0]])
    P.op('pe', lambda e: e.matmul(banks[1][:, 0:nch], onesf[:], lf[:, 0:nch], start=True, stop=True), reads=[onesf, lf], writes=[banks[1]])
    P.op('dve', lambda e: e.tensor_copy(bcs[:, 0:nch], banks[0][:, 0:nch]), excl=[banks[0]], writes=[bcs])
    P.op('act', lambda e: e.activation(eb[:, 0:nch], bcs[:, 0:nch], AF.Exp), reads=[bcs], writes=[eb])
    P.op('act', lambda e: e.activation(ebL[:, 0:nch], banks[1][:, 0:nch], AF.Exp), excl=[banks[1]], writes=[ebL])
    P.op('dve', lambda e: e.tensor_tensor(ek[:, 0:nch], gates[:, 0:nch, 0], bcs[:, 0:nch], op=ALU.subtract), reads=[gates, bcs], writes=[ek])
    P.op('act', lambda e: e.activation(ek[:, 0:nch], ek[:, 0:nch], AF.Exp, bias=spt[:, 10:11], scale=1.0), reads=[ek, spt], writes=[ek])

    Dst = [P.sb([128, 129], F32, f"Dst{i}") for i in range(2)]
    Cb = [P.sb([128, 129], BF16, f"Cb{i}") for i in range(2)]
    ktok = [P.sb([128, 128], BF16, f"ktok{i}") for i in range(2)]
    vp = [P.sb([128, 129], BF16, f"vp{i}") for i in range(2)]
    ATb = [P.sb([128, 128], BF16, f"ATb{i}") for i in range(2)]
    hh = P.sb([128, 128], F32, "hh")
    hsm = P.sb([128, 4], F32, "hsm")
    st6 = P.sb([128, 6], F32, "st6")
    mvv = P.sb([128, 2], F32, "mvv")
    rstd = P.sb([128, 1], F32, "rstd")
    ho = [P.sb([128, 128], BF16, f"ho{i}") for i in range(2)]
    pKt = banks[2].t.bitcast(BF16)
    for c in range(nch):
        sl = slice(c * 128, (c + 1) * 128)
        kt, v_, at = ktok[c % 2], vp[c % 2], ATb[c % 2]
        P.op('pe', lambda e, sl=sl: e.transpose(bass.AP(pKt, 0, [[1024, 128], [1, 128]]), mlkT[:, sl], ident[:]),
             reads=[mlkT, ident], writes=[banks[2]])
        P.op('act', lambda e, kt=kt: e.copy(kt[:], bass.AP(pKt, 0, [[1024, 128], [1, 128]])), excl=[banks[2]], writes=[kt])
        P.op('pool', lambda e, c=c, v_=v_: e.tensor_scalar(v_[:], mlV[:, c, :], ek[:, c:c + 1], None, op0=ALU.mult),
             reads=[mlV, ek], writes=[v_])
        P.op('pe', lambda e, kt=kt, v_=v_: e.matmul(banks[3][:, 0:129], kt[:], v_[:], start=True, stop=True),
             reads=[kt, v_], writes=[banks[3]])
        P.op('pe', lambda e, sl=sl: e.matmul(banks[4][:, 0:128], mlkT[:, sl], mlqT[:, sl], start=True, stop=True),
             reads=[mlkT, mlqT], writes=[banks[4]])
        P.op('dve', lambda e, c=c, at=at: e.scalar_tensor_tensor(out=at[:], in0=banks[4][:, 0:128], scalar=ek[:, c:c + 1], in1=trif[:],
                                                                 op0=ALU.mult, op1=ALU.mult), reads=[ek, trif], excl=[banks[4]], writes=[at])
        pH = banks[5 + c % 2]
        if c > 0:
            P.op('pe', lambda e, sl=sl, c=c, pH=pH: e.matmul(pH[:, 0:129], mlqT[:, sl], Cb[(c - 1) % 2][:], start=True, stop=False),
                 reads=[mlqT, Cb[(c - 1) % 2]], writes=[pH])
        P.op('pe', lambda e, at=at, c=c, pH=pH: e.matmul(pH[:, 0:129], at[:], mlV[:, c, :], start=(c == 0), stop=True),
             reads=[at, mlV], writes=[pH])
        Dc = Dst[c % 2]
        if c == 0:
            P.op('dve', lambda e, Dc=Dc: e.tensor_copy(Dc[:], banks[3][:, 0:129]), excl=[banks[3]], writes=[Dc])
        else:
            Dp = Dst[(c - 1) % 2]
            P.op('dve', lambda e, Dc=Dc, Dp=Dp, c=c: e.scalar_tensor_tensor(out=Dc[:], in0=Dp[:], scalar=ebL[:, c - 1:c], in1=banks[3][:, 0:129],
                                                                          op0=ALU.mult, op1=ALU.add), reads=[Dp, ebL], excl=[banks[3]], writes=[Dc])
        P.op('act', lambda e, Dc=Dc, c=c: e.activation(Cb[c % 2][:], Dc[:], AF.Identity, scale=ebL[:, c:c + 1]), reads=[Dc, ebL], writes=[Cb[c % 2]])
        P.op('dve', lambda e, c=c, pH=pH: e.tensor_tensor(hsm[:, 0:1], pH[:, 128:129], eb[:, c:c + 1], op=ALU.mult), reads=[eb], excl=[pH], writes=[hsm])
        P.op('dve', lambda e: e.tensor_scalar(hsm[:, 1:2], hsm[:, 0:1], -1.0, 1.0, op0=ALU.mult, op1=ALU.max), reads=[hsm], writes=[hsm])
        P.op('dve', lambda e: e.tensor_scalar(hsm[:, 2:3], hsm[:, 0:1], 1.0, None, op0=ALU.max), reads=[hsm], writes=[hsm])
        P.op('dve', lambda e: e.tensor_tensor(hsm[:, 1:2], hsm[:, 1:2], hsm[:, 2:3], op=ALU.max), reads=[hsm], writes=[hsm])
        P.op('dve', lambda e: e.reciprocal(hsm[:, 2:3], hsm[:, 1:2]), reads=[hsm], writes=[hsm])
        P.op('dve', lambda e, c=c: e.tensor_tensor(hsm[:, 3:4], hsm[:, 2:3], eb[:, c:c + 1], op=ALU.mult), reads=[hsm, eb], writes=[hsm])
        P.op('dve', lambda e, pH=pH: e.tensor_scalar(hh[:], pH[:, 0:128], hsm[:, 3:4], None, op0=ALU.mult), reads=[hsm], excl=[pH], writes=[hh])
        P.op('dve', lambda e: e.bn_stats(st6[:], hh[:]), reads=[hh], writes=[st6])
        P.op('dve', lambda e: e.bn_aggr(mvv[:], st6[:]), reads=[st6], writes=[mvv])
        P.op('dve', lambda e: e.tensor_scalar(rstd[:], mvv[:, 1:2], LN_EPS, None, op0=ALU.add), reads=[mvv], writes=[rstd])
        P.op('act', lambda e: e.activation(rstd[:], rstd[:], AF.Sqrt), reads=[rstd], writes=[rstd])
        P.op('dve', lambda e: e.reciprocal(rstd[:], rstd[:]), reads=[rstd], writes=[rstd])
        P.op('dve', lambda e: e.tensor_scalar(hh[:], hh[:], mvv[:, 0:1], rstd[:, 0:1], op0=ALU.subtract, op1=ALU.mult),
             reads=[hh, mvv, rstd], writes=[hh])
        P.op('pool', lambda e: e.tensor_tensor(hh[:], hh[:], gml[:], op=ALU.mult), reads=[hh, gml], writes=[hh])
        hoo = ho[c % 2]
        P.op('pool', lambda e, c=c, hoo=hoo: e.tensor_tensor(hoo[:], hh[:], sigo[:, c, :], op=ALU.mult), reads=[hh, sigo], writes=[hoo])
        hT_ = oT_sb[c % 2]
        P.op('pe', lambda e, hoo=hoo: e.transpose(bass.AP(pT7, 0, [[1024, 128], [1, 128]]), hoo[:], ident[:]),
             reads=[hoo, ident], writes=[banks[7]])
        P.op('act', lambda e, hT_=hT_: e.copy(hT_[:], bass.AP(pT7, 0, [[1024, 128], [1, 128]])), excl=[banks[7]], writes=[hT_])
        jj, qq, ww = c // 16, (c % 16) // 4, c % 4
        P.dma('sp', mixsrc_h[jj].ap()[(qq * 2 + 1) * 128:(qq * 2 + 2) * 128, ww * 128:(ww + 1) * 128], hT_[:], reads=[hT_], writes=[mixsrc_res])


def phase_b(P, nc, C, H, xin_h, xin_res, mixg_h, mixg_res, midx_h, xout_h, xout_res, xsrc_h, xsrc_res, ntiles):
    ident, ones_bf, iota16, thr16 = C['ident'], C['ones_bf'], C['iota16'], C['thr16']
    D = 1024
    lnp = P.sb([128, 6 * D], F32, "lnp")
    P.dma('sp', lnp[:], bass.AP(H['lnp'], 0, [[0, 128], [1, 6 * D]]), writes=[lnp])

    r = P.sb([128, D], F32, "r")
    xt = P.sb([128, D], F32, "xt")
    x1 = P.sb([128, D], F32, "x1")
    x2 = P.sb([128, D], F32, "x2")
    stg = [r, xt, x1, x2]
    stg_i = [0]

    def load_w(handle, n, name, dst=None):
        wb = dst if dst is not None else P.sb([128, 8, n], BF16, name)
        wap = handle.ap().rearrange("(c p) n -> p c n", p=128)
        for kc in range(8):
            for pc in range(n // 1024):
                s = stg[stg_i[0] % 4]
                stg_i[0] += 1
                P.dma('sp', s[:, :], wap[:, kc, pc * 1024:(pc + 1) * 1024], writes=[s])
                if stg_i[0] % 2 == 0:
                    P.op('act', lambda e, s=s, kc=kc, pc=pc: e.copy(wb[:, kc, pc * 1024:(pc + 1) * 1024], s[:, :]), reads=[s], writes=[wb])
                else:
                    P.op('dve', lambda e, s=s, kc=kc, pc=pc: e.tensor_copy(wb[:, kc, pc * 1024:(pc + 1) * 1024], s[:, :]), reads=[s], writes=[wb])
        return wb

    wout_b = load_w(H['wout'], 1024, "wout")
    wq_b = load_w(H['wq'], 1024, "wq")
    wo_b = load_w(H['wo'], 1024, "wo")
    wpq_b = load_w(H['wpq'], 2048, "wpq")

    pA = P.ps([128, 2048], F32, "pA")
    pB = P.ps([128, 1024], F32, "pB")
    pC = P.ps([128, 512], F32, "pC")
    pT = P.ps([128, 8, 128], BF16, "pT")

    KT_b = P.sb([128, 8, 256], BF16, "KT")
    V_b = P.sb([128, 2, 1024], BF16, "V")
    skT_b = P.sb([128, 2, 128], BF16, "skT")
    NROW = 4
    gbuf = P.sb([128, 2 * NROW, 1024], F32, "gbuf")
    gb_bf = gbuf.t.bitcast(BF16)

    def wkv_ap(kc, c0, c1):
        return bass.AP(gb_bf, kc * 2048 + c0, [[2 * NROW * 1024 * 2, 128], [1, c1 - c0]])

    wkvap = H['wkv'].ap().rearrange("(c p) n -> p c n", p=128)
    for kc in range(8):
        for pc in range(2):
            s = stg[stg_i[0] % 4]
            stg_i[0] += 1
            P.dma('sp', s[:, :], wkvap[:, kc, pc * 1024:(pc + 1) * 1024], writes=[s])
            P.op('dve', lambda e, s=s, kc=kc, pc=pc: e.tensor_copy(wkv_ap(kc, pc * 1024, (pc + 1) * 1024), s[:, :]), reads=[s], writes=[gbuf])
    sc = P.sb([128, 16, 128], F32, "sc")
    w8bf = sc.t.bitcast(BF16)
    memT_b = sc

    def memT_ap(kc, m0, m1):
        return bass.AP(w8bf, kc * 256 + m0, [[4096, 128], [1, m1 - m0]])

    def pqT_ap(hc):
        return bass.AP(w8bf, hc * 128, [[4096, 128], [1, 128]])
    mT_ap = H['memT'].ap().rearrange("(c p) n -> p c n", p=128)
    for kc in range(8):
        s = stg[stg_i[0] % 4]
        stg_i[0] += 1
        P.dma('sp', s[:, 0:256], mT_ap[:, kc, :], writes=[s])
        P.op('dve', lambda e, s=s, kc=kc: e.tensor_copy(memT_ap(kc, 0, 256), s[:, 0:256]), reads=[s], writes=[memT_b])
    s = stg[stg_i[0] % 4]
    stg_i[0] += 1
    P.dma('sp', s[:, 0:256].rearrange("p (c n) -> p c n", c=2), H['skT'].ap().rearrange("c d n -> d c n"), writes=[s])
    P.op('dve', lambda e, s=s: e.tensor_copy(skT_b[:], s[:, 0:256].rearrange("p (c n) -> p c n", c=2)), reads=[s], writes=[skT_b])

    for j in range(8):
        for kc in range(8):
            P.op('pe', lambda e, j=j, kc=kc: e.matmul(pB[:, 0:256], wkv_ap(kc, j * 128, (j + 1) * 128), memT_ap(kc, 0, 256),
                                                      start=(kc == 0), stop=(kc == 7)),
                 reads=[gbuf, memT_b], writes=[pB])
        P.op('dve', lambda e, j=j: e.tensor_copy(KT_b[:, j, :], pB[:, 0:256]), excl=[pB], writes=[KT_b])
    for mc in range(2):
        for half in range(2):
            for kc in range(8):
                P.op('pe', lambda e, mc=mc, half=half, kc=kc: e.matmul(
                    pB[:, 0:512], memT_ap(kc, mc * 128, (mc + 1) * 128),
                    wkv_ap(kc, 1024 + half * 512, 1024 + (half + 1) * 512), start=(kc == 0), stop=(kc == 7)),
                     reads=[gbuf, memT_b], writes=[pB])
            P.op('dve', lambda e, mc=mc, half=half: e.tensor_copy(V_b[:, mc, half * 512:(half + 1) * 512], pB[:, 0:512]),
                 excl=[pB], writes=[V_b])

    mixb = P.sb([128, 8, 512], BF16, "mixb")
    xb = P.sb([128, D], BF16, "xb")
    xT = P.sb([128, 8, 128], BF16, "xT")
    qT = P.sb([128, 8, 128], BF16, "qT")
    E_b = P.sb([128, 8, 128], BF16, "E")
    rz = P.sb([128, 4, 128], F32, "rz")
    oT = qT
    top = P.sb([128, 16, 16], F32, "top")
    idxu = P.sb([128, 16, 16], U32, "idxu")
    idxf = P.sb([128, 16, 16], F32, "idxf")
    cs = P.sb([128, 8, 16], F32, "cs")
    ciu = P.sb([128, 8, 16], U32, "ciu")
    cif = P.sb([128, 128], F32, "cif")
    big = sc
    big2 = sc
    pqT = sc
    mixf = r
    x3 = r
    junk = r
    cand = sc
    k0f = P.sb([128, 128], F32, "k0f")
    k1f = P.sb([128, 128], F32, "k1f")
    i0s = P.sb([128, 128], F32, "i0s")
    i1s = P.sb([128, 128], F32, "i1s")
    ef = P.sb([128, 128], F32, "ef")
    eu = P.sb([128, 128], U32, "eu")
    gex = P.sb([128, 8, 16], F32, "gex")
    gz = P.sb([128, 8], F32, "gz")
    aact = P.sb([128, 128], F32, "aact")
    wgt = P.sb([128, 128], F32, "wgt")
    acc = x1
    st = P.sb([128, 12], F32, "st")
    mv = P.sb([128, 2], F32, "mv")
    rstd = P.sb([128, 1], F32, "rstd")
    grow = [Res(f"grow{i}") for i in range(16)]
    dgs = [P.sb([128, 128], BF16, f"dg{i}") for i in range(4)]
    gel = P.sb([128, 128], F32, "gel")
    acol = [Res() for _ in range(128)]
    gcol = [Res() for _ in range(128)]
    wcol = [Res() for _ in range(128)]

    def layernorm(src, dst, li):
        for c in range(2):
            P.op('dve', lambda e, c=c: e.bn_stats(st[:, c * 6:(c + 1) * 6], src[:, c * 512:(c + 1) * 512]),
                 reads=[src], writes=[st])
        P.op('dve', lambda e: e.bn_aggr(mv[:], st[:]), reads=[st], writes=[mv])
        P.op('dve', lambda e: e.tensor_scalar(rstd[:], mv[:, 1:2], LN_EPS, None, op0=ALU.add), reads=[mv], writes=[rstd])
        P.op('act', lambda e: e.activation(rstd[:], rstd[:], AF.Sqrt), reads=[rstd], writes=[rstd])
        P.op('dve', lambda e: e.reciprocal(rstd[:], rstd[:]), reads=[rstd], writes=[rstd])
        P.op('dve', lambda e: e.tensor_scalar(dst[:], src[:], mv[:, 0:1], rstd[:, 0:1], op0=ALU.subtract, op1=ALU.mult),
             reads=[src, mv, rstd], writes=[dst])
        P.op('dve', lambda e: e.tensor_tensor(dst[:], dst[:], lnp[:, (2 * li) * D:(2 * li + 1) * D], op=ALU.mult),
             reads=[dst, lnp], writes=[dst])
        P.op('dve', lambda e: e.tensor_tensor(dst[:], dst[:], lnp[:, (2 * li + 1) * D:(2 * li + 2) * D], op=ALU.add),
             reads=[dst, lnp], writes=[dst])

    def to_T(src):
        P.op('act', lambda e: e.copy(xb[:], src[:]), reads=[src], writes=[xb])
        for c in range(8):
            P.op('pe', lambda e, c=c: e.transpose(pT[:, c, :], xb[:, c * 128:(c + 1) * 128], ident[:]),
                 reads=[xb, ident], writes=[pT])
        P.op('dve', lambda e: e.tensor_copy(xT[:], pT[:]), excl=[pT], writes=[xT])

    def linear(lhsT_tile, w_b, n, pdst, off=0):
        for half in range(n // 512):
            for kc in range(8):
                P.op('pe', lambda e, half=half, kc=kc: e.matmul(
                    pdst[:, half * 512:(half + 1) * 512], lhsT_tile[:, kc, off:off + 128], w_b[:, kc, half * 512:(half + 1) * 512],
                    start=(kc == 0), stop=(kc == 7)), reads=[lhsT_tile, w_b], writes=[pdst])

    midx = P.sb([128, 32], U32, "midx")
    P.dma('sp', midx[:], midx_h.ap(), writes=[midx])

    def brow(rr):
        return bass.AP(gb_bf, rr * 1024, [[16384, 128], [1, 1024]])

    P.barrier()
    for i in range(ntiles):
        t0 = i * 128
        P.dma('sp', xt[:], xin_h.ap()[t0:t0 + 128, :], reads=[xin_res], writes=[xt])
        if i % 4 == 0:
            for kc in range(8):
                P.dma('pool', None, None, reads=[mixg_res, midx], writes=[mixb],
                      fn=lambda e, kc=kc, i=i: e.indirect_dma_start(
                          out=mixb[:, kc, :], out_offset=None, in_=mixg_h.ap(),
                          in_offset=bass.IndirectOffsetOnAxis(ap=midx[:, (i // 4) * 8 + kc:(i // 4) * 8 + kc + 1], axis=0)))
        linear(mixb, wout_b, 1024, pB, off=(i % 4) * 128)
        P.op('dve', lambda e: e.scalar_tensor_tensor(out=r[:], in0=xt[:], scalar=ALPHA, in1=pB[:], op0=ALU.mult, op1=ALU.add),
             reads=[xt], excl=[pB], writes=[r])
        layernorm(r, x1, 0)
        to_T(x1)
        for j in range(8):
            for kc in range(8):
                P.op('pe', lambda e, j=j, kc=kc: e.matmul(pA[:, j * 128:(j + 1) * 128], wq_b[:, kc, j * 128:(j + 1) * 128],
                                                          xT[:, kc, :], start=(kc == 0), stop=(kc == 7)),
                     reads=[wq_b, xT], writes=[pA])
        P.op('act', lambda e: e.copy(qT[:], pA[:, 0:1024].rearrange("p (j t) -> p j t", j=8)), excl=[pA], writes=[qT])
        for h in range(4):
            for mc in range(2):
                for dc in range(2):
                    P.op('pe', lambda e, h=h, mc=mc, dc=dc: e.matmul(
                        pB[:, (h * 2 + mc) * 128:(h * 2 + mc + 1) * 128],
                        KT_b[:, h * 2 + dc, mc * 128:(mc + 1) * 128], qT[:, h * 2 + dc, :],
                        start=(dc == 0), stop=(dc == 1)), reads=[KT_b, qT], writes=[pB])
        P.op('act', lambda e: e.activation(E_b[:], pB[:].rearrange("p (j t) -> p j t", j=8), AF.Exp, scale=1.0 / 16.0),
             excl=[pB], writes=[E_b])
        for h in range(4):
            for mc in range(2):
                P.op('pe', lambda e, h=h, mc=mc: e.matmul(pC[:, h * 128:(h + 1) * 128], ones_bf[:], E_b[:, h * 2 + mc, :],
                                                          start=(mc == 0), stop=(mc == 1)),
                     reads=[ones_bf, E_b], writes=[pC])
        P.op('dve', lambda e: e.reciprocal(rz[:], pC[:].rearrange("p (h t) -> p h t", h=4)), excl=[pC], writes=[rz])
        for h in range(4):
            for dc in range(2):
                for mc in range(2):
                    P.op('pe', lambda e, h=h, dc=dc, mc=mc: e.matmul(
                        pA[:, 1024 + (h * 2 + dc) * 128:1024 + (h * 2 + dc + 1) * 128],
                        V_b[:, mc, h * 256 + dc * 128:h * 256 + (dc + 1) * 128], E_b[:, h * 2 + mc, :],
                        start=(mc == 0), stop=(mc == 1)), reads=[V_b, E_b], writes=[pA])
        for j in range(8):
            P.op('dve', lambda e, j=j: e.tensor_tensor(oT[:, j, :], pA[:, 1024 + j * 128:1024 + (j + 1) * 128], rz[:, j // 2, :],
                                                       op=ALU.mult), reads=[rz], excl=[pA], writes=[oT])
        linear(oT, wo_b, 1024, pB)
        P.op('dve', lambda e: e.scalar_tensor_tensor(out=r[:], in0=x1[:], scalar=ALPHA, in1=pB[:], op0=ALU.mult, op1=ALU.add),
             reads=[x1], excl=[pB], writes=[r])
        layernorm(r, x2, 1)
        to_T(x2)
        for hc in range(16):
            for kc in range(8):
                P.op('pe', lambda e, hc=hc, kc=kc: e.matmul(pA[:, hc * 128:(hc + 1) * 128], wpq_b[:, kc, hc * 128:(hc + 1) * 128],
                                                            xT[:, kc, :], start=(kc == 0), stop=(kc == 7)),
                     reads=[wpq_b, xT], writes=[pA])
        P.op('act', lambda e: e.copy(bass.AP(w8bf, 0, [[4096, 128], [1, 2048]]), pA[:]), excl=[pA], writes=[pqT])
        for hc in range(16):
            P.op('pe', lambda e, hc=hc: e.matmul(pA[:, hc * 128:(hc + 1) * 128], pqT_ap(hc), skT_b[:, hc % 2, :],
                                                 start=True, stop=True), reads=[pqT, skT_b], writes=[pA])
        P.op('act', lambda e: e.copy(sc[:], pA[:].rearrange("p (j t) -> p j t", j=16)), excl=[pA], writes=[sc])
        for hc in range(16):
            P.op('dve', lambda e, hc=hc: e.max(top[:, hc, 0:8], sc[:, hc, :]), reads=[sc], writes=[top])
            P.op('dve', lambda e, hc=hc: e.max_index(idxu[:, hc, 0:8], top[:, hc, 0:8], sc[:, hc, :]), reads=[sc, top], writes=[idxu])
            P.op('dve', lambda e, hc=hc: e.match_replace(sc[:, hc, :], top[:, hc, 0:8], sc[:, hc, :], -1e30),
                 reads=[sc, top], writes=[sc])
            P.op('dve', lambda e, hc=hc: e.max(top[:, hc, 8:16], sc[:, hc, :]), reads=[sc], writes=[top])
            P.op('dve', lambda e, hc=hc: e.max_index(idxu[:, hc, 8:16], top[:, hc, 8:16], sc[:, hc, :]), reads=[sc, top], writes=[idxu])
        P.op('dve', lambda e: e.tensor_copy(idxf[:], idxu[:]), reads=[idxu], writes=[idxf])
        P.op('dve', lambda e: e.tensor_tensor(AP(cand, 0, [[2048, 128], [256, 8], [16, 16], [1, 16]]),
                                              AP(top, 0, [[256, 128], [32, 8], [1, 16], [0, 16]]),
                                              AP(top, 16, [[256, 128], [32, 8], [0, 16], [1, 16]]), op=ALU.add),
             reads=[top], writes=[cand])
        for h in range(8):
            P.op('dve', lambda e, h=h: e.max(cs[:, h, 0:8], sc.t.rearrange('p a b -> p (a b)')[:, h * 256:(h + 1) * 256]), reads=[cand], writes=[cs])
            P.op('dve', lambda e, h=h: e.max_index(ciu[:, h, 0:8], cs[:, h, 0:8], sc.t.rearrange('p a b -> p (a b)')[:, h * 256:(h + 1) * 256]), reads=[cand, cs], writes=[ciu])
            P.op('dve', lambda e, h=h: e.match_replace(sc.t.rearrange('p a b -> p (a b)')[:, h * 256:(h + 1) * 256], cs[:, h, 0:8], sc.t.rearrange('p a b -> p (a b)')[:, h * 256:(h + 1) * 256], -1e30),
                 reads=[cand, cs], writes=[cand])
            P.op('dve', lambda e, h=h: e.max(cs[:, h, 8:16], sc.t.rearrange('p a b -> p (a b)')[:, h * 256:(h + 1) * 256]), reads=[cand], writes=[cs])
            P.op('dve', lambda e, h=h: e.max_index(ciu[:, h, 8:16], cs[:, h, 8:16], sc.t.rearrange('p a b -> p (a b)')[:, h * 256:(h + 1) * 256]), reads=[cand, cs], writes=[ciu])
        P.op('dve', lambda e: e.tensor_copy(cif[:], ciu[:].rearrange("p h k -> p (h k)")), reads=[ciu], writes=[cif])
        P.op('dve', lambda e: e.tensor_tensor(AP(big, 0, [[2048, 128], [16, 128], [1, 16]]),
                                              AP(cif, 0, [[128, 128], [1, 128], [0, 16]]),
                                              AP(thr16, 0, [[16, 128], [0, 128], [1, 16]]), op=ALU.is_ge),
             reads=[cif, thr16], writes=[big])
        P.op('dve', lambda e: e.tensor_reduce(k0f[:], AP(big, 0, [[2048, 128], [16, 128], [1, 16]]), axis=AX.X, op=ALU.add),
             reads=[big], writes=[k0f])
        P.op('dve', lambda e: e.scalar_tensor_tensor(out=k1f[:], in0=k0f[:], scalar=-16.0, in1=cif[:], op0=ALU.mult, op1=ALU.add),
             reads=[k0f, cif], writes=[k1f])
        for (kf, c, dst) in ((k0f, 0, i0s), (k1f, 1, i1s)):
            P.op('dve', lambda e, kf=kf: e.tensor_tensor(AP(big, 0, [[2048, 128], [16, 128], [1, 16]]),
                                                         AP(kf, 0, [[128, 128], [1, 128], [0, 16]]),
                                                         AP(iota16, 0, [[16, 128], [0, 128], [1, 16]]), op=ALU.is_equal),
                 reads=[kf, iota16], writes=[big])
            P.op('dve', lambda e, c=c: e.tensor_tensor(AP(big2, 0, [[2048, 128], [256, 8], [16, 16], [1, 16]]),
                                                       AP(big, 0, [[2048, 128], [256, 8], [16, 16], [1, 16]]),
                                                       AP(idxf, c * 16, [[256, 128], [32, 8], [0, 16], [1, 16]]), op=ALU.mult),
                 reads=[big, idxf], writes=[big2])
            P.op('dve', lambda e, dst=dst: e.tensor_reduce(dst[:], AP(big2, 0, [[2048, 128], [16, 128], [1, 16]]), axis=AX.X, op=ALU.add),
                 reads=[big2], writes=[dst])
        P.op('dve', lambda e: e.scalar_tensor_tensor(out=ef[:], in0=i0s[:], scalar=128.0, in1=i1s[:], op0=ALU.mult, op1=ALU.add),
             reads=[i0s, i1s], writes=[ef])
        P.op('dve', lambda e: e.tensor_copy(eu[:], ef[:]), reads=[ef], writes=[eu])
        P.op('dve', lambda e: e.tensor_tensor(gex[:], cs[:], AP(cs, 0, [[128, 128], [16, 8], [0, 16]]), op=ALU.subtract),
             reads=[cs], writes=[gex])
        P.op('act', lambda e: e.activation(gex[:], gex[:], AF.Exp), reads=[gex], writes=[gex])
        P.op('dve', lambda e: e.tensor_reduce(gz[:], gex[:], axis=AX.X, op=ALU.add), reads=[gex], writes=[gz])
        P.op('dve', lambda e: e.reciprocal(gz[:], gz[:]), reads=[gz], writes=[gz])
        P.op('dve', lambda e: e.tensor_tensor(gex[:], gex[:], AP(gz, 0, [[8, 128], [1, 8], [0, 16]]), op=ALU.mult),
             reads=[gex, gz], writes=[gex])
        gexf = gex.t.rearrange("p h k -> p (h k)")

        def rowap(rr, off, n):
            if rr < 8:
                return bass.AP(gb_bf, rr * 2048 + off, [[16384, 128], [1, n]])
            if rr < 10:
                return bass.AP(w8bf, (rr - 8) * 2048 + off, [[4096, 128], [1, n]])
            return bass.AP((xt if rr == 10 else x1).t.bitcast(BF16), off, [[2048, 128], [1, n]])

        def rowdep(rr):
            if rr < 8:
                return [grow[rr]]
            return [grow[rr], sc if rr < 10 else (xt if rr == 10 else x1)]
        for s_ in range(128 + 3):
            if s_ < 128:
                hk, rr = s_, s_ % 12
                P.dma('pool', None, None, reads=[eu] + rowdep(rr)[1:], writes=[grow[rr]],
                      fn=lambda e, hk=hk, rr=rr: e.indirect_dma_start(
                          out=rowap(rr, 0, 2048), out_offset=None, in_=H['uvb'].ap(),
                          in_offset=bass.IndirectOffsetOnAxis(ap=eu[:, hk:hk + 1], axis=0)))
                P.op('dve', lambda e, hk=hk, rr=rr: e.scalar_tensor_tensor(
                    out=junk[:], in0=rowap(rr, 0, 1024), scalar=1.0, in1=x2[:],
                    op0=ALU.mult, op1=ALU.mult, accum_out=aact[:, hk:hk + 1]), reads=rowdep(rr) + [x2], writes=[junk, acol[hk]])
            if 0 <= s_ - 1 < 128:
                hk = s_ - 1
                P.op('act', lambda e, hk=hk: e.activation(gel[:, hk:hk + 1], aact[:, hk:hk + 1], AF.Gelu),
                     reads=[acol[hk]], writes=[gcol[hk]])
            if 0 <= s_ - 2 < 128:
                hk = s_ - 2
                P.op('dve', lambda e, hk=hk: e.tensor_tensor(wgt[:, hk:hk + 1], gel[:, hk:hk + 1], gexf[:, hk:hk + 1], op=ALU.mult),
                     reads=[gcol[hk], gex], writes=[wcol[hk]])
            if 0 <= s_ - 3 < 128:
                hk = s_ - 3
                rr = hk % 12
                dg = dgs[hk % 4]
                P.op('act', lambda e, hk=hk, dg=dg: e.activation(dg[:], ident[:], AF.Identity, scale=wgt[:, hk:hk + 1]),
                     reads=[ident, wcol[hk]], writes=[dg])
                for half in range(2):
                    P.op('pe', lambda e, hk=hk, rr=rr, dg=dg, half=half: e.matmul(
                        pB[:, half * 512:(half + 1) * 512], dg[:],
                        rowap(rr, 1024 + half * 512, 512),
                        start=(hk == 0), stop=(hk == 127)), reads=[dg] + rowdep(rr), writes=[pB])
        P.op('dve', lambda e: e.scalar_tensor_tensor(out=acc[:], in0=x2[:], scalar=ALPHA, in1=pB[:], op0=ALU.mult, op1=ALU.add),
             reads=[x2], excl=[pB], writes=[acc])
        layernorm(acc, x3, 2)
        P.dma('sp', xout_h.ap()[t0:t0 + 128, :], x3[:], reads=[x3], writes=[xout_res])
        if xsrc_h is not None:
            to_T(x3)
            for c4 in range(4):
                P.dma('sp', xsrc_h[c4].ap().rearrange("(k p) t -> p k t", p=128)[:, :, t0:t0 + 128], xT[:, 2 * c4:2 * c4 + 2, :],
                      reads=[xT], writes=[xsrc_res])

def build_fused(depth=4, nq_tiles=32, ntiles=16):
    nc = bass.Bass("TRN2", target_bir_lowering=False)
    D = 1024
    xT0_h = nc.dram_tensor("xT0", [D, S], F32, kind="ExternalInput")
    x0_h = nc.dram_tensor("x0", [NTOK, D], F32, kind="ExternalInput")
    memT_h = nc.dram_tensor("memT", [D, 256], F32, kind="ExternalInput")
    midx_h = nc.dram_tensor("midx", [128, 32], U32, kind="ExternalInput")
    uvb_h = nc.dram_tensor("uvb_scr", [16384, 2 * D], BF16)
    HA, HB = [], []
    for l in range(depth):
        HA.append({"wa": nc.dram_tensor(f"wa{l}", [D, NWA], F32, kind="ExternalInput"),
                   "sp": nc.dram_tensor(f"sp{l}", [128, 16], F32, kind="ExternalInput"),
                   "lamqk": nc.dram_tensor(f"lamqk{l}", [128, 256], F32, kind="ExternalInput"),
                   "gda": nc.dram_tensor(f"gda{l}", [128, 128], F32, kind="ExternalInput"),
                   "gml": nc.dram_tensor(f"gml{l}", [128, 128], F32, kind="ExternalInput")})
        HB.append({"wout": nc.dram_tensor(f"wout{l}", [D, D], F32, kind="ExternalInput"),
                   "wq": nc.dram_tensor(f"wq{l}", [D, D], F32, kind="ExternalInput"),
                   "wkv": nc.dram_tensor(f"wkv{l}", [D, 2 * D], F32, kind="ExternalInput"),
                   "wo": nc.dram_tensor(f"wo{l}", [D, D], F32, kind="ExternalInput"),
                   "wpq": nc.dram_tensor(f"wpq{l}", [D, 2 * D], F32, kind="ExternalInput"),
                   "skT": nc.dram_tensor(f"skT{l}", [2, 128, 128], F32, kind="ExternalInput"),
                   "lnp": nc.dram_tensor(f"lnp{l}", [6, D], F32, kind="ExternalInput"),
                   "u": nc.dram_tensor(f"u{l}", [16384, D], F32, kind="ExternalInput"),
                   "v": nc.dram_tensor(f"v{l}", [16384, D], F32, kind="ExternalInput"),
                   "memT": memT_h, "uvb": uvb_h})
        HA[l]["u"], HA[l]["v"], HA[l]["uvb"] = HB[l]["u"], HB[l]["v"], uvb_h
    out_h = nc.dram_tensor("out", [NTOK, D], F32, kind="ExternalOutput")
    mixsrc = [[nc.dram_tensor(f"mixsrc{l}_{c}", [1024, 512], BF16) for c in range(4)] for l in range(depth)]
    mixgc = [[nc.dram_tensor(f"mixgc{l}_{c}", [4096, 512], BF16) for c in range(4)] for l in range(depth)]
    mixg = [nc.dram_tensor(f"mixg{l}", [16384, 512], BF16) for l in range(depth)]
    xsrc = [[nc.dram_tensor(f"xsrc{l}_{c}", [256, NTOK], BF16) for c in range(4)] for l in range(depth - 1)]
    xg = [None] + [[nc.dram_tensor(f"xg{l}_{c}", [1024, NTOK], BF16) for c in range(4)] for l in range(1, depth)]
    xres = [None] + [nc.dram_tensor(f"xres{l}", [NTOK, D], F32) for l in range(1, depth)]
    GROUPS = [[0, 1, 2, 3], [4, 5, 6, 7]]

    P = Prog(nc, n_dma_sems={'sp': 8, 'act': 4, 'pool': 20})
    C = make_consts(P)
    out_res = Res("out")
    xg_res = [Res(f"xg{l}") for l in range(depth)]
    xres_res = [Res(f"xres{l}") for l in range(depth)]
    for l in range(depth):
        mixsrc_res, mixg_res, xsrc_res = Res("mixsrc"), Res("mixg"), Res("xsrc")
        with P.scope():
            phase_a(P, nc, C, HA[l], xT0_h, xg[l], xg_res[l], mixsrc[l], mixsrc_res, nq_tiles)
        for c in range(4):
            gc_res = Res("mixgc")
            P.cc("AllGather", GROUPS, mixsrc[l][c].ap(), mixgc[l][c].ap(), reads=[mixsrc_res], writes=[gc_res], inc=1)
            P.dma('sp', mixg[l].ap()[c * 4096:(c + 1) * 4096, :].rearrange("(a b) n -> a (b n)", a=128),
                  mixgc[l][c].ap().rearrange("(a b) n -> a (b n)", a=128), reads=[gc_res], writes=[mixg_res])
        last = (l == depth - 1)
        with P.scope():
            phase_b(P, nc, C, HB[l], x0_h if l == 0 else xres[l], xres_res[l], mixg[l], mixg_res, midx_h,
                    out_h if last else xres[l + 1], out_res if last else xres_res[l + 1],
                    None if last else xsrc[l], xsrc_res, ntiles)
        if not last:
            for c in range(4):
                P.cc("AllGather", GROUPS, xsrc[l][c].ap(), xg[l + 1][c].ap(), reads=[xsrc_res], writes=[xg_res[l + 1]], inc=1)
    P.finish([out_res])
    P.emit()
    P.close()
    return nc


DEPTH = 4
_NC_CACHE = {}


def _head_inputs(w_in_l, conv_w_l, conv_b_l, i_bias_l, f_bias_l, lam_qk_l, da_g_l, ml_g_l, l, h):
    r = np.arange(h * 128, (h + 1) * 128)
    cols = np.concatenate([r, 512 + r, 1024 + r, 1536 + r, 2048 + r, 2560 + r, 3072 + r, [3584 + h], [3588 + h]])
    wa = np.ascontiguousarray(w_in_l[:, cols])
    lam_init = 0.8 - 0.6 * math.exp(-0.3 * l)
    sp = np.zeros((128, 16), np.float32)
    for g in range(2):
        ch = g * 512 + r
        for j in range(4):
            sp[:, g * 4 + j] = conv_w_l[j, ch]
        sp[:, 8 + g] = conv_b_l[ch]
    sp[:, 10] = i_bias_l[h]
    sp[:, 11] = f_bias_l[h]
    sp[:, 12] = lam_init
    sp[:, 13] = 1.0 - lam_init
    lamqk = np.ascontiguousarray(np.broadcast_to(lam_qk_l.reshape(1, 256), (128, 256)))
    gda = np.ascontiguousarray(np.broadcast_to(da_g_l[None, :], (128, 128)))
    gml = np.ascontiguousarray(np.broadcast_to(ml_g_l[None, r], (128, 128)))
    return {f"wa{l}": wa, f"sp{l}": sp, f"lamqk{l}": lamqk, f"gda{l}": gda, f"gml{l}": gml}


def make_in_maps(depth, x, mem, w_in, i_bias, f_bias, conv_w, conv_b, lam_qk, da_norm_g, ml_norm_g, w_out,
                 ln1_g, ln1_b, wq_mem, wkv_mem, wo_mem, ln2_g, ln2_b, w_pq, sub_keys, u_tab, v_tab, ln3_g, ln3_b):
    f = lambda a: np.asarray(a, dtype=np.float32)
    x = f(x); mem = f(mem)
    B = x.shape[0]
    shared = {}
    for l in range(depth):
        shared[f"wout{l}"] = f(w_out[l]); shared[f"wq{l}"] = f(wq_mem[l]); shared[f"wkv{l}"] = f(wkv_mem[l])
        shared[f"wo{l}"] = f(wo_mem[l]); shared[f"wpq{l}"] = f(w_pq[l])
        shared[f"skT{l}"] = np.ascontiguousarray(f(sub_keys[l]).transpose(0, 2, 1))
        shared[f"lnp{l}"] = np.ascontiguousarray(np.stack([f(ln1_g[l]), f(ln1_b[l]), f(ln2_g[l]), f(ln2_b[l]), f(ln3_g[l]), f(ln3_b[l])]))
        shared[f"u{l}"] = f(u_tab[l]); shared[f"v{l}"] = f(v_tab[l])
    in_maps = []
    for b in range(B):
        xT = np.ascontiguousarray(x[b].T)
        memT = np.ascontiguousarray(mem[b].T)
        for r_ in range(4):
            m = dict(shared)
            m["xT0"] = xT
            m["x0"] = np.ascontiguousarray(x[b, r_ * 2048:(r_ + 1) * 2048])
            m["memT"] = memT
            p = np.arange(128, dtype=np.int64)[:, None, None]
            qr = np.arange(4, dtype=np.int64)[None, :, None]
            kc = np.arange(8, dtype=np.int64)[None, None, :]
            midx = ((((r_ * 4 + (kc % 4)) * 4 + qr) * 2 + (kc // 4)) * 128) + p
            m["midx"] = np.ascontiguousarray(midx.reshape(128, 32).astype(np.uint32))
            for l in range(depth):
                m.update(_head_inputs(f(w_in[l]), f(conv_w[l]), f(conv_b[l]), f(i_bias[l]), f(f_bias[l]), f(lam_qk[l]),
                                      f(da_norm_g[l]), f(ml_norm_g[l]), l, r_))
            in_maps.append(m)
    return in_maps


def kernel(x, mem, w_in, i_bias, f_bias, conv_w, conv_b, lam_qk, da_norm_g, ml_norm_g, w_out,
           ln1_g, ln1_b, wq_mem, wkv_mem, wo_mem, ln2_g, ln2_b, w_pq, sub_keys, u_tab, v_tab,
           ln3_g, ln3_b):
    if "f" not in _NC_CACHE:
        _NC_CACHE["f"] = build_fused(DEPTH)
    in_maps = make_in_maps(DEPTH, x, mem, w_in, i_bias, f_bias, conv_w, conv_b, lam_qk, da_norm_g, ml_norm_g, w_out,
                           ln1_g, ln1_b, wq_mem, wkv_mem, wo_mem, ln2_g, ln2_b, w_pq, sub_keys, u_tab, v_tab, ln3_g, ln3_b)
    res = run_bass_kernel_spmd(_NC_CACHE["f"], in_maps, core_ids=list(range(8))).results
    B, S_, D = np.asarray(x).shape
    out = np.empty((B, S_, D), np.float32)
    for b in range(B):
        for r_ in range(4):
            out[b, r_ * 2048:(r_ + 1) * 2048] = res[b * 4 + r_]["out"]
    return out
```
